# Optimizing a Trainium2 kernel written in Bass

```python
import math
import jax, jax.numpy as jnp
from jax import lax
import numpy as np

D_MODEL = 1024
BATCH = 8
SEQ = 2048
DEPTH = 4
DEC_BATCH = 128
DEC_SEQ = 8
PAST_LEN = 16384
PAGE_SIZE = 128

N_MIXERS = 2
N_HGRN = (DEPTH + 1) // 2
N_RWKV = DEPTH // 2
MIX_W = D_MODEL
HG_HEADS = 8
HG_DK = MIX_W // HG_HEADS
HG_DV = MIX_W // HG_HEADS
HG_CHUNK = 64
RW_N = 64
RW_HEADS = MIX_W // RW_N
RW_DECAY_LORA = 64
RW_ICL_LORA = 64
RW_VRES_LORA = 32
RW_LNX_EPS = 64e-5
MEM_LEN = 256
X_HEADS = 4
X_DH = 128
CROSS_W = X_HEADS * X_DH
BRANCH_W = MIX_W + CROSS_W
IN_COLS = 3 * MIX_W + CROSS_W + BRANCH_W
NORM_EPS = 1e-6
LOG_FLOOR = 1e-30

kernel_name = 'hgrn2_rwkv7_memxattn_decode_step'

F32 = jnp.float32


def rmsnorm(x, g):
    xf = x.astype(F32)
    xf = xf * lax.rsqrt(jnp.mean(xf * xf, axis=-1, keepdims=True) + NORM_EPS)
    return (xf * g.astype(F32)).astype(x.dtype)


def memory_kv(mem, g, w):
    kv = rmsnorm(mem, g) @ w
    k, v = jnp.split(kv, 2, axis=-1)
    B, M = mem.shape[0], mem.shape[1]
    return k.reshape(B, M, X_HEADS, X_DH), v.reshape(B, M, X_HEADS, X_DH)


def cross_attend(q, mk, mv):
    B, T = q.shape[0], q.shape[1]
    q = q.reshape(B, T, X_HEADS, X_DH)
    s = jnp.einsum('bthd,bmhd->bhtm', q, mk.astype(q.dtype)).astype(F32) * (X_DH ** -0.5)
    p = jax.nn.softmax(s, axis=-1).astype(q.dtype)
    o = jnp.einsum('bhtm,bmhd->bthd', p, mv.astype(q.dtype))
    return o.reshape(B, T, CROSS_W)


def hgrn2_scan(q, k, v, logf, s0):
    B, T, H, DK = q.shape
    C = math.gcd(T, HG_CHUNK)
    n = T // C

    def to_chunks(a):
        return a.reshape(B, n, C, H, a.shape[-1]).transpose(1, 0, 3, 2, 4)

    causal = jnp.tril(jnp.ones((C, C), dtype=bool))[:, :, None]

    def step(S, inp):
        qc, kc, vc, gc = inp
        b = jnp.cumsum(gc, axis=2)
        diff = b[:, :, :, None, :] - b[:, :, None, :, :]
        decay = jnp.where(causal, jnp.exp(jnp.where(causal, diff, 0.0)), 0.0)
        att = jnp.einsum('bhtsd,bhsd->bhts', qc[:, :, :, None, :] * decay, kc)
        o = jnp.einsum('bhts,bhse->bhte', att, vc) + jnp.einsum('bhtd,bhde->bhte', qc * jnp.exp(b), S)
        b_last = b[:, :, -1:, :]
        S = jnp.exp(b_last[:, :, 0, :])[..., None] * S + jnp.einsum('bhsd,bhse->bhde', kc * jnp.exp(b_last - b), vc)
        return S, o

    S, o = lax.scan(step, s0, (to_chunks(q), to_chunks(k), to_chunks(v), to_chunks(logf)))
    o = o.transpose(1, 0, 3, 2, 4).reshape(B, T, H, v.shape[-1])
    return o, S


def hgrn2_mixer(h, w_in_l, lb, onorm_g, s0):
    B, T, _ = h.shape
    proj = h @ w_in_l
    qz = proj[..., :MIX_W].astype(F32)
    fz = proj[..., MIX_W:2 * MIX_W].astype(F32)
    iv = proj[..., 2 * MIX_W:3 * MIX_W].astype(F32)
    rest = proj[..., 3 * MIX_W:]
    q = jax.nn.silu(qz)
    f = lb + (1.0 - lb) * jax.nn.sigmoid(fz)
    logf = jnp.log(jnp.maximum(f, LOG_FLOOR))
    k = (1.0 - lb) * jax.nn.sigmoid(-fz)
    hd = lambda t: t.reshape(B, T, HG_HEADS, -1)
    o, S = hgrn2_scan(hd(q), hd(k), hd(iv), hd(logf), s0.astype(F32))
    o = rmsnorm(o, onorm_g)
    return o.reshape(B, T, MIX_W).astype(h.dtype), rest, S.astype(s0.dtype)


def rwkv7_scan(r, w, k, v, a, b, s0):
    def step(S, inp):
        r_t, w_t, k_t, v_t, a_t, b_t = inp
        sa = jnp.einsum('bhij,bhj->bhi', S, a_t)
        S = S * w_t[:, :, None, :] + sa[..., None] * b_t[:, :, None, :] + v_t[..., None] * k_t[:, :, None, :]
        y = jnp.einsum('bhij,bhj->bhi', S, r_t)
        return S, y

    tm = lambda t: t.transpose(1, 0, 2, 3)
    S, y = lax.scan(step, s0, (tm(r), tm(w), tm(k), tm(v), tm(a), tm(b)))
    return tm(y), S


def rwkv7_mixer(h, prev, w_in_l, j, p, v_first, s0):
    B, T, D = h.shape
    h_prev = jnp.concatenate([prev[:, None, :].astype(h.dtype), h[:, :-1]], axis=1)
    dx = h_prev - h
    mu = p['rw_mu'][j].astype(h.dtype)
    xr = h + dx * mu[0]
    xw = h + dx * mu[1]
    xk = h + dx * mu[2]
    xv = h + dx * mu[3]
    xa = h + dx * mu[4]
    r = (xr @ w_in_l[:, :MIX_W]).astype(F32)
    k = (xk @ w_in_l[:, MIX_W:2 * MIX_W]).astype(F32)
    v = (xv @ w_in_l[:, 2 * MIX_W:3 * MIX_W]).astype(F32)
    rest = h @ w_in_l[:, 3 * MIX_W:]
    w = -jax.nn.softplus(-(p['rw_w0'][j] + jnp.tanh(xw @ p['rw_w1'][j]) @ p['rw_w2'][j]).astype(F32)) - 0.5
    decay = jnp.exp(-jnp.exp(w))
    if v_first is None:
        v_first = v
    else:
        m = j - 1
        v = v + (v_first - v) * jax.nn.sigmoid((p['rw_v0'][m] + (xv @ p['rw_v1'][m]) @ p['rw_v2'][m]).astype(F32))
    a = jax.nn.sigmoid((p['rw_a0'][j] + (xa @ p['rw_a1'][j]) @ p['rw_a2'][j]).astype(F32))
    hd = lambda t: t.reshape(B, T, RW_HEADS, RW_N)
    kk = hd(k * p['rw_kk'][j].astype(F32))
    kk = kk / jnp.maximum(jnp.sqrt(jnp.sum(kk * kk, axis=-1, keepdims=True)), 1e-12)
    k = k * (1.0 + (a - 1.0) * p['rw_ka'][j].astype(F32))
    rh, kh, vh = hd(r), hd(k), hd(v)
    y, S = rwkv7_scan(rh, hd(decay), kh, vh, -kk, kk * hd(a), s0.astype(F32))
    mean = jnp.mean(y, axis=-1, keepdims=True)
    var = jnp.mean(jnp.square(y - mean), axis=-1, keepdims=True)
    y = ((y - mean) * lax.rsqrt(var + RW_LNX_EPS)).reshape(B, T, D)
    y = y * p['rw_lnx_g'][j].astype(F32) + p['rw_lnx_b'][j].astype(F32)
    bonus = jnp.sum(rh * kh * p['rw_rk'][j].astype(F32), axis=-1, keepdims=True) * vh
    y = y + bonus.reshape(B, T, D)
    return y.astype(h.dtype), rest, S.astype(s0.dtype), h[:, -1], v_first


def trunk(x, mem_k, mem_v, s_hgrn, s_rwkv, s_shift, p):
    lbs = jax.nn.softmax(p['hg_lb'].astype(F32), axis=0)
    lbs = jnp.cumsum(lbs, axis=0) - lbs[0]
    new_h, new_r, new_s = [], [], []
    v_first = None
    for i in range(DEPTH):
        h = rmsnorm(x, p['norm_g'][i])
        j = i // N_MIXERS
        if i % N_MIXERS == 0:
            mix, rest, S = hgrn2_mixer(h, p['w_in'][i], lbs[j], p['hg_onorm_g'][j], s_hgrn[j])
            new_h.append(S)
        else:
            mix, rest, S, last, v_first = rwkv7_mixer(h, s_shift[j], p['w_in'][i], j, p, v_first, s_rwkv[j])
            new_r.append(S)
            new_s.append(last.astype(s_shift.dtype))
        xq = rest[..., :CROSS_W]
        gate = rest[..., CROSS_W:]
        xo = cross_attend(xq, mem_k[i], mem_v[i])
        branch = jnp.concatenate([mix, xo], axis=-1) * jax.nn.silu(gate)
        x = x + branch @ p['w_out'][i]
    y = rmsnorm(x, p['final_g'])
    return y, jnp.stack(new_h), jnp.stack(new_r), jnp.stack(new_s)


def setup_inputs(seed: int = 0) -> dict:
    key = jax.random.key(seed)
    ks = iter(jax.random.split(key, 40))
    nrm = lambda shape, scale: jax.random.normal(next(ks), shape, F32) * scale
    uni = lambda shape, lo, hi: jax.random.uniform(next(ks), shape, F32, lo, hi)
    return {
        'x_prompt': nrm((BATCH, SEQ, D_MODEL), 1.0),
        'x_sample': nrm((DEC_BATCH, DEC_SEQ, D_MODEL), 1.0),
        'mem_prompt': nrm((BATCH, MEM_LEN, D_MODEL), 1.0),
        'state_hgrn': nrm((N_HGRN, DEC_BATCH, HG_HEADS, HG_DK, HG_DV), 0.5),
        'state_rwkv': nrm((N_RWKV, DEC_BATCH, RW_HEADS, RW_N, RW_N), 0.3),
        'state_shift': nrm((N_RWKV, DEC_BATCH, D_MODEL), 1.0),
        'cache_mem_k': nrm((DEPTH, DEC_BATCH, MEM_LEN, X_HEADS, X_DH), 1.0),
        'cache_mem_v': nrm((DEPTH, DEC_BATCH, MEM_LEN, X_HEADS, X_DH), 1.0),
        'norm_g': 1.0 + nrm((DEPTH, D_MODEL), 0.02),
        'w_in': nrm((DEPTH, D_MODEL, IN_COLS), D_MODEL ** -0.5),
        'w_out': nrm((DEPTH, BRANCH_W, D_MODEL), BRANCH_W ** -0.5),
        'mem_norm_g': 1.0 + nrm((DEPTH, D_MODEL), 0.02),
        'w_mem_kv': nrm((DEPTH, D_MODEL, 2 * CROSS_W), D_MODEL ** -0.5),
        'hg_lb': nrm((N_HGRN, MIX_W), 0.1),
        'hg_onorm_g': 1.0 + nrm((N_HGRN, HG_DV), 0.02),
        'rw_mu': uni((N_RWKV, 5, D_MODEL), 0.0, 1.0),
        'rw_w0': uni((N_RWKV, D_MODEL), -4.0, 0.0),
        'rw_w1': nrm((N_RWKV, D_MODEL, RW_DECAY_LORA), D_MODEL ** -0.5),
        'rw_w2': nrm((N_RWKV, RW_DECAY_LORA, D_MODEL), 0.5 * RW_DECAY_LORA ** -0.5),
        'rw_a0': nrm((N_RWKV, D_MODEL), 0.1),
        'rw_a1': nrm((N_RWKV, D_MODEL, RW_ICL_LORA), D_MODEL ** -0.5),
        'rw_a2': nrm((N_RWKV, RW_ICL_LORA, D_MODEL), 0.5 * RW_ICL_LORA ** -0.5),
        'rw_v0': nrm((N_RWKV - 1, D_MODEL), 0.1),
        'rw_v1': nrm((N_RWKV - 1, D_MODEL, RW_VRES_LORA), D_MODEL ** -0.5),
        'rw_v2': nrm((N_RWKV - 1, RW_VRES_LORA, D_MODEL), 0.5 * RW_VRES_LORA ** -0.5),
        'rw_kk': 0.85 + nrm((N_RWKV, D_MODEL), 0.02),
        'rw_ka': 1.0 + nrm((N_RWKV, D_MODEL), 0.02),
        'rw_rk': nrm((N_RWKV, RW_HEADS, RW_N), 0.1),
        'rw_lnx_g': 1.0 + nrm((N_RWKV, D_MODEL), 0.02),
        'rw_lnx_b': nrm((N_RWKV, D_MODEL), 0.02),
        'final_g': 1.0 + nrm((D_MODEL,), 0.02),
    }


def reference(x_prompt, x_sample, mem_prompt, state_hgrn, state_rwkv, state_shift, cache_mem_k, cache_mem_v,
              norm_g, w_in, w_out, mem_norm_g, w_mem_kv, hg_lb, hg_onorm_g, rw_mu, rw_w0, rw_w1, rw_w2,
              rw_a0, rw_a1, rw_a2, rw_v0, rw_v1, rw_v2, rw_kk, rw_ka, rw_rk, rw_lnx_g, rw_lnx_b, final_g):
    p = dict(norm_g=norm_g, w_in=w_in, w_out=w_out, hg_lb=hg_lb, hg_onorm_g=hg_onorm_g, rw_mu=rw_mu,
             rw_w0=rw_w0, rw_w1=rw_w1, rw_w2=rw_w2, rw_a0=rw_a0, rw_a1=rw_a1, rw_a2=rw_a2,
             rw_v0=rw_v0, rw_v1=rw_v1, rw_v2=rw_v2, rw_kk=rw_kk, rw_ka=rw_ka, rw_rk=rw_rk,
             rw_lnx_g=rw_lnx_g, rw_lnx_b=rw_lnx_b, final_g=final_g)
    mk, mv = [], []
    for i in range(DEPTH):
        k_i, v_i = memory_kv(mem_prompt, mem_norm_g[i], w_mem_kv[i])
        mk.append(k_i)
        mv.append(v_i)
    mem_k_p = jnp.stack(mk)
    mem_v_p = jnp.stack(mv)
    B = x_prompt.shape[0]
    dt = x_prompt.dtype
    z_h = jnp.zeros((N_HGRN, B, HG_HEADS, HG_DK, HG_DV), dt)
    z_r = jnp.zeros((N_RWKV, B, RW_HEADS, RW_N, RW_N), dt)
    z_s = jnp.zeros((N_RWKV, B, D_MODEL), dt)
    y_prompt, sh_p, sr_p, ss_p = trunk(x_prompt, mem_k_p, mem_v_p, z_h, z_r, z_s, p)
    y_sample, sh_s, sr_s, ss_s = trunk(x_sample, cache_mem_k, cache_mem_v, state_hgrn, state_rwkv, state_shift, p)
    return (y_prompt, y_sample, sh_p, sr_p, ss_p, mem_k_p, mem_v_p, sh_s, sr_s, ss_s)
```

```python
import contextlib
import numpy as np
import ml_dtypes
import concourse.bass as bass
import concourse.mybir as mybir
from concourse.bass_utils import run_bass_kernel_spmd

F32 = mybir.dt.float32
F32R = mybir.dt.float32r
BF16 = mybir.dt.bfloat16
AF = mybir.ActivationFunctionType
ALU = mybir.AluOpType
AX = mybir.AxisListType

ENGS = ("pe", "act", "dve", "pool", "sp")
DMA_K = 6
SEM_WRAP = 30000
SAME_ENG_RAW_WAITS = True
PIPE = True


class Prog:
    def __init__(self, nc):
        self.nc = nc
        self.ops = []
        self.last_w = {}
        self.readers = {}
        self.excl = set()
        self.seg = None
        self.nseg = 0
        self.merges = []

    def begin_seg(self):
        self.nseg += 1
        self.seg = self.nseg
        return self.seg

    def end_seg(self):
        self.seg = None

    def op(self, eng, fn, reads=(), writes=(), dma=False):
        i = len(self.ops)
        ex = [k for k in reads if k in self.excl]
        if ex:
            writes = list(writes) + ex
        deps = {}
        for k in reads:
            w = self.last_w.get(k)
            if w is not None:
                deps[w] = True
        for k in writes:
            w = self.last_w.get(k)
            if w is not None:
                deps.setdefault(w, False)
            for r in self.readers.get(k, ()):
                deps.setdefault(r, False)
        o = dict(i=i, eng=eng, fn=fn, dma=dma, deps=deps, seg=self.seg)
        self.ops.append(o)
        for k in reads:
            self.readers.setdefault(k, []).append(i)
        for k in writes:
            self.last_w[k] = i
            self.readers[k] = []
        return i

    def barrier(self):
        lastc = {e: None for e in ENGS}
        lastd = {e: [] for e in ENGS}
        for o in self.ops:
            if o["dma"]:
                lastd[o["eng"]].append(o["i"])
            else:
                lastc[o["eng"]] = o["i"]
        extra = [v for v in lastc.values() if v is not None]
        for e in ENGS:
            extra += lastd[e][-DMA_K:]
        for e in ENGS:
            i = self.op(e, lambda eng: eng.nop(), reads=(), writes=())
            for x in extra:
                self.ops[i]["deps"][x] = True

    def _order(self):
        n = len(self.ops)
        order = []
        segops = {}
        for o in self.ops:
            if o["seg"] is not None:
                segops.setdefault(o["seg"], []).append(o["i"])
        partner = {x: y for x, y in self.merges}
        done_seg = set()
        i = 0
        while i < n:
            o = self.ops[i]
            sg = o["seg"]
            if sg is not None and sg in partner and sg not in done_seg:
                X = segops[sg]; Y = segops[partner[sg]]
                assert X[-1] + 1 == Y[0] and X == list(range(X[0], X[-1] + 1)) and Y == list(range(Y[0], Y[-1] + 1))
                emitted = set()
                xi = yi = 0
                xset = set(X)
                turn = 0
                while xi < len(X) or yi < len(Y):
                    pick_y = False
                    if yi < len(Y) and (turn == 1 or xi >= len(X)):
                        yo = self.ops[Y[yi]]
                        if all((d not in xset) or (d in emitted) for d in yo["deps"]):
                            pick_y = True
                    if pick_y:
                        order.append(Y[yi]); yi += 1
                    else:
                        order.append(X[xi]); emitted.add(X[xi]); xi += 1
                    turn ^= 1
                done_seg.add(sg); done_seg.add(partner[sg])
                i = Y[-1] + 1
                continue
            order.append(i)
            i += 1
        assert sorted(order) == list(range(n))
        return order

    def emit(self, final_wait_eng="sp"):
        nc = self.nc
        order = self._order()
        ops = [self.ops[i] for i in order]
        newidx = {o["i"]: k for k, o in enumerate(ops)}
        for k, o in enumerate(ops):
            o["deps"] = {newidx[d]: r for d, r in o["deps"].items()}
            assert all(d < k for d in o["deps"])
            o["i"] = k
        per_eng = {e: [] for e in ENGS}
        ndma = {e: 0 for e in ENGS}
        lastslot = {}
        for o in ops:
            e = o["eng"]
            o["pos"] = len(per_eng[e])
            per_eng[e].append(o["i"])
            if o["dma"]:
                nn = ndma[e]
                o["dslot"] = nn % DMA_K
                o["dval"] = 16 * (nn // DMA_K + 1)
                o["dprev"] = lastslot.get((e, o["dslot"]))
                lastslot[(e, o["dslot"])] = o["i"]
                ndma[e] = nn + 1
        known = {e: {f: -1 for f in ENGS} for e in ENGS}
        known_d = {e: {} for e in ENGS}
        milestone = set()
        for o in ops:
            e = o["eng"]
            wl = []
            deps = dict(o["deps"])
            if o["dma"] and o["dprev"] is not None:
                deps[o["dprev"]] = True
            for j, raw in deps.items():
                p = ops[j]
                f = p["eng"]
                if p["dma"]:
                    key = (f, p["dslot"])
                    if known_d[e].get(key, 0) < p["dval"]:
                        known_d[e][key] = p["dval"]
                        wl.append(("d", f, p["dslot"], p["dval"]))
                else:
                    if f == e and not o["dma"] and e != "pool":
                        if e == "pe" or not raw or not SAME_ENG_RAW_WAITS:
                            continue
                    if known[e][f] < p["pos"]:
                        wl.append(("c", f, p["pos"], j))
            best = {}
            out = []
            for w in wl:
                if w[0] == "c":
                    if w[1] not in best or best[w[1]][2] < w[2]:
                        best[w[1]] = w
                else:
                    out.append(w)
            for f, w in best.items():
                known[e][f] = w[2]
                milestone.add(w[3])
                out.append(w)
            o["waits"] = out
        cnt = {e: 0 for e in ENGS}
        for o in ops:
            if o["dma"]:
                continue
            if o["i"] in milestone:
                cnt[o["eng"]] += 1
                o["ms"] = cnt[o["eng"]]
            else:
                o["ms"] = None
        nsem = {e: max(1, (cnt[e] + SEM_WRAP - 1) // SEM_WRAP) for e in ENGS}
        nwaits = 0
        with contextlib.ExitStack() as st:
            csem = {e: [st.enter_context(nc.semaphore(f"c_{e}_{k}")) for k in range(nsem[e])] for e in ENGS}
            dsem = {e: [st.enter_context(nc.semaphore(f"d_{e}_{k}")) for k in range(DMA_K)]
                    for e in ENGS if ndma[e] > 0}
            block = st.enter_context(nc.Block())

            def mk(ename):
                def body(eng):
                    nonlocal nwaits
                    for i in per_eng[ename]:
                        o = ops[i]
                        for w in o["waits"]:
                            nwaits += 1
                            if w[0] == "c":
                                ms = ops[w[3]]["ms"]
                                k, v = (ms - 1) // SEM_WRAP, (ms - 1) % SEM_WRAP + 1
                                eng.wait_ge(csem[w[1]][k], v)
                            else:
                                eng.wait_ge(dsem[w[1]][w[2]], w[3])
                        ins = o["fn"](eng)
                        if o["dma"]:
                            ins.then_inc(dsem[ename][o["dslot"]], 16)
                        elif o["ms"] is not None:
                            k = (o["ms"] - 1) // SEM_WRAP
                            ins.then_inc(csem[ename][k], 1)
                    if ename == final_wait_eng:
                        for f in dsem:
                            last = {}
                            for j in per_eng[f]:
                                if ops[j]["dma"]:
                                    last[ops[j]["dslot"]] = ops[j]["dval"]
                            for s, v in last.items():
                                eng.wait_ge(dsem[f][s], v)
                return body

            block.tensor(mk("pe"))
            block.scalar(mk("act"))
            block.vector(mk("dve"))
            block.gpsimd(mk("pool"))
            block.sync(mk("sp"))
        return dict(n_ops=len(ops), per_eng={e: len(v) for e, v in per_eng.items()},
                    milestones=cnt, nwaits=nwaits)


class V:
    def __init__(self, ap, keys, r=False):
        self.ap = ap
        self.keys = tuple(keys)
        self.r = r

    def __getitem__(self, idx):
        return V(self.ap[idx], self.keys, self.r)

    def pat(self, pat, off=0):
        a = self.ap
        return V(bass.AP(a.tensor, a.offset + off, [list(a.ap[0])] + [list(p) for p in pat]), self.keys, self.r)

    def k(self, *keys):
        return V(self.ap, keys, self.r)

    def nr(self):
        return V(self.ap, self.keys, False)

    def rr(self):
        return V(self.ap, self.keys, True)


D = 1024
NKC = 8
SEQ = 2048
NPT = 16
NS = 16
TS = 8
MEM = 256
XH = 4
INC = 5120
CDEC = -float(np.exp(-0.5))
NORM_EPS = 1e-6
LNX_EPS = 64e-5
SC_ATT = 128 ** -0.5

CO = {}
_o = 0
for _n, _w in [("ident", 128), ("ones", 128),
               ("mb1", 512), ("sl1", 128), ("tic1", 128), ("tsc1", 128), ("boc1", 128),
               ("mb16", 512), ("sl16", 128), ("bo16", 128), ("tic16", 128), ("tsc16", 128), ("boc16", 128),
               ("iu2", 128), ("bo2", 128), ("sel16", 16), ("sel16c", 16), ("negc", 2), ("cm0", 1), ("cm1", 1),
               ("hw", 1), ("ha", 1)]:
    CO[_n] = (_o, _w)
    _o += _w
NCF = _o
CB = {"ident": (0, 128)}
NCB = 128


def make_consts():
    c = np.zeros((128, NCF), np.float32)
    s = np.arange(128)[:, None]
    t = np.arange(128)[None, :]
    def put(n, a):
        o, w = CO[n]
        c[:, o:o + w] = a
    put("ident", (s == t))
    put("ones", 1.0)
    for cfg, blk in ((1, 128), (16, 8), (2, 64)):
        same = (s // blk) == (t // blk)
        iu = same & (s <= t)
        su = same & (s < t)
        sl = same & (s > t)
        if cfg == 2:
            put("iu2", iu); put("bo2", same)
            continue
        put(f"mb{cfg}", np.concatenate([su, su, iu, iu], 1))
        put(f"sl{cfg}", sl)
        if cfg == 16:
            put("bo16", same)
        put(f"tic{cfg}", CDEC * iu); put(f"tsc{cfg}", CDEC * su); put(f"boc{cfg}", CDEC * same)
    sel = (np.arange(128)[:, None] // 8) == np.arange(16)[None, :]
    put("sel16", sel); put("sel16c", CDEC * sel); put("negc", CDEC)
    put("cm0", (np.arange(128) < 64)[:, None]); put("cm1", (np.arange(128) >= 64)[:, None])
    put("hw", (np.arange(128) < 64)[:, None]); put("ha", (np.arange(128) >= 64)[:, None])
    cb = np.zeros((128, NCB), np.float32)
    cb[:, 0:128] = np.eye(128)
    return c, cb.astype(ml_dtypes.bfloat16)


def build(depth=4, groups=None, dbg=False, memkv=True, phases=(1, 2, 3, 4), mks=9, levo=None):
    if groups is None:
        groups = [[0, 1, 2, 3], [4, 5, 6, 7], [8, 9, 10, 11], [12, 13, 14, 15], ["s"]]
    GMAX = max(len(g) for g in groups)
    NTM = GMAX * 128
    nc = bass.Bass("TRN2", target_bir_lowering=False)
    din = lambda n, s, d=F32: nc.dram_tensor(n, list(s), d, kind="ExternalInput").ap()
    dout = lambda n, s: nc.dram_tensor(n, list(s), F32, kind="ExternalOutput").ap()
    xp = din("xp", [SEQ, D]); xs = din("xs", [128, D]); mem = din("mem", [MEM, D])
    sth = din("sth", [2, NS, 8, 128, 128]); strw = din("strw", [2, NS, 16, 64, 64]); sts = din("sts", [2, NS, D])
    ck = din("ck", [4, NS, MEM, 512]); cv = din("cv", [4, NS, MEM, 512])
    norm_g = din("norm_g", [4, D]); w_in = din("w_in", [4, D, INC]); w_out = din("w_out", [4, 1536, D])
    mem_norm_g = din("mem_norm_g", [4, D]); w_mem_kv = din("w_mem_kv", [4, D, 1024])
    hg_lb = din("hg_lb", [2, D]); hg_og = din("hg_onorm_g", [2, 128])
    rw_mu = din("rw_mu", [2, 5, D]); rw_w0 = din("rw_w0", [2, D]); rw_w1 = din("rw_w1", [2, D, 64]); rw_w2 = din("rw_w2", [2, 64, D])
    rw_a0 = din("rw_a0", [2, D]); rw_a1 = din("rw_a1", [2, D, 64]); rw_a2 = din("rw_a2", [2, 64, D])
    rw_v0 = din("rw_v0", [1, D]); rw_v1 = din("rw_v1", [1, D, 32]); rw_v2 = din("rw_v2", [1, 32, D])
    rw_kk = din("rw_kk", [2, D]); rw_ka = din("rw_ka", [2, D]); rw_rk = din("rw_rk", [2, D])
    rw_lg = din("rw_lnx_g", [2, D]); rw_lb = din("rw_lnx_b", [2, D]); final_g = din("final_g", [1, D])
    cf_d = din("cf", [128, NCF]); cb_d = din("cb", [128, NCB], BF16)
    yp = dout("yp", [SEQ, D]); ys = dout("ys", [128, D])
    shp = dout("shp", [2, 8, 128, 128]); srp = dout("srp", [2, 16, 64, 64]); ssp = dout("ssp", [2, D])
    mkp = dout("mkp", [4, MEM, 512]); mvp = dout("mvp", [4, MEM, 512])
    shs = dout("shs", [2, NS, 8, 128, 128]); srs = dout("srs", [2, NS, 16, 64, 64]); sss = dout("sss", [2, NS, D])

    with contextlib.ExitStack() as st:
        P = Prog(nc)

        def sb(name, shape, dt=F32, r=False):
            t = st.enter_context(nc.sbuf_tensor(name, list(shape), dt))
            return V(t[:], [name], r and dt == F32)

        class Rot:
            def __init__(self, name, n, shape, dt=F32, r=False):
                self.bufs = [sb(f"{name}{i}", shape, dt, r) for i in range(n)]
                self.i = 0

            def get(self):
                b = self.bufs[self.i % len(self.bufs)]
                self.i += 1
                return b

        class PSR:
            def __init__(self):
                self.bufs = []
                for i in range(8):
                    t = st.enter_context(nc.psum_tensor(f"ps{i}", [128, 512], F32))
                    self.bufs.append(V(t[:], [f"ps{i}"]))
                    P.excl.add(f"ps{i}")
                self.i = 0

            def get(self):
                b = self.bufs[self.i % 8]
                self.i += 1
                return b

        PS = PSR()

        def bfv(v):
            return V(v.ap.bitcast(BF16), v.keys)

        def rk(*vs):
            out = []
            for v in vs:
                if isinstance(v, V):
                    out += list(v.keys)
            return out

        def a_(x):
            return x.ap if isinstance(x, V) else x

        def o_(x):
            return x.ap.bitcast(F32R) if x.r else x.ap

        def mm(o, l, r, start=True, stop=True):
            la, ra = l.ap, r.ap
            if l.r and r.r:
                la, ra = la.bitcast(F32R), ra.bitcast(F32R)
            P.op("pe", lambda e: e.matmul(o.ap, lhsT=la, rhs=ra, start=start, stop=stop), reads=rk(l, r), writes=rk(o))

        def tr(o, i, idn):
            P.op("pe", lambda e: e.transpose(out=o.ap, in_=i.ap, identity=idn.ap), reads=rk(i, idn), writes=rk(o))

        def act(o, i, f, bias=None, scale=None, acc=None):
            kw = {}
            if bias is not None:
                kw["bias"] = a_(bias)
            if scale is not None:
                kw["scale"] = a_(scale)
            if acc is not None:
                kw["accum_out"] = acc.ap
            oa = o_(o)
            P.op("act", lambda e: e.activation(out=oa, in_=i.ap, func=f, **kw), reads=rk(i, bias, scale), writes=rk(o, acc))

        def tt(o, a, b, op, eng="dve"):
            if eng == "pool" and o.r:
                eng = "dve"
            oa = o_(o)
            P.op(eng, lambda e: e.tensor_tensor(out=oa, in0=a.ap, in1=b.ap, op=op), reads=rk(a, b), writes=rk(o))

        def ts(o, a, s1, op0, s2=None, op1=None, eng="dve"):
            oa = o_(o)

            def f(e):
                if op1 is None:
                    return e.tensor_scalar(out=oa, in0=a.ap, scalar1=a_(s1), scalar2=None, op0=op0)
                return e.tensor_scalar(out=oa, in0=a.ap, scalar1=a_(s1), scalar2=a_(s2), op0=op0, op1=op1)
            P.op(eng, f, reads=rk(a, s1, s2), writes=rk(o))

        def stt(o, a, s, b, op0, op1):
            oa = o_(o)
            P.op("dve", lambda e: e.scalar_tensor_tensor(out=oa, in0=a.ap, scalar=a_(s), in1=b.ap, op0=op0, op1=op1),
                 reads=rk(a, s, b), writes=rk(o))

        def red(o, a, op, negate=False):
            P.op("dve", lambda e: e.tensor_reduce(out=o.ap, in_=a.ap, axis=AX.X, op=op, negate=negate), reads=rk(a), writes=rk(o))

        def recip(o, a):
            oa = o_(o)
            P.op("dve", lambda e: e.reciprocal(out=oa, in_=a.ap), reads=rk(a), writes=rk(o))

        def cp(o, i, eng="act"):
            if eng == "pool" and o.r:
                eng = "dve"
            oa = o_(o)
            if eng == "act":
                P.op("act", lambda e: e.copy(out=oa, in_=i.ap), reads=rk(i), writes=rk(o))
            else:
                P.op(eng, lambda e: e.tensor_copy(out=oa, in_=i.ap), reads=rk(i), writes=rk(o))

        def mset(o, val, eng="pool"):
            P.op(eng, lambda e: e.memset(o.ap, val), writes=rk(o))

        def dma(o, i, eng="sp", slow=False):
            oa = o.ap if isinstance(o, V) else o
            ia = i.ap if isinstance(i, V) else i
            kw = dict(allow_slow_non_contiguous=True) if slow else {}
            P.op(eng, lambda e: e.dma_start(out=oa, in_=ia, **kw), reads=rk(i), writes=rk(o), dma=True)

        MUL, ADD, SUB, MAX = ALU.mult, ALU.add, ALU.subtract, ALU.max

        cF = sb("cF", [128, NCF], r=True); cB = sb("cB", [128, NCB], BF16)
        dma(cB, cb_d)
        C = lambda n: cF[:, CO[n][0]:CO[n][0] + CO[n][1]]
        identF = C("ident"); identB = cB[:, 0:128]; onesF = C("ones")

        Sp = [sb(f"Sp{j}", [128, 8, 128], r=True) for j in range(2)]
        Mp = [sb(f"Mp{j}", [128, 8, 64], r=True) for j in range(2)]
        hcar = [sb(f"hcar{j}", [128, 8], BF16) for j in range(2)]
        for j in range(2):
            mset(Sp[j], 0.0); mset(Mp[j], 0.0)

        ARENA = 62 * 256
        arena_t = st.enter_context(nc.sbuf_tensor("arena", [128, ARENA], F32))
        arena = arena_t[:]
        aoff = {"P": 0, "S": 0}

        def carve(lay, name, shape, dt=F32):
            n = int(np.prod(shape[1:]))
            n32 = n if dt == F32 else (n + 1) // 2
            o = aoff[lay]
            aoff[lay] = o + n32
            assert aoff[lay] <= ARENA, (lay, name, aoff[lay])
            a = arena[:, o:o + n32]
            if dt != F32:
                a = a.bitcast(dt)
            pat = []
            stride = 1
            for d in reversed(shape[1:]):
                pat.insert(0, [stride, d])
                stride *= d
            return V(bass.AP(a.tensor, a.offset, [list(a.ap[0])] + pat), [name])

        LAY = {}
        for lay, ntm in (("P", GMAX * 128), ("S", 128)):
            Ld = dict(NTM=ntm)
            Ld["l1T"] = carve(lay, lay + "l1T", [128, 2, ntm])
            Ld["l1vT"] = carve(lay, lay + "l1vT", [128, ntm])
            if lay == "P":
                Ld["QZ"] = carve(lay, "QZ", [128, 256])
            Ld["xT"] = carve(lay, lay + "xT", [128, NKC, ntm])
            Ld["hT"] = carve(lay, lay + "hT", [128, NKC, 1 + ntm + 1], BF16)
            Ld["brT"] = carve(lay, lay + "brT", [128, 12, ntm], BF16)
            Ld["qx"] = [carve(lay, lay + f"qx{i}", [128, ntm], BF16) for i in range(2)]
            if lay == "P":
                Ld["mkT"] = carve(lay, "mkT", [128, 4, XH, MEM], BF16)
                Ld["mvb"] = carve(lay, "mvb", [128, 4, 2, 512], BF16)
            else:
                Ld["hpS"] = carve(lay, "hpS", [128, NKC, 128], BF16)
                Ld["Z1"] = carve(lay, "Z1", [128, 2048])
                Ld["Z1b"] = carve(lay, "Z1b", [128, 2048], BF16)
                Ld["Z2b"] = carve(lay, "Z2b", [128, 2048], BF16)
                Ld["M0b"] = carve(lay, "M0b", [128, 16, 64], BF16)
                o_save = aoff[lay]
                Ld["S0q"] = [carve(lay, f"S0q{i}", [128, 512]).k("salias") for i in range(4)]
                aoff[lay] = o_save
                Ld["M0all"] = carve(lay, "M0all", [128, 16, 64]).k("salias")
                Ld["SBD"] = [carve(lay, f"SBD{i}", [128, 512]).k("salias") for i in range(2)]
                Ld["MBD"] = [carve(lay, f"MBD{i}", [128, 512]) for i in range(2)]
                Ld["KVB"] = [carve(lay, f"KVB{i}", [128, 2048], BF16) for i in range(3)]
                Ld["QXM"] = carve(lay, "QXM", [128, 2048], BF16)
            LAY[lay] = Ld
        print("arena use (KiB):", {k: v / 256 for k, v in aoff.items()})

        T128 = Rot("t128_", 17, [128, 128], r=True)
        T256 = Rot("t256_", 2, [128, 256], r=True)
        T512 = Rot("t512_", 4, [128, 512], r=True)
        T1K = Rot("t1k_", 2, [128, 1024], r=True)
        TB256 = Rot("tb256_", 2, [128, 256], BF16)
        TB512 = Rot("tb512_", 2, [128, 512], BF16)
        TS8 = Rot("ts8_", 10, [128, 24])
        TS8s = TS8
        OG = sb("OG", [128, 128]); LBC = sb("LBC", [128, 24])
        BIGV = [sb("bigv0", [128, D], r=True), sb("bigv1", [128, D], r=True)]
        MUC = sb("MUC", [128, 80])
        L1a = sb("L1a", [128, NKC, 128], BF16); L1b = sb("L1b", [128, NKC, 128], BF16)
        L1va = sb("L1va", [128, NKC, 128], BF16); L1vb = sb("L1vb", [128, NKC, 128], BF16)
        PVB = sb("PVB", [128, 8, 128])
        ZP = [sb(f"zp{i}", [128, 256], BF16) for i in range(8)]
        PMBD = sb("PMBD", [128, 128])
        for zb in ZP + [PMBD]:
            mset(zb, 0.0)
        WM = Rot("wm_", 2, [128, NKC, 384], BF16)
        WM2 = Rot("wm2_", 2, [128, NKC, 384], BF16)
        WR = Rot("wr_", 2, [128, NKC, 512], BF16)
        WO = Rot("wo_", 1, [128, 12, 128], BF16)
        XIN = Rot("xin_", 1, [128, D])
        NRV = Rot("nrv_", 2, [128, 128], BF16)
        BT128 = Rot("b128_", 8, [128, 128], BF16)
        BT256 = Rot("b256_", 3, [128, 256], BF16)
        BT512 = Rot("b512_", 5, [128, 512], BF16)
        BT1K = Rot("b1k_", 2, [128, 1024], BF16)
        MK = [sb(f"mk{i}", [128, 512], BF16) for i in range(4)]
        MpB = [sb(f"MpB{j}", [128, 8, 64], BF16) for j in range(2)]
        for j in range(2):
            mset(MpB[j], 0.0)
        vfd = nc.dram_tensor("vfd", [17 * 128, D], BF16, kind="Internal").ap()

        for c0 in range(0, NCF, 1024):
            n = min(1024, NCF - c0)
            xin = XIN.get()
            dma(xin[:, 0:n], cf_d[:, c0:c0 + n])
            cp(cF[:, c0:c0 + n], xin[:, 0:n])

        def wview(w2d, c0, ncols):
            return w2d[:, c0:c0 + ncols].rearrange("(kc p) c -> p kc c", p=128)

        def xkey(li):
            return ("xT", li)

        def hkey(li):
            return ("hT", li)

        def mem_kv():
            LP = LAY["P"]
            mkT, mvb = LP["mkT"], LP["mvb"]
            memnT = V(LP["xT"].ap[:, :, 0:MEM], [xkey(0), xkey(1)])
            hmT = V(LP["hT"].ap[:, :, 1:1 + MEM], [hkey(0), hkey(1)])
            for mc in range(2):
                xin = XIN.get()
                dma(xin, mem[mc * 128:(mc + 1) * 128, :])
                ss = TS8.get()
                junk = T1K.get()
                act(junk, xin, AF.Square, acc=ss[:, 0:1])
                ts(ss[:, 1:2], ss[:, 0:1], 1.0 / D, MUL, NORM_EPS, ADD)
                act(ss[:, 2:3], ss[:, 1:2], AF.Ln)
                act(ss[:, 3:4], ss[:, 2:3], AF.Exp, scale=-0.5)
                xn = T1K.get()
                ts(xn, xin, ss[:, 3:4], MUL)
                for half in range(2):
                    ps = PS.get()
                    for q in range(4):
                        kc = half * 4 + q
                        tr(ps[:, q * 128:(q + 1) * 128], xn[:, kc * 128:(kc + 1) * 128], identF)
                    cp(memnT[:, half * 4:(half + 1) * 4, mc * 128:(mc + 1) * 128], ps.pat([[128, 4], [1, 128]]))
            for l in range(depth):
                if mks < 1:
                    break
                gcol = TS8.get()[:, 0:8]
                dma(gcol, mem_norm_g[l].rearrange("(kc p) -> p kc", p=128), slow=True)
                for kc in range(NKC):
                    ts(hmT[:, kc, :], memnT[:, kc, :], gcol[:, kc:kc + 1], MUL)
                if mks < 2:
                    continue
                for cb in range(2):
                    wb = WR.get()
                    dma(wb, wview(w_mem_kv[l], cb * 512, 512), eng="pool")
                    if mks < 3:
                        continue
                    if cb == 0:
                        for hx in range(XH):
                            ps = PS.get()
                            for kc in range(NKC):
                                mm(ps[:, 0:256], wb[:, kc, hx * 128:(hx + 1) * 128], hmT[:, kc, :], start=kc == 0, stop=kc == NKC - 1)
                            cp(mkT[:, l, hx, :], ps[:, 0:256])
                    if mks < 4:
                        continue
                    for mc in range(2):
                        ps = PS.get()
                        for kc in range(NKC):
                            mm(ps, hmT[:, kc, mc * 128:(mc + 1) * 128], wb[:, kc, :], start=kc == 0, stop=kc == NKC - 1)
                        stg = T512.get()
                        cp(stg, ps)
                        if mks >= 6:
                            dma((mkp if cb == 0 else mvp)[l, mc * 128:(mc + 1) * 128, :], stg)
                        if cb == 1 and mks >= 7:
                            cp(mvb[:, l, mc, :], ps, eng="dve")

        if memkv:
            mem_kv()
        mset(LAY["P"]["QZ"], 0.0)

        def process_group(gi, grp):
            ptiles = [t for t in grp if t != "s"]
            has_s = "s" in grp
            assert not (has_s and ptiles)
            L = LAY["S" if has_s else "P"]
            NTM = L["NTM"]; HS = NTM + 2
            xT, hT, brT, l1T, l1vT = L["xT"], L["hT"], L["brT"], L["l1T"], L["l1vT"]
            if has_s:
                hpS = L["hpS"]; Z1 = L["Z1"]; Z1b = L["Z1b"]; Z2b = L["Z2b"]; QXM = L["QXM"]
                P.barrier()
                for zb in [Z1, Z1b, Z2b, QXM] + L["SBD"] + L["MBD"]:
                    mset(zb, 0.0)
                QM = Z1
            else:
                mkT, mvb, QZ = L["mkT"], L["mvb"], L["QZ"]
            qxi = [0]
            npt = len(ptiles)
            first_group = bool(ptiles) and ptiles[0] == 0
            pchunks = []
            c = 0
            while c < npt * 128:
                n = min(512, npt * 128 - c)
                pchunks.append((c, n))
                c += n
            scol = npt * 128
            chunks = pchunks + ([(scol, 128)] if has_s else [])

            def tkeys(c0, n, fn):
                return tuple(fn(li) for li in range(c0 // 128, (c0 + n + 127) // 128))

            def xTv(kc, c0, n):
                return V(xT.ap[:, kc, c0:c0 + n], tkeys(c0, n, xkey))

            def xTt(li):
                return V(xT.ap[:, :, li * 128:(li + 1) * 128], [xkey(li)])

            def hTv(kc, c0, n):
                return V(hT.ap[:, kc, 1 + c0:1 + c0 + n], tkeys(c0, n, hkey))

            def hprevv(kc, c0, n):
                if has_s and c0 == scol:
                    return hpS[:, kc, :]
                ks = tkeys(c0, n, hkey) + ((hkey(c0 // 128 - 1),) if c0 > 0 else (("hT", "c0"),))
                return V(hT.ap[:, kc, c0:c0 + n], ks)

            def brv(kc, c0, n):
                return V(brT.ap[:, kc, c0:c0 + n], tkeys(c0, n, lambda li: ("br", kc, li)))

            for li, t in enumerate(grp):
                xin = XIN.get()
                dma(xin, xs if t == "s" else xp[t * 128:(t + 1) * 128, :])
                for half in range(2):
                    ps = PS.get()
                    for q in range(4):
                        kc = half * 4 + q
                        tr(ps[:, q * 128:(q + 1) * 128], xin[:, kc * 128:(kc + 1) * 128], identF)
                    cp(V(xT.ap[:, half * 4:(half + 1) * 4, li * 128:(li + 1) * 128], [xkey(li)]), ps.pat([[128, 4], [1, 128]]))

            def rms_tile(li):
                sq = T1K.get()
                act(sq.pat([[128, 8], [1, 128]]), xTt(li), AF.Square)
                ps = PS.get()
                for kc in range(NKC):
                    mm(ps[:, 0:128], onesF, sq[:, kc * 128:(kc + 1) * 128], start=kc == 0, stop=kc == NKC - 1)
                rs = T128.get()
                act(rs, ps[:, 0:128], AF.Ln, bias=NORM_EPS, scale=1.0 / D)
                rstd = T128.get()
                act(rstd, rs, AF.Exp, scale=-0.5)
                return rstd

            def hgrn_layer(l, j):
                og = OG
                dma(og, hg_og[j:j + 1, :].partition_broadcast(128))
                lbT = LBC[:, 0:8]; omT = LBC[:, 8:16]; nomT = LBC[:, 16:24]
                nom_bc = BIGV[0]
                if j == 0:
                    mset(omT, 1.0); mset(nomT, -1.0); mset(nom_bc, -1.0)
                else:
                    a0 = TS8.get()[:, 0:8]; a1 = TS8.get()[:, 0:8]
                    dma(a0, hg_lb[0].rearrange("(h p) -> p h", p=128), slow=True)
                    dma(a1, hg_lb[1].rearrange("(h p) -> p h", p=128), slow=True)
                    tt(a1, a1, a0, SUB)
                    act(lbT, a1, AF.Sigmoid)
                    ts(omT, lbT, -1.0, MUL, 1.0, ADD)
                    ts(nomT, lbT, 1.0, SUB)
                    b0 = XIN.get()
                    dma(b0, hg_lb[1:2, :].partition_broadcast(128))
                    cp(nom_bc, b0)
                    b0 = XIN.get()
                    dma(b0, hg_lb[0:1, :].partition_broadcast(128))
                    tt(nom_bc, nom_bc, b0, SUB)
                    act(nom_bc, nom_bc, AF.Sigmoid)
                    ts(nom_bc, nom_bc, 1.0, SUB)
                for h in range(8):
                    wblk = WM.get()
                    for q in range(3):
                        dma(wblk[:, :, q * 128:(q + 1) * 128], wview(w_in[l], q * 1024 + h * 128, 128), eng="pool")
                    Sph = V(Sp[j].ap[:, h, :], [("Sp", j, h)])
                    if has_s:
                        S0q = L["S0q"]
                        for q in range(4):
                            dma(S0q[q].pat([[128, 4], [1, 128]]), sth[j, 4 * q:4 * q + 4, h].rearrange("s k v -> k s v"))
                    hsegB = None
                    for li, t in enumerate(grp):
                        samp = (t == "s")
                        c0 = li * 128
                        hsegA = P.begin_seg()
                        if hsegB is not None and PIPE:
                            P.merges.append((hsegB, hsegA))
                        IU = C("mb16")[:, 256:384] if samp else C("iu2")
                        BO = C("bo16") if samp else C("bo2")
                        psq = PS.get()
                        for kc in range(NKC):
                            mm(psq[:, 0:128], wblk[:, kc, 0:128], hTv(kc, c0, 128), start=kc == 0, stop=kc == NKC - 1)
                        for kc in range(NKC):
                            mm(psq[:, 128:256], wblk[:, kc, 128:256], hTv(kc, c0, 128), start=kc == 0, stop=kc == NKC - 1)
                        psfv = PS.get()
                        for kc in range(NKC):
                            mm(psfv[:, 0:256], hTv(kc, c0, 128), wblk[:, kc, 128:384], start=kc == 0, stop=kc == NKC - 1)
                        qsg = T128.get(); act(qsg, psq[:, 0:128], AF.Sigmoid)
                        qs = T128.get(); tt(qs, qsg, psq[:, 0:128], MUL)
                        sgT = T128.get(); act(sgT, psq[:, 128:256], AF.Sigmoid)
                        kT = T128.get(); ts(kT, sgT, nomT[:, h:h + 1], MUL, omT[:, h:h + 1], ADD)
                        sg = T128.get(); act(sg, psfv[:, 0:128], AF.Sigmoid)
                        kt = T128.get(); stt(kt, sg, -1.0, nom_bc[:, h * 128:(h + 1) * 128], ADD, MUL)
                        g = T128.get(); act(g, kt, AF.Ln, bias=1.0, scale=-1.0)
                        vt = T128.get(); cp(vt, psfv[:, 128:256])
                        ps3 = PS.get()
                        mm(ps3[:, 0:128], IU, g)
                        mm(ps3[:, 128:256], g, IU)
                        mm(ps3[:, 256:384], BO, g)
                        eb = T128.get(); act(eb, ps3[:, 128:256], AF.Exp)
                        enb = T128.get(); act(enb, ps3[:, 128:256], AF.Exp, scale=-1.0)
                        qtT = T128.get(); tt(qtT, qs, eb, MUL)
                        ktT = T128.get(); tt(ktT, kT, enb, MUL)
                        bsb = T128.get(); cp(bsb, ps3[:, 0:128])
                        dd = T128.get(); tt(dd, ps3[:, 256:384], bsb, SUB)
                        ed = T128.get(); act(ed, dd, AF.Exp)
                        psA = PS.get()
                        mm(psA[:, 0:128], ktT, qtT)
                        att = T128.get(); tt(att, psA[:, 0:128], IU, MUL)
                        P.end_seg()
                        hsegB = P.begin_seg()
                        psO = PS.get()
                        mm(psO[:, 0:128], att, vt, start=True, stop=False)
                        if not samp:
                            tt(QZ.pat([[192, 2], [1, 64]]), qs.pat([[64, 2], [1, 64]]), eb.pat([[64, 2], [1, 64]]), MUL)
                            kh0 = T128.get(); stt(kh0, kt, C("cm0"), ed, MUL, MUL)
                            kh1 = T128.get(); stt(kh1, kt, C("cm1"), ed, MUL, MUL)
                            psS = PS.get()
                            mm(psS[:, 0:128], kh0, vt)
                            mm(psS[:, 128:256], kh1, vt)
                            mm(psO[:, 0:128], QZ[:, 0:128], Sph, start=False, stop=False)
                            S1 = T128.get(); stt(S1, Sph, eb[:, 63:64], psS[:, 0:128], MUL, ADD)
                            mm(psO[:, 0:128], QZ[:, 128:256], S1, start=False, stop=True)
                            stt(Sph, S1, eb[:, 127:128], psS[:, 128:256], MUL, ADD)
                        else:
                            tt(QM.pat([[136, 16], [1, 8]]), qs.pat([[8, 16], [1, 8]]), eb.pat([[8, 16], [1, 8]]), MUL)
                            for s in range(NS):
                                mm(psO[:, 0:128], QM[:, s * 128:(s + 1) * 128], S0q[s // 4][:, (s % 4) * 128:(s % 4 + 1) * 128], start=False, stop=(s == NS - 1))
                            kh = T128.get(); tt(kh, kt, ed, MUL)
                            for q in range(4):
                                vm = T512.get()
                                tt(vm.pat([[128, 4], [1, 128]]), vt.pat([[0, 4], [1, 128]]), C("sel16").pat([[1, 4], [0, 128]], off=4 * q), MUL)
                                psW = PS.get()
                                mm(psW, kh, vm)
                                tmp = T512.get()
                                tt(tmp.pat([[128, 4], [1, 128]]), S0q[q].pat([[128, 4], [1, 128]]),
                                   eb.pat([[8, 4], [0, 128]], off=7 + 32 * q), MUL)
                                Sn = T512.get()
                                tt(Sn, tmp, psW, ADD)
                                dma(shs[j, 4 * q:4 * q + 4, h].rearrange("s k v -> k s v"), Sn.pat([[128, 4], [1, 128]]))
                        ss = TS8.get()
                        junk = T128.get()
                        act(junk, psO[:, 0:128], AF.Square, acc=ss[:, 0:1])
                        ts(ss[:, 1:2], ss[:, 0:1], 1.0 / 128, MUL, NORM_EPS, ADD)
                        act(ss[:, 2:3], ss[:, 1:2], AF.Ln)
                        act(ss[:, 3:4], ss[:, 2:3], AF.Exp, scale=-0.5)
                        on = T128.get(); stt(on, psO[:, 0:128], ss[:, 3:4], og, MUL, MUL)
                        psT = PS.get()
                        tr(psT[:, 0:128], on, identF)
                        cp(brv(h, c0, 128), psT[:, 0:128])
                        P.end_seg()
                if first_last_prompt_group:
                    for h in range(8):
                        dma(shp[j, h], V(Sp[j].ap[:, h, :], [("Sp", j, h)]))

            def rest_layer(l):
                if 2 not in phases:
                    return
                wq = WR.get()
                dma(wq, wview(w_in[l], 3072, 512), eng="pool")
                for hx in range(XH):
                    qxT = L['qx'][qxi[0] % 2]; qxi[0] += 1
                    for (c0, n) in chunks:
                        ps = PS.get()
                        for kc in range(NKC):
                            mm(ps[:, 0:n], wq[:, kc, hx * 128:(hx + 1) * 128], hTv(kc, c0, n), start=kc == 0, stop=kc == NKC - 1)
                        cp(qxT[:, c0:c0 + n], ps[:, 0:n])
                    if has_s:
                        KVB = L["KVB"]
                        kvi = [0]
                        def kvget():
                            b = KVB[kvi[0] % 3]; kvi[0] += 1
                            return b
                        mkTs = []; vbs = []
                        for hf in range(2):
                            kb = kvget()
                            dma(kb.pat([[256, 8], [128, 2], [1, 128]]), ck[l][8 * hf:8 * hf + 8, :, hx * 128:(hx + 1) * 128].rearrange("s (mc p) d -> p s mc d", p=128), eng="pool")
                            mk_ = kvget()
                            for b4 in range(2):
                                psb = bfv(PS.get())
                                for q in range(8):
                                    idx = b4 * 8 + q
                                    tr(psb[:, q * 128:(q + 1) * 128], kb[:, idx * 128:(idx + 1) * 128], identB)
                                cp(mk_[:, b4 * 1024:(b4 + 1) * 1024], psb)
                            mkTs.append(mk_)
                    for li, t in enumerate(grp):
                        samp = (t == "s")
                        c0 = li * 128
                        ps = PS.get()
                        if not samp:
                            mm(ps[:, 0:256], qxT[:, c0:c0 + 128], mkT[:, l, hx, :])
                        else:
                            cp(QXM.pat([[136, 16], [1, 8]]), qxT[:, c0:c0 + 128].pat([[8, 16], [1, 8]]), eng="dve")
                            for s in range(NS):
                                mm(ps[:, 0:256], QXM[:, s * 128:(s + 1) * 128], mkTs[s // 8][:, (s % 8) * 256:(s % 8 + 1) * 256], start=(s == 0), stop=(s == NS - 1))
                        ss = TS8.get()
                        red(ss[:, 0:1], ps[:, 0:256], MAX, negate=True)
                        ts(ss[:, 1:2], ss[:, 0:1], SC_ATT, MUL)
                        pe_ = T512.get()
                        act(pe_[:, 0:256], ps[:, 0:256], AF.Exp, bias=ss[:, 1:2], scale=SC_ATT, acc=ss[:, 2:3])
                        recip(ss[:, 3:4], ss[:, 2:3])
                        pn = TB256.get()
                        ts(pn, pe_[:, 0:256], ss[:, 3:4], MUL)
                        psb = bfv(PS.get())
                        tr(psb[:, 0:128], pn[:, 0:128], identB)
                        tr(psb[:, 128:256], pn[:, 128:256], identB)
                        pT = TB256.get()
                        cp(pT, psb[:, 0:256])
                        psO = PS.get()
                        if not samp:
                            for mc in range(2):
                                mm(psO[:, 0:128], mvb[:, l, mc, hx * 128:(hx + 1) * 128], pT[:, mc * 128:(mc + 1) * 128], start=mc == 0, stop=mc == 1)
                        else:
                            for hf in range(2):
                                vb_ = kvget()
                                dma(vb_.pat([[256, 8], [128, 2], [1, 128]]), cv[l][8 * hf:8 * hf + 8, :, hx * 128:(hx + 1) * 128].rearrange("s (mc p) d -> p s mc d", p=128), eng="pool")
                                vbs.append(vb_)
                            for s in range(NS):
                                for mc in range(2):
                                    vb = vbs[s // 8]
                                    mm(psO[:, s * 8:(s + 1) * 8], vb[:, ((s % 8) * 2 + mc) * 128:((s % 8) * 2 + mc + 1) * 128],
                                       pT[:, mc * 128 + s * 8:mc * 128 + (s + 1) * 8], start=mc == 0, stop=mc == 1)
                        cp(brv(8 + hx, c0, 128), psO[:, 0:128])
                if 3 not in phases:
                    return
                for gb in range(3):
                    wg = WR.get()
                    dma(wg, wview(w_in[l], 3584 + gb * 512, 512), eng="pool")
                    for sub in range(4):
                        bi = gb * 4 + sub
                        for (c0, n) in chunks:
                            ps = PS.get()
                            for kc in range(NKC):
                                mm(ps[:, 0:n], wg[:, kc, sub * 128:(sub + 1) * 128], hTv(kc, c0, n), start=kc == 0, stop=kc == NKC - 1)
                            sgt = TB512.get()
                            act(sgt[:, 0:n], ps[:, 0:n], AF.Sigmoid)
                            tt(sgt[:, 0:n], sgt[:, 0:n], ps[:, 0:n], MUL)
                            tt(brv(bi, c0, n), brv(bi, c0, n), sgt[:, 0:n], MUL)
                if 4 not in phases:
                    return
                for ob in range(NKC):
                    wo = WO.get()
                    dma(wo, w_out[l][:, ob * 128:(ob + 1) * 128].rearrange("(kc p) c -> p kc c", p=128), eng="pool")
                    for (c0, n) in chunks:
                        ps = PS.get()
                        for kc in range(12):
                            mm(ps[:, 0:n], wo[:, kc, :], brv(kc, c0, n), start=kc == 0, stop=kc == 11)
                        tt(xTv(ob, c0, n), xTv(ob, c0, n), ps[:, 0:n], ADD)

            def rwkv_layer(l, j):
                m = j - 1
                cfgn = "16" if has_s else "1"
                MB = C("mb" + cfgn); SL = C("sl" + cfgn)
                TIC = C("tic" + cfgn); TSC = C("tsc" + cfgn); BOC = C("boc" + cfgn)
                LEV = 3 if has_s else 7
                if levo is not None:
                    LEV = levo
                muT = MUC[:, 0:40].pat([[8, 5], [1, 8]]); omuT = MUC[:, 40:80].pat([[8, 5], [1, 8]])
                dma(muT, rw_mu[j].rearrange("k (kc p) -> p k kc", p=128), slow=True)
                ts(omuT, muT, -1.0, MUL, 1.0, ADD)
                raw = WR.get()
                dma(raw[:, :, 0:64], rw_w1[j].rearrange("(kc p) c -> p kc c", p=128), eng="pool")
                dma(raw[:, :, 64:128], rw_a1[j].rearrange("(kc p) c -> p kc c", p=128), eng="pool")
                if j >= 1:
                    mset(raw[:, :, 160:256], 0.0)
                    dma(raw[:, :, 128:160], rw_v1[m].rearrange("(kc p) c -> p kc c", p=128), eng="pool")
                for (c_lo, kind) in ((0, 1), (64, 4)):
                    src = raw[:, :, c_lo:c_lo + 64]
                    tt(L1a[:, :, c_lo:c_lo + 64], src, omuT[:, kind, :].pat([[1, 8], [0, 64]]), MUL)
                    tt(L1b[:, :, c_lo:c_lo + 64], src, muT[:, kind, :].pat([[1, 8], [0, 64]]), MUL)
                if j >= 1:
                    src = raw[:, :, 128:256]
                    tt(L1va, src, omuT[:, 3, :].pat([[1, 8], [0, 128]]), MUL)
                    tt(L1vb, src, muT[:, 3, :].pat([[1, 8], [0, 128]]), MUL)
                W2 = BIGV[0]; V2 = BIGV[1]
                xin = XIN.get()
                dma(xin[0:64, :], rw_w2[j]); dma(xin[64:128, :], rw_a2[j])
                cp(W2, xin)
                if j >= 1:
                    xin = XIN.get()
                    mset(xin, 0.0)
                    dma(xin[0:32, :], rw_v2[m])
                    cp(V2, xin, eng="dve")
                for (c0, n) in chunks:
                    ps = PS.get()
                    for kc in range(NKC):
                        mm(ps[:, 0:n], L1a[:, kc, :], hTv(kc, c0, n), start=kc == 0, stop=False)
                    for kc in range(NKC):
                        mm(ps[:, 0:n], L1b[:, kc, :], hprevv(kc, c0, n), start=False, stop=kc == NKC - 1)
                    th = T512.get()
                    act(th[:, 0:n], ps[:, 0:n], AF.Tanh)
                    ts(V(l1T.ap[:, 0, c0:c0 + n], tkeys(c0, n, lambda li: ("l1", li))), th[:, 0:n], C("hw"), MUL)
                    ts(V(l1T.ap[:, 1, c0:c0 + n], tkeys(c0, n, lambda li: ("l1", li))), ps[:, 0:n], C("ha"), MUL)
                    if j >= 1:
                        ps = PS.get()
                        for kc in range(NKC):
                            mm(ps[:, 0:n], L1va[:, kc, :], hTv(kc, c0, n), start=kc == 0, stop=False)
                        for kc in range(NKC):
                            mm(ps[:, 0:n], L1vb[:, kc, :], hprevv(kc, c0, n), start=False, stop=kc == NKC - 1)
                        cp(V(l1vT.ap[:, c0:c0 + n], tkeys(c0, n, lambda li: ("l1v", li))), ps[:, 0:n])

                for pr in range(8):
                    cc = pr * 128
                    wa = WM.get(); wb = WM2.get()
                    for q in range(3):
                        dma(wa[:, :, q * 128:(q + 1) * 128], wview(w_in[l], q * 1024 + cc, 128), eng="pool")
                    for q, kind in enumerate((0, 2, 3)):
                        tt(wb[:, :, q * 128:(q + 1) * 128], wa[:, :, q * 128:(q + 1) * 128], muT[:, kind, :].pat([[1, 8], [0, 128]]), MUL)
                    tt(wa, wa, wb, SUB)
                    pv = PVB
                    vecs = [rw_w0[j:j + 1], rw_a0[j:j + 1], (rw_v0[m:m + 1] if j >= 1 else rw_a0[j:j + 1]), rw_kk[j:j + 1], rw_ka[j:j + 1], rw_rk[j:j + 1], rw_lg[j:j + 1], rw_lb[j:j + 1]]
                    for k8, vv in enumerate(vecs):
                        dma(pv[:, k8, :], vv[:, cc:cc + 128].partition_broadcast(128))
                    Mpp = V(Mp[j].ap[:, pr, :], [("Mp", j, pr)], True)
                    MppB = V(MpB[j].ap[:, pr, :], [("MpB", j, pr)])
                    if has_s:
                        M0b = L["M0b"]
                        M0all = L["M0all"]
                        for q in range(4):
                            SBD = L["SBD"][q % 2]
                            for hh in range(2):
                                dma(SBD[hh * 64:(hh + 1) * 64, :].pat([[128, 4], [1, 64]], off=hh * 64),
                                    strw[j, 4 * q:4 * q + 4, 2 * pr + hh].rearrange("s i k -> i s k"))
                            ps = PS.get()
                            for s4 in range(4):
                                tr(ps[:, s4 * 128:(s4 + 1) * 128], SBD[:, s4 * 128:(s4 + 1) * 128], identF)
                            for hh in range(2):
                                cp(M0all[hh * 64:(hh + 1) * 64, 4 * q:4 * q + 4, :], ps[hh * 64:(hh + 1) * 64, :].pat([[128, 4], [1, 64]], off=hh * 64),
                                   eng=("act" if hh == 0 else "dve"))
                                cp(M0b[hh * 64:(hh + 1) * 64, 4 * q:4 * q + 4, :], ps[hh * 64:(hh + 1) * 64, :].pat([[128, 4], [1, 64]], off=hh * 64),
                                   eng=("dve" if hh == 0 else "act"))
                    segB_prev = None
                    for li, t in enumerate(grp):
                        samp = (t == "s")
                        c0 = li * 128
                        gt = 16 if samp else t
                        segA = P.begin_seg()
                        if segB_prev is not None and PIPE:
                            P.merges.append((segB_prev, segA))
                        psR = PS.get()
                        for kc in range(NKC):
                            mm(psR[:, 0:384], hTv(kc, c0, 128), wa[:, kc, :], start=kc == 0, stop=False)
                        for kc in range(NKC):
                            mm(psR[:, 0:384], hprevv(kc, c0, 128), wb[:, kc, :], start=False, stop=kc == NKC - 1)
                        rP, kP, vP = psR[:, 0:128], psR[:, 128:256], psR[:, 256:384]
                        l1k = ("l1", li)
                        psL = PS.get()
                        mm(psL[:, 0:128], V(l1T.ap[:, 0, c0:c0 + 128], [l1k]), W2[:, cc:cc + 128])
                        mm(psL[:, 128:256], V(l1T.ap[:, 1, c0:c0 + 128], [l1k]), W2[:, cc:cc + 128])
                        if j >= 1:
                            mm(psL[:, 256:384], V(l1vT.ap[:, c0:c0 + 128], [("l1v", li)]), V2[:, cc:cc + 128])
                        wpre = T128.get(); tt(wpre, psL[:, 0:128], pv[:, 0, :], ADD)
                        sg = T128.get(); act(sg, wpre, AF.Sigmoid)
                        apre = T128.get(); tt(apre, psL[:, 128:256], pv[:, 1, :], ADD)
                        ag = T128.get(); act(ag, apre, AF.Sigmoid)
                        Vt = BT128.get()
                        vkey = ("vfd", gt, pr)
                        if j == 0:
                            cp(Vt, vP)
                            P.op("sp", (lambda e, o=vfd[gt * 128:(gt + 1) * 128, cc:cc + 128], i_=Vt.ap: e.dma_start(out=o, in_=i_)),
                                 reads=list(Vt.keys), writes=[vkey], dma=True)
                        else:
                            vf = NRV.get()
                            P.op("sp", (lambda e, i_=vfd[gt * 128:(gt + 1) * 128, cc:cc + 128], o=vf.ap: e.dma_start(out=o, in_=i_)),
                                 reads=[vkey], writes=list(vf.keys), dma=True)
                            vpre = T128.get(); tt(vpre, psL[:, 256:384], pv[:, 2, :], ADD)
                            vg = T128.get(); act(vg, vpre, AF.Sigmoid)
                            d1 = T128.get(); tt(d1, vf, vP, SUB)
                            tt(d1, d1, vg, MUL)
                            tt(Vt, d1, vP, ADD)
                        kk = T128.get(); tt(kk, kP, pv[:, 3, :], MUL)
                        sq = T128.get(); tt(sq, kk, kk, MUL)
                        sm = TS8.get()
                        red(sm[:, 0:2], sq.pat([[64, 2], [1, 64]]), ADD)
                        act(sm[:, 2:4], sm[:, 0:2], AF.Ln)
                        act(sm[:, 4:6], sm[:, 2:4], AF.Exp, scale=-0.5)
                        ts(sm[:, 4:6], sm[:, 4:6], 1e12, ALU.min)
                        kkn = T128.get()
                        tt(kkn.pat([[64, 2], [1, 64]]), kk.pat([[64, 2], [1, 64]]), sm[:, 4:6].pat([[1, 2], [0, 64]]), MUL)
                        t1 = T128.get(); stt(t1, ag, -1.0, pv[:, 4, :], ADD, MUL)
                        kp = T128.get(); stt(kp, t1, 1.0, kP, ADD, MUL)
                        bb = T128.get(); tt(bb, kkn, ag, MUL)
                        t2 = T128.get(); tt(t2, rP, kp, MUL)
                        tt(t2, t2, pv[:, 5, :], MUL)
                        red(sm[:, 6:8], t2.pat([[64, 2], [1, 64]]), ADD)
                        psC = PS.get()
                        mm(psC[:, 0:128], TIC, sg)
                        mm(psC[:, 128:256], TSC, sg)
                        mm(psC[:, 256:384], BOC, sg)
                        nsq = 16 if samp else 1
                        nsq = 16 if samp else 2
                        mm(psC[:, 384:384 + nsq], sg, (C("sel16c") if samp else C("negc")))
                        E1 = T128.get(); act(E1, psC[:, 0:128], AF.Exp)
                        E2 = T128.get(); act(E2, psC[:, 0:128], AF.Exp, scale=-1.0)
                        E3 = T128.get(); act(E3, psC[:, 128:256], AF.Exp)
                        E5 = T128.get(); act(E5, psC[:, 256:384], AF.Exp)
                        gC = TS8s.get(); act(gC[:, 0:nsq], psC[:, 384:384 + nsq], AF.Exp)
                        dg = [[192, 2], [1, 64]]; nd = [[64, 2], [1, 64]]
                        Am, Bm, Km, Rm = ZP[0:4]; BHm, KHm = ZP[4 + 2 * (li % 2):6 + 2 * (li % 2)]
                        stt(Am.pat(dg), kkn.pat(nd), -1.0, E3.pat(nd), MUL, MUL)
                        tt(Bm.pat(dg), bb.pat(nd), E2.pat(nd), MUL)
                        tt(Km.pat(dg), kp.pat(nd), E2.pat(nd), MUL)
                        tt(Rm.pat(dg), rP.pat(nd), E1.pat(nd), MUL)
                        tt(BHm.pat(dg), Bm.pat(dg), E5.pat(nd), MUL)
                        tt(KHm.pat(dg), Km.pat(dg), E5.pat(nd), MUL)
                        XT = BT1K.get()
                        psb = bfv(PS.get())
                        for q, Xm in enumerate((Am, Am, Bm, Bm, Km, Km, Rm, Rm)):
                            hh = q % 2
                            tr(psb[:, q * 128:(q + 1) * 128], Xm[:, hh * 128:(hh + 1) * 128], identB)
                        cp(XT[:, 0:512], psb[:, 0:512]); cp(XT[:, 512:1024], psb[:, 512:1024], eng="dve")
                        AT = lambda hh: XT[:, (0 + hh) * 128:(1 + hh) * 128]
                        BT = lambda hh: XT[:, (2 + hh) * 128:(3 + hh) * 128]
                        KT = lambda hh: XT[:, (4 + hh) * 128:(5 + hh) * 128]
                        RT = lambda hh: XT[:, (6 + hh) * 128:(7 + hh) * 128]
                        Mk = []; XY = BT512.get(); WT = BT256.get()
                        for hh in range(2):
                            psM = PS.get()
                            mm(psM[:, 0:128], BT(hh), AT(hh))
                            mm(psM[:, 128:256], KT(hh), AT(hh))
                            mm(psM[:, 256:384], BT(hh), RT(hh))
                            mm(psM[:, 384:512], KT(hh), RT(hh))
                            mk_ = MK[2 * (li % 2) + hh]; tt(mk_, psM, MB, MUL)
                            Mk.append(mk_)
                            psN = PS.get()
                            mm(psN[:, 0:128], AT(hh), BT(hh))
                            cp(XY[:, hh * 256:hh * 256 + 128], mk_[:, 0:128], eng="pool")
                            tt(XY[:, hh * 256 + 128:hh * 256 + 256], psN[:, 0:128], SL, MUL)
                            tt(WT[:, hh * 128:(hh + 1) * 128], mk_[:, 0:128], identF, ADD)
                        for lev in range(1, LEV):
                            psV = PS.get()
                            for hh in range(2):
                                Xp = XY[:, hh * 256:hh * 256 + 128]; Yp = XY[:, hh * 256 + 128:hh * 256 + 256]
                                mm(psV[:, hh * 256 + 128:hh * 256 + 256], Xp, Yp)
                                if lev < LEV - 1:
                                    mm(psV[:, hh * 256:hh * 256 + 128], Yp, Xp)
                            XYn = BT512.get()
                            if lev < LEV - 1:
                                cp(XYn, psV)
                            else:
                                cp(XYn.pat([[256, 2], [1, 128]], off=128), psV.pat([[256, 2], [1, 128]], off=128))
                            psW = PS.get()
                            for hh in range(2):
                                mm(psW[:, hh * 128:(hh + 1) * 128], XYn[:, hh * 256 + 128:hh * 256 + 256], WT[:, hh * 128:(hh + 1) * 128])
                            WTn = BT256.get()
                            tt(WTn, psW[:, 0:256], WT, ADD)
                            XY = XYn; WT = WTn
                        P.end_seg()
                        segB_prev = P.begin_seg()
                        if samp:
                            cp(Z1b.pat([[136, 16], [1, 8]]), AT(0).pat([[8, 16], [1, 8]]), eng="pool")
                            cp(Z2b.pat([[136, 16], [1, 8]]), AT(1).pat([[8, 16], [1, 8]]), eng="pool")
                        psP = PS.get()
                        for hh in range(2):
                            o_ = psP[:, hh * 64:(hh + 1) * 64]
                            if not samp:
                                mm(o_, AT(hh), MppB, start=True, stop=False)
                            else:
                                ZZ = Z1b if hh == 0 else Z2b
                                for s in range(NS):
                                    mm(o_, ZZ[:, s * 128:(s + 1) * 128], M0b[:, s, :], start=(s == 0), stop=False)
                            mm(o_, Mk[hh][:, 128:256], Vt[:, hh * 64:(hh + 1) * 64], start=False, stop=True)
                        P1 = BT128.get(); cp(P1, psP[:, 0:128])
                        psU = PS.get()
                        for hh in range(2):
                            mm(psU[:, hh * 64:(hh + 1) * 64], WT[:, hh * 128:(hh + 1) * 128], P1[:, hh * 64:(hh + 1) * 64])
                        U = BT128.get(); cp(U, psU[:, 0:128], eng="dve")
                        if samp:
                            cp(Z1b.pat([[136, 16], [1, 8]]), RT(0).pat([[8, 16], [1, 8]]), eng="pool")
                            cp(Z2b.pat([[136, 16], [1, 8]]), RT(1).pat([[8, 16], [1, 8]]), eng="pool")
                        psY = PS.get()
                        for hh in range(2):
                            o_ = psY[:, hh * 64:(hh + 1) * 64]
                            if not samp:
                                mm(o_, RT(hh), MppB, start=True, stop=False)
                            else:
                                ZZ = Z1b if hh == 0 else Z2b
                                for s in range(NS):
                                    mm(o_, ZZ[:, s * 128:(s + 1) * 128], M0b[:, s, :], start=(s == 0), stop=False)
                            mm(o_, Mk[hh][:, 256:384], U[:, hh * 64:(hh + 1) * 64], start=False, stop=False)
                            mm(o_, Mk[hh][:, 384:512], Vt[:, hh * 64:(hh + 1) * 64], start=False, stop=True)
                        if not samp:
                            psS = PS.get()
                            mm(psS[:, 0:64], BHm[:, 0:128], U[:, 0:64], start=True, stop=False)
                            mm(psS[:, 0:64], BHm[:, 128:256], U[:, 64:128], start=False, stop=False)
                            mm(psS[:, 0:64], KHm[:, 0:128], Vt[:, 0:64], start=False, stop=False)
                            mm(psS[:, 0:64], KHm[:, 128:256], Vt[:, 64:128], start=False, stop=True)
                            stt(Mpp, Mpp, gC[:, 0:1], psS[:, 0:64], MUL, ADD)
                            cp(MppB, Mpp)
                        else:
                            for q in range(2):
                                Uw = [BT512.get(), BT512.get()]; Vw = [BT512.get(), BT512.get()]
                                for hh in range(2):
                                    tt(Uw[hh].pat([[64, 8], [1, 64]]), U[:, hh * 64:(hh + 1) * 64].pat([[0, 8], [1, 64]]),
                                       C("sel16").pat([[1, 8], [0, 64]], off=8 * q), MUL)
                                    tt(Vw[hh].pat([[64, 8], [1, 64]]), Vt[:, hh * 64:(hh + 1) * 64].pat([[0, 8], [1, 64]]),
                                       C("sel16").pat([[1, 8], [0, 64]], off=8 * q), MUL, eng="pool")
                                psS = PS.get()
                                mm(psS, BHm[:, 0:128], Uw[0], start=True, stop=False)
                                mm(psS, BHm[:, 128:256], Uw[1], start=False, stop=False)
                                mm(psS, KHm[:, 0:128], Vw[0], start=False, stop=False)
                                mm(psS, KHm[:, 128:256], Vw[1], start=False, stop=True)
                                tmp = T512.get()
                                tt(tmp.pat([[64, 8], [1, 64]]), M0all[:, 8 * q:8 * q + 8, :], gC[:, 8 * q:8 * q + 8].pat([[1, 8], [0, 64]]), MUL)
                                Mn = T512.get()
                                tt(Mn, tmp, psS, ADD)
                                for q2 in range(2):
                                    MBD = L["MBD"][q2]
                                    for hh in range(2):
                                        cp(MBD[hh * 64:(hh + 1) * 64, :].pat([[128, 4], [1, 64]], off=hh * 64),
                                           Mn[hh * 64:(hh + 1) * 64, q2 * 256:(q2 + 1) * 256].pat([[64, 4], [1, 64]]), eng=("act" if hh == 0 else "pool"))
                                    ps = PS.get()
                                    for s4 in range(4):
                                        tr(ps[:, s4 * 128:(s4 + 1) * 128], MBD[:, s4 * 128:(s4 + 1) * 128], identF)
                                    So = T256.get()
                                    for hh in range(2):
                                        cp(So[hh * 64:(hh + 1) * 64, :].pat([[64, 4], [1, 64]]),
                                           ps[hh * 64:(hh + 1) * 64, :].pat([[128, 4], [1, 64]], off=hh * 64), eng=("act" if hh == 0 else "dve"))
                                    sb0 = 8 * q + 4 * q2
                                    for hh in range(2):
                                        dma(srs[j, sb0:sb0 + 4, 2 * pr + hh].rearrange("s i k -> i s k"),
                                            So[hh * 64:(hh + 1) * 64, :].pat([[64, 4], [1, 64]]))
                        g2 = [[64, 2], [1, 64]]
                        red(sm[:, 8:10], psY[:, 0:128].pat(g2), ADD)
                        ysq = T128.get(); act(ysq, psY[:, 0:128], AF.Square)
                        red(sm[:, 10:12], ysq.pat(g2), ADD)
                        ts(sm[:, 12:14], sm[:, 8:10], 1.0 / 64, MUL)
                        tt(sm[:, 14:16], sm[:, 12:14], sm[:, 12:14], MUL)
                        stt(sm[:, 16:18], sm[:, 10:12], 1.0 / 64, sm[:, 14:16], MUL, SUB)
                        act(sm[:, 18:20], sm[:, 16:18], AF.Ln, bias=LNX_EPS, scale=1.0)
                        act(sm[:, 20:22], sm[:, 18:20], AF.Exp, scale=-0.5)
                        yc = T128.get()
                        tt(yc.pat(g2), psY[:, 0:128].pat(g2), sm[:, 12:14].pat([[1, 2], [0, 64]]), SUB)
                        tt(yc.pat(g2), yc.pat(g2), sm[:, 20:22].pat([[1, 2], [0, 64]]), MUL)
                        tt(yc, yc, pv[:, 6, :], MUL)
                        tt(yc, yc, pv[:, 7, :], ADD)
                        yb = T128.get()
                        tt(yb.pat(g2), Vt.pat(g2), sm[:, 6:8].pat([[1, 2], [0, 64]]), MUL)
                        tt(yc, yc, yb, ADD)
                        psT = PS.get()
                        tr(psT[:, 0:128], yc, identF)
                        cp(brv(pr, c0, 128), psT[:, 0:128])
                        P.end_seg()
                if first_last_prompt_group:
                    for pr in range(8):
                        MBD = PMBD
                        Mpp = V(Mp[j].ap[:, pr, :], [("Mp", j, pr)])
                        for hh in range(2):
                            cp(MBD[hh * 64:(hh + 1) * 64, hh * 64:(hh + 1) * 64], Mpp[hh * 64:(hh + 1) * 64, :], eng=("act" if hh == 0 else "pool"))
                        ps = PS.get()
                        tr(ps[:, 0:128], MBD, identF)
                        So = T128.get()
                        for hh in range(2):
                            cp(So[hh * 64:(hh + 1) * 64, 0:64], ps[hh * 64:(hh + 1) * 64, hh * 64:(hh + 1) * 64], eng=("act" if hh == 0 else "dve"))
                        for hh in range(2):
                            dma(srp[j, 2 * pr + hh], So[hh * 64:(hh + 1) * 64, 0:64])

            first_last_prompt_group = (15 in ptiles)
            for l in range(depth):
                is_rw = (l % 2 == 1)
                j = l // 2
                gcol = TS8.get()[:, 0:8]
                dma(gcol, norm_g[l].rearrange("(kc p) -> p kc", p=128), slow=True)
                if is_rw:
                    if first_group:
                        mset(V(hT.ap[:, :, 0:1], [("hT", "c0")]), 0.0)
                    elif npt > 0:
                        cp(V(hT.ap[:, :, 0], [("hT", "c0")]), hcar[j], eng="pool")
                for li, t in enumerate(grp):
                    rstd = rms_tile(li)
                    for kc in range(NKC):
                        stt(V(hT.ap[:, kc, 1 + li * 128:1 + (li + 1) * 128], [hkey(li)]), xTv(kc, li * 128, 128), gcol[:, kc:kc + 1], rstd, MUL, MUL)
                    if is_rw and t == 15:
                        hl = TS8.get()[:, 0:8]
                        ts(hl, V(xT.ap[:, :, li * 128 + 127], [xkey(li)]), rstd[:, 127:128], MUL)
                        tt(hl, hl, gcol, MUL)
                        dma(ssp[j].rearrange("(kc p) -> p kc", p=128), hl, slow=True)
                    if is_rw and t == "s":
                        hl = T128.get()
                        hl3 = hl.pat([[16, 8], [1, 16]])
                        xv = xTt(li).pat([[NTM, 8], [8, 16]], off=7)
                        rv = rstd.pat([[0, 8], [8, 16]], off=7)
                        tt(hl3, xv, rv, MUL)
                        tt(hl3, hl3, gcol.pat([[1, 8], [0, 16]]), MUL)
                        ps = PS.get()
                        tr(ps[:, 0:128], hl, identF)
                        ho = T128.get()
                        cp(ho, ps[:, 0:128])
                        for kc in range(NKC):
                            dma(sss[j, :, kc * 128:(kc + 1) * 128], ho[kc * 16:(kc + 1) * 16, :])
                        stin = XIN.get()
                        mset(stin, 0.0)
                        dma(stin[0:16, :], sts[j])
                        for half in range(2):
                            ps = PS.get()
                            for q in range(4):
                                kc = half * 4 + q
                                tr(ps[:, q * 128:(q + 1) * 128], stin[:, kc * 128:(kc + 1) * 128], identF)
                            cp(hpS.pat([[128, 4], [8, 16]], off=half * 4 * 128), ps.pat([[128, 4], [1, 16]]))
                        hsv = V(hT.ap[:, :, 1 + li * 128:1 + (li + 1) * 128], [hkey(li)])
                        cp(hpS.pat([[128, 8], [8, 16], [1, 7]], off=1), hsv.pat([[HS, 8], [8, 16], [1, 7]]), eng="pool")
                if is_rw and npt > 0:
                    cp(hcar[j], V(hT.ap[:, :, npt * 128], [hkey(npt - 1)]), eng="pool")
                if 1 in phases:
                    if not is_rw:
                        hgrn_layer(l, j)
                    else:
                        rwkv_layer(l, j)
                rest_layer(l)

            fg = TS8.get()[:, 0:8]
            dma(fg, final_g[0].rearrange("(kc p) -> p kc", p=128), slow=True)
            for li, t in enumerate(grp):
                rstd = rms_tile(li)
                yT = T1K.get()
                for kc in range(NKC):
                    stt(yT[:, kc * 128:(kc + 1) * 128], xTv(kc, li * 128, 128), fg[:, kc:kc + 1], rstd, MUL, MUL)
                yo = T1K.get()
                for half in range(2):
                    ps = PS.get()
                    for q in range(4):
                        kc = half * 4 + q
                        tr(ps[:, q * 128:(q + 1) * 128], yT[:, kc * 128:(kc + 1) * 128], identF)
                    cp(yo[:, half * 512:(half + 1) * 512], ps)
                dma(ys if t == "s" else yp[t * 128:(t + 1) * 128, :], yo)

        for gi, grp in enumerate(groups):
            process_group(gi, grp)

        info = P.emit()
    return nc, info


def _shard_inputs(inp):
    cf, cb = make_consts()
    maps = []
    for c in range(8):
        b = slice(16 * c, 16 * c + 16)
        m = {
            "xp": np.ascontiguousarray(inp["x_prompt"][c]),
            "xs": np.ascontiguousarray(inp["x_sample"][b]).reshape(128, D),
            "mem": np.ascontiguousarray(inp["mem_prompt"][c]),
            "sth": np.ascontiguousarray(inp["state_hgrn"][:, b]),
            "strw": np.ascontiguousarray(inp["state_rwkv"][:, b]),
            "sts": np.ascontiguousarray(inp["state_shift"][:, b]),
            "ck": np.ascontiguousarray(inp["cache_mem_k"][:, b]).reshape(4, 16, MEM, 512),
            "cv": np.ascontiguousarray(inp["cache_mem_v"][:, b]).reshape(4, 16, MEM, 512),
            "rw_rk": np.ascontiguousarray(inp["rw_rk"]).reshape(2, D),
            "final_g": np.ascontiguousarray(inp["final_g"]).reshape(1, D),
            "cf": cf, "cb": cb,
        }
        for k in ("norm_g", "w_in", "w_out", "mem_norm_g", "w_mem_kv", "hg_lb", "hg_onorm_g", "rw_mu", "rw_w0", "rw_w1",
                  "rw_w2", "rw_a0", "rw_a1", "rw_a2", "rw_v0", "rw_v1", "rw_v2", "rw_kk", "rw_ka", "rw_lnx_g", "rw_lnx_b"):
            m[k] = np.ascontiguousarray(inp[k])
        maps.append(m)
    return maps


_NC_CACHE = {}


def kernel(**inputs):
    inp = {k: np.asarray(v, dtype=np.float32) for k, v in inputs.items()}
    if "nc" not in _NC_CACHE:
        _NC_CACHE["nc"] = build()[0]
    nc = _NC_CACHE["nc"]
    maps = _shard_inputs(inp)
    res = run_bass_kernel_spmd(nc, maps, core_ids=list(range(8)))
    R = res.results
    y_p = np.stack([R[c]["yp"] for c in range(8)], 0)
    y_s = np.concatenate([R[c]["ys"].reshape(16, 8, D) for c in range(8)], 0)
    sh_p = np.stack([R[c]["shp"] for c in range(8)], 1)
    sr_p = np.stack([R[c]["srp"] for c in range(8)], 1)
    ss_p = np.stack([R[c]["ssp"] for c in range(8)], 1)
    mk_p = np.stack([R[c]["mkp"].reshape(4, MEM, 4, 128) for c in range(8)], 1)
    mv_p = np.stack([R[c]["mvp"].reshape(4, MEM, 4, 128) for c in range(8)], 1)
    sh_s = np.concatenate([R[c]["shs"] for c in range(8)], 1)
    sr_s = np.concatenate([R[c]["srs"] for c in range(8)], 1)
    ss_s = np.concatenate([R[c]["sss"] for c in range(8)], 1)
    return (y_p, y_s, sh_p, sr_p, ss_p, mk_p, mv_p, sh_s, sr_s, ss_s)
```

```python
import contextlib
import numpy as np
import ml_dtypes
import concourse.bass as bass
import concourse.mybir as mybir
from concourse.bass_utils import run_bass_kernel_spmd

F32 = mybir.dt.float32
F32R = mybir.dt.float32r
BF16 = mybir.dt.bfloat16
AF = mybir.ActivationFunctionType
ALU = mybir.AluOpType
AX = mybir.AxisListType

ENGS = ("pe", "act", "dve", "pool", "sp")
DMA_K = 6
SEM_WRAP = 30000
SAME_ENG_RAW_WAITS = True
PIPE = True


class Prog:
    def __init__(self, nc):
        self.nc = nc
        self.ops = []
        self.last_w = {}
        self.readers = {}
        self.excl = set()
        self.seg = None
        self.nseg = 0
        self.merges = []

    def begin_seg(self):
        self.nseg += 1
        self.seg = self.nseg
        return self.seg

    def end_seg(self):
        self.seg = None

    def op(self, eng, fn, reads=(), writes=(), dma=False):
        i = len(self.ops)
        ex = [k for k in reads if k in self.excl]
        if ex:
            writes = list(writes) + ex
        deps = {}
        for k in reads:
            w = self.last_w.get(k)
            if w is not None:
                deps[w] = True
        for k in writes:
            w = self.last_w.get(k)
            if w is not None:
                deps.setdefault(w, False)
            for r in self.readers.get(k, ()):
                deps.setdefault(r, False)
        o = dict(i=i, eng=eng, fn=fn, dma=dma, deps=deps, seg=self.seg)
        self.ops.append(o)
        for k in reads:
            self.readers.setdefault(k, []).append(i)
        for k in writes:
            self.last_w[k] = i
            self.readers[k] = []
        return i

    def barrier(self):
        lastc = {e: None for e in ENGS}
        lastd = {e: [] for e in ENGS}
        for o in self.ops:
            if o["dma"]:
                lastd[o["eng"]].append(o["i"])
            else:
                lastc[o["eng"]] = o["i"]
        extra = [v for v in lastc.values() if v is not None]
        for e in ENGS:
            extra += lastd[e][-DMA_K:]
        for e in ENGS:
            i = self.op(e, lambda eng: eng.nop(), reads=(), writes=())
            for x in extra:
                self.ops[i]["deps"][x] = True

    def _order(self):
        n = len(self.ops)
        order = []
        segops = {}
        for o in self.ops:
            if o["seg"] is not None:
                segops.setdefault(o["seg"], []).append(o["i"])
        partner = {x: y for x, y in self.merges}
        done_seg = set()
        i = 0
        while i < n:
            o = self.ops[i]
            sg = o["seg"]
            if sg is not None and sg in partner and sg not in done_seg:
                X = segops[sg]; Y = segops[partner[sg]]
                assert X[-1] + 1 == Y[0] and X == list(range(X[0], X[-1] + 1)) and Y == list(range(Y[0], Y[-1] + 1))
                emitted = set()
                xi = yi = 0
                xset = set(X)
                turn = 0
                while xi < len(X) or yi < len(Y):
                    pick_y = False
                    if yi < len(Y) and (turn == 1 or xi >= len(X)):
                        yo = self.ops[Y[yi]]
                        if all((d not in xset) or (d in emitted) for d in yo["deps"]):
                            pick_y = True
                    if pick_y:
                        order.append(Y[yi]); yi += 1
                    else:
                        order.append(X[xi]); emitted.add(X[xi]); xi += 1
                    turn ^= 1
                done_seg.add(sg); done_seg.add(partner[sg])
                i = Y[-1] + 1
                continue
            order.append(i)
            i += 1
        assert sorted(order) == list(range(n))
        return order

    def emit(self, final_wait_eng="sp"):
        nc = self.nc
        order = self._order()
        ops = [self.ops[i] for i in order]
        newidx = {o["i"]: k for k, o in enumerate(ops)}
        for k, o in enumerate(ops):
            o["deps"] = {newidx[d]: r for d, r in o["deps"].items()}
            assert all(d < k for d in o["deps"])
            o["i"] = k
        per_eng = {e: [] for e in ENGS}
        ndma = {e: 0 for e in ENGS}
        lastslot = {}
        for o in ops:
            e = o["eng"]
            o["pos"] = len(per_eng[e])
            per_eng[e].append(o["i"])
            if o["dma"]:
                nn = ndma[e]
                o["dslot"] = nn % DMA_K
                o["dval"] = 16 * (nn // DMA_K + 1)
                o["dprev"] = lastslot.get((e, o["dslot"]))
                lastslot[(e, o["dslot"])] = o["i"]
                ndma[e] = nn + 1
        known = {e: {f: -1 for f in ENGS} for e in ENGS}
        known_d = {e: {} for e in ENGS}
        milestone = set()
        for o in ops:
            e = o["eng"]
            wl = []
            deps = dict(o["deps"])
            if o["dma"] and o["dprev"] is not None:
                deps[o["dprev"]] = True
            for j, raw in deps.items():
                p = ops[j]
                f = p["eng"]
                if p["dma"]:
                    key = (f, p["dslot"])
                    if known_d[e].get(key, 0) < p["dval"]:
                        known_d[e][key] = p["dval"]
                        wl.append(("d", f, p["dslot"], p["dval"]))
                else:
                    if f == e and not o["dma"] and e != "pool":
                        if e == "pe" or not raw or not SAME_ENG_RAW_WAITS:
                            continue
                    if known[e][f] < p["pos"]:
                        wl.append(("c", f, p["pos"], j))
            best = {}
            out = []
            for w in wl:
                if w[0] == "c":
                    if w[1] not in best or best[w[1]][2] < w[2]:
                        best[w[1]] = w
                else:
                    out.append(w)
            for f, w in best.items():
                known[e][f] = w[2]
                milestone.add(w[3])
                out.append(w)
            o["waits"] = out
        cnt = {e: 0 for e in ENGS}
        for o in ops:
            if o["dma"]:
                continue
            if o["i"] in milestone:
                cnt[o["eng"]] += 1
                o["ms"] = cnt[o["eng"]]
            else:
                o["ms"] = None
        nsem = {e: max(1, (cnt[e] + SEM_WRAP - 1) // SEM_WRAP) for e in ENGS}
        nwaits = 0
        with contextlib.ExitStack() as st:
            csem = {e: [st.enter_context(nc.semaphore(f"c_{e}_{k}")) for k in range(nsem[e])] for e in ENGS}
            dsem = {e: [st.enter_context(nc.semaphore(f"d_{e}_{k}")) for k in range(DMA_K)]
                    for e in ENGS if ndma[e] > 0}
            block = st.enter_context(nc.Block())

            def mk(ename):
                def body(eng):
                    nonlocal nwaits
                    for i in per_eng[ename]:
                        o = ops[i]
                        for w in o["waits"]:
                            nwaits += 1
                            if w[0] == "c":
                                ms = ops[w[3]]["ms"]
                                k, v = (ms - 1) // SEM_WRAP, (ms - 1) % SEM_WRAP + 1
                                eng.wait_ge(csem[w[1]][k], v)
                            else:
                                eng.wait_ge(dsem[w[1]][w[2]], w[3])
                        ins = o["fn"](eng)
                        if o["dma"]:
                            ins.then_inc(dsem[ename][o["dslot"]], 16)
                        elif o["ms"] is not None:
                            k = (o["ms"] - 1) // SEM_WRAP
                            ins.then_inc(csem[ename][k], 1)
                    if ename == final_wait_eng:
                        for f in dsem:
                            last = {}
                            for j in per_eng[f]:
                                if ops[j]["dma"]:
                                    last[ops[j]["dslot"]] = ops[j]["dval"]
                            for s, v in last.items():
                                eng.wait_ge(dsem[f][s], v)
                return body

            block.tensor(mk("pe"))
            block.scalar(mk("act"))
            block.vector(mk("dve"))
            block.gpsimd(mk("pool"))
            block.sync(mk("sp"))
        return dict(n_ops=len(ops), per_eng={e: len(v) for e, v in per_eng.items()},
                    milestones=cnt, nwaits=nwaits)


class V:
    def __init__(self, ap, keys, r=False):
        self.ap = ap
        self.keys = tuple(keys)
        self.r = r

    def __getitem__(self, idx):
        return V(self.ap[idx], self.keys, self.r)

    def pat(self, pat, off=0):
        a = self.ap
        return V(bass.AP(a.tensor, a.offset + off, [list(a.ap[0])] + [list(p) for p in pat]), self.keys, self.r)

    def k(self, *keys):
        return V(self.ap, keys, self.r)

    def nr(self):
        return V(self.ap, self.keys, False)

    def rr(self):
        return V(self.ap, self.keys, True)


D = 1024
NKC = 8
SEQ = 2048
NPT = 16
NS = 16
TS = 8
MEM = 256
XH = 4
INC = 5120
CDEC = -float(np.exp(-0.5))
NORM_EPS = 1e-6
LNX_EPS = 64e-5
SC_ATT = 128 ** -0.5

CO = {}
_o = 0
for _n, _w in [("ident", 128), ("ones", 128),
               ("mb1", 512), ("sl1", 128), ("tic1", 128), ("tsc1", 128), ("boc1", 128),
               ("mb16", 512), ("sl16", 128), ("bo16", 128), ("tic16", 128), ("tsc16", 128), ("boc16", 128),
               ("iu2", 128), ("bo2", 128), ("sel16", 16), ("sel16c", 16), ("negc", 2), ("cm0", 1), ("cm1", 1),
               ("hw", 1), ("ha", 1)]:
    CO[_n] = (_o, _w)
    _o += _w
NCF = _o
CB = {"ident": (0, 128)}
NCB = 128


def make_consts():
    c = np.zeros((128, NCF), np.float32)
    s = np.arange(128)[:, None]
    t = np.arange(128)[None, :]
    def put(n, a):
        o, w = CO[n]
        c[:, o:o + w] = a
    put("ident", (s == t))
    put("ones", 1.0)
    for cfg, blk in ((1, 128), (16, 8), (2, 64)):
        same = (s // blk) == (t // blk)
        iu = same & (s <= t)
        su = same & (s < t)
        sl = same & (s > t)
        if cfg == 2:
            put("iu2", iu); put("bo2", same)
            continue
        put(f"mb{cfg}", np.concatenate([su, su, iu, iu], 1))
        put(f"sl{cfg}", sl)
        if cfg == 16:
            put("bo16", same)
        put(f"tic{cfg}", CDEC * iu); put(f"tsc{cfg}", CDEC * su); put(f"boc{cfg}", CDEC * same)
    sel = (np.arange(128)[:, None] // 8) == np.arange(16)[None, :]
    put("sel16", sel); put("sel16c", CDEC * sel); put("negc", CDEC)
    put("cm0", (np.arange(128) < 64)[:, None]); put("cm1", (np.arange(128) >= 64)[:, None])
    put("hw", (np.arange(128) < 64)[:, None]); put("ha", (np.arange(128) >= 64)[:, None])
    cb = np.zeros((128, NCB), np.float32)
    cb[:, 0:128] = np.eye(128)
    return c, cb.astype(ml_dtypes.bfloat16)


def build(depth=4, groups=None, dbg=False, memkv=True, phases=(1, 2, 3, 4), mks=9, levo=None):
    if groups is None:
        groups = [[0, 1, 2, 3], [4, 5, 6, 7], [8, 9, 10, 11], [12, 13, 14, 15], ["s"]]
    GMAX = max(len(g) for g in groups)
    NTM = GMAX * 128
    nc = bass.Bass("TRN2", target_bir_lowering=False)
    din = lambda n, s, d=F32: nc.dram_tensor(n, list(s), d, kind="ExternalInput").ap()
    dout = lambda n, s: nc.dram_tensor(n, list(s), F32, kind="ExternalOutput").ap()
    xp = din("xp", [SEQ, D]); xs = din("xs", [128, D]); mem = din("mem", [MEM, D])
    sth = din("sth", [2, NS, 8, 128, 128]); strw = din("strw", [2, NS, 16, 64, 64]); sts = din("sts", [2, NS, D])
    ck = din("ck", [4, NS, MEM, 512]); cv = din("cv", [4, NS, MEM, 512])
    norm_g = din("norm_g", [4, D]); w_in = din("w_in", [4, D, INC]); w_out = din("w_out", [4, 1536, D])
    mem_norm_g = din("mem_norm_g", [4, D]); w_mem_kv = din("w_mem_kv", [4, D, 1024])
    hg_lb = din("hg_lb", [2, D]); hg_og = din("hg_onorm_g", [2, 128])
    rw_mu = din("rw_mu", [2, 5, D]); rw_w0 = din("rw_w0", [2, D]); rw_w1 = din("rw_w1", [2, D, 64]); rw_w2 = din("rw_w2", [2, 64, D])
    rw_a0 = din("rw_a0", [2, D]); rw_a1 = din("rw_a1", [2, D, 64]); rw_a2 = din("rw_a2", [2, 64, D])
    rw_v0 = din("rw_v0", [1, D]); rw_v1 = din("rw_v1", [1, D, 32]); rw_v2 = din("rw_v2", [1, 32, D])
    rw_kk = din("rw_kk", [2, D]); rw_ka = din("rw_ka", [2, D]); rw_rk = din("rw_rk", [2, D])
    rw_lg = din("rw_lnx_g", [2, D]); rw_lb = din("rw_lnx_b", [2, D]); final_g = din("final_g", [1, D])
    cf_d = din("cf", [128, NCF]); cb_d = din("cb", [128, NCB], BF16)
    yp = dout("yp", [SEQ, D]); ys = dout("ys", [128, D])
    shp = dout("shp", [2, 8, 128, 128]); srp = dout("srp", [2, 16, 64, 64]); ssp = dout("ssp", [2, D])
    mkp = dout("mkp", [4, MEM, 512]); mvp = dout("mvp", [4, MEM, 512])
    shs = dout("shs", [2, NS, 8, 128, 128]); srs = dout("srs", [2, NS, 16, 64, 64]); sss = dout("sss", [2, NS, D])

    with contextlib.ExitStack() as st:
        P = Prog(nc)

        def sb(name, shape, dt=F32, r=False):
            t = st.enter_context(nc.sbuf_tensor(name, list(shape), dt))
            return V(t[:], [name], r and dt == F32)

        class Rot:
            def __init__(self, name, n, shape, dt=F32, r=False):
                self.bufs = [sb(f"{name}{i}", shape, dt, r) for i in range(n)]
                self.i = 0

            def get(self):
                b = self.bufs[self.i % len(self.bufs)]
                self.i += 1
                return b

        class PSR:
            def __init__(self):
                self.bufs = []
                for i in range(8):
                    t = st.enter_context(nc.psum_tensor(f"ps{i}", [128, 512], F32))
                    self.bufs.append(V(t[:], [f"ps{i}"]))
                    P.excl.add(f"ps{i}")
                self.i = 0

            def get(self):
                b = self.bufs[self.i % 8]
                self.i += 1
                return b

        PS = PSR()

        def bfv(v):
            return V(v.ap.bitcast(BF16), v.keys)

        def rk(*vs):
            out = []
            for v in vs:
                if isinstance(v, V):
                    out += list(v.keys)
            return out

        def a_(x):
            return x.ap if isinstance(x, V) else x

        def o_(x):
            return x.ap.bitcast(F32R) if x.r else x.ap

        def mm(o, l, r, start=True, stop=True):
            la, ra = l.ap, r.ap
            if l.r and r.r:
                la, ra = la.bitcast(F32R), ra.bitcast(F32R)
            P.op("pe", lambda e: e.matmul(o.ap, lhsT=la, rhs=ra, start=start, stop=stop), reads=rk(l, r), writes=rk(o))

        def tr(o, i, idn):
            P.op("pe", lambda e: e.transpose(out=o.ap, in_=i.ap, identity=idn.ap), reads=rk(i, idn), writes=rk(o))

        def act(o, i, f, bias=None, scale=None, acc=None):
            kw = {}
            if bias is not None:
                kw["bias"] = a_(bias)
            if scale is not None:
                kw["scale"] = a_(scale)
            if acc is not None:
                kw["accum_out"] = acc.ap
            oa = o_(o)
            P.op("act", lambda e: e.activation(out=oa, in_=i.ap, func=f, **kw), reads=rk(i, bias, scale), writes=rk(o, acc))

        def tt(o, a, b, op, eng="dve"):
            if eng == "pool" and o.r:
                eng = "dve"
            oa = o_(o)
            P.op(eng, lambda e: e.tensor_tensor(out=oa, in0=a.ap, in1=b.ap, op=op), reads=rk(a, b), writes=rk(o))

        def ts(o, a, s1, op0, s2=None, op1=None, eng="dve"):
            oa = o_(o)

            def f(e):
                if op1 is None:
                    return e.tensor_scalar(out=oa, in0=a.ap, scalar1=a_(s1), scalar2=None, op0=op0)
                return e.tensor_scalar(out=oa, in0=a.ap, scalar1=a_(s1), scalar2=a_(s2), op0=op0, op1=op1)
            P.op(eng, f, reads=rk(a, s1, s2), writes=rk(o))

        def stt(o, a, s, b, op0, op1):
            oa = o_(o)
            P.op("dve", lambda e: e.scalar_tensor_tensor(out=oa, in0=a.ap, scalar=a_(s), in1=b.ap, op0=op0, op1=op1),
                 reads=rk(a, s, b), writes=rk(o))

        def red(o, a, op, negate=False):
            P.op("dve", lambda e: e.tensor_reduce(out=o.ap, in_=a.ap, axis=AX.X, op=op, negate=negate), reads=rk(a), writes=rk(o))

        def recip(o, a):
            oa = o_(o)
            P.op("dve", lambda e: e.reciprocal(out=oa, in_=a.ap), reads=rk(a), writes=rk(o))

        def cp(o, i, eng="act"):
            if eng == "pool" and o.r:
                eng = "dve"
            oa = o_(o)
            if eng == "act":
                P.op("act", lambda e: e.copy(out=oa, in_=i.ap), reads=rk(i), writes=rk(o))
            else:
                P.op(eng, lambda e: e.tensor_copy(out=oa, in_=i.ap), reads=rk(i), writes=rk(o))

        def mset(o, val, eng="pool"):
            P.op(eng, lambda e: e.memset(o.ap, val), writes=rk(o))

        def dma(o, i, eng="sp", slow=False):
            oa = o.ap if isinstance(o, V) else o
            ia = i.ap if isinstance(i, V) else i
            kw = dict(allow_slow_non_contiguous=True) if slow else {}
            P.op(eng, lambda e: e.dma_start(out=oa, in_=ia, **kw), reads=rk(i), writes=rk(o), dma=True)

        MUL, ADD, SUB, MAX = ALU.mult, ALU.add, ALU.subtract, ALU.max

        cF = sb("cF", [128, NCF], r=True); cB = sb("cB", [128, NCB], BF16)
        dma(cB, cb_d)
        C = lambda n: cF[:, CO[n][0]:CO[n][0] + CO[n][1]]
        identF = C("ident"); identB = cB[:, 0:128]; onesF = C("ones")

        Sp = [sb(f"Sp{j}", [128, 8, 128], r=True) for j in range(2)]
        Mp = [sb(f"Mp{j}", [128, 8, 64], r=True) for j in range(2)]
        hcar = [sb(f"hcar{j}", [128, 8], BF16) for j in range(2)]
        for j in range(2):
            mset(Sp[j], 0.0); mset(Mp[j], 0.0)

        ARENA = 62 * 256
        arena_t = st.enter_context(nc.sbuf_tensor("arena", [128, ARENA], F32))
        arena = arena_t[:]
        aoff = {"P": 0, "S": 0}

        def carve(lay, name, shape, dt=F32):
            n = int(np.prod(shape[1:]))
            n32 = n if dt == F32 else (n + 1) // 2
            o = aoff[lay]
            aoff[lay] = o + n32
            assert aoff[lay] <= ARENA, (lay, name, aoff[lay])
            a = arena[:, o:o + n32]
            if dt != F32:
                a = a.bitcast(dt)
            pat = []
            stride = 1
            for d in reversed(shape[1:]):
                pat.insert(0, [stride, d])
                stride *= d
            return V(bass.AP(a.tensor, a.offset, [list(a.ap[0])] + pat), [name])

        LAY = {}
        for lay, ntm in (("P", GMAX * 128), ("S", 128)):
            Ld = dict(NTM=ntm)
            Ld["l1T"] = carve(lay, lay + "l1T", [128, 2, ntm])
            Ld["l1vT"] = carve(lay, lay + "l1vT", [128, ntm])
            if lay == "P":
                Ld["QZ"] = carve(lay, "QZ", [128, 256])
            Ld["xT"] = carve(lay, lay + "xT", [128, NKC, ntm])
            Ld["hT"] = carve(lay, lay + "hT", [128, NKC, 1 + ntm + 1], BF16)
            Ld["brT"] = carve(lay, lay + "brT", [128, 12, ntm], BF16)
            Ld["qx"] = [carve(lay, lay + f"qx{i}", [128, ntm], BF16) for i in range(2)]
            if lay == "P":
                Ld["mkT"] = carve(lay, "mkT", [128, 4, XH, MEM], BF16)
                Ld["mvb"] = carve(lay, "mvb", [128, 4, 2, 512], BF16)
            else:
                Ld["hpS"] = carve(lay, "hpS", [128, NKC, 128], BF16)
                Ld["Z1"] = carve(lay, "Z1", [128, 2048])
                Ld["Z1b"] = carve(lay, "Z1b", [128, 2048], BF16)
                Ld["Z2b"] = carve(lay, "Z2b", [128, 2048], BF16)
                Ld["M0b"] = carve(lay, "M0b", [128, 16, 64], BF16)
                o_save = aoff[lay]
                Ld["S0q"] = [carve(lay, f"S0q{i}", [128, 512]).k("salias") for i in range(4)]
                aoff[lay] = o_save
                Ld["M0all"] = carve(lay, "M0all", [128, 16, 64]).k("salias")
                Ld["SBD"] = [carve(lay, f"SBD{i}", [128, 512]).k("salias") for i in range(2)]
                Ld["MBD"] = [carve(lay, f"MBD{i}", [128, 512]) for i in range(2)]
                Ld["KVB"] = [carve(lay, f"KVB{i}", [128, 2048], BF16) for i in range(3)]
                Ld["QXM"] = carve(lay, "QXM", [128, 2048], BF16)
            LAY[lay] = Ld
        print("arena use (KiB):", {k: v / 256 for k, v in aoff.items()})

        T128 = Rot("t128_", 16, [128, 128], r=True)
        T256 = Rot("t256_", 3, [128, 256], r=True)
        T512 = Rot("t512_", 4, [128, 512], r=True)
        T1K = Rot("t1k_", 2, [128, 1024], r=True)
        TB256 = Rot("tb256_", 2, [128, 256], BF16)
        TB512 = Rot("tb512_", 2, [128, 512], BF16)
        TS8 = Rot("ts8_", 8, [128, 24])
        TS8s = TS8
        OG = sb("OG", [128, 128]); LBC = sb("LBC", [128, 24])
        BIGV = [sb("bigv0", [128, D], r=True), sb("bigv1", [128, D], r=True)]
        MUC = sb("MUC", [128, 80])
        L1a = sb("L1a", [128, NKC, 128], BF16); L1b = sb("L1b", [128, NKC, 128], BF16)
        L1va = sb("L1va", [128, NKC, 128], BF16); L1vb = sb("L1vb", [128, NKC, 128], BF16)
        PVB = sb("PVB", [128, 8, 128])
        ZP = [sb(f"zp{i}", [128, 256], BF16) for i in range(2)]
        ZBK = sb("zbk", [128, 512], BF16)
        ZBKH = [sb(f"zbkh{i}", [128, 512], BF16) for i in range(2)]
        PMBD = sb("PMBD", [128, 128])
        for zb in ZP + [ZBK, PMBD] + ZBKH:
            mset(zb, 0.0)
        WM = Rot("wm_", 2, [128, NKC, 384], BF16)
        WM2 = Rot("wm2_", 2, [128, NKC, 384], BF16)
        WR = Rot("wr_", 2, [128, NKC, 512], BF16)
        WO = Rot("wo_", 1, [128, 12, 128], BF16)
        XIN = Rot("xin_", 1, [128, D])
        NRV = Rot("nrv_", 2, [128, 128], BF16)
        BT128 = Rot("b128_", 8, [128, 128], BF16)
        BT256 = Rot("b256_", 3, [128, 256], BF16)
        BT512 = Rot("b512_", 5, [128, 512], BF16)
        BT1K = Rot("b1k_", 2, [128, 1024], BF16)
        MK = [sb(f"mk{i}", [128, 512], BF16) for i in range(4)]
        MpB = [sb(f"MpB{j}", [128, 8, 64], BF16) for j in range(2)]
        for j in range(2):
            mset(MpB[j], 0.0)
        vfd = nc.dram_tensor("vfd", [17 * 128, D], BF16, kind="Internal").ap()

        for c0 in range(0, NCF, 1024):
            n = min(1024, NCF - c0)
            xin = XIN.get()
            dma(xin[:, 0:n], cf_d[:, c0:c0 + n])
            cp(cF[:, c0:c0 + n], xin[:, 0:n])

        def wview(w2d, c0, ncols):
            return w2d[:, c0:c0 + ncols].rearrange("(kc p) c -> p kc c", p=128)

        def xkey(li):
            return ("xT", li)

        def hkey(li):
            return ("hT", li)

        def mem_kv():
            LP = LAY["P"]
            mkT, mvb = LP["mkT"], LP["mvb"]
            memnT = V(LP["xT"].ap[:, :, 0:MEM], [xkey(0), xkey(1)])
            hmT = V(LP["hT"].ap[:, :, 1:1 + MEM], [hkey(0), hkey(1)])
            for mc in range(2):
                xin = XIN.get()
                dma(xin, mem[mc * 128:(mc + 1) * 128, :])
                ss = TS8.get()
                junk = T1K.get()
                act(junk, xin, AF.Square, acc=ss[:, 0:1])
                ts(ss[:, 1:2], ss[:, 0:1], 1.0 / D, MUL, NORM_EPS, ADD)
                act(ss[:, 2:3], ss[:, 1:2], AF.Ln)
                act(ss[:, 3:4], ss[:, 2:3], AF.Exp, scale=-0.5)
                xn = T1K.get()
                ts(xn, xin, ss[:, 3:4], MUL)
                for half in range(2):
                    ps = PS.get()
                    for q in range(4):
                        kc = half * 4 + q
                        tr(ps[:, q * 128:(q + 1) * 128], xn[:, kc * 128:(kc + 1) * 128], identF)
                    cp(memnT[:, half * 4:(half + 1) * 4, mc * 128:(mc + 1) * 128], ps.pat([[128, 4], [1, 128]]))
            for l in range(depth):
                if mks < 1:
                    break
                gcol = TS8.get()[:, 0:8]
                dma(gcol, mem_norm_g[l].rearrange("(kc p) -> p kc", p=128), slow=True)
                for kc in range(NKC):
                    ts(hmT[:, kc, :], memnT[:, kc, :], gcol[:, kc:kc + 1], MUL)
                if mks < 2:
                    continue
                for cb in range(2):
                    wb = WR.get()
                    dma(wb, wview(w_mem_kv[l], cb * 512, 512), eng="pool")
                    if mks < 3:
                        continue
                    if cb == 0:
                        for hx in range(XH):
                            ps = PS.get()
                            for kc in range(NKC):
                                mm(ps[:, 0:256], wb[:, kc, hx * 128:(hx + 1) * 128], hmT[:, kc, :], start=kc == 0, stop=kc == NKC - 1)
                            cp(mkT[:, l, hx, :], ps[:, 0:256])
                    if mks < 4:
                        continue
                    for mc in range(2):
                        ps = PS.get()
                        for kc in range(NKC):
                            mm(ps, hmT[:, kc, mc * 128:(mc + 1) * 128], wb[:, kc, :], start=kc == 0, stop=kc == NKC - 1)
                        stg = T512.get()
                        cp(stg, ps)
                        if mks >= 6:
                            dma((mkp if cb == 0 else mvp)[l, mc * 128:(mc + 1) * 128, :], stg)
                        if cb == 1 and mks >= 7:
                            cp(mvb[:, l, mc, :], ps, eng="dve")

        if memkv:
            mem_kv()
        mset(LAY["P"]["QZ"], 0.0)

        def process_group(gi, grp):
            ptiles = [t for t in grp if t != "s"]
            has_s = "s" in grp
            assert not (has_s and ptiles)
            L = LAY["S" if has_s else "P"]
            NTM = L["NTM"]; HS = NTM + 2
            xT, hT, brT, l1T, l1vT = L["xT"], L["hT"], L["brT"], L["l1T"], L["l1vT"]
            if has_s:
                hpS = L["hpS"]; Z1 = L["Z1"]; Z1b = L["Z1b"]; Z2b = L["Z2b"]; QXM = L["QXM"]
                P.barrier()
                for zb in [Z1, Z1b, Z2b, QXM] + L["SBD"] + L["MBD"]:
                    mset(zb, 0.0)
                QM = Z1
            else:
                mkT, mvb, QZ = L["mkT"], L["mvb"], L["QZ"]
            qxi = [0]
            npt = len(ptiles)
            first_group = bool(ptiles) and ptiles[0] == 0
            pchunks = []
            c = 0
            while c < npt * 128:
                n = min(512, npt * 128 - c)
                pchunks.append((c, n))
                c += n
            scol = npt * 128
            chunks = pchunks + ([(scol, 128)] if has_s else [])

            def tkeys(c0, n, fn):
                return tuple(fn(li) for li in range(c0 // 128, (c0 + n + 127) // 128))

            def xTv(kc, c0, n):
                return V(xT.ap[:, kc, c0:c0 + n], tkeys(c0, n, xkey))

            def xTt(li):
                return V(xT.ap[:, :, li * 128:(li + 1) * 128], [xkey(li)])

            def hTv(kc, c0, n):
                return V(hT.ap[:, kc, 1 + c0:1 + c0 + n], tkeys(c0, n, hkey))

            def hprevv(kc, c0, n):
                if has_s and c0 == scol:
                    return hpS[:, kc, :]
                ks = tkeys(c0, n, hkey) + ((hkey(c0 // 128 - 1),) if c0 > 0 else (("hT", "c0"),))
                return V(hT.ap[:, kc, c0:c0 + n], ks)

            def brv(kc, c0, n):
                return V(brT.ap[:, kc, c0:c0 + n], tkeys(c0, n, lambda li: ("br", kc, li)))

            for li, t in enumerate(grp):
                xin = XIN.get()
                dma(xin, xs if t == "s" else xp[t * 128:(t + 1) * 128, :])
                for half in range(2):
                    ps = PS.get()
                    for q in range(4):
                        kc = half * 4 + q
                        tr(ps[:, q * 128:(q + 1) * 128], xin[:, kc * 128:(kc + 1) * 128], identF)
                    cp(V(xT.ap[:, half * 4:(half + 1) * 4, li * 128:(li + 1) * 128], [xkey(li)]), ps.pat([[128, 4], [1, 128]]))

            def rms_tile(li):
                sq = T1K.get()
                act(sq.pat([[128, 8], [1, 128]]), xTt(li), AF.Square)
                ps = PS.get()
                for kc in range(NKC):
                    mm(ps[:, 0:128], onesF, sq[:, kc * 128:(kc + 1) * 128], start=kc == 0, stop=kc == NKC - 1)
                rs = T128.get()
                act(rs, ps[:, 0:128], AF.Ln, bias=NORM_EPS, scale=1.0 / D)
                rstd = T128.get()
                act(rstd, rs, AF.Exp, scale=-0.5)
                return rstd

            def hgrn_layer(l, j):
                og = OG
                dma(og, hg_og[j:j + 1, :].partition_broadcast(128))
                lbT = LBC[:, 0:8]; omT = LBC[:, 8:16]; nomT = LBC[:, 16:24]
                nom_bc = BIGV[0]
                if j == 0:
                    mset(omT, 1.0); mset(nomT, -1.0); mset(nom_bc, -1.0)
                else:
                    a0 = TS8.get()[:, 0:8]; a1 = TS8.get()[:, 0:8]
                    dma(a0, hg_lb[0].rearrange("(h p) -> p h", p=128), slow=True)
                    dma(a1, hg_lb[1].rearrange("(h p) -> p h", p=128), slow=True)
                    tt(a1, a1, a0, SUB)
                    act(lbT, a1, AF.Sigmoid)
                    ts(omT, lbT, -1.0, MUL, 1.0, ADD)
                    ts(nomT, lbT, 1.0, SUB)
                    b0 = XIN.get()
                    dma(b0, hg_lb[1:2, :].partition_broadcast(128))
                    cp(nom_bc, b0)
                    b0 = XIN.get()
                    dma(b0, hg_lb[0:1, :].partition_broadcast(128))
                    tt(nom_bc, nom_bc, b0, SUB)
                    act(nom_bc, nom_bc, AF.Sigmoid)
                    ts(nom_bc, nom_bc, 1.0, SUB)
                for h in range(8):
                    wblk = WM.get()
                    for q in range(3):
                        dma(wblk[:, :, q * 128:(q + 1) * 128], wview(w_in[l], q * 1024 + h * 128, 128), eng="pool")
                    Sph = V(Sp[j].ap[:, h, :], [("Sp", j, h)])
                    if has_s:
                        S0q = L["S0q"]
                        for q in range(4):
                            dma(S0q[q].pat([[128, 4], [1, 128]]), sth[j, 4 * q:4 * q + 4, h].rearrange("s k v -> k s v"))
                    hsegB = None
                    for li, t in enumerate(grp):
                        samp = (t == "s")
                        c0 = li * 128
                        hsegA = P.begin_seg()
                        if hsegB is not None and PIPE:
                            P.merges.append((hsegB, hsegA))
                        IU = C("mb16")[:, 256:384] if samp else C("iu2")
                        BO = C("bo16") if samp else C("bo2")
                        psq = PS.get()
                        for kc in range(NKC):
                            mm(psq[:, 0:128], wblk[:, kc, 0:128], hTv(kc, c0, 128), start=kc == 0, stop=kc == NKC - 1)
                        for kc in range(NKC):
                            mm(psq[:, 128:256], wblk[:, kc, 128:256], hTv(kc, c0, 128), start=kc == 0, stop=kc == NKC - 1)
                        psfv = PS.get()
                        for kc in range(NKC):
                            mm(psfv[:, 0:256], hTv(kc, c0, 128), wblk[:, kc, 128:384], start=kc == 0, stop=kc == NKC - 1)
                        qsg = T128.get(); act(qsg, psq[:, 0:128], AF.Sigmoid)
                        qs = T128.get(); tt(qs, qsg, psq[:, 0:128], MUL)
                        sgT = T128.get(); act(sgT, psq[:, 128:256], AF.Sigmoid)
                        kT = T128.get(); ts(kT, sgT, nomT[:, h:h + 1], MUL, omT[:, h:h + 1], ADD)
                        sg = T128.get(); act(sg, psfv[:, 0:128], AF.Sigmoid)
                        kt = T128.get(); stt(kt, sg, -1.0, nom_bc[:, h * 128:(h + 1) * 128], ADD, MUL)
                        g = T128.get(); act(g, kt, AF.Ln, bias=1.0, scale=-1.0)
                        vt = T128.get(); cp(vt, psfv[:, 128:256])
                        ps3 = PS.get()
                        mm(ps3[:, 0:128], IU, g)
                        mm(ps3[:, 128:256], g, IU)
                        mm(ps3[:, 256:384], BO, g)
                        eb = T128.get(); act(eb, ps3[:, 128:256], AF.Exp)
                        enb = T128.get(); act(enb, ps3[:, 128:256], AF.Exp, scale=-1.0)
                        qtT = T128.get(); tt(qtT, qs, eb, MUL)
                        ktT = T128.get(); tt(ktT, kT, enb, MUL)
                        bsb = T128.get(); cp(bsb, ps3[:, 0:128])
                        dd = T128.get(); tt(dd, ps3[:, 256:384], bsb, SUB)
                        ed = T128.get(); act(ed, dd, AF.Exp)
                        psA = PS.get()
                        mm(psA[:, 0:128], ktT, qtT)
                        att = T128.get(); tt(att, psA[:, 0:128], IU, MUL)
                        P.end_seg()
                        hsegB = P.begin_seg()
                        psO = PS.get()
                        mm(psO[:, 0:128], att, vt, start=True, stop=False)
                        if not samp:
                            tt(QZ.pat([[192, 2], [1, 64]]), qs.pat([[64, 2], [1, 64]]), eb.pat([[64, 2], [1, 64]]), MUL)
                            kh0 = T128.get(); stt(kh0, kt, C("cm0"), ed, MUL, MUL)
                            kh1 = T128.get(); stt(kh1, kt, C("cm1"), ed, MUL, MUL)
                            psS = PS.get()
                            mm(psS[:, 0:128], kh0, vt)
                            mm(psS[:, 128:256], kh1, vt)
                            mm(psO[:, 0:128], QZ[:, 0:128], Sph, start=False, stop=False)
                            S1 = T128.get(); stt(S1, Sph, eb[:, 63:64], psS[:, 0:128], MUL, ADD)
                            mm(psO[:, 0:128], QZ[:, 128:256], S1, start=False, stop=True)
                            stt(Sph, S1, eb[:, 127:128], psS[:, 128:256], MUL, ADD)
                        else:
                            tt(QM.pat([[136, 16], [1, 8]]), qs.pat([[8, 16], [1, 8]]), eb.pat([[8, 16], [1, 8]]), MUL)
                            for s in range(NS):
                                mm(psO[:, 0:128], QM[:, s * 128:(s + 1) * 128], S0q[s // 4][:, (s % 4) * 128:(s % 4 + 1) * 128], start=False, stop=(s == NS - 1))
                            kh = T128.get(); tt(kh, kt, ed, MUL)
                            for q in range(4):
                                vm = T512.get()
                                tt(vm.pat([[128, 4], [1, 128]]), vt.pat([[0, 4], [1, 128]]), C("sel16").pat([[1, 4], [0, 128]], off=4 * q), MUL)
                                psW = PS.get()
                                mm(psW, kh, vm)
                                tmp = T512.get()
                                tt(tmp.pat([[128, 4], [1, 128]]), S0q[q].pat([[128, 4], [1, 128]]),
                                   eb.pat([[8, 4], [0, 128]], off=7 + 32 * q), MUL)
                                Sn = T512.get()
                                tt(Sn, tmp, psW, ADD)
                                dma(shs[j, 4 * q:4 * q + 4, h].rearrange("s k v -> k s v"), Sn.pat([[128, 4], [1, 128]]))
                        ss = TS8.get()
                        junk = T128.get()
                        act(junk, psO[:, 0:128], AF.Square, acc=ss[:, 0:1])
                        ts(ss[:, 1:2], ss[:, 0:1], 1.0 / 128, MUL, NORM_EPS, ADD)
                        act(ss[:, 2:3], ss[:, 1:2], AF.Ln)
                        act(ss[:, 3:4], ss[:, 2:3], AF.Exp, scale=-0.5)
                        on = T128.get(); stt(on, psO[:, 0:128], ss[:, 3:4], og, MUL, MUL)
                        psT = PS.get()
                        tr(psT[:, 0:128], on, identF)
                        cp(brv(h, c0, 128), psT[:, 0:128])
                        P.end_seg()
                if first_last_prompt_group:
                    for h in range(8):
                        dma(shp[j, h], V(Sp[j].ap[:, h, :], [("Sp", j, h)]))

            def rest_layer(l):
                if 2 not in phases:
                    return
                wq = WR.get()
                dma(wq, wview(w_in[l], 3072, 512), eng="pool")
                for hx in range(XH):
                    qxT = L['qx'][qxi[0] % 2]; qxi[0] += 1
                    for (c0, n) in chunks:
                        ps = PS.get()
                        for kc in range(NKC):
                            mm(ps[:, 0:n], wq[:, kc, hx * 128:(hx + 1) * 128], hTv(kc, c0, n), start=kc == 0, stop=kc == NKC - 1)
                        cp(qxT[:, c0:c0 + n], ps[:, 0:n])
                    if has_s:
                        KVB = L["KVB"]
                        kvi = [0]
                        def kvget():
                            b = KVB[kvi[0] % 3]; kvi[0] += 1
                            return b
                        mkTs = []; vbs = []
                        for hf in range(2):
                            kb = kvget()
                            dma(kb.pat([[256, 8], [128, 2], [1, 128]]), ck[l][8 * hf:8 * hf + 8, :, hx * 128:(hx + 1) * 128].rearrange("s (mc p) d -> p s mc d", p=128), eng="pool")
                            mk_ = kvget()
                            for b4 in range(2):
                                psb = bfv(PS.get())
                                for q in range(8):
                                    idx = b4 * 8 + q
                                    tr(psb[:, q * 128:(q + 1) * 128], kb[:, idx * 128:(idx + 1) * 128], identB)
                                cp(mk_[:, b4 * 1024:(b4 + 1) * 1024], psb)
                            mkTs.append(mk_)
                    for li, t in enumerate(grp):
                        samp = (t == "s")
                        c0 = li * 128
                        ps = PS.get()
                        if not samp:
                            mm(ps[:, 0:256], qxT[:, c0:c0 + 128], mkT[:, l, hx, :])
                        else:
                            cp(QXM.pat([[136, 16], [1, 8]]), qxT[:, c0:c0 + 128].pat([[8, 16], [1, 8]]), eng="dve")
                            for s in range(NS):
                                mm(ps[:, 0:256], QXM[:, s * 128:(s + 1) * 128], mkTs[s // 8][:, (s % 8) * 256:(s % 8 + 1) * 256], start=(s == 0), stop=(s == NS - 1))
                        ss = TS8.get()
                        red(ss[:, 0:1], ps[:, 0:256], MAX, negate=True)
                        ts(ss[:, 1:2], ss[:, 0:1], SC_ATT, MUL)
                        pe_ = T512.get()
                        act(pe_[:, 0:256], ps[:, 0:256], AF.Exp, bias=ss[:, 1:2], scale=SC_ATT, acc=ss[:, 2:3])
                        recip(ss[:, 3:4], ss[:, 2:3])
                        pn = TB256.get()
                        ts(pn, pe_[:, 0:256], ss[:, 3:4], MUL)
                        psb = bfv(PS.get())
                        tr(psb[:, 0:128], pn[:, 0:128], identB)
                        tr(psb[:, 128:256], pn[:, 128:256], identB)
                        pT = TB256.get()
                        cp(pT, psb[:, 0:256])
                        psO = PS.get()
                        if not samp:
                            for mc in range(2):
                                mm(psO[:, 0:128], mvb[:, l, mc, hx * 128:(hx + 1) * 128], pT[:, mc * 128:(mc + 1) * 128], start=mc == 0, stop=mc == 1)
                        else:
                            for hf in range(2):
                                vb_ = kvget()
                                dma(vb_.pat([[256, 8], [128, 2], [1, 128]]), cv[l][8 * hf:8 * hf + 8, :, hx * 128:(hx + 1) * 128].rearrange("s (mc p) d -> p s mc d", p=128), eng="pool")
                                vbs.append(vb_)
                            for s in range(NS):
                                for mc in range(2):
                                    vb = vbs[s // 8]
                                    mm(psO[:, s * 8:(s + 1) * 8], vb[:, ((s % 8) * 2 + mc) * 128:((s % 8) * 2 + mc + 1) * 128],
                                       pT[:, mc * 128 + s * 8:mc * 128 + (s + 1) * 8], start=mc == 0, stop=mc == 1)
                        cp(brv(8 + hx, c0, 128), psO[:, 0:128])
                if 3 not in phases:
                    return
                for gb in range(3):
                    wg = WR.get()
                    dma(wg, wview(w_in[l], 3584 + gb * 512, 512), eng="pool")
                    for sub in range(4):
                        bi = gb * 4 + sub
                        for (c0, n) in chunks:
                            ps = PS.get()
                            for kc in range(NKC):
                                mm(ps[:, 0:n], wg[:, kc, sub * 128:(sub + 1) * 128], hTv(kc, c0, n), start=kc == 0, stop=kc == NKC - 1)
                            sgt = TB512.get()
                            act(sgt[:, 0:n], ps[:, 0:n], AF.Sigmoid)
                            tt(sgt[:, 0:n], sgt[:, 0:n], ps[:, 0:n], MUL)
                            tt(brv(bi, c0, n), brv(bi, c0, n), sgt[:, 0:n], MUL)
                if 4 not in phases:
                    return
                for ob in range(NKC):
                    wo = WO.get()
                    dma(wo, w_out[l][:, ob * 128:(ob + 1) * 128].rearrange("(kc p) c -> p kc c", p=128), eng="pool")
                    for (c0, n) in chunks:
                        ps = PS.get()
                        for kc in range(12):
                            mm(ps[:, 0:n], wo[:, kc, :], brv(kc, c0, n), start=kc == 0, stop=kc == 11)
                        tt(xTv(ob, c0, n), xTv(ob, c0, n), ps[:, 0:n], ADD)

            def rwkv_layer(l, j):
                m = j - 1
                cfgn = "16" if has_s else "1"
                MB = C("mb" + cfgn); SL = C("sl" + cfgn)
                TIC = C("tic" + cfgn); TSC = C("tsc" + cfgn); BOC = C("boc" + cfgn)
                LEV = 3 if has_s else 7
                if levo is not None:
                    LEV = levo
                muT = MUC[:, 0:40].pat([[8, 5], [1, 8]]); omuT = MUC[:, 40:80].pat([[8, 5], [1, 8]])
                dma(muT, rw_mu[j].rearrange("k (kc p) -> p k kc", p=128), slow=True)
                ts(omuT, muT, -1.0, MUL, 1.0, ADD)
                raw = WR.get()
                dma(raw[:, :, 0:64], rw_w1[j].rearrange("(kc p) c -> p kc c", p=128), eng="pool")
                dma(raw[:, :, 64:128], rw_a1[j].rearrange("(kc p) c -> p kc c", p=128), eng="pool")
                if j >= 1:
                    mset(raw[:, :, 160:256], 0.0)
                    dma(raw[:, :, 128:160], rw_v1[m].rearrange("(kc p) c -> p kc c", p=128), eng="pool")
                for (c_lo, kind) in ((0, 1), (64, 4)):
                    src = raw[:, :, c_lo:c_lo + 64]
                    tt(L1a[:, :, c_lo:c_lo + 64], src, omuT[:, kind, :].pat([[1, 8], [0, 64]]), MUL)
                    tt(L1b[:, :, c_lo:c_lo + 64], src, muT[:, kind, :].pat([[1, 8], [0, 64]]), MUL)
                if j >= 1:
                    src = raw[:, :, 128:256]
                    tt(L1va, src, omuT[:, 3, :].pat([[1, 8], [0, 128]]), MUL)
                    tt(L1vb, src, muT[:, 3, :].pat([[1, 8], [0, 128]]), MUL)
                W2 = BIGV[0]; V2 = BIGV[1]
                xin = XIN.get()
                dma(xin[0:64, :], rw_w2[j]); dma(xin[64:128, :], rw_a2[j])
                cp(W2, xin)
                if j >= 1:
                    xin = XIN.get()
                    mset(xin, 0.0)
                    dma(xin[0:32, :], rw_v2[m])
                    cp(V2, xin, eng="dve")
                for (c0, n) in chunks:
                    ps = PS.get()
                    for kc in range(NKC):
                        mm(ps[:, 0:n], L1a[:, kc, :], hTv(kc, c0, n), start=kc == 0, stop=False)
                    for kc in range(NKC):
                        mm(ps[:, 0:n], L1b[:, kc, :], hprevv(kc, c0, n), start=False, stop=kc == NKC - 1)
                    th = T512.get()
                    act(th[:, 0:n], ps[:, 0:n], AF.Tanh)
                    ts(V(l1T.ap[:, 0, c0:c0 + n], tkeys(c0, n, lambda li: ("l1", li))), th[:, 0:n], C("hw"), MUL)
                    ts(V(l1T.ap[:, 1, c0:c0 + n], tkeys(c0, n, lambda li: ("l1", li))), ps[:, 0:n], C("ha"), MUL)
                    if j >= 1:
                        ps = PS.get()
                        for kc in range(NKC):
                            mm(ps[:, 0:n], L1va[:, kc, :], hTv(kc, c0, n), start=kc == 0, stop=False)
                        for kc in range(NKC):
                            mm(ps[:, 0:n], L1vb[:, kc, :], hprevv(kc, c0, n), start=False, stop=kc == NKC - 1)
                        cp(V(l1vT.ap[:, c0:c0 + n], tkeys(c0, n, lambda li: ("l1v", li))), ps[:, 0:n])

                for pr in range(8):
                    cc = pr * 128
                    wa = WM.get(); wb = WM2.get()
                    for q in range(3):
                        dma(wa[:, :, q * 128:(q + 1) * 128], wview(w_in[l], q * 1024 + cc, 128), eng="pool")
                    for q, kind in enumerate((0, 2, 3)):
                        tt(wb[:, :, q * 128:(q + 1) * 128], wa[:, :, q * 128:(q + 1) * 128], muT[:, kind, :].pat([[1, 8], [0, 128]]), MUL)
                    tt(wa, wa, wb, SUB)
                    pv = PVB
                    vecs = [rw_w0[j:j + 1], rw_a0[j:j + 1], (rw_v0[m:m + 1] if j >= 1 else rw_a0[j:j + 1]), rw_kk[j:j + 1], rw_ka[j:j + 1], rw_rk[j:j + 1], rw_lg[j:j + 1], rw_lb[j:j + 1]]
                    for k8, vv in enumerate(vecs):
                        dma(pv[:, k8, :], vv[:, cc:cc + 128].partition_broadcast(128))
                    Mpp = V(Mp[j].ap[:, pr, :], [("Mp", j, pr)], True)
                    MppB = V(MpB[j].ap[:, pr, :], [("MpB", j, pr)])
                    if has_s:
                        M0b = L["M0b"]
                        M0all = L["M0all"]
                        for q in range(4):
                            SBD = L["SBD"][q % 2]
                            for hh in range(2):
                                dma(SBD[hh * 64:(hh + 1) * 64, :].pat([[128, 4], [1, 64]], off=hh * 64),
                                    strw[j, 4 * q:4 * q + 4, 2 * pr + hh].rearrange("s i k -> i s k"))
                            ps = PS.get()
                            for s4 in range(4):
                                tr(ps[:, s4 * 128:(s4 + 1) * 128], SBD[:, s4 * 128:(s4 + 1) * 128], identF)
                            for hh in range(2):
                                cp(M0all[hh * 64:(hh + 1) * 64, 4 * q:4 * q + 4, :], ps[hh * 64:(hh + 1) * 64, :].pat([[128, 4], [1, 64]], off=hh * 64),
                                   eng=("act" if hh == 0 else "dve"))
                                cp(M0b[hh * 64:(hh + 1) * 64, 4 * q:4 * q + 4, :], ps[hh * 64:(hh + 1) * 64, :].pat([[128, 4], [1, 64]], off=hh * 64),
                                   eng=("dve" if hh == 0 else "act"))
                    segB_prev = None
                    for li, t in enumerate(grp):
                        samp = (t == "s")
                        c0 = li * 128
                        gt = 16 if samp else t
                        segA = P.begin_seg()
                        if segB_prev is not None and PIPE:
                            P.merges.append((segB_prev, segA))
                        psR = PS.get()
                        for kc in range(NKC):
                            mm(psR[:, 0:384], hTv(kc, c0, 128), wa[:, kc, :], start=kc == 0, stop=False)
                        for kc in range(NKC):
                            mm(psR[:, 0:384], hprevv(kc, c0, 128), wb[:, kc, :], start=False, stop=kc == NKC - 1)
                        rP, kP, vP = psR[:, 0:128], psR[:, 128:256], psR[:, 256:384]
                        l1k = ("l1", li)
                        psL = PS.get()
                        mm(psL[:, 0:128], V(l1T.ap[:, 0, c0:c0 + 128], [l1k]), W2[:, cc:cc + 128])
                        mm(psL[:, 128:256], V(l1T.ap[:, 1, c0:c0 + 128], [l1k]), W2[:, cc:cc + 128])
                        if j >= 1:
                            mm(psL[:, 256:384], V(l1vT.ap[:, c0:c0 + 128], [("l1v", li)]), V2[:, cc:cc + 128])
                        npre = 384 if j >= 1 else 256
                        pre = T512.get(); tt(pre[:, 0:npre], psL[:, 0:npre], pv.pat([[1, npre]]), ADD)
                        sgs = T512.get(); act(sgs[:, 0:npre], pre[:, 0:npre], AF.Sigmoid)
                        sg = sgs[:, 0:128]; ag = sgs[:, 128:256]
                        Vt = BT128.get()
                        vkey = ("vfd", gt, pr)
                        if j == 0:
                            cp(Vt, vP)
                            P.op("sp", (lambda e, o=vfd[gt * 128:(gt + 1) * 128, cc:cc + 128], i_=Vt.ap: e.dma_start(out=o, in_=i_)),
                                 reads=list(Vt.keys), writes=[vkey], dma=True)
                        else:
                            vf = NRV.get()
                            P.op("sp", (lambda e, i_=vfd[gt * 128:(gt + 1) * 128, cc:cc + 128], o=vf.ap: e.dma_start(out=o, in_=i_)),
                                 reads=[vkey], writes=list(vf.keys), dma=True)
                            vg = sgs[:, 256:384]
                            d1 = T128.get(); tt(d1, vf, vP, SUB)
                            tt(d1, d1, vg, MUL)
                            tt(Vt, d1, vP, ADD)
                        kk = T128.get(); tt(kk, kP, pv[:, 3, :], MUL)
                        sq = T128.get(); tt(sq, kk, kk, MUL)
                        sm = TS8.get()
                        red(sm[:, 0:2], sq.pat([[64, 2], [1, 64]]), ADD)
                        act(sm[:, 2:4], sm[:, 0:2], AF.Ln)
                        act(sm[:, 4:6], sm[:, 2:4], AF.Exp, scale=-0.5)
                        ts(sm[:, 4:6], sm[:, 4:6], 1e12, ALU.min)
                        kkn = T128.get()
                        tt(kkn.pat([[64, 2], [1, 64]]), kk.pat([[64, 2], [1, 64]]), sm[:, 4:6].pat([[1, 2], [0, 64]]), MUL)
                        t1 = T128.get(); stt(t1, ag, -1.0, pv[:, 4, :], ADD, MUL)
                        BK = T256.get()
                        kp = BK[:, 128:256]; stt(kp, t1, 1.0, kP, ADD, MUL)
                        bb = BK[:, 0:128]; tt(bb, kkn, ag, MUL)
                        t2 = T128.get(); tt(t2, rP, kp, MUL)
                        tt(t2, t2, pv[:, 5, :], MUL)
                        red(sm[:, 6:8], t2.pat([[64, 2], [1, 64]]), ADD)
                        psC = PS.get()
                        mm(psC[:, 0:128], TIC, sg)
                        mm(psC[:, 128:256], TSC, sg)
                        mm(psC[:, 256:384], BOC, sg)
                        nsq = 16 if samp else 1
                        nsq = 16 if samp else 2
                        mm(psC[:, 384:384 + nsq], sg, (C("sel16c") if samp else C("negc")))
                        EE = T512.get(); act(EE[:, 0:384], psC[:, 0:384], AF.Exp)
                        E1 = EE[:, 0:128]; E3 = EE[:, 128:256]; E5 = EE[:, 256:384]
                        E2 = T128.get(); act(E2, psC[:, 0:128], AF.Exp, scale=-1.0)
                        gC = TS8s.get(); act(gC[:, 0:nsq], psC[:, 384:384 + nsq], AF.Exp)
                        dg = [[192, 2], [1, 64]]; nd = [[64, 2], [1, 64]]
                        Am, Rm = ZP; Bm = ZBK[:, 0:256]; Km = ZBK[:, 256:512]
                        ZH = ZBKH[li % 2]; BHm = ZH[:, 0:256]; KHm = ZH[:, 256:512]
                        dg2 = [[256, 2], [192, 2], [1, 64]]
                        stt(Am.pat(dg), kkn.pat(nd), -1.0, E3.pat(nd), MUL, MUL)
                        tt(ZBK.pat(dg2), BK.pat([[128, 2], [64, 2], [1, 64]]), E2.pat([[0, 2], [64, 2], [1, 64]]), MUL)
                        tt(Rm.pat(dg), rP.pat(nd), E1.pat(nd), MUL)
                        tt(ZH.pat(dg2), ZBK.pat(dg2), E5.pat([[0, 2], [64, 2], [1, 64]]), MUL)
                        XT = BT1K.get()
                        psb = bfv(PS.get())
                        for q, Xm in enumerate((Am, Am, Bm, Bm, Km, Km, Rm, Rm)):
                            hh = q % 2
                            tr(psb[:, q * 128:(q + 1) * 128], Xm[:, hh * 128:(hh + 1) * 128], identB)
                        cp(XT[:, 0:512], psb[:, 0:512]); cp(XT[:, 512:1024], psb[:, 512:1024], eng="dve")
                        AT = lambda hh: XT[:, (0 + hh) * 128:(1 + hh) * 128]
                        BT = lambda hh: XT[:, (2 + hh) * 128:(3 + hh) * 128]
                        KT = lambda hh: XT[:, (4 + hh) * 128:(5 + hh) * 128]
                        RT = lambda hh: XT[:, (6 + hh) * 128:(7 + hh) * 128]
                        Mk = []; XY = BT512.get(); WT = BT256.get()
                        for hh in range(2):
                            psM = PS.get()
                            mm(psM[:, 0:128], BT(hh), AT(hh))
                            mm(psM[:, 128:256], KT(hh), AT(hh))
                            mm(psM[:, 256:384], BT(hh), RT(hh))
                            mm(psM[:, 384:512], KT(hh), RT(hh))
                            mk_ = MK[2 * (li % 2) + hh]; tt(mk_, psM, MB, MUL)
                            Mk.append(mk_)
                            psN = PS.get()
                            mm(psN[:, 0:128], AT(hh), BT(hh))
                            tt(XY[:, hh * 256 + 128:hh * 256 + 256], psN[:, 0:128], SL, MUL)
                            tt(WT[:, hh * 128:(hh + 1) * 128], mk_[:, 0:128], identF, ADD)
                        for lev in range(1, LEV):
                            psV = PS.get()
                            for hh in range(2):
                                Xp = Mk[hh][:, 0:128] if lev == 1 else XY[:, hh * 256:hh * 256 + 128]
                                Yp = XY[:, hh * 256 + 128:hh * 256 + 256]
                                mm(psV[:, hh * 256 + 128:hh * 256 + 256], Xp, Yp)
                                if lev < LEV - 1:
                                    mm(psV[:, hh * 256:hh * 256 + 128], Yp, Xp)
                            XYn = BT512.get()
                            if lev < LEV - 1:
                                cp(XYn, psV)
                            else:
                                cp(XYn.pat([[256, 2], [1, 128]], off=128), psV.pat([[256, 2], [1, 128]], off=128))
                            psW = PS.get()
                            for hh in range(2):
                                mm(psW[:, hh * 128:(hh + 1) * 128], XYn[:, hh * 256 + 128:hh * 256 + 256], WT[:, hh * 128:(hh + 1) * 128])
                            WTn = BT256.get()
                            tt(WTn, psW[:, 0:256], WT, ADD)
                            XY = XYn; WT = WTn
                        P.end_seg()
                        segB_prev = P.begin_seg()
                        if samp:
                            cp(Z1b.pat([[136, 16], [1, 8]]), AT(0).pat([[8, 16], [1, 8]]), eng="pool")
                            cp(Z2b.pat([[136, 16], [1, 8]]), AT(1).pat([[8, 16], [1, 8]]), eng="pool")
                        psP = PS.get()
                        for hh in range(2):
                            o_ = psP[:, hh * 64:(hh + 1) * 64]
                            if not samp:
                                mm(o_, AT(hh), MppB, start=True, stop=False)
                            else:
                                ZZ = Z1b if hh == 0 else Z2b
                                for s in range(NS):
                                    mm(o_, ZZ[:, s * 128:(s + 1) * 128], M0b[:, s, :], start=(s == 0), stop=False)
                            mm(o_, Mk[hh][:, 128:256], Vt[:, hh * 64:(hh + 1) * 64], start=False, stop=True)
                        P1 = BT128.get(); cp(P1, psP[:, 0:128])
                        psU = PS.get()
                        for hh in range(2):
                            mm(psU[:, hh * 64:(hh + 1) * 64], WT[:, hh * 128:(hh + 1) * 128], P1[:, hh * 64:(hh + 1) * 64])
                        U = BT128.get(); cp(U, psU[:, 0:128], eng="dve")
                        if samp:
                            cp(Z1b.pat([[136, 16], [1, 8]]), RT(0).pat([[8, 16], [1, 8]]), eng="pool")
                            cp(Z2b.pat([[136, 16], [1, 8]]), RT(1).pat([[8, 16], [1, 8]]), eng="pool")
                        psY = PS.get()
                        for hh in range(2):
                            o_ = psY[:, hh * 64:(hh + 1) * 64]
                            if not samp:
                                mm(o_, RT(hh), MppB, start=True, stop=False)
                            else:
                                ZZ = Z1b if hh == 0 else Z2b
                                for s in range(NS):
                                    mm(o_, ZZ[:, s * 128:(s + 1) * 128], M0b[:, s, :], start=(s == 0), stop=False)
                            mm(o_, Mk[hh][:, 256:384], U[:, hh * 64:(hh + 1) * 64], start=False, stop=False)
                            mm(o_, Mk[hh][:, 384:512], Vt[:, hh * 64:(hh + 1) * 64], start=False, stop=True)
                        if not samp:
                            psS = PS.get()
                            mm(psS[:, 0:64], BHm[:, 0:128], U[:, 0:64], start=True, stop=False)
                            mm(psS[:, 0:64], BHm[:, 128:256], U[:, 64:128], start=False, stop=False)
                            mm(psS[:, 0:64], KHm[:, 0:128], Vt[:, 0:64], start=False, stop=False)
                            mm(psS[:, 0:64], KHm[:, 128:256], Vt[:, 64:128], start=False, stop=True)
                            stt(Mpp, Mpp, gC[:, 0:1], psS[:, 0:64], MUL, ADD)
                            cp(MppB, Mpp)
                        else:
                            for q in range(2):
                                Uw = [BT512.get(), BT512.get()]; Vw = [BT512.get(), BT512.get()]
                                for hh in range(2):
                                    tt(Uw[hh].pat([[64, 8], [1, 64]]), U[:, hh * 64:(hh + 1) * 64].pat([[0, 8], [1, 64]]),
                                       C("sel16").pat([[1, 8], [0, 64]], off=8 * q), MUL)
                                    tt(Vw[hh].pat([[64, 8], [1, 64]]), Vt[:, hh * 64:(hh + 1) * 64].pat([[0, 8], [1, 64]]),
                                       C("sel16").pat([[1, 8], [0, 64]], off=8 * q), MUL, eng="pool")
                                psS = PS.get()
                                mm(psS, BHm[:, 0:128], Uw[0], start=True, stop=False)
                                mm(psS, BHm[:, 128:256], Uw[1], start=False, stop=False)
                                mm(psS, KHm[:, 0:128], Vw[0], start=False, stop=False)
                                mm(psS, KHm[:, 128:256], Vw[1], start=False, stop=True)
                                tmp = T512.get()
                                tt(tmp.pat([[64, 8], [1, 64]]), M0all[:, 8 * q:8 * q + 8, :], gC[:, 8 * q:8 * q + 8].pat([[1, 8], [0, 64]]), MUL)
                                Mn = T512.get()
                                tt(Mn, tmp, psS, ADD)
                                for q2 in range(2):
                                    MBD = L["MBD"][q2]
                                    for hh in range(2):
                                        cp(MBD[hh * 64:(hh + 1) * 64, :].pat([[128, 4], [1, 64]], off=hh * 64),
                                           Mn[hh * 64:(hh + 1) * 64, q2 * 256:(q2 + 1) * 256].pat([[64, 4], [1, 64]]), eng=("act" if hh == 0 else "pool"))
                                    ps = PS.get()
                                    for s4 in range(4):
                                        tr(ps[:, s4 * 128:(s4 + 1) * 128], MBD[:, s4 * 128:(s4 + 1) * 128], identF)
                                    So = T256.get()
                                    for hh in range(2):
                                        cp(So[hh * 64:(hh + 1) * 64, :].pat([[64, 4], [1, 64]]),
                                           ps[hh * 64:(hh + 1) * 64, :].pat([[128, 4], [1, 64]], off=hh * 64), eng=("act" if hh == 0 else "dve"))
                                    sb0 = 8 * q + 4 * q2
                                    for hh in range(2):
                                        dma(srs[j, sb0:sb0 + 4, 2 * pr + hh].rearrange("s i k -> i s k"),
                                            So[hh * 64:(hh + 1) * 64, :].pat([[64, 4], [1, 64]]))
                        g2 = [[64, 2], [1, 64]]
                        red(sm[:, 8:10], psY[:, 0:128].pat(g2), ADD)
                        ysq = T128.get(); act(ysq, psY[:, 0:128], AF.Square)
                        red(sm[:, 10:12], ysq.pat(g2), ADD)
                        ts(sm[:, 12:14], sm[:, 8:10], 1.0 / 64, MUL)
                        tt(sm[:, 14:16], sm[:, 12:14], sm[:, 12:14], MUL)
                        stt(sm[:, 16:18], sm[:, 10:12], 1.0 / 64, sm[:, 14:16], MUL, SUB)
                        act(sm[:, 18:20], sm[:, 16:18], AF.Ln, bias=LNX_EPS, scale=1.0)
                        act(sm[:, 20:22], sm[:, 18:20], AF.Exp, scale=-0.5)
                        yc = T128.get()
                        tt(yc.pat(g2), psY[:, 0:128].pat(g2), sm[:, 12:14].pat([[1, 2], [0, 64]]), SUB)
                        tt(yc.pat(g2), yc.pat(g2), sm[:, 20:22].pat([[1, 2], [0, 64]]), MUL)
                        tt(yc, yc, pv[:, 6, :], MUL)
                        tt(yc, yc, pv[:, 7, :], ADD)
                        yb = T128.get()
                        tt(yb.pat(g2), Vt.pat(g2), sm[:, 6:8].pat([[1, 2], [0, 64]]), MUL)
                        tt(yc, yc, yb, ADD)
                        psT = PS.get()
                        tr(psT[:, 0:128], yc, identF)
                        cp(brv(pr, c0, 128), psT[:, 0:128])
                        P.end_seg()
                if first_last_prompt_group:
                    for pr in range(8):
                        MBD = PMBD
                        Mpp = V(Mp[j].ap[:, pr, :], [("Mp", j, pr)])
                        for hh in range(2):
                            cp(MBD[hh * 64:(hh + 1) * 64, hh * 64:(hh + 1) * 64], Mpp[hh * 64:(hh + 1) * 64, :], eng=("act" if hh == 0 else "pool"))
                        ps = PS.get()
                        tr(ps[:, 0:128], MBD, identF)
                        So = T128.get()
                        for hh in range(2):
                            cp(So[hh * 64:(hh + 1) * 64, 0:64], ps[hh * 64:(hh + 1) * 64, hh * 64:(hh + 1) * 64], eng=("act" if hh == 0 else "dve"))
                        for hh in range(2):
                            dma(srp[j, 2 * pr + hh], So[hh * 64:(hh + 1) * 64, 0:64])

            first_last_prompt_group = (15 in ptiles)
            for l in range(depth):
                is_rw = (l % 2 == 1)
                j = l // 2
                gcol = TS8.get()[:, 0:8]
                dma(gcol, norm_g[l].rearrange("(kc p) -> p kc", p=128), slow=True)
                if is_rw:
                    if first_group:
                        mset(V(hT.ap[:, :, 0:1], [("hT", "c0")]), 0.0)
                    elif npt > 0:
                        cp(V(hT.ap[:, :, 0], [("hT", "c0")]), hcar[j], eng="pool")
                for li, t in enumerate(grp):
                    rstd = rms_tile(li)
                    for kc in range(NKC):
                        stt(V(hT.ap[:, kc, 1 + li * 128:1 + (li + 1) * 128], [hkey(li)]), xTv(kc, li * 128, 128), gcol[:, kc:kc + 1], rstd, MUL, MUL)
                    if is_rw and t == 15:
                        hl = TS8.get()[:, 0:8]
                        ts(hl, V(xT.ap[:, :, li * 128 + 127], [xkey(li)]), rstd[:, 127:128], MUL)
                        tt(hl, hl, gcol, MUL)
                        dma(ssp[j].rearrange("(kc p) -> p kc", p=128), hl, slow=True)
                    if is_rw and t == "s":
                        hl = T128.get()
                        hl3 = hl.pat([[16, 8], [1, 16]])
                        xv = xTt(li).pat([[NTM, 8], [8, 16]], off=7)
                        rv = rstd.pat([[0, 8], [8, 16]], off=7)
                        tt(hl3, xv, rv, MUL)
                        tt(hl3, hl3, gcol.pat([[1, 8], [0, 16]]), MUL)
                        ps = PS.get()
                        tr(ps[:, 0:128], hl, identF)
                        ho = T128.get()
                        cp(ho, ps[:, 0:128])
                        for kc in range(NKC):
                            dma(sss[j, :, kc * 128:(kc + 1) * 128], ho[kc * 16:(kc + 1) * 16, :])
                        stin = XIN.get()
                        mset(stin, 0.0)
                        dma(stin[0:16, :], sts[j])
                        for half in range(2):
                            ps = PS.get()
                            for q in range(4):
                                kc = half * 4 + q
                                tr(ps[:, q * 128:(q + 1) * 128], stin[:, kc * 128:(kc + 1) * 128], identF)
                            cp(hpS.pat([[128, 4], [8, 16]], off=half * 4 * 128), ps.pat([[128, 4], [1, 16]]))
                        hsv = V(hT.ap[:, :, 1 + li * 128:1 + (li + 1) * 128], [hkey(li)])
                        cp(hpS.pat([[128, 8], [8, 16], [1, 7]], off=1), hsv.pat([[HS, 8], [8, 16], [1, 7]]), eng="pool")
                if is_rw and npt > 0:
                    cp(hcar[j], V(hT.ap[:, :, npt * 128], [hkey(npt - 1)]), eng="pool")
                if 1 in phases:
                    if not is_rw:
                        hgrn_layer(l, j)
                    else:
                        rwkv_layer(l, j)
                rest_layer(l)

            fg = TS8.get()[:, 0:8]
            dma(fg, final_g[0].rearrange("(kc p) -> p kc", p=128), slow=True)
            for li, t in enumerate(grp):
                rstd = rms_tile(li)
                yT = T1K.get()
                for kc in range(NKC):
                    stt(yT[:, kc * 128:(kc + 1) * 128], xTv(kc, li * 128, 128), fg[:, kc:kc + 1], rstd, MUL, MUL)
                yo = T1K.get()
                for half in range(2):
                    ps = PS.get()
                    for q in range(4):
                        kc = half * 4 + q
                        tr(ps[:, q * 128:(q + 1) * 128], yT[:, kc * 128:(kc + 1) * 128], identF)
                    cp(yo[:, half * 512:(half + 1) * 512], ps)
                dma(ys if t == "s" else yp[t * 128:(t + 1) * 128, :], yo)

        for gi, grp in enumerate(groups):
            process_group(gi, grp)

        info = P.emit()
    return nc, info


def _shard_inputs(inp):
    cf, cb = make_consts()
    maps = []
    for c in range(8):
        b = slice(16 * c, 16 * c + 16)
        m = {
            "xp": np.ascontiguousarray(inp["x_prompt"][c]),
            "xs": np.ascontiguousarray(inp["x_sample"][b]).reshape(128, D),
            "mem": np.ascontiguousarray(inp["mem_prompt"][c]),
            "sth": np.ascontiguousarray(inp["state_hgrn"][:, b]),
            "strw": np.ascontiguousarray(inp["state_rwkv"][:, b]),
            "sts": np.ascontiguousarray(inp["state_shift"][:, b]),
            "ck": np.ascontiguousarray(inp["cache_mem_k"][:, b]).reshape(4, 16, MEM, 512),
            "cv": np.ascontiguousarray(inp["cache_mem_v"][:, b]).reshape(4, 16, MEM, 512),
            "rw_rk": np.ascontiguousarray(inp["rw_rk"]).reshape(2, D),
            "final_g": np.ascontiguousarray(inp["final_g"]).reshape(1, D),
            "cf": cf, "cb": cb,
        }
        for k in ("norm_g", "w_in", "w_out", "mem_norm_g", "w_mem_kv", "hg_lb", "hg_onorm_g", "rw_mu", "rw_w0", "rw_w1",
                  "rw_w2", "rw_a0", "rw_a1", "rw_a2", "rw_v0", "rw_v1", "rw_v2", "rw_kk", "rw_ka", "rw_lnx_g", "rw_lnx_b"):
            m[k] = np.ascontiguousarray(inp[k])
        maps.append(m)
    return maps


_NC_CACHE = {}


def kernel(**inputs):
    inp = {k: np.asarray(v, dtype=np.float32) for k, v in inputs.items()}
    if "nc" not in _NC_CACHE:
        _NC_CACHE["nc"] = build()[0]
    nc = _NC_CACHE["nc"]
    maps = _shard_inputs(inp)
    res = run_bass_kernel_spmd(nc, maps, core_ids=list(range(8)))
    R = res.results
    y_p = np.stack([R[c]["yp"] for c in range(8)], 0)
    y_s = np.concatenate([R[c]["ys"].reshape(16, 8, D) for c in range(8)], 0)
    sh_p = np.stack([R[c]["shp"] for c in range(8)], 1)
    sr_p = np.stack([R[c]["srp"] for c in range(8)], 1)
    ss_p = np.stack([R[c]["ssp"] for c in range(8)], 1)
    mk_p = np.stack([R[c]["mkp"].reshape(4, MEM, 4, 128) for c in range(8)], 1)
    mv_p = np.stack([R[c]["mvp"].reshape(4, MEM, 4, 128) for c in range(8)], 1)
    sh_s = np.concatenate([R[c]["shs"] for c in range(8)], 1)
    sr_s = np.concatenate([R[c]["srs"] for c in range(8)], 1)
    ss_s = np.concatenate([R[c]["sss"] for c in range(8)], 1)
    return (y_p, y_s, sh_p, sr_p, ss_p, mk_p, mv_p, sh_s, sr_s, ss_s)
```

```python
import contextlib
import numpy as np
import ml_dtypes
import concourse.bass as bass
import concourse.mybir as mybir
from concourse.bass_utils import run_bass_kernel_spmd

F32 = mybir.dt.float32
F32R = mybir.dt.float32r
BF16 = mybir.dt.bfloat16
AF = mybir.ActivationFunctionType
ALU = mybir.AluOpType
AX = mybir.AxisListType

ENGS = ("pe", "act", "dve", "pool", "sp")
DMA_K = 6
SEM_WRAP = 30000
SAME_ENG_RAW_WAITS = True
PIPE = True


class Prog:
    def __init__(self, nc):
        self.nc = nc
        self.ops = []
        self.last_w = {}
        self.readers = {}
        self.excl = set()
        self.seg = None
        self.nseg = 0
        self.merges = []

    def begin_seg(self):
        self.nseg += 1
        self.seg = self.nseg
        return self.seg

    def end_seg(self):
        self.seg = None

    def op(self, eng, fn, reads=(), writes=(), dma=False):
        i = len(self.ops)
        ex = [k for k in reads if k in self.excl]
        if ex:
            writes = list(writes) + ex
        deps = {}
        for k in reads:
            w = self.last_w.get(k)
            if w is not None:
                deps[w] = True
        for k in writes:
            w = self.last_w.get(k)
            if w is not None:
                deps.setdefault(w, False)
            for r in self.readers.get(k, ()):
                deps.setdefault(r, False)
        o = dict(i=i, eng=eng, fn=fn, dma=dma, deps=deps, seg=self.seg)
        self.ops.append(o)
        for k in reads:
            self.readers.setdefault(k, []).append(i)
        for k in writes:
            self.last_w[k] = i
            self.readers[k] = []
        return i

    def barrier(self):
        lastc = {e: None for e in ENGS}
        lastd = {e: [] for e in ENGS}
        for o in self.ops:
            if o["dma"]:
                lastd[o["eng"]].append(o["i"])
            else:
                lastc[o["eng"]] = o["i"]
        extra = [v for v in lastc.values() if v is not None]
        for e in ENGS:
            extra += lastd[e][-DMA_K:]
        for e in ENGS:
            i = self.op(e, lambda eng: eng.nop(), reads=(), writes=())
            for x in extra:
                self.ops[i]["deps"][x] = True

    def _order(self):
        n = len(self.ops)
        order = []
        segops = {}
        for o in self.ops:
            if o["seg"] is not None:
                segops.setdefault(o["seg"], []).append(o["i"])
        partner = {x: y for x, y in self.merges}
        done_seg = set()
        i = 0
        while i < n:
            o = self.ops[i]
            sg = o["seg"]
            if sg is not None and sg in partner and sg not in done_seg:
                X = segops[sg]; Y = segops[partner[sg]]
                assert X[-1] + 1 == Y[0] and X == list(range(X[0], X[-1] + 1)) and Y == list(range(Y[0], Y[-1] + 1))
                emitted = set()
                xi = yi = 0
                xset = set(X)
                turn = 0
                while xi < len(X) or yi < len(Y):
                    pick_y = False
                    if yi < len(Y) and (turn == 1 or xi >= len(X)):
                        yo = self.ops[Y[yi]]
                        if all((d not in xset) or (d in emitted) for d in yo["deps"]):
                            pick_y = True
                    if pick_y:
                        order.append(Y[yi]); yi += 1
                    else:
                        order.append(X[xi]); emitted.add(X[xi]); xi += 1
                    turn ^= 1
                done_seg.add(sg); done_seg.add(partner[sg])
                i = Y[-1] + 1
                continue
            order.append(i)
            i += 1
        assert sorted(order) == list(range(n))
        return order

    def emit(self, final_wait_eng="sp"):
        nc = self.nc
        order = self._order()
        ops = [self.ops[i] for i in order]
        newidx = {o["i"]: k for k, o in enumerate(ops)}
        for k, o in enumerate(ops):
            o["deps"] = {newidx[d]: r for d, r in o["deps"].items()}
            assert all(d < k for d in o["deps"])
            o["i"] = k
        per_eng = {e: [] for e in ENGS}
        ndma = {e: 0 for e in ENGS}
        lastslot = {}
        for o in ops:
            e = o["eng"]
            o["pos"] = len(per_eng[e])
            per_eng[e].append(o["i"])
            if o["dma"]:
                nn = ndma[e]
                o["dslot"] = nn % DMA_K
                o["dval"] = 16 * (nn // DMA_K + 1)
                o["dprev"] = lastslot.get((e, o["dslot"]))
                lastslot[(e, o["dslot"])] = o["i"]
                ndma[e] = nn + 1
        known = {e: {f: -1 for f in ENGS} for e in ENGS}
        known_d = {e: {} for e in ENGS}
        milestone = set()
        for o in ops:
            e = o["eng"]
            wl = []
            deps = dict(o["deps"])
            if o["dma"] and o["dprev"] is not None:
                deps[o["dprev"]] = True
            for j, raw in deps.items():
                p = ops[j]
                f = p["eng"]
                if p["dma"]:
                    key = (f, p["dslot"])
                    if known_d[e].get(key, 0) < p["dval"]:
                        known_d[e][key] = p["dval"]
                        wl.append(("d", f, p["dslot"], p["dval"]))
                else:
                    if f == e and not o["dma"] and e != "pool":
                        if e == "pe" or not raw or not SAME_ENG_RAW_WAITS:
                            continue
                    if known[e][f] < p["pos"]:
                        wl.append(("c", f, p["pos"], j))
            best = {}
            out = []
            for w in wl:
                if w[0] == "c":
                    if w[1] not in best or best[w[1]][2] < w[2]:
                        best[w[1]] = w
                else:
                    out.append(w)
            for f, w in best.items():
                known[e][f] = w[2]
                milestone.add(w[3])
                out.append(w)
            o["waits"] = out
        cnt = {e: 0 for e in ENGS}
        for o in ops:
            if o["dma"]:
                continue
            if o["i"] in milestone:
                cnt[o["eng"]] += 1
                o["ms"] = cnt[o["eng"]]
            else:
                o["ms"] = None
        nsem = {e: max(1, (cnt[e] + SEM_WRAP - 1) // SEM_WRAP) for e in ENGS}
        nwaits = 0
        with contextlib.ExitStack() as st:
            csem = {e: [st.enter_context(nc.semaphore(f"c_{e}_{k}")) for k in range(nsem[e])] for e in ENGS}
            dsem = {e: [st.enter_context(nc.semaphore(f"d_{e}_{k}")) for k in range(DMA_K)]
                    for e in ENGS if ndma[e] > 0}
            block = st.enter_context(nc.Block())

            def mk(ename):
                def body(eng):
                    nonlocal nwaits
                    for i in per_eng[ename]:
                        o = ops[i]
                        for w in o["waits"]:
                            nwaits += 1
                            if w[0] == "c":
                                ms = ops[w[3]]["ms"]
                                k, v = (ms - 1) // SEM_WRAP, (ms - 1) % SEM_WRAP + 1
                                eng.wait_ge(csem[w[1]][k], v)
                            else:
                                eng.wait_ge(dsem[w[1]][w[2]], w[3])
                        ins = o["fn"](eng)
                        if o["dma"]:
                            ins.then_inc(dsem[ename][o["dslot"]], 16)
                        elif o["ms"] is not None:
                            k = (o["ms"] - 1) // SEM_WRAP
                            ins.then_inc(csem[ename][k], 1)
                    if ename == final_wait_eng:
                        for f in dsem:
                            last = {}
                            for j in per_eng[f]:
                                if ops[j]["dma"]:
                                    last[ops[j]["dslot"]] = ops[j]["dval"]
                            for s, v in last.items():
                                eng.wait_ge(dsem[f][s], v)
                return body

            block.tensor(mk("pe"))
            block.scalar(mk("act"))
            block.vector(mk("dve"))
            block.gpsimd(mk("pool"))
            block.sync(mk("sp"))
        return dict(n_ops=len(ops), per_eng={e: len(v) for e, v in per_eng.items()},
                    milestones=cnt, nwaits=nwaits)


class V:
    def __init__(self, ap, keys, r=False):
        self.ap = ap
        self.keys = tuple(keys)
        self.r = r

    def __getitem__(self, idx):
        return V(self.ap[idx], self.keys, self.r)

    def pat(self, pat, off=0):
        a = self.ap
        return V(bass.AP(a.tensor, a.offset + off, [list(a.ap[0])] + [list(p) for p in pat]), self.keys, self.r)

    def k(self, *keys):
        return V(self.ap, keys, self.r)

    def nr(self):
        return V(self.ap, self.keys, False)

    def rr(self):
        return V(self.ap, self.keys, True)


D = 1024
NKC = 8
SEQ = 2048
NPT = 16
NS = 16
TS = 8
MEM = 256
XH = 4
INC = 5120
CDEC = -float(np.exp(-0.5))
NORM_EPS = 1e-6
LNX_EPS = 64e-5
SC_ATT = 128 ** -0.5

CO = {}
_o = 0
for _n, _w in [("ident", 128), ("ones", 128),
               ("mb1", 512), ("sl1", 128), ("tic1", 128), ("tsc1", 128), ("boc1", 128),
               ("mb16", 512), ("sl16", 128), ("bo16", 128), ("tic16", 128), ("tsc16", 128), ("boc16", 128),
               ("iu2", 128), ("bo2", 128), ("sel16", 16), ("sel16c", 16), ("negc", 2), ("cm0", 1), ("cm1", 1),
               ("hw", 1), ("ha", 1)]:
    CO[_n] = (_o, _w)
    _o += _w
NCF = _o
CB = {"ident": (0, 128)}
NCB = 128


def make_consts():
    c = np.zeros((128, NCF), np.float32)
    s = np.arange(128)[:, None]
    t = np.arange(128)[None, :]
    def put(n, a):
        o, w = CO[n]
        c[:, o:o + w] = a
    put("ident", (s == t))
    put("ones", 1.0)
    for cfg, blk in ((1, 128), (16, 8), (2, 64)):
        same = (s // blk) == (t // blk)
        iu = same & (s <= t)
        su = same & (s < t)
        sl = same & (s > t)
        if cfg == 2:
            put("iu2", iu); put("bo2", same)
            continue
        put(f"mb{cfg}", np.concatenate([su, su, iu, iu], 1))
        put(f"sl{cfg}", sl)
        if cfg == 16:
            put("bo16", same)
        put(f"tic{cfg}", CDEC * iu); put(f"tsc{cfg}", CDEC * su); put(f"boc{cfg}", CDEC * same)
    sel = (np.arange(128)[:, None] // 8) == np.arange(16)[None, :]
    put("sel16", sel); put("sel16c", CDEC * sel); put("negc", CDEC)
    put("cm0", (np.arange(128) < 64)[:, None]); put("cm1", (np.arange(128) >= 64)[:, None])
    put("hw", (np.arange(128) < 64)[:, None]); put("ha", (np.arange(128) >= 64)[:, None])
    cb = np.zeros((128, NCB), np.float32)
    cb[:, 0:128] = np.eye(128)
    return c, cb.astype(ml_dtypes.bfloat16)


def build(depth=4, groups=None, dbg=False, memkv=True, phases=(1, 2, 3, 4), mks=9, levo=None):
    if groups is None:
        groups = [[0, 1, 2, 3], [4, 5, 6, 7], [8, 9, 10, 11], [12, 13, 14, 15], ["s"]]
    GMAX = max(len(g) for g in groups)
    NTM = GMAX * 128
    nc = bass.Bass("TRN2", target_bir_lowering=False)
    din = lambda n, s, d=F32: nc.dram_tensor(n, list(s), d, kind="ExternalInput").ap()
    dout = lambda n, s: nc.dram_tensor(n, list(s), F32, kind="ExternalOutput").ap()
    xp = din("xp", [SEQ, D]); xs = din("xs", [128, D]); mem = din("mem", [MEM, D])
    sth = din("sth", [2, NS, 8, 128, 128]); strw = din("strw", [2, NS, 16, 64, 64]); sts = din("sts", [2, NS, D])
    ck = din("ck", [4, NS, MEM, 512]); cv = din("cv", [4, NS, MEM, 512])
    norm_g = din("norm_g", [4, D]); w_in = din("w_in", [4, D, INC]); w_out = din("w_out", [4, 1536, D])
    mem_norm_g = din("mem_norm_g", [4, D]); w_mem_kv = din("w_mem_kv", [4, D, 1024])
    hg_lb = din("hg_lb", [2, D]); hg_og = din("hg_onorm_g", [2, 128])
    rw_mu = din("rw_mu", [2, 5, D]); rw_w0 = din("rw_w0", [2, D]); rw_w1 = din("rw_w1", [2, D, 64]); rw_w2 = din("rw_w2", [2, 64, D])
    rw_a0 = din("rw_a0", [2, D]); rw_a1 = din("rw_a1", [2, D, 64]); rw_a2 = din("rw_a2", [2, 64, D])
    rw_v0 = din("rw_v0", [1, D]); rw_v1 = din("rw_v1", [1, D, 32]); rw_v2 = din("rw_v2", [1, 32, D])
    rw_kk = din("rw_kk", [2, D]); rw_ka = din("rw_ka", [2, D]); rw_rk = din("rw_rk", [2, D])
    rw_lg = din("rw_lnx_g", [2, D]); rw_lb = din("rw_lnx_b", [2, D]); final_g = din("final_g", [1, D])
    cf_d = din("cf", [128, NCF]); cb_d = din("cb", [128, NCB], BF16)
    yp = dout("yp", [SEQ, D]); ys = dout("ys", [128, D])
    shp = dout("shp", [2, 8, 128, 128]); srp = dout("srp", [2, 16, 64, 64]); ssp = dout("ssp", [2, D])
    mkp = dout("mkp", [4, MEM, 512]); mvp = dout("mvp", [4, MEM, 512])
    shs = dout("shs", [2, NS, 8, 128, 128]); srs = dout("srs", [2, NS, 16, 64, 64]); sss = dout("sss", [2, NS, D])

    with contextlib.ExitStack() as st:
        P = Prog(nc)

        def sb(name, shape, dt=F32, r=False):
            t = st.enter_context(nc.sbuf_tensor(name, list(shape), dt))
            return V(t[:], [name], r and dt == F32)

        class Rot:
            def __init__(self, name, n, shape, dt=F32, r=False):
                self.bufs = [sb(f"{name}{i}", shape, dt, r) for i in range(n)]
                self.i = 0

            def get(self):
                b = self.bufs[self.i % len(self.bufs)]
                self.i += 1
                return b

        class PSR:
            def __init__(self):
                self.bufs = []
                for i in range(8):
                    t = st.enter_context(nc.psum_tensor(f"ps{i}", [128, 512], F32))
                    self.bufs.append(V(t[:], [f"ps{i}"]))
                    P.excl.add(f"ps{i}")
                self.i = 0

            def get(self):
                b = self.bufs[self.i % 8]
                self.i += 1
                return b

        PS = PSR()

        def bfv(v):
            return V(v.ap.bitcast(BF16), v.keys)

        def rk(*vs):
            out = []
            for v in vs:
                if isinstance(v, V):
                    out += list(v.keys)
            return out

        def a_(x):
            return x.ap if isinstance(x, V) else x

        def o_(x):
            return x.ap.bitcast(F32R) if x.r else x.ap

        def mm(o, l, r, start=True, stop=True):
            la, ra = l.ap, r.ap
            if l.r and r.r:
                la, ra = la.bitcast(F32R), ra.bitcast(F32R)
            P.op("pe", lambda e: e.matmul(o.ap, lhsT=la, rhs=ra, start=start, stop=stop), reads=rk(l, r), writes=rk(o))

        def tr(o, i, idn):
            P.op("pe", lambda e: e.transpose(out=o.ap, in_=i.ap, identity=idn.ap), reads=rk(i, idn), writes=rk(o))

        def act(o, i, f, bias=None, scale=None, acc=None):
            kw = {}
            if bias is not None:
                kw["bias"] = a_(bias)
            if scale is not None:
                kw["scale"] = a_(scale)
            if acc is not None:
                kw["accum_out"] = acc.ap
            oa = o_(o)
            P.op("act", lambda e: e.activation(out=oa, in_=i.ap, func=f, **kw), reads=rk(i, bias, scale), writes=rk(o, acc))

        def tt(o, a, b, op, eng="dve"):
            if eng == "pool" and o.r:
                eng = "dve"
            oa = o_(o)
            P.op(eng, lambda e: e.tensor_tensor(out=oa, in0=a.ap, in1=b.ap, op=op), reads=rk(a, b), writes=rk(o))

        def ts(o, a, s1, op0, s2=None, op1=None, eng="dve"):
            oa = o_(o)

            def f(e):
                if op1 is None:
                    return e.tensor_scalar(out=oa, in0=a.ap, scalar1=a_(s1), scalar2=None, op0=op0)
                return e.tensor_scalar(out=oa, in0=a.ap, scalar1=a_(s1), scalar2=a_(s2), op0=op0, op1=op1)
            P.op(eng, f, reads=rk(a, s1, s2), writes=rk(o))

        def stt(o, a, s, b, op0, op1):
            oa = o_(o)
            P.op("dve", lambda e: e.scalar_tensor_tensor(out=oa, in0=a.ap, scalar=a_(s), in1=b.ap, op0=op0, op1=op1),
                 reads=rk(a, s, b), writes=rk(o))

        def red(o, a, op, negate=False):
            P.op("dve", lambda e: e.tensor_reduce(out=o.ap, in_=a.ap, axis=AX.X, op=op, negate=negate), reads=rk(a), writes=rk(o))

        def recip(o, a):
            oa = o_(o)
            P.op("dve", lambda e: e.reciprocal(out=oa, in_=a.ap), reads=rk(a), writes=rk(o))

        def cp(o, i, eng="act"):
            if eng == "pool" and o.r:
                eng = "dve"
            oa = o_(o)
            if eng == "act":
                P.op("act", lambda e: e.copy(out=oa, in_=i.ap), reads=rk(i), writes=rk(o))
            else:
                P.op(eng, lambda e: e.tensor_copy(out=oa, in_=i.ap), reads=rk(i), writes=rk(o))

        def mset(o, val, eng="pool"):
            P.op(eng, lambda e: e.memset(o.ap, val), writes=rk(o))

        def dma(o, i, eng="sp", slow=False):
            oa = o.ap if isinstance(o, V) else o
            ia = i.ap if isinstance(i, V) else i
            kw = dict(allow_slow_non_contiguous=True) if slow else {}
            P.op(eng, lambda e: e.dma_start(out=oa, in_=ia, **kw), reads=rk(i), writes=rk(o), dma=True)

        MUL, ADD, SUB, MAX = ALU.mult, ALU.add, ALU.subtract, ALU.max

        cF = sb("cF", [128, NCF], r=True); cB = sb("cB", [128, NCB], BF16)
        dma(cB, cb_d)
        C = lambda n: cF[:, CO[n][0]:CO[n][0] + CO[n][1]]
        identF = C("ident"); identB = cB[:, 0:128]; onesF = C("ones")

        Sp = [sb(f"Sp{j}", [128, 8, 128], r=True) for j in range(2)]
        Mp = [sb(f"Mp{j}", [128, 8, 64], r=True) for j in range(2)]
        hcar = [sb(f"hcar{j}", [128, 8], BF16) for j in range(2)]
        for j in range(2):
            mset(Sp[j], 0.0); mset(Mp[j], 0.0)

        ARENA = 62 * 256
        arena_t = st.enter_context(nc.sbuf_tensor("arena", [128, ARENA], F32))
        arena = arena_t[:]
        aoff = {"P": 0, "S": 0}

        def carve(lay, name, shape, dt=F32):
            n = int(np.prod(shape[1:]))
            n32 = n if dt == F32 else (n + 1) // 2
            o = aoff[lay]
            aoff[lay] = o + n32
            assert aoff[lay] <= ARENA, (lay, name, aoff[lay])
            a = arena[:, o:o + n32]
            if dt != F32:
                a = a.bitcast(dt)
            pat = []
            stride = 1
            for d in reversed(shape[1:]):
                pat.insert(0, [stride, d])
                stride *= d
            return V(bass.AP(a.tensor, a.offset, [list(a.ap[0])] + pat), [name])

        LAY = {}
        for lay, ntm in (("P", GMAX * 128), ("S", 128)):
            Ld = dict(NTM=ntm)
            Ld["l1T"] = carve(lay, lay + "l1T", [128, 2, ntm])
            Ld["l1vT"] = carve(lay, lay + "l1vT", [128, ntm])
            if lay == "P":
                Ld["QZ"] = carve(lay, "QZ", [128, 256])
            Ld["xT"] = carve(lay, lay + "xT", [128, NKC, ntm])
            Ld["hT"] = carve(lay, lay + "hT", [128, NKC, 1 + ntm + 1], BF16)
            Ld["brT"] = carve(lay, lay + "brT", [128, 12, ntm], BF16)
            Ld["qx"] = [carve(lay, lay + f"qx{i}", [128, ntm], BF16) for i in range(2)]
            if lay == "P":
                Ld["mkT"] = carve(lay, "mkT", [128, 4, XH, MEM], BF16)
                Ld["mvb"] = carve(lay, "mvb", [128, 4, 2, 512], BF16)
            else:
                Ld["hpS"] = carve(lay, "hpS", [128, NKC, 128], BF16)
                Ld["Z1"] = carve(lay, "Z1", [128, 2048])
                Ld["Z1b"] = carve(lay, "Z1b", [128, 2048], BF16)
                Ld["Z2b"] = carve(lay, "Z2b", [128, 2048], BF16)
                Ld["M0b"] = carve(lay, "M0b", [128, 16, 64], BF16)
                o_save = aoff[lay]
                Ld["S0q"] = [carve(lay, f"S0q{i}", [128, 512]).k("salias") for i in range(4)]
                aoff[lay] = o_save
                Ld["M0all"] = carve(lay, "M0all", [128, 16, 64]).k("salias")
                Ld["SBD"] = [carve(lay, f"SBD{i}", [128, 512]).k("salias") for i in range(2)]
                Ld["MBD"] = [carve(lay, f"MBD{i}", [128, 512]) for i in range(2)]
                Ld["KVB"] = [carve(lay, f"KVB{i}", [128, 2048], BF16) for i in range(3)]
                Ld["QXM"] = carve(lay, "QXM", [128, 2048], BF16)
            LAY[lay] = Ld
        print("arena use (KiB):", {k: v / 256 for k, v in aoff.items()})

        T128 = Rot("t128_", 16, [128, 128], r=True)
        T256 = Rot("t256_", 3, [128, 256], r=True)
        T512 = Rot("t512_", 4, [128, 512], r=True)
        T1K = Rot("t1k_", 2, [128, 1024], r=True)
        TB256 = Rot("tb256_", 2, [128, 256], BF16)
        TB512 = Rot("tb512_", 2, [128, 512], BF16)
        TS8 = Rot("ts8_", 8, [128, 24])
        TS8s = TS8
        OG = sb("OG", [128, 128]); LBC = sb("LBC", [128, 24])
        BIGV = [sb("bigv0", [128, D], r=True), sb("bigv1", [128, D], r=True)]
        MUC = sb("MUC", [128, 80])
        L1a = sb("L1a", [128, NKC, 128], BF16); L1b = sb("L1b", [128, NKC, 128], BF16)
        L1va = sb("L1va", [128, NKC, 128], BF16); L1vb = sb("L1vb", [128, NKC, 128], BF16)
        PVB = sb("PVB", [128, 8, 128])
        ZP = [sb(f"zp{i}", [128, 256], BF16) for i in range(2)]
        ZBK = sb("zbk", [128, 512], BF16)
        ZBKH = [sb(f"zbkh{i}", [128, 512], BF16) for i in range(2)]
        PMBD = sb("PMBD", [128, 128])
        for zb in ZP + [ZBK, PMBD] + ZBKH:
            mset(zb, 0.0)
        WM = Rot("wm_", 2, [128, NKC, 384], BF16)
        WM2 = Rot("wm2_", 2, [128, NKC, 384], BF16)
        WR = Rot("wr_", 2, [128, NKC, 512], BF16)
        WO = Rot("wo_", 1, [128, 12, 128], BF16)
        XIN = Rot("xin_", 1, [128, D])
        _xb = XIN.bufs[0].ap.bitcast(BF16)
        WO.bufs.append(V(bass.AP(_xb.tensor, _xb.offset, [list(_xb.ap[0]), [128, 12], [1, 128]]), XIN.bufs[0].keys))
        NRV = Rot("nrv_", 2, [128, 128], BF16)
        BT128 = Rot("b128_", 8, [128, 128], BF16)
        BT256 = Rot("b256_", 3, [128, 256], BF16)
        BT512 = Rot("b512_", 5, [128, 512], BF16)
        BT1K = Rot("b1k_", 2, [128, 1024], BF16)
        MK = [sb(f"mk{i}", [128, 512], BF16) for i in range(4)]
        MpB = [sb(f"MpB{j}", [128, 8, 64], BF16) for j in range(2)]
        for j in range(2):
            mset(MpB[j], 0.0)
        vfd = nc.dram_tensor("vfd", [17 * 128, D], BF16, kind="Internal").ap()

        for c0 in range(0, NCF, 1024):
            n = min(1024, NCF - c0)
            xin = XIN.get()
            dma(xin[:, 0:n], cf_d[:, c0:c0 + n])
            cp(cF[:, c0:c0 + n], xin[:, 0:n])

        def wview(w2d, c0, ncols):
            return w2d[:, c0:c0 + ncols].rearrange("(kc p) c -> p kc c", p=128)

        def xkey(li):
            return ("xT", li)

        def hkey(li):
            return ("hT", li)

        def mem_kv():
            LP = LAY["P"]
            mkT, mvb = LP["mkT"], LP["mvb"]
            memnT = V(LP["xT"].ap[:, :, 0:MEM], [xkey(0), xkey(1)])
            hmT = V(LP["hT"].ap[:, :, 1:1 + MEM], [hkey(0), hkey(1)])
            for mc in range(2):
                xin = XIN.get()
                dma(xin, mem[mc * 128:(mc + 1) * 128, :])
                ss = TS8.get()
                junk = T1K.get()
                act(junk, xin, AF.Square, acc=ss[:, 0:1])
                ts(ss[:, 1:2], ss[:, 0:1], 1.0 / D, MUL, NORM_EPS, ADD)
                act(ss[:, 2:3], ss[:, 1:2], AF.Ln)
                act(ss[:, 3:4], ss[:, 2:3], AF.Exp, scale=-0.5)
                xn = T1K.get()
                ts(xn, xin, ss[:, 3:4], MUL)
                for half in range(2):
                    ps = PS.get()
                    for q in range(4):
                        kc = half * 4 + q
                        tr(ps[:, q * 128:(q + 1) * 128], xn[:, kc * 128:(kc + 1) * 128], identF)
                    cp(memnT[:, half * 4:(half + 1) * 4, mc * 128:(mc + 1) * 128], ps.pat([[128, 4], [1, 128]]))
            for l in range(depth):
                if mks < 1:
                    break
                gcol = TS8.get()[:, 0:8]
                dma(gcol, mem_norm_g[l].rearrange("(kc p) -> p kc", p=128), slow=True)
                for kc in range(NKC):
                    ts(hmT[:, kc, :], memnT[:, kc, :], gcol[:, kc:kc + 1], MUL)
                if mks < 2:
                    continue
                for cb in range(2):
                    wb = WR.get()
                    dma(wb, wview(w_mem_kv[l], cb * 512, 512), eng="pool")
                    if mks < 3:
                        continue
                    if cb == 0:
                        for hx in range(XH):
                            ps = PS.get()
                            for kc in range(NKC):
                                mm(ps[:, 0:256], wb[:, kc, hx * 128:(hx + 1) * 128], hmT[:, kc, :], start=kc == 0, stop=kc == NKC - 1)
                            cp(mkT[:, l, hx, :], ps[:, 0:256])
                    if mks < 4:
                        continue
                    for mc in range(2):
                        ps = PS.get()
                        for kc in range(NKC):
                            mm(ps, hmT[:, kc, mc * 128:(mc + 1) * 128], wb[:, kc, :], start=kc == 0, stop=kc == NKC - 1)
                        stg = T512.get()
                        cp(stg, ps)
                        if mks >= 6:
                            dma((mkp if cb == 0 else mvp)[l, mc * 128:(mc + 1) * 128, :], stg)
                        if cb == 1 and mks >= 7:
                            cp(mvb[:, l, mc, :], ps, eng="dve")

        if memkv:
            mem_kv()
        mset(LAY["P"]["QZ"], 0.0)

        def process_group(gi, grp):
            ptiles = [t for t in grp if t != "s"]
            has_s = "s" in grp
            assert not (has_s and ptiles)
            L = LAY["S" if has_s else "P"]
            NTM = L["NTM"]; HS = NTM + 2
            xT, hT, brT, l1T, l1vT = L["xT"], L["hT"], L["brT"], L["l1T"], L["l1vT"]
            if has_s:
                hpS = L["hpS"]; Z1 = L["Z1"]; Z1b = L["Z1b"]; Z2b = L["Z2b"]; QXM = L["QXM"]
                P.barrier()
                for zb in [Z1, Z1b, Z2b, QXM] + L["SBD"] + L["MBD"]:
                    mset(zb, 0.0)
                QM = Z1
            else:
                mkT, mvb, QZ = L["mkT"], L["mvb"], L["QZ"]
            qxi = [0]
            npt = len(ptiles)
            first_group = bool(ptiles) and ptiles[0] == 0
            pchunks = []
            c = 0
            while c < npt * 128:
                n = min(512, npt * 128 - c)
                pchunks.append((c, n))
                c += n
            scol = npt * 128
            chunks = pchunks + ([(scol, 128)] if has_s else [])

            def tkeys(c0, n, fn):
                return tuple(fn(li) for li in range(c0 // 128, (c0 + n + 127) // 128))

            def xTv(kc, c0, n):
                return V(xT.ap[:, kc, c0:c0 + n], tkeys(c0, n, xkey))

            def xTt(li):
                return V(xT.ap[:, :, li * 128:(li + 1) * 128], [xkey(li)])

            def hTv(kc, c0, n):
                return V(hT.ap[:, kc, 1 + c0:1 + c0 + n], tkeys(c0, n, hkey))

            def hprevv(kc, c0, n):
                if has_s and c0 == scol:
                    return hpS[:, kc, :]
                ks = tkeys(c0, n, hkey) + ((hkey(c0 // 128 - 1),) if c0 > 0 else (("hT", "c0"),))
                return V(hT.ap[:, kc, c0:c0 + n], ks)

            def brv(kc, c0, n):
                return V(brT.ap[:, kc, c0:c0 + n], tkeys(c0, n, lambda li: ("br", kc, li)))

            for li, t in enumerate(grp):
                xin = XIN.get()
                dma(xin, xs if t == "s" else xp[t * 128:(t + 1) * 128, :])
                for half in range(2):
                    ps = PS.get()
                    for q in range(4):
                        kc = half * 4 + q
                        tr(ps[:, q * 128:(q + 1) * 128], xin[:, kc * 128:(kc + 1) * 128], identF)
                    cp(V(xT.ap[:, half * 4:(half + 1) * 4, li * 128:(li + 1) * 128], [xkey(li)]), ps.pat([[128, 4], [1, 128]]))

            def rms_tile(li):
                sq = T1K.get()
                act(sq.pat([[128, 8], [1, 128]]), xTt(li), AF.Square)
                ps = PS.get()
                for kc in range(NKC):
                    mm(ps[:, 0:128], onesF, sq[:, kc * 128:(kc + 1) * 128], start=kc == 0, stop=kc == NKC - 1)
                rs = T128.get()
                act(rs, ps[:, 0:128], AF.Ln, bias=NORM_EPS, scale=1.0 / D)
                rstd = T128.get()
                act(rstd, rs, AF.Exp, scale=-0.5)
                return rstd

            def hgrn_layer(l, j):
                og = OG
                dma(og, hg_og[j:j + 1, :].partition_broadcast(128))
                lbT = LBC[:, 0:8]; omT = LBC[:, 8:16]; nomT = LBC[:, 16:24]
                nom_bc = BIGV[0]
                if j == 0:
                    mset(omT, 1.0); mset(nomT, -1.0); mset(nom_bc, -1.0)
                else:
                    a0 = TS8.get()[:, 0:8]; a1 = TS8.get()[:, 0:8]
                    dma(a0, hg_lb[0].rearrange("(h p) -> p h", p=128), slow=True)
                    dma(a1, hg_lb[1].rearrange("(h p) -> p h", p=128), slow=True)
                    tt(a1, a1, a0, SUB)
                    act(lbT, a1, AF.Sigmoid)
                    ts(omT, lbT, -1.0, MUL, 1.0, ADD)
                    ts(nomT, lbT, 1.0, SUB)
                    b0 = XIN.get()
                    dma(b0, hg_lb[1:2, :].partition_broadcast(128))
                    cp(nom_bc, b0)
                    b0 = XIN.get()
                    dma(b0, hg_lb[0:1, :].partition_broadcast(128))
                    tt(nom_bc, nom_bc, b0, SUB)
                    act(nom_bc, nom_bc, AF.Sigmoid)
                    ts(nom_bc, nom_bc, 1.0, SUB)
                for h in range(8):
                    wblk = WM.get()
                    for q in range(3):
                        dma(wblk[:, :, q * 128:(q + 1) * 128], wview(w_in[l], q * 1024 + h * 128, 128), eng="pool")
                    Sph = V(Sp[j].ap[:, h, :], [("Sp", j, h)])
                    if has_s:
                        S0q = L["S0q"]
                        for q in range(4):
                            dma(S0q[q].pat([[128, 4], [1, 128]]), sth[j, 4 * q:4 * q + 4, h].rearrange("s k v -> k s v"))
                    hsegB = None
                    for li, t in enumerate(grp):
                        samp = (t == "s")
                        c0 = li * 128
                        hsegA = P.begin_seg()
                        if hsegB is not None and PIPE:
                            P.merges.append((hsegB, hsegA))
                        IU = C("mb16")[:, 256:384] if samp else C("iu2")
                        BO = C("bo16") if samp else C("bo2")
                        psq = PS.get()
                        for kc in range(NKC):
                            mm(psq[:, 0:128], wblk[:, kc, 0:128], hTv(kc, c0, 128), start=kc == 0, stop=kc == NKC - 1)
                        for kc in range(NKC):
                            mm(psq[:, 128:256], wblk[:, kc, 128:256], hTv(kc, c0, 128), start=kc == 0, stop=kc == NKC - 1)
                        psfv = PS.get()
                        for kc in range(NKC):
                            mm(psfv[:, 0:256], hTv(kc, c0, 128), wblk[:, kc, 128:384], start=kc == 0, stop=kc == NKC - 1)
                        qsg = T128.get(); act(qsg, psq[:, 0:128], AF.Sigmoid)
                        qs = T128.get(); tt(qs, qsg, psq[:, 0:128], MUL)
                        sgT = T128.get(); act(sgT, psq[:, 128:256], AF.Sigmoid)
                        kT = T128.get(); ts(kT, sgT, nomT[:, h:h + 1], MUL, omT[:, h:h + 1], ADD)
                        sg = T128.get(); act(sg, psfv[:, 0:128], AF.Sigmoid)
                        kt = T128.get(); stt(kt, sg, -1.0, nom_bc[:, h * 128:(h + 1) * 128], ADD, MUL)
                        g = T128.get(); act(g, kt, AF.Ln, bias=1.0, scale=-1.0)
                        vt = T128.get(); cp(vt, psfv[:, 128:256])
                        ps3 = PS.get()
                        mm(ps3[:, 0:128], IU, g)
                        mm(ps3[:, 128:256], g, IU)
                        mm(ps3[:, 256:384], BO, g)
                        eb = T128.get(); act(eb, ps3[:, 128:256], AF.Exp)
                        enb = T128.get(); act(enb, ps3[:, 128:256], AF.Exp, scale=-1.0)
                        qtT = T128.get(); tt(qtT, qs, eb, MUL)
                        ktT = T128.get(); tt(ktT, kT, enb, MUL)
                        bsb = T128.get(); cp(bsb, ps3[:, 0:128])
                        dd = T128.get(); tt(dd, ps3[:, 256:384], bsb, SUB)
                        ed = T128.get(); act(ed, dd, AF.Exp)
                        psA = PS.get()
                        mm(psA[:, 0:128], ktT, qtT)
                        att = T128.get(); tt(att, psA[:, 0:128], IU, MUL)
                        P.end_seg()
                        hsegB = P.begin_seg()
                        psO = PS.get()
                        mm(psO[:, 0:128], att, vt, start=True, stop=False)
                        if not samp:
                            tt(QZ.pat([[192, 2], [1, 64]]), qs.pat([[64, 2], [1, 64]]), eb.pat([[64, 2], [1, 64]]), MUL)
                            kh0 = T128.get(); stt(kh0, kt, C("cm0"), ed, MUL, MUL)
                            kh1 = T128.get(); stt(kh1, kt, C("cm1"), ed, MUL, MUL)
                            psS = PS.get()
                            mm(psS[:, 0:128], kh0, vt)
                            mm(psS[:, 128:256], kh1, vt)
                            mm(psO[:, 0:128], QZ[:, 0:128], Sph, start=False, stop=False)
                            S1 = T128.get(); stt(S1, Sph, eb[:, 63:64], psS[:, 0:128], MUL, ADD)
                            mm(psO[:, 0:128], QZ[:, 128:256], S1, start=False, stop=True)
                            stt(Sph, S1, eb[:, 127:128], psS[:, 128:256], MUL, ADD)
                        else:
                            tt(QM.pat([[136, 16], [1, 8]]), qs.pat([[8, 16], [1, 8]]), eb.pat([[8, 16], [1, 8]]), MUL)
                            for s in range(NS):
                                mm(psO[:, 0:128], QM[:, s * 128:(s + 1) * 128], S0q[s // 4][:, (s % 4) * 128:(s % 4 + 1) * 128], start=False, stop=(s == NS - 1))
                            kh = T128.get(); tt(kh, kt, ed, MUL)
                            for q in range(4):
                                vm = T512.get()
                                tt(vm.pat([[128, 4], [1, 128]]), vt.pat([[0, 4], [1, 128]]), C("sel16").pat([[1, 4], [0, 128]], off=4 * q), MUL)
                                psW = PS.get()
                                mm(psW, kh, vm)
                                tmp = T512.get()
                                tt(tmp.pat([[128, 4], [1, 128]]), S0q[q].pat([[128, 4], [1, 128]]),
                                   eb.pat([[8, 4], [0, 128]], off=7 + 32 * q), MUL)
                                Sn = T512.get()
                                tt(Sn, tmp, psW, ADD)
                                dma(shs[j, 4 * q:4 * q + 4, h].rearrange("s k v -> k s v"), Sn.pat([[128, 4], [1, 128]]))
                        ss = TS8.get()
                        junk = T128.get()
                        act(junk, psO[:, 0:128], AF.Square, acc=ss[:, 0:1])
                        ts(ss[:, 1:2], ss[:, 0:1], 1.0 / 128, MUL, NORM_EPS, ADD)
                        act(ss[:, 2:3], ss[:, 1:2], AF.Ln)
                        act(ss[:, 3:4], ss[:, 2:3], AF.Exp, scale=-0.5)
                        on = T128.get(); stt(on, psO[:, 0:128], ss[:, 3:4], og, MUL, MUL)
                        psT = PS.get()
                        tr(psT[:, 0:128], on, identF)
                        cp(brv(h, c0, 128), psT[:, 0:128])
                        P.end_seg()
                if first_last_prompt_group:
                    for h in range(8):
                        dma(shp[j, h], V(Sp[j].ap[:, h, :], [("Sp", j, h)]))

            def rest_layer(l):
                if 2 not in phases:
                    return
                wq = WR.get()
                dma(wq, wview(w_in[l], 3072, 512), eng="pool")
                for hx in range(XH):
                    qxT = L['qx'][qxi[0] % 2]; qxi[0] += 1
                    for (c0, n) in chunks:
                        ps = PS.get()
                        for kc in range(NKC):
                            mm(ps[:, 0:n], wq[:, kc, hx * 128:(hx + 1) * 128], hTv(kc, c0, n), start=kc == 0, stop=kc == NKC - 1)
                        cp(qxT[:, c0:c0 + n], ps[:, 0:n])
                    if has_s:
                        KVB = L["KVB"]
                        kvi = [0]
                        def kvget():
                            b = KVB[kvi[0] % 3]; kvi[0] += 1
                            return b
                        mkTs = []; vbs = []
                        for hf in range(2):
                            kb = kvget()
                            dma(kb.pat([[256, 8], [128, 2], [1, 128]]), ck[l][8 * hf:8 * hf + 8, :, hx * 128:(hx + 1) * 128].rearrange("s (mc p) d -> p s mc d", p=128), eng="pool")
                            mk_ = kvget()
                            for b4 in range(2):
                                psb = bfv(PS.get())
                                for q in range(8):
                                    idx = b4 * 8 + q
                                    tr(psb[:, q * 128:(q + 1) * 128], kb[:, idx * 128:(idx + 1) * 128], identB)
                                cp(mk_[:, b4 * 1024:(b4 + 1) * 1024], psb)
                            mkTs.append(mk_)
                    for li, t in enumerate(grp):
                        samp = (t == "s")
                        c0 = li * 128
                        ps = PS.get()
                        if not samp:
                            mm(ps[:, 0:256], qxT[:, c0:c0 + 128], mkT[:, l, hx, :])
                        else:
                            cp(QXM.pat([[136, 16], [1, 8]]), qxT[:, c0:c0 + 128].pat([[8, 16], [1, 8]]), eng="dve")
                            for s in range(NS):
                                mm(ps[:, 0:256], QXM[:, s * 128:(s + 1) * 128], mkTs[s // 8][:, (s % 8) * 256:(s % 8 + 1) * 256], start=(s == 0), stop=(s == NS - 1))
                        ss = TS8.get()
                        red(ss[:, 0:1], ps[:, 0:256], MAX, negate=True)
                        ts(ss[:, 1:2], ss[:, 0:1], SC_ATT, MUL)
                        pe_ = T512.get()
                        act(pe_[:, 0:256], ps[:, 0:256], AF.Exp, bias=ss[:, 1:2], scale=SC_ATT, acc=ss[:, 2:3])
                        recip(ss[:, 3:4], ss[:, 2:3])
                        pn = TB256.get()
                        ts(pn, pe_[:, 0:256], ss[:, 3:4], MUL)
                        psb = bfv(PS.get())
                        tr(psb[:, 0:128], pn[:, 0:128], identB)
                        tr(psb[:, 128:256], pn[:, 128:256], identB)
                        pT = TB256.get()
                        cp(pT, psb[:, 0:256])
                        psO = PS.get()
                        if not samp:
                            for mc in range(2):
                                mm(psO[:, 0:128], mvb[:, l, mc, hx * 128:(hx + 1) * 128], pT[:, mc * 128:(mc + 1) * 128], start=mc == 0, stop=mc == 1)
                        else:
                            for hf in range(2):
                                vb_ = kvget()
                                dma(vb_.pat([[256, 8], [128, 2], [1, 128]]), cv[l][8 * hf:8 * hf + 8, :, hx * 128:(hx + 1) * 128].rearrange("s (mc p) d -> p s mc d", p=128), eng="pool")
                                vbs.append(vb_)
                            for s in range(NS):
                                for mc in range(2):
                                    vb = vbs[s // 8]
                                    mm(psO[:, s * 8:(s + 1) * 8], vb[:, ((s % 8) * 2 + mc) * 128:((s % 8) * 2 + mc + 1) * 128],
                                       pT[:, mc * 128 + s * 8:mc * 128 + (s + 1) * 8], start=mc == 0, stop=mc == 1)
                        cp(brv(8 + hx, c0, 128), psO[:, 0:128])
                if 3 not in phases:
                    return
                for gb in range(3):
                    wg = WR.get()
                    dma(wg, wview(w_in[l], 3584 + gb * 512, 512), eng="pool")
                    for sub in range(4):
                        bi = gb * 4 + sub
                        for (c0, n) in chunks:
                            ps = PS.get()
                            for kc in range(NKC):
                                mm(ps[:, 0:n], wg[:, kc, sub * 128:(sub + 1) * 128], hTv(kc, c0, n), start=kc == 0, stop=kc == NKC - 1)
                            sgt = TB512.get()
                            act(sgt[:, 0:n], ps[:, 0:n], AF.Sigmoid)
                            tt(sgt[:, 0:n], sgt[:, 0:n], ps[:, 0:n], MUL)
                            tt(brv(bi, c0, n), brv(bi, c0, n), sgt[:, 0:n], MUL)
                if 4 not in phases:
                    return
                for ob in range(NKC):
                    wo = WO.get()
                    dma(wo, w_out[l][:, ob * 128:(ob + 1) * 128].rearrange("(kc p) c -> p kc c", p=128), eng="pool")
                    for (c0, n) in chunks:
                        ps = PS.get()
                        for kc in range(12):
                            mm(ps[:, 0:n], wo[:, kc, :], brv(kc, c0, n), start=kc == 0, stop=kc == 11)
                        tt(xTv(ob, c0, n), xTv(ob, c0, n), ps[:, 0:n], ADD)

            def rwkv_layer(l, j):
                m = j - 1
                cfgn = "16" if has_s else "1"
                MB = C("mb" + cfgn); SL = C("sl" + cfgn)
                TIC = C("tic" + cfgn); TSC = C("tsc" + cfgn); BOC = C("boc" + cfgn)
                LEV = 3 if has_s else 7
                if levo is not None:
                    LEV = levo
                muT = MUC[:, 0:40].pat([[8, 5], [1, 8]]); omuT = MUC[:, 40:80].pat([[8, 5], [1, 8]])
                dma(muT, rw_mu[j].rearrange("k (kc p) -> p k kc", p=128), slow=True)
                ts(omuT, muT, -1.0, MUL, 1.0, ADD)
                raw = WR.get()
                dma(raw[:, :, 0:64], rw_w1[j].rearrange("(kc p) c -> p kc c", p=128), eng="pool")
                dma(raw[:, :, 64:128], rw_a1[j].rearrange("(kc p) c -> p kc c", p=128), eng="pool")
                if j >= 1:
                    mset(raw[:, :, 160:256], 0.0)
                    dma(raw[:, :, 128:160], rw_v1[m].rearrange("(kc p) c -> p kc c", p=128), eng="pool")
                for (c_lo, kind) in ((0, 1), (64, 4)):
                    src = raw[:, :, c_lo:c_lo + 64]
                    tt(L1a[:, :, c_lo:c_lo + 64], src, omuT[:, kind, :].pat([[1, 8], [0, 64]]), MUL)
                    tt(L1b[:, :, c_lo:c_lo + 64], src, muT[:, kind, :].pat([[1, 8], [0, 64]]), MUL)
                if j >= 1:
                    src = raw[:, :, 128:256]
                    tt(L1va, src, omuT[:, 3, :].pat([[1, 8], [0, 128]]), MUL)
                    tt(L1vb, src, muT[:, 3, :].pat([[1, 8], [0, 128]]), MUL)
                W2 = BIGV[0]; V2 = BIGV[1]
                xin = XIN.get()
                dma(xin[0:64, :], rw_w2[j]); dma(xin[64:128, :], rw_a2[j])
                cp(W2, xin)
                if j >= 1:
                    xin = XIN.get()
                    mset(xin, 0.0)
                    dma(xin[0:32, :], rw_v2[m])
                    cp(V2, xin, eng="dve")
                for (c0, n) in chunks:
                    ps = PS.get()
                    for kc in range(NKC):
                        mm(ps[:, 0:n], L1a[:, kc, :], hTv(kc, c0, n), start=kc == 0, stop=False)
                    for kc in range(NKC):
                        mm(ps[:, 0:n], L1b[:, kc, :], hprevv(kc, c0, n), start=False, stop=kc == NKC - 1)
                    th = T512.get()
                    act(th[:, 0:n], ps[:, 0:n], AF.Tanh)
                    ts(V(l1T.ap[:, 0, c0:c0 + n], tkeys(c0, n, lambda li: ("l1", li))), th[:, 0:n], C("hw"), MUL)
                    ts(V(l1T.ap[:, 1, c0:c0 + n], tkeys(c0, n, lambda li: ("l1", li))), ps[:, 0:n], C("ha"), MUL)
                    if j >= 1:
                        ps = PS.get()
                        for kc in range(NKC):
                            mm(ps[:, 0:n], L1va[:, kc, :], hTv(kc, c0, n), start=kc == 0, stop=False)
                        for kc in range(NKC):
                            mm(ps[:, 0:n], L1vb[:, kc, :], hprevv(kc, c0, n), start=False, stop=kc == NKC - 1)
                        cp(V(l1vT.ap[:, c0:c0 + n], tkeys(c0, n, lambda li: ("l1v", li))), ps[:, 0:n])

                for pr in range(8):
                    cc = pr * 128
                    wa = WM.get(); wb = WM2.get()
                    for q in range(3):
                        dma(wa[:, :, q * 128:(q + 1) * 128], wview(w_in[l], q * 1024 + cc, 128), eng="pool")
                    for q, kind in enumerate((0, 2, 3)):
                        tt(wb[:, :, q * 128:(q + 1) * 128], wa[:, :, q * 128:(q + 1) * 128], muT[:, kind, :].pat([[1, 8], [0, 128]]), MUL)
                    tt(wa, wa, wb, SUB)
                    pv = PVB
                    vecs = [rw_w0[j:j + 1], rw_a0[j:j + 1], (rw_v0[m:m + 1] if j >= 1 else rw_a0[j:j + 1]), rw_kk[j:j + 1], rw_ka[j:j + 1], rw_rk[j:j + 1], rw_lg[j:j + 1], rw_lb[j:j + 1]]
                    for k8, vv in enumerate(vecs):
                        dma(pv[:, k8, :], vv[:, cc:cc + 128].partition_broadcast(128))
                    Mpp = V(Mp[j].ap[:, pr, :], [("Mp", j, pr)], True)
                    MppB = V(MpB[j].ap[:, pr, :], [("MpB", j, pr)])
                    if has_s:
                        M0b = L["M0b"]
                        M0all = L["M0all"]
                        for q in range(4):
                            SBD = L["SBD"][q % 2]
                            for hh in range(2):
                                dma(SBD[hh * 64:(hh + 1) * 64, :].pat([[128, 4], [1, 64]], off=hh * 64),
                                    strw[j, 4 * q:4 * q + 4, 2 * pr + hh].rearrange("s i k -> i s k"))
                            ps = PS.get()
                            for s4 in range(4):
                                tr(ps[:, s4 * 128:(s4 + 1) * 128], SBD[:, s4 * 128:(s4 + 1) * 128], identF)
                            for hh in range(2):
                                cp(M0all[hh * 64:(hh + 1) * 64, 4 * q:4 * q + 4, :], ps[hh * 64:(hh + 1) * 64, :].pat([[128, 4], [1, 64]], off=hh * 64),
                                   eng=("act" if hh == 0 else "dve"))
                                cp(M0b[hh * 64:(hh + 1) * 64, 4 * q:4 * q + 4, :], ps[hh * 64:(hh + 1) * 64, :].pat([[128, 4], [1, 64]], off=hh * 64),
                                   eng=("dve" if hh == 0 else "act"))
                    segB_prev = None
                    for li, t in enumerate(grp):
                        samp = (t == "s")
                        c0 = li * 128
                        gt = 16 if samp else t
                        segA = P.begin_seg()
                        if segB_prev is not None and PIPE:
                            P.merges.append((segB_prev, segA))
                        psR = PS.get()
                        for kc in range(NKC):
                            mm(psR[:, 0:384], hTv(kc, c0, 128), wa[:, kc, :], start=kc == 0, stop=False)
                        for kc in range(NKC):
                            mm(psR[:, 0:384], hprevv(kc, c0, 128), wb[:, kc, :], start=False, stop=kc == NKC - 1)
                        rP, kP, vP = psR[:, 0:128], psR[:, 128:256], psR[:, 256:384]
                        l1k = ("l1", li)
                        psL = PS.get()
                        mm(psL[:, 0:128], V(l1T.ap[:, 0, c0:c0 + 128], [l1k]), W2[:, cc:cc + 128])
                        mm(psL[:, 128:256], V(l1T.ap[:, 1, c0:c0 + 128], [l1k]), W2[:, cc:cc + 128])
                        if j >= 1:
                            mm(psL[:, 256:384], V(l1vT.ap[:, c0:c0 + 128], [("l1v", li)]), V2[:, cc:cc + 128])
                        npre = 384 if j >= 1 else 256
                        pre = T512.get(); tt(pre[:, 0:npre], psL[:, 0:npre], pv.pat([[1, npre]]), ADD)
                        sgs = T512.get(); act(sgs[:, 0:npre], pre[:, 0:npre], AF.Sigmoid)
                        sg = sgs[:, 0:128]; ag = sgs[:, 128:256]
                        Vt = BT128.get()
                        vkey = ("vfd", gt, pr)
                        if j == 0:
                            cp(Vt, vP)
                            P.op("sp", (lambda e, o=vfd[gt * 128:(gt + 1) * 128, cc:cc + 128], i_=Vt.ap: e.dma_start(out=o, in_=i_)),
                                 reads=list(Vt.keys), writes=[vkey], dma=True)
                        else:
                            vf = NRV.get()
                            P.op("sp", (lambda e, i_=vfd[gt * 128:(gt + 1) * 128, cc:cc + 128], o=vf.ap: e.dma_start(out=o, in_=i_)),
                                 reads=[vkey], writes=list(vf.keys), dma=True)
                            vg = sgs[:, 256:384]
                            d1 = T128.get(); tt(d1, vf, vP, SUB)
                            tt(d1, d1, vg, MUL)
                            tt(Vt, d1, vP, ADD)
                        kk = T128.get(); tt(kk, kP, pv[:, 3, :], MUL)
                        sq = T128.get(); tt(sq, kk, kk, MUL)
                        sm = TS8.get()
                        red(sm[:, 0:2], sq.pat([[64, 2], [1, 64]]), ADD)
                        act(sm[:, 2:4], sm[:, 0:2], AF.Ln)
                        act(sm[:, 4:6], sm[:, 2:4], AF.Exp, scale=-0.5)
                        ts(sm[:, 4:6], sm[:, 4:6], 1e12, ALU.min)
                        kkn = T128.get()
                        tt(kkn.pat([[64, 2], [1, 64]]), kk.pat([[64, 2], [1, 64]]), sm[:, 4:6].pat([[1, 2], [0, 64]]), MUL)
                        t1 = T128.get(); stt(t1, ag, -1.0, pv[:, 4, :], ADD, MUL)
                        BK = T256.get()
                        kp = BK[:, 128:256]; stt(kp, t1, 1.0, kP, ADD, MUL)
                        bb = BK[:, 0:128]; tt(bb, kkn, ag, MUL)
                        t2 = T128.get(); tt(t2, rP, kp, MUL)
                        tt(t2, t2, pv[:, 5, :], MUL)
                        red(sm[:, 6:8], t2.pat([[64, 2], [1, 64]]), ADD)
                        psC = PS.get()
                        mm(psC[:, 0:128], TIC, sg)
                        mm(psC[:, 128:256], TSC, sg)
                        mm(psC[:, 256:384], BOC, sg)
                        nsq = 16 if samp else 1
                        nsq = 16 if samp else 2
                        mm(psC[:, 384:384 + nsq], sg, (C("sel16c") if samp else C("negc")))
                        EE = T512.get(); act(EE[:, 0:384], psC[:, 0:384], AF.Exp)
                        E1 = EE[:, 0:128]; E3 = EE[:, 128:256]; E5 = EE[:, 256:384]
                        E2 = T128.get(); act(E2, psC[:, 0:128], AF.Exp, scale=-1.0)
                        gC = TS8s.get(); act(gC[:, 0:nsq], psC[:, 384:384 + nsq], AF.Exp)
                        dg = [[192, 2], [1, 64]]; nd = [[64, 2], [1, 64]]
                        Am, Rm = ZP; Bm = ZBK[:, 0:256]; Km = ZBK[:, 256:512]
                        ZH = ZBKH[li % 2]; BHm = ZH[:, 0:256]; KHm = ZH[:, 256:512]
                        dg2 = [[256, 2], [192, 2], [1, 64]]
                        stt(Am.pat(dg), kkn.pat(nd), -1.0, E3.pat(nd), MUL, MUL)
                        tt(ZBK.pat(dg2), BK.pat([[128, 2], [64, 2], [1, 64]]), E2.pat([[0, 2], [64, 2], [1, 64]]), MUL)
                        tt(Rm.pat(dg), rP.pat(nd), E1.pat(nd), MUL)
                        tt(ZH.pat(dg2), ZBK.pat(dg2), E5.pat([[0, 2], [64, 2], [1, 64]]), MUL)
                        XT = BT1K.get()
                        psb = bfv(PS.get())
                        for q, Xm in enumerate((Am, Am, Bm, Bm, Km, Km, Rm, Rm)):
                            hh = q % 2
                            tr(psb[:, q * 128:(q + 1) * 128], Xm[:, hh * 128:(hh + 1) * 128], identB)
                        cp(XT[:, 0:512], psb[:, 0:512]); cp(XT[:, 512:1024], psb[:, 512:1024], eng="dve")
                        AT = lambda hh: XT[:, (0 + hh) * 128:(1 + hh) * 128]
                        BT = lambda hh: XT[:, (2 + hh) * 128:(3 + hh) * 128]
                        KT = lambda hh: XT[:, (4 + hh) * 128:(5 + hh) * 128]
                        RT = lambda hh: XT[:, (6 + hh) * 128:(7 + hh) * 128]
                        Mk = []; XY = BT512.get(); WT = BT256.get()
                        for hh in range(2):
                            psM = PS.get()
                            mm(psM[:, 0:128], BT(hh), AT(hh))
                            mm(psM[:, 128:256], KT(hh), AT(hh))
                            mm(psM[:, 256:384], BT(hh), RT(hh))
                            mm(psM[:, 384:512], KT(hh), RT(hh))
                            mk_ = MK[2 * (li % 2) + hh]; tt(mk_, psM, MB, MUL)
                            Mk.append(mk_)
                            psN = PS.get()
                            mm(psN[:, 0:128], AT(hh), BT(hh))
                            tt(XY[:, hh * 256 + 128:hh * 256 + 256], psN[:, 0:128], SL, MUL)
                            tt(WT[:, hh * 128:(hh + 1) * 128], mk_[:, 0:128], identF, ADD)
                        for lev in range(1, LEV):
                            psV = PS.get()
                            for hh in range(2):
                                Xp = Mk[hh][:, 0:128] if lev == 1 else XY[:, hh * 256:hh * 256 + 128]
                                Yp = XY[:, hh * 256 + 128:hh * 256 + 256]
                                mm(psV[:, hh * 256 + 128:hh * 256 + 256], Xp, Yp)
                                if lev < LEV - 1:
                                    mm(psV[:, hh * 256:hh * 256 + 128], Yp, Xp)
                            XYn = BT512.get()
                            if lev < LEV - 1:
                                cp(XYn, psV)
                            else:
                                cp(XYn.pat([[256, 2], [1, 128]], off=128), psV.pat([[256, 2], [1, 128]], off=128))
                            psW = PS.get()
                            for hh in range(2):
                                mm(psW[:, hh * 128:(hh + 1) * 128], XYn[:, hh * 256 + 128:hh * 256 + 256], WT[:, hh * 128:(hh + 1) * 128])
                            WTn = BT256.get()
                            tt(WTn, psW[:, 0:256], WT, ADD)
                            XY = XYn; WT = WTn
                        P.end_seg()
                        segB_prev = P.begin_seg()
                        if samp:
                            cp(Z1b.pat([[136, 16], [1, 8]]), AT(0).pat([[8, 16], [1, 8]]), eng="pool")
                            cp(Z2b.pat([[136, 16], [1, 8]]), AT(1).pat([[8, 16], [1, 8]]), eng="pool")
                        psP = PS.get()
                        for hh in range(2):
                            o_ = psP[:, hh * 64:(hh + 1) * 64]
                            if not samp:
                                mm(o_, AT(hh), MppB, start=True, stop=False)
                            else:
                                ZZ = Z1b if hh == 0 else Z2b
                                for s in range(NS):
                                    mm(o_, ZZ[:, s * 128:(s + 1) * 128], M0b[:, s, :], start=(s == 0), stop=False)
                            mm(o_, Mk[hh][:, 128:256], Vt[:, hh * 64:(hh + 1) * 64], start=False, stop=True)
                        P1 = BT128.get(); cp(P1, psP[:, 0:128])
                        psU = PS.get()
                        for hh in range(2):
                            mm(psU[:, hh * 64:(hh + 1) * 64], WT[:, hh * 128:(hh + 1) * 128], P1[:, hh * 64:(hh + 1) * 64])
                        U = BT128.get(); cp(U, psU[:, 0:128], eng="dve")
                        if samp:
                            cp(Z1b.pat([[136, 16], [1, 8]]), RT(0).pat([[8, 16], [1, 8]]), eng="pool")
                            cp(Z2b.pat([[136, 16], [1, 8]]), RT(1).pat([[8, 16], [1, 8]]), eng="pool")
                        psY = PS.get()
                        for hh in range(2):
                            o_ = psY[:, hh * 64:(hh + 1) * 64]
                            if not samp:
                                mm(o_, RT(hh), MppB, start=True, stop=False)
                            else:
                                ZZ = Z1b if hh == 0 else Z2b
                                for s in range(NS):
                                    mm(o_, ZZ[:, s * 128:(s + 1) * 128], M0b[:, s, :], start=(s == 0), stop=False)
                            mm(o_, Mk[hh][:, 256:384], U[:, hh * 64:(hh + 1) * 64], start=False, stop=False)
                            mm(o_, Mk[hh][:, 384:512], Vt[:, hh * 64:(hh + 1) * 64], start=False, stop=True)
                        if not samp:
                            psS = PS.get()
                            mm(psS[:, 0:64], BHm[:, 0:128], U[:, 0:64], start=True, stop=False)
                            mm(psS[:, 0:64], BHm[:, 128:256], U[:, 64:128], start=False, stop=False)
                            mm(psS[:, 0:64], KHm[:, 0:128], Vt[:, 0:64], start=False, stop=False)
                            mm(psS[:, 0:64], KHm[:, 128:256], Vt[:, 64:128], start=False, stop=True)
                            stt(Mpp, Mpp, gC[:, 0:1], psS[:, 0:64], MUL, ADD)
                            cp(MppB, Mpp)
                        else:
                            for q in range(2):
                                Uw = [BT512.get(), BT512.get()]; Vw = [BT512.get(), BT512.get()]
                                for hh in range(2):
                                    tt(Uw[hh].pat([[64, 8], [1, 64]]), U[:, hh * 64:(hh + 1) * 64].pat([[0, 8], [1, 64]]),
                                       C("sel16").pat([[1, 8], [0, 64]], off=8 * q), MUL)
                                    tt(Vw[hh].pat([[64, 8], [1, 64]]), Vt[:, hh * 64:(hh + 1) * 64].pat([[0, 8], [1, 64]]),
                                       C("sel16").pat([[1, 8], [0, 64]], off=8 * q), MUL, eng="pool")
                                psS = PS.get()
                                mm(psS, BHm[:, 0:128], Uw[0], start=True, stop=False)
                                mm(psS, BHm[:, 128:256], Uw[1], start=False, stop=False)
                                mm(psS, KHm[:, 0:128], Vw[0], start=False, stop=False)
                                mm(psS, KHm[:, 128:256], Vw[1], start=False, stop=True)
                                tmp = T512.get()
                                tt(tmp.pat([[64, 8], [1, 64]]), M0all[:, 8 * q:8 * q + 8, :], gC[:, 8 * q:8 * q + 8].pat([[1, 8], [0, 64]]), MUL)
                                Mn = T512.get()
                                tt(Mn, tmp, psS, ADD)
                                for q2 in range(2):
                                    MBD = L["MBD"][q2]
                                    for hh in range(2):
                                        cp(MBD[hh * 64:(hh + 1) * 64, :].pat([[128, 4], [1, 64]], off=hh * 64),
                                           Mn[hh * 64:(hh + 1) * 64, q2 * 256:(q2 + 1) * 256].pat([[64, 4], [1, 64]]), eng=("act" if hh == 0 else "pool"))
                                    ps = PS.get()
                                    for s4 in range(4):
                                        tr(ps[:, s4 * 128:(s4 + 1) * 128], MBD[:, s4 * 128:(s4 + 1) * 128], identF)
                                    So = T256.get()
                                    for hh in range(2):
                                        cp(So[hh * 64:(hh + 1) * 64, :].pat([[64, 4], [1, 64]]),
                                           ps[hh * 64:(hh + 1) * 64, :].pat([[128, 4], [1, 64]], off=hh * 64), eng=("act" if hh == 0 else "dve"))
                                    sb0 = 8 * q + 4 * q2
                                    for hh in range(2):
                                        dma(srs[j, sb0:sb0 + 4, 2 * pr + hh].rearrange("s i k -> i s k"),
                                            So[hh * 64:(hh + 1) * 64, :].pat([[64, 4], [1, 64]]))
                        g2 = [[64, 2], [1, 64]]
                        red(sm[:, 8:10], psY[:, 0:128].pat(g2), ADD)
                        ysq = T128.get(); act(ysq, psY[:, 0:128], AF.Square)
                        red(sm[:, 10:12], ysq.pat(g2), ADD)
                        ts(sm[:, 12:14], sm[:, 8:10], 1.0 / 64, MUL)
                        tt(sm[:, 14:16], sm[:, 12:14], sm[:, 12:14], MUL)
                        stt(sm[:, 16:18], sm[:, 10:12], 1.0 / 64, sm[:, 14:16], MUL, SUB)
                        act(sm[:, 18:20], sm[:, 16:18], AF.Ln, bias=LNX_EPS, scale=1.0)
                        act(sm[:, 20:22], sm[:, 18:20], AF.Exp, scale=-0.5)
                        yc = T128.get()
                        tt(yc.pat(g2), psY[:, 0:128].pat(g2), sm[:, 12:14].pat([[1, 2], [0, 64]]), SUB)
                        tt(yc.pat(g2), yc.pat(g2), sm[:, 20:22].pat([[1, 2], [0, 64]]), MUL)
                        tt(yc, yc, pv[:, 6, :], MUL)
                        tt(yc, yc, pv[:, 7, :], ADD)
                        yb = T128.get()
                        tt(yb.pat(g2), Vt.pat(g2), sm[:, 6:8].pat([[1, 2], [0, 64]]), MUL)
                        tt(yc, yc, yb, ADD)
                        psT = PS.get()
                        tr(psT[:, 0:128], yc, identF)
                        cp(brv(pr, c0, 128), psT[:, 0:128])
                        P.end_seg()
                if first_last_prompt_group:
                    for pr in range(8):
                        MBD = PMBD
                        Mpp = V(Mp[j].ap[:, pr, :], [("Mp", j, pr)])
                        for hh in range(2):
                            cp(MBD[hh * 64:(hh + 1) * 64, hh * 64:(hh + 1) * 64], Mpp[hh * 64:(hh + 1) * 64, :], eng=("act" if hh == 0 else "pool"))
                        ps = PS.get()
                        tr(ps[:, 0:128], MBD, identF)
                        So = T128.get()
                        for hh in range(2):
                            cp(So[hh * 64:(hh + 1) * 64, 0:64], ps[hh * 64:(hh + 1) * 64, hh * 64:(hh + 1) * 64], eng=("act" if hh == 0 else "dve"))
                        for hh in range(2):
                            dma(srp[j, 2 * pr + hh], So[hh * 64:(hh + 1) * 64, 0:64])

            first_last_prompt_group = (15 in ptiles)
            for l in range(depth):
                is_rw = (l % 2 == 1)
                j = l // 2
                gcol = TS8.get()[:, 0:8]
                dma(gcol, norm_g[l].rearrange("(kc p) -> p kc", p=128), slow=True)
                if is_rw:
                    if first_group:
                        mset(V(hT.ap[:, :, 0:1], [("hT", "c0")]), 0.0)
                    elif npt > 0:
                        cp(V(hT.ap[:, :, 0], [("hT", "c0")]), hcar[j], eng="pool")
                for li, t in enumerate(grp):
                    rstd = rms_tile(li)
                    for kc in range(NKC):
                        stt(V(hT.ap[:, kc, 1 + li * 128:1 + (li + 1) * 128], [hkey(li)]), xTv(kc, li * 128, 128), gcol[:, kc:kc + 1], rstd, MUL, MUL)
                    if is_rw and t == 15:
                        hl = TS8.get()[:, 0:8]
                        ts(hl, V(xT.ap[:, :, li * 128 + 127], [xkey(li)]), rstd[:, 127:128], MUL)
                        tt(hl, hl, gcol, MUL)
                        dma(ssp[j].rearrange("(kc p) -> p kc", p=128), hl, slow=True)
                    if is_rw and t == "s":
                        hl = T128.get()
                        hl3 = hl.pat([[16, 8], [1, 16]])
                        xv = xTt(li).pat([[NTM, 8], [8, 16]], off=7)
                        rv = rstd.pat([[0, 8], [8, 16]], off=7)
                        tt(hl3, xv, rv, MUL)
                        tt(hl3, hl3, gcol.pat([[1, 8], [0, 16]]), MUL)
                        ps = PS.get()
                        tr(ps[:, 0:128], hl, identF)
                        ho = T128.get()
                        cp(ho, ps[:, 0:128])
                        for kc in range(NKC):
                            dma(sss[j, :, kc * 128:(kc + 1) * 128], ho[kc * 16:(kc + 1) * 16, :])
                        stin = XIN.get()
                        mset(stin, 0.0)
                        dma(stin[0:16, :], sts[j])
                        for half in range(2):
                            ps = PS.get()
                            for q in range(4):
                                kc = half * 4 + q
                                tr(ps[:, q * 128:(q + 1) * 128], stin[:, kc * 128:(kc + 1) * 128], identF)
                            cp(hpS.pat([[128, 4], [8, 16]], off=half * 4 * 128), ps.pat([[128, 4], [1, 16]]))
                        hsv = V(hT.ap[:, :, 1 + li * 128:1 + (li + 1) * 128], [hkey(li)])
                        cp(hpS.pat([[128, 8], [8, 16], [1, 7]], off=1), hsv.pat([[HS, 8], [8, 16], [1, 7]]), eng="pool")
                if is_rw and npt > 0:
                    cp(hcar[j], V(hT.ap[:, :, npt * 128], [hkey(npt - 1)]), eng="pool")
                if 1 in phases:
                    if not is_rw:
                        hgrn_layer(l, j)
                    else:
                        rwkv_layer(l, j)
                rest_layer(l)

            fg = TS8.get()[:, 0:8]
            dma(fg, final_g[0].rearrange("(kc p) -> p kc", p=128), slow=True)
            for li, t in enumerate(grp):
                rstd = rms_tile(li)
                yT = T1K.get()
                for kc in range(NKC):
                    stt(yT[:, kc * 128:(kc + 1) * 128], xTv(kc, li * 128, 128), fg[:, kc:kc + 1], rstd, MUL, MUL)
                yo = T1K.get()
                for half in range(2):
                    ps = PS.get()
                    for q in range(4):
                        kc = half * 4 + q
                        tr(ps[:, q * 128:(q + 1) * 128], yT[:, kc * 128:(kc + 1) * 128], identF)
                    cp(yo[:, half * 512:(half + 1) * 512], ps)
                dma(ys if t == "s" else yp[t * 128:(t + 1) * 128, :], yo)

        for gi, grp in enumerate(groups):
            process_group(gi, grp)

        info = P.emit()
    return nc, info


def _shard_inputs(inp):
    cf, cb = make_consts()
    maps = []
    for c in range(8):
        b = slice(16 * c, 16 * c + 16)
        m = {
            "xp": np.ascontiguousarray(inp["x_prompt"][c]),
            "xs": np.ascontiguousarray(inp["x_sample"][b]).reshape(128, D),
            "mem": np.ascontiguousarray(inp["mem_prompt"][c]),
            "sth": np.ascontiguousarray(inp["state_hgrn"][:, b]),
            "strw": np.ascontiguousarray(inp["state_rwkv"][:, b]),
            "sts": np.ascontiguousarray(inp["state_shift"][:, b]),
            "ck": np.ascontiguousarray(inp["cache_mem_k"][:, b]).reshape(4, 16, MEM, 512),
            "cv": np.ascontiguousarray(inp["cache_mem_v"][:, b]).reshape(4, 16, MEM, 512),
            "rw_rk": np.ascontiguousarray(inp["rw_rk"]).reshape(2, D),
            "final_g": np.ascontiguousarray(inp["final_g"]).reshape(1, D),
            "cf": cf, "cb": cb,
        }
        for k in ("norm_g", "w_in", "w_out", "mem_norm_g", "w_mem_kv", "hg_lb", "hg_onorm_g", "rw_mu", "rw_w0", "rw_w1",
                  "rw_w2", "rw_a0", "rw_a1", "rw_a2", "rw_v0", "rw_v1", "rw_v2", "rw_kk", "rw_ka", "rw_lnx_g", "rw_lnx_b"):
            m[k] = np.ascontiguousarray(inp[k])
        maps.append(m)
    return maps


_NC_CACHE = {}


def kernel(**inputs):
    inp = {k: np.asarray(v, dtype=np.float32) for k, v in inputs.items()}
    if "nc" not in _NC_CACHE:
        _NC_CACHE["nc"] = build()[0]
    nc = _NC_CACHE["nc"]
    maps = _shard_inputs(inp)
    res = run_bass_kernel_spmd(nc, maps, core_ids=list(range(8)))
    R = res.results
    y_p = np.stack([R[c]["yp"] for c in range(8)], 0)
    y_s = np.concatenate([R[c]["ys"].reshape(16, 8, D) for c in range(8)], 0)
    sh_p = np.stack([R[c]["shp"] for c in range(8)], 1)
    sr_p = np.stack([R[c]["srp"] for c in range(8)], 1)
    ss_p = np.stack([R[c]["ssp"] for c in range(8)], 1)
    mk_p = np.stack([R[c]["mkp"].reshape(4, MEM, 4, 128) for c in range(8)], 1)
    mv_p = np.stack([R[c]["mvp"].reshape(4, MEM, 4, 128) for c in range(8)], 1)
    sh_s = np.concatenate([R[c]["shs"] for c in range(8)], 1)
    sr_s = np.concatenate([R[c]["srs"] for c in range(8)], 1)
    ss_s = np.concatenate([R[c]["sss"] for c in range(8)], 1)
    return (y_p, y_s, sh_p, sr_p, ss_p, mk_p, mv_p, sh_s, sr_s, ss_s)
```

```python
import contextlib
import numpy as np
import ml_dtypes
import concourse.bass as bass
import concourse.mybir as mybir
from concourse.bass_utils import run_bass_kernel_spmd

F32 = mybir.dt.float32
F32R = mybir.dt.float32r
BF16 = mybir.dt.bfloat16
AF = mybir.ActivationFunctionType
ALU = mybir.AluOpType
AX = mybir.AxisListType

ENGS = ("pe", "act", "dve", "pool", "sp")
DMA_K = 6
SEM_WRAP = 30000
SAME_ENG_RAW_WAITS = True
PIPE = True


class Prog:
    def __init__(self, nc):
        self.nc = nc
        self.ops = []
        self.last_w = {}
        self.readers = {}
        self.excl = set()
        self.seg = None
        self.nseg = 0
        self.merges = []

    def begin_seg(self):
        self.nseg += 1
        self.seg = self.nseg
        return self.seg

    def end_seg(self):
        self.seg = None

    def op(self, eng, fn, reads=(), writes=(), dma=False):
        i = len(self.ops)
        ex = [k for k in reads if k in self.excl]
        if ex:
            writes = list(writes) + ex
        deps = {}
        for k in reads:
            w = self.last_w.get(k)
            if w is not None:
                deps[w] = True
        for k in writes:
            w = self.last_w.get(k)
            if w is not None:
                deps.setdefault(w, False)
            for r in self.readers.get(k, ()):
                deps.setdefault(r, False)
        o = dict(i=i, eng=eng, fn=fn, dma=dma, deps=deps, seg=self.seg)
        self.ops.append(o)
        for k in reads:
            self.readers.setdefault(k, []).append(i)
        for k in writes:
            self.last_w[k] = i
            self.readers[k] = []
        return i

    def barrier(self):
        lastc = {e: None for e in ENGS}
        lastd = {e: [] for e in ENGS}
        for o in self.ops:
            if o["dma"]:
                lastd[o["eng"]].append(o["i"])
            else:
                lastc[o["eng"]] = o["i"]
        extra = [v for v in lastc.values() if v is not None]
        for e in ENGS:
            extra += lastd[e][-DMA_K:]
        for e in ENGS:
            i = self.op(e, lambda eng: eng.nop(), reads=(), writes=())
            for x in extra:
                self.ops[i]["deps"][x] = True

    def _order(self):
        n = len(self.ops)
        order = []
        segops = {}
        for o in self.ops:
            if o["seg"] is not None:
                segops.setdefault(o["seg"], []).append(o["i"])
        partner = {x: y for x, y in self.merges}
        done_seg = set()
        i = 0
        while i < n:
            o = self.ops[i]
            sg = o["seg"]
            if sg is not None and sg in partner and sg not in done_seg:
                X = segops[sg]; Y = segops[partner[sg]]
                assert X[-1] + 1 == Y[0] and X == list(range(X[0], X[-1] + 1)) and Y == list(range(Y[0], Y[-1] + 1))
                emitted = set()
                xi = yi = 0
                xset = set(X)
                turn = 0
                while xi < len(X) or yi < len(Y):
                    pick_y = False
                    if yi < len(Y) and (turn == 1 or xi >= len(X)):
                        yo = self.ops[Y[yi]]
                        if all((d not in xset) or (d in emitted) for d in yo["deps"]):
                            pick_y = True
                    if pick_y:
                        order.append(Y[yi]); yi += 1
                    else:
                        order.append(X[xi]); emitted.add(X[xi]); xi += 1
                    turn ^= 1
                done_seg.add(sg); done_seg.add(partner[sg])
                i = Y[-1] + 1
                continue
            order.append(i)
            i += 1
        assert sorted(order) == list(range(n))
        return order

    def emit(self, final_wait_eng="sp"):
        nc = self.nc
        order = self._order()
        ops = [self.ops[i] for i in order]
        newidx = {o["i"]: k for k, o in enumerate(ops)}
        for k, o in enumerate(ops):
            o["deps"] = {newidx[d]: r for d, r in o["deps"].items()}
            assert all(d < k for d in o["deps"])
            o["i"] = k
        per_eng = {e: [] for e in ENGS}
        ndma = {e: 0 for e in ENGS}
        lastslot = {}
        for o in ops:
            e = o["eng"]
            o["pos"] = len(per_eng[e])
            per_eng[e].append(o["i"])
            if o["dma"]:
                nn = ndma[e]
                o["dslot"] = nn % DMA_K
                o["dval"] = 16 * (nn // DMA_K + 1)
                o["dprev"] = lastslot.get((e, o["dslot"]))
                lastslot[(e, o["dslot"])] = o["i"]
                ndma[e] = nn + 1
        known = {e: {f: -1 for f in ENGS} for e in ENGS}
        known_d = {e: {} for e in ENGS}
        milestone = set()
        for o in ops:
            e = o["eng"]
            wl = []
            deps = dict(o["deps"])
            if o["dma"] and o["dprev"] is not None:
                deps[o["dprev"]] = True
            for j, raw in deps.items():
                p = ops[j]
                f = p["eng"]
                if p["dma"]:
                    key = (f, p["dslot"])
                    if known_d[e].get(key, 0) < p["dval"]:
                        known_d[e][key] = p["dval"]
                        wl.append(("d", f, p["dslot"], p["dval"]))
                else:
                    if f == e and not o["dma"] and e != "pool":
                        if e == "pe" or not raw or not SAME_ENG_RAW_WAITS:
                            continue
                    if known[e][f] < p["pos"]:
                        wl.append(("c", f, p["pos"], j))
            best = {}
            out = []
            for w in wl:
                if w[0] == "c":
                    if w[1] not in best or best[w[1]][2] < w[2]:
                        best[w[1]] = w
                else:
                    out.append(w)
            for f, w in best.items():
                known[e][f] = w[2]
                milestone.add(w[3])
                out.append(w)
            o["waits"] = out
        cnt = {e: 0 for e in ENGS}
        for o in ops:
            if o["dma"]:
                continue
            if o["i"] in milestone:
                cnt[o["eng"]] += 1
                o["ms"] = cnt[o["eng"]]
            else:
                o["ms"] = None
        nsem = {e: max(1, (cnt[e] + SEM_WRAP - 1) // SEM_WRAP) for e in ENGS}
        nwaits = 0
        with contextlib.ExitStack() as st:
            csem = {e: [st.enter_context(nc.semaphore(f"c_{e}_{k}")) for k in range(nsem[e])] for e in ENGS}
            dsem = {e: [st.enter_context(nc.semaphore(f"d_{e}_{k}")) for k in range(DMA_K)]
                    for e in ENGS if ndma[e] > 0}
            block = st.enter_context(nc.Block())

            def mk(ename):
                def body(eng):
                    nonlocal nwaits
                    for i in per_eng[ename]:
                        o = ops[i]
                        for w in o["waits"]:
                            nwaits += 1
                            if w[0] == "c":
                                ms = ops[w[3]]["ms"]
                                k, v = (ms - 1) // SEM_WRAP, (ms - 1) % SEM_WRAP + 1
                                eng.wait_ge(csem[w[1]][k], v)
                            else:
                                eng.wait_ge(dsem[w[1]][w[2]], w[3])
                        ins = o["fn"](eng)
                        if o["dma"]:
                            ins.then_inc(dsem[ename][o["dslot"]], 16)
                        elif o["ms"] is not None:
                            k = (o["ms"] - 1) // SEM_WRAP
                            ins.then_inc(csem[ename][k], 1)
                    if ename == final_wait_eng:
                        for f in dsem:
                            last = {}
                            for j in per_eng[f]:
                                if ops[j]["dma"]:
                                    last[ops[j]["dslot"]] = ops[j]["dval"]
                            for s, v in last.items():
                                eng.wait_ge(dsem[f][s], v)
                return body

            block.tensor(mk("pe"))
            block.scalar(mk("act"))
            block.vector(mk("dve"))
            block.gpsimd(mk("pool"))
            block.sync(mk("sp"))
        return dict(n_ops=len(ops), per_eng={e: len(v) for e, v in per_eng.items()},
                    milestones=cnt, nwaits=nwaits)


class V:
    def __init__(self, ap, keys, r=False):
        self.ap = ap
        self.keys = tuple(keys)
        self.r = r

    def __getitem__(self, idx):
        return V(self.ap[idx], self.keys, self.r)

    def pat(self, pat, off=0):
        a = self.ap
        return V(bass.AP(a.tensor, a.offset + off, [list(a.ap[0])] + [list(p) for p in pat]), self.keys, self.r)

    def k(self, *keys):
        return V(self.ap, keys, self.r)

    def nr(self):
        return V(self.ap, self.keys, False)

    def rr(self):
        return V(self.ap, self.keys, True)


D = 1024
NKC = 8
SEQ = 2048
NPT = 16
NS = 16
TS = 8
MEM = 256
XH = 4
INC = 5120
CDEC = -float(np.exp(-0.5))
NORM_EPS = 1e-6
LNX_EPS = 64e-5
SC_ATT = 128 ** -0.5

CO = {}
_o = 0
for _n, _w in [("ident", 128), ("ones", 128),
               ("mb1", 512), ("sl1", 128), ("tic1", 128), ("tsc1", 128), ("boc1", 128),
               ("mb16", 512), ("sl16", 128), ("bo16", 128), ("tic16", 128), ("tsc16", 128), ("boc16", 128),
               ("iu2", 128), ("bo2", 128), ("sel16", 16), ("sel16c", 16), ("negc", 2), ("cm0", 1), ("cm1", 1),
               ("hw", 1), ("ha", 1)]:
    CO[_n] = (_o, _w)
    _o += _w
NCF = _o
CB = {"ident": (0, 128)}
NCB = 128


def make_consts():
    c = np.zeros((128, NCF), np.float32)
    s = np.arange(128)[:, None]
    t = np.arange(128)[None, :]
    def put(n, a):
        o, w = CO[n]
        c[:, o:o + w] = a
    put("ident", (s == t))
    put("ones", 1.0)
    for cfg, blk in ((1, 128), (16, 8), (2, 64)):
        same = (s // blk) == (t // blk)
        iu = same & (s <= t)
        su = same & (s < t)
        sl = same & (s > t)
        if cfg == 2:
            put("iu2", iu); put("bo2", same)
            continue
        put(f"mb{cfg}", np.concatenate([su, su, iu, iu], 1))
        put(f"sl{cfg}", sl)
        if cfg == 16:
            put("bo16", same)
        put(f"tic{cfg}", CDEC * iu); put(f"tsc{cfg}", CDEC * su); put(f"boc{cfg}", CDEC * same)
    sel = (np.arange(128)[:, None] // 8) == np.arange(16)[None, :]
    put("sel16", sel); put("sel16c", CDEC * sel); put("negc", CDEC)
    put("cm0", (np.arange(128) < 64)[:, None]); put("cm1", (np.arange(128) >= 64)[:, None])
    put("hw", (np.arange(128) < 64)[:, None]); put("ha", (np.arange(128) >= 64)[:, None])
    cb = np.zeros((128, NCB), np.float32)
    cb[:, 0:128] = np.eye(128)
    return c, cb.astype(ml_dtypes.bfloat16)


def build(depth=4, groups=None, dbg=False, memkv=True, phases=(1, 2, 3, 4), mks=9, levo=None):
    if groups is None:
        groups = [[0, 1, 2, 3], [4, 5, 6, 7], [8, 9, 10, 11], [12, 13, 14, 15], ["s"]]
    GMAX = max(len(g) for g in groups)
    NTM = GMAX * 128
    nc = bass.Bass("TRN2", target_bir_lowering=False)
    din = lambda n, s, d=F32: nc.dram_tensor(n, list(s), d, kind="ExternalInput").ap()
    dout = lambda n, s: nc.dram_tensor(n, list(s), F32, kind="ExternalOutput").ap()
    xp = din("xp", [SEQ, D]); xs = din("xs", [128, D]); mem = din("mem", [MEM, D])
    sth = din("sth", [2, NS, 8, 128, 128]); strw = din("strw", [2, NS, 16, 64, 64]); sts = din("sts", [2, NS, D])
    ck = din("ck", [4, NS, MEM, 512]); cv = din("cv", [4, NS, MEM, 512])
    norm_g = din("norm_g", [4, D]); w_in = din("w_in", [4, D, INC]); w_out = din("w_out", [4, 1536, D])
    mem_norm_g = din("mem_norm_g", [4, D]); w_mem_kv = din("w_mem_kv", [4, D, 1024])
    hg_lb = din("hg_lb", [2, D]); hg_og = din("hg_onorm_g", [2, 128])
    rw_mu = din("rw_mu", [2, 5, D]); rw_w0 = din("rw_w0", [2, D]); rw_w1 = din("rw_w1", [2, D, 64]); rw_w2 = din("rw_w2", [2, 64, D])
    rw_a0 = din("rw_a0", [2, D]); rw_a1 = din("rw_a1", [2, D, 64]); rw_a2 = din("rw_a2", [2, 64, D])
    rw_v0 = din("rw_v0", [1, D]); rw_v1 = din("rw_v1", [1, D, 32]); rw_v2 = din("rw_v2", [1, 32, D])
    rw_kk = din("rw_kk", [2, D]); rw_ka = din("rw_ka", [2, D]); rw_rk = din("rw_rk", [2, D])
    rw_lg = din("rw_lnx_g", [2, D]); rw_lb = din("rw_lnx_b", [2, D]); final_g = din("final_g", [1, D])
    cf_d = din("cf", [128, NCF]); cb_d = din("cb", [128, NCB], BF16)
    yp = dout("yp", [SEQ, D]); ys = dout("ys", [128, D])
    shp = dout("shp", [2, 8, 128, 128]); srp = dout("srp", [2, 16, 64, 64]); ssp = dout("ssp", [2, D])
    mkp = dout("mkp", [4, MEM, 512]); mvp = dout("mvp", [4, MEM, 512])
    shs = dout("shs", [2, NS, 8, 128, 128]); srs = dout("srs", [2, NS, 16, 64, 64]); sss = dout("sss", [2, NS, D])

    with contextlib.ExitStack() as st:
        P = Prog(nc)

        def sb(name, shape, dt=F32, r=False):
            t = st.enter_context(nc.sbuf_tensor(name, list(shape), dt))
            return V(t[:], [name], r and dt == F32)

        class Rot:
            def __init__(self, name, n, shape, dt=F32, r=False):
                self.bufs = [sb(f"{name}{i}", shape, dt, r) for i in range(n)]
                self.i = 0

            def get(self):
                b = self.bufs[self.i % len(self.bufs)]
                self.i += 1
                return b

        class PSR:
            def __init__(self):
                self.bufs = []
                for i in range(8):
                    t = st.enter_context(nc.psum_tensor(f"ps{i}", [128, 512], F32))
                    self.bufs.append(V(t[:], [f"ps{i}"]))
                    P.excl.add(f"ps{i}")
                self.i = 0

            def get(self):
                b = self.bufs[self.i % 8]
                self.i += 1
                return b

        PS = PSR()

        def bfv(v):
            return V(v.ap.bitcast(BF16), v.keys)

        def rk(*vs):
            out = []
            for v in vs:
                if isinstance(v, V):
                    out += list(v.keys)
            return out

        def a_(x):
            return x.ap if isinstance(x, V) else x

        def o_(x):
            return x.ap.bitcast(F32R) if x.r else x.ap

        def mm(o, l, r, start=True, stop=True):
            la, ra = l.ap, r.ap
            if l.r and r.r:
                la, ra = la.bitcast(F32R), ra.bitcast(F32R)
            P.op("pe", lambda e: e.matmul(o.ap, lhsT=la, rhs=ra, start=start, stop=stop), reads=rk(l, r), writes=rk(o))

        def tr(o, i, idn):
            P.op("pe", lambda e: e.transpose(out=o.ap, in_=i.ap, identity=idn.ap), reads=rk(i, idn), writes=rk(o))

        def act(o, i, f, bias=None, scale=None, acc=None):
            kw = {}
            if bias is not None:
                kw["bias"] = a_(bias)
            if scale is not None:
                kw["scale"] = a_(scale)
            if acc is not None:
                kw["accum_out"] = acc.ap
            oa = o_(o)
            P.op("act", lambda e: e.activation(out=oa, in_=i.ap, func=f, **kw), reads=rk(i, bias, scale), writes=rk(o, acc))

        def tt(o, a, b, op, eng="dve"):
            if eng == "pool" and o.r:
                eng = "dve"
            oa = o_(o)
            P.op(eng, lambda e: e.tensor_tensor(out=oa, in0=a.ap, in1=b.ap, op=op), reads=rk(a, b), writes=rk(o))

        def ts(o, a, s1, op0, s2=None, op1=None, eng="dve"):
            oa = o_(o)

            def f(e):
                if op1 is None:
                    return e.tensor_scalar(out=oa, in0=a.ap, scalar1=a_(s1), scalar2=None, op0=op0)
                return e.tensor_scalar(out=oa, in0=a.ap, scalar1=a_(s1), scalar2=a_(s2), op0=op0, op1=op1)
            P.op(eng, f, reads=rk(a, s1, s2), writes=rk(o))

        def stt(o, a, s, b, op0, op1):
            oa = o_(o)
            P.op("dve", lambda e: e.scalar_tensor_tensor(out=oa, in0=a.ap, scalar=a_(s), in1=b.ap, op0=op0, op1=op1),
                 reads=rk(a, s, b), writes=rk(o))

        def red(o, a, op, negate=False):
            P.op("dve", lambda e: e.tensor_reduce(out=o.ap, in_=a.ap, axis=AX.X, op=op, negate=negate), reads=rk(a), writes=rk(o))

        def recip(o, a):
            oa = o_(o)
            P.op("dve", lambda e: e.reciprocal(out=oa, in_=a.ap), reads=rk(a), writes=rk(o))

        def cp(o, i, eng="act"):
            if eng == "pool" and o.r:
                eng = "dve"
            oa = o_(o)
            if eng == "act":
                P.op("act", lambda e: e.copy(out=oa, in_=i.ap), reads=rk(i), writes=rk(o))
            else:
                P.op(eng, lambda e: e.tensor_copy(out=oa, in_=i.ap), reads=rk(i), writes=rk(o))

        def mset(o, val, eng="pool"):
            P.op(eng, lambda e: e.memset(o.ap, val), writes=rk(o))

        def dma(o, i, eng="sp", slow=False):
            oa = o.ap if isinstance(o, V) else o
            ia = i.ap if isinstance(i, V) else i
            kw = dict(allow_slow_non_contiguous=True) if slow else {}
            P.op(eng, lambda e: e.dma_start(out=oa, in_=ia, **kw), reads=rk(i), writes=rk(o), dma=True)

        MUL, ADD, SUB, MAX = ALU.mult, ALU.add, ALU.subtract, ALU.max

        cF = sb("cF", [128, NCF], r=True); cB = sb("cB", [128, NCB], BF16)
        dma(cB, cb_d)
        C = lambda n: cF[:, CO[n][0]:CO[n][0] + CO[n][1]]
        identF = C("ident"); identB = cB[:, 0:128]; onesF = C("ones")

        Sp = [sb(f"Sp{j}", [128, 8, 128], r=True) for j in range(2)]
        Mp = [sb(f"Mp{j}", [128, 8, 64], r=True) for j in range(2)]
        hcar = [sb(f"hcar{j}", [128, 8], BF16) for j in range(2)]
        for j in range(2):
            mset(Sp[j], 0.0); mset(Mp[j], 0.0)

        ARENA = 62 * 256
        arena_t = st.enter_context(nc.sbuf_tensor("arena", [128, ARENA], F32))
        arena = arena_t[:]
        aoff = {"P": 0, "S": 0}

        def carve(lay, name, shape, dt=F32):
            n = int(np.prod(shape[1:]))
            n32 = n if dt == F32 else (n + 1) // 2
            o = aoff[lay]
            aoff[lay] = o + n32
            assert aoff[lay] <= ARENA, (lay, name, aoff[lay])
            a = arena[:, o:o + n32]
            if dt != F32:
                a = a.bitcast(dt)
            pat = []
            stride = 1
            for d in reversed(shape[1:]):
                pat.insert(0, [stride, d])
                stride *= d
            return V(bass.AP(a.tensor, a.offset, [list(a.ap[0])] + pat), [name])

        LAY = {}
        for lay, ntm in (("P", GMAX * 128), ("S", 128)):
            Ld = dict(NTM=ntm)
            Ld["l1T"] = carve(lay, lay + "l1T", [128, 2, ntm])
            Ld["l1vT"] = carve(lay, lay + "l1vT", [128, ntm])
            if lay == "P":
                Ld["QZ"] = carve(lay, "QZ", [128, 256])
            Ld["xT"] = carve(lay, lay + "xT", [128, NKC, ntm])
            Ld["hT"] = carve(lay, lay + "hT", [128, NKC, 1 + ntm + 1], BF16)
            Ld["brT"] = carve(lay, lay + "brT", [128, 12, ntm], BF16)
            Ld["qx"] = [carve(lay, lay + f"qx{i}", [128, ntm], BF16) for i in range(2)]
            if lay == "P":
                Ld["mkT"] = carve(lay, "mkT", [128, 4, XH, MEM], BF16)
                Ld["mvb"] = carve(lay, "mvb", [128, 4, 2, 512], BF16)
            else:
                Ld["hpS"] = carve(lay, "hpS", [128, NKC, 128], BF16)
                Ld["Z1"] = carve(lay, "Z1", [128, 2048])
                Ld["Z1b"] = carve(lay, "Z1b", [128, 2048], BF16)
                Ld["Z2b"] = carve(lay, "Z2b", [128, 2048], BF16)
                Ld["M0b"] = carve(lay, "M0b", [128, 16, 64], BF16)
                o_save = aoff[lay]
                Ld["S0q"] = [carve(lay, f"S0q{i}", [128, 512]).k("salias") for i in range(4)]
                aoff[lay] = o_save
                Ld["M0all"] = carve(lay, "M0all", [128, 16, 64]).k("salias")
                Ld["SBD"] = [carve(lay, f"SBD{i}", [128, 512]).k("salias") for i in range(2)]
                Ld["MBD"] = [carve(lay, f"MBD{i}", [128, 512]) for i in range(2)]
                Ld["KVB"] = [carve(lay, f"KVB{i}", [128, 2048], BF16) for i in range(3)]
                Ld["QXM"] = carve(lay, "QXM", [128, 2048], BF16)
            LAY[lay] = Ld
        print("arena use (KiB):", {k: v / 256 for k, v in aoff.items()})

        T128 = Rot("t128_", 16, [128, 128], r=True)
        T256 = Rot("t256_", 3, [128, 256], r=True)
        T512 = Rot("t512_", 4, [128, 512], r=True)
        T1K = Rot("t1k_", 2, [128, 1024], r=True)
        TB256 = Rot("tb256_", 2, [128, 256], BF16)
        TB512 = Rot("tb512_", 2, [128, 512], BF16)
        TS8 = Rot("ts8_", 8, [128, 24])
        TS8s = TS8
        OG = sb("OG", [128, 128]); LBC = sb("LBC", [128, 24])
        BIGV = [sb("bigv0", [128, D], r=True), sb("bigv1", [128, D], r=True)]
        MUC = sb("MUC", [128, 80])
        L1a = sb("L1a", [128, NKC, 128], BF16); L1b = sb("L1b", [128, NKC, 128], BF16)
        L1va = sb("L1va", [128, NKC, 128], BF16); L1vb = sb("L1vb", [128, NKC, 128], BF16)
        PVB = sb("PVB", [128, 8, 128])
        ZP = [sb(f"zp{i}", [128, 256], BF16) for i in range(2)]
        ZBK = sb("zbk", [128, 512], BF16)
        ZBKH = [sb(f"zbkh{i}", [128, 512], BF16) for i in range(2)]
        PMBD = sb("PMBD", [128, 128])
        for zb in ZP + [ZBK, PMBD] + ZBKH:
            mset(zb, 0.0)
        WM = Rot("wm_", 2, [128, NKC, 384], BF16)
        WM2 = Rot("wm2_", 2, [128, NKC, 384], BF16)
        WR = Rot("wr_", 2, [128, NKC, 512], BF16)
        WO = Rot("wo_", 1, [128, 12, 128], BF16)
        XIN = Rot("xin_", 1, [128, D])
        _xa = XIN.bufs[0].ap
        PVS = [PVB, V(bass.AP(_xa.tensor, _xa.offset, [list(_xa.ap[0]), [128, 8], [1, 128]]), XIN.bufs[0].keys)]
        _xb = XIN.bufs[0].ap.bitcast(BF16)
        WO.bufs.append(V(bass.AP(_xb.tensor, _xb.offset, [list(_xb.ap[0]), [128, 12], [1, 128]]), XIN.bufs[0].keys))
        NRV = Rot("nrv_", 2, [128, 128], BF16)
        BT128 = Rot("b128_", 8, [128, 128], BF16)
        BT256 = Rot("b256_", 3, [128, 256], BF16)
        BT512 = Rot("b512_", 5, [128, 512], BF16)
        BT1K = Rot("b1k_", 2, [128, 1024], BF16)
        MK = [sb(f"mk{i}", [128, 512], BF16) for i in range(4)]
        MpB = [sb(f"MpB{j}", [128, 8, 64], BF16) for j in range(2)]
        for j in range(2):
            mset(MpB[j], 0.0)
        vfd = nc.dram_tensor("vfd", [17 * 128, D], BF16, kind="Internal").ap()

        for c0 in range(0, NCF, 1024):
            n = min(1024, NCF - c0)
            xin = XIN.get()
            dma(xin[:, 0:n], cf_d[:, c0:c0 + n])
            cp(cF[:, c0:c0 + n], xin[:, 0:n])

        def wview(w2d, c0, ncols):
            return w2d[:, c0:c0 + ncols].rearrange("(kc p) c -> p kc c", p=128)

        def xkey(li):
            return ("xT", li)

        def hkey(li):
            return ("hT", li)

        def mem_kv():
            LP = LAY["P"]
            mkT, mvb = LP["mkT"], LP["mvb"]
            memnT = V(LP["xT"].ap[:, :, 0:MEM], [xkey(0), xkey(1)])
            hmT = V(LP["hT"].ap[:, :, 1:1 + MEM], [hkey(0), hkey(1)])
            for mc in range(2):
                xin = XIN.get()
                dma(xin, mem[mc * 128:(mc + 1) * 128, :])
                ss = TS8.get()
                junk = T1K.get()
                act(junk, xin, AF.Square, acc=ss[:, 0:1])
                ts(ss[:, 1:2], ss[:, 0:1], 1.0 / D, MUL, NORM_EPS, ADD)
                act(ss[:, 2:3], ss[:, 1:2], AF.Ln)
                act(ss[:, 3:4], ss[:, 2:3], AF.Exp, scale=-0.5)
                xn = T1K.get()
                ts(xn, xin, ss[:, 3:4], MUL)
                for half in range(2):
                    ps = PS.get()
                    for q in range(4):
                        kc = half * 4 + q
                        tr(ps[:, q * 128:(q + 1) * 128], xn[:, kc * 128:(kc + 1) * 128], identF)
                    cp(memnT[:, half * 4:(half + 1) * 4, mc * 128:(mc + 1) * 128], ps.pat([[128, 4], [1, 128]]))
            for l in range(depth):
                if mks < 1:
                    break
                gcol = TS8.get()[:, 0:8]
                dma(gcol, mem_norm_g[l].rearrange("(kc p) -> p kc", p=128), slow=True)
                for kc in range(NKC):
                    ts(hmT[:, kc, :], memnT[:, kc, :], gcol[:, kc:kc + 1], MUL)
                if mks < 2:
                    continue
                for cb in range(2):
                    wb = WR.get()
                    dma(wb, wview(w_mem_kv[l], cb * 512, 512), eng="pool")
                    if mks < 3:
                        continue
                    if cb == 0:
                        for hx in range(XH):
                            ps = PS.get()
                            for kc in range(NKC):
                                mm(ps[:, 0:256], wb[:, kc, hx * 128:(hx + 1) * 128], hmT[:, kc, :], start=kc == 0, stop=kc == NKC - 1)
                            cp(mkT[:, l, hx, :], ps[:, 0:256])
                    if mks < 4:
                        continue
                    for mc in range(2):
                        ps = PS.get()
                        for kc in range(NKC):
                            mm(ps, hmT[:, kc, mc * 128:(mc + 1) * 128], wb[:, kc, :], start=kc == 0, stop=kc == NKC - 1)
                        stg = T512.get()
                        cp(stg, ps)
                        if mks >= 6:
                            dma((mkp if cb == 0 else mvp)[l, mc * 128:(mc + 1) * 128, :], stg)
                        if cb == 1 and mks >= 7:
                            cp(mvb[:, l, mc, :], ps, eng="dve")

        if memkv:
            mem_kv()
        mset(LAY["P"]["QZ"], 0.0)

        def process_group(gi, grp):
            ptiles = [t for t in grp if t != "s"]
            has_s = "s" in grp
            assert not (has_s and ptiles)
            L = LAY["S" if has_s else "P"]
            NTM = L["NTM"]; HS = NTM + 2
            xT, hT, brT, l1T, l1vT = L["xT"], L["hT"], L["brT"], L["l1T"], L["l1vT"]
            if has_s:
                hpS = L["hpS"]; Z1 = L["Z1"]; Z1b = L["Z1b"]; Z2b = L["Z2b"]; QXM = L["QXM"]
                P.barrier()
                for zb in [Z1, Z1b, Z2b, QXM] + L["SBD"] + L["MBD"]:
                    mset(zb, 0.0)
                QM = Z1
            else:
                mkT, mvb, QZ = L["mkT"], L["mvb"], L["QZ"]
            qxi = [0]
            npt = len(ptiles)
            first_group = bool(ptiles) and ptiles[0] == 0
            pchunks = []
            c = 0
            while c < npt * 128:
                n = min(512, npt * 128 - c)
                pchunks.append((c, n))
                c += n
            scol = npt * 128
            chunks = pchunks + ([(scol, 128)] if has_s else [])

            def tkeys(c0, n, fn):
                return tuple(fn(li) for li in range(c0 // 128, (c0 + n + 127) // 128))

            def xTv(kc, c0, n):
                return V(xT.ap[:, kc, c0:c0 + n], tkeys(c0, n, xkey))

            def xTt(li):
                return V(xT.ap[:, :, li * 128:(li + 1) * 128], [xkey(li)])

            def hTv(kc, c0, n):
                return V(hT.ap[:, kc, 1 + c0:1 + c0 + n], tkeys(c0, n, hkey))

            def hprevv(kc, c0, n):
                if has_s and c0 == scol:
                    return hpS[:, kc, :]
                ks = tkeys(c0, n, hkey) + ((hkey(c0 // 128 - 1),) if c0 > 0 else (("hT", "c0"),))
                return V(hT.ap[:, kc, c0:c0 + n], ks)

            def brv(kc, c0, n):
                return V(brT.ap[:, kc, c0:c0 + n], tkeys(c0, n, lambda li: ("br", kc, li)))

            for li, t in enumerate(grp):
                xin = XIN.get()
                dma(xin, xs if t == "s" else xp[t * 128:(t + 1) * 128, :])
                for half in range(2):
                    ps = PS.get()
                    for q in range(4):
                        kc = half * 4 + q
                        tr(ps[:, q * 128:(q + 1) * 128], xin[:, kc * 128:(kc + 1) * 128], identF)
                    cp(V(xT.ap[:, half * 4:(half + 1) * 4, li * 128:(li + 1) * 128], [xkey(li)]), ps.pat([[128, 4], [1, 128]]))

            def rms_tile(li):
                sq = T1K.get()
                act(sq.pat([[128, 8], [1, 128]]), xTt(li), AF.Square)
                ps = PS.get()
                for kc in range(NKC):
                    mm(ps[:, 0:128], onesF, sq[:, kc * 128:(kc + 1) * 128], start=kc == 0, stop=kc == NKC - 1)
                rs = T128.get()
                act(rs, ps[:, 0:128], AF.Ln, bias=NORM_EPS, scale=1.0 / D)
                rstd = T128.get()
                act(rstd, rs, AF.Exp, scale=-0.5)
                return rstd

            def hgrn_layer(l, j):
                og = OG
                dma(og, hg_og[j:j + 1, :].partition_broadcast(128))
                lbT = LBC[:, 0:8]; omT = LBC[:, 8:16]; nomT = LBC[:, 16:24]
                nom_bc = BIGV[0]
                if j == 0:
                    mset(omT, 1.0); mset(nomT, -1.0); mset(nom_bc, -1.0)
                else:
                    a0 = TS8.get()[:, 0:8]; a1 = TS8.get()[:, 0:8]
                    dma(a0, hg_lb[0].rearrange("(h p) -> p h", p=128), slow=True)
                    dma(a1, hg_lb[1].rearrange("(h p) -> p h", p=128), slow=True)
                    tt(a1, a1, a0, SUB)
                    act(lbT, a1, AF.Sigmoid)
                    ts(omT, lbT, -1.0, MUL, 1.0, ADD)
                    ts(nomT, lbT, 1.0, SUB)
                    b0 = XIN.get()
                    dma(b0, hg_lb[1:2, :].partition_broadcast(128))
                    cp(nom_bc, b0)
                    b0 = XIN.get()
                    dma(b0, hg_lb[0:1, :].partition_broadcast(128))
                    tt(nom_bc, nom_bc, b0, SUB)
                    act(nom_bc, nom_bc, AF.Sigmoid)
                    ts(nom_bc, nom_bc, 1.0, SUB)
                for h in range(8):
                    wblk = WM.get()
                    for q in range(3):
                        dma(wblk[:, :, q * 128:(q + 1) * 128], wview(w_in[l], q * 1024 + h * 128, 128), eng="pool")
                    Sph = V(Sp[j].ap[:, h, :], [("Sp", j, h)])
                    if has_s:
                        S0q = L["S0q"]
                        for q in range(4):
                            dma(S0q[q].pat([[128, 4], [1, 128]]), sth[j, 4 * q:4 * q + 4, h].rearrange("s k v -> k s v"))
                    hsegB = None
                    for li, t in enumerate(grp):
                        samp = (t == "s")
                        c0 = li * 128
                        hsegA = P.begin_seg()
                        if hsegB is not None and PIPE:
                            P.merges.append((hsegB, hsegA))
                        IU = C("mb16")[:, 256:384] if samp else C("iu2")
                        BO = C("bo16") if samp else C("bo2")
                        psq = PS.get()
                        for kc in range(NKC):
                            mm(psq[:, 0:128], wblk[:, kc, 0:128], hTv(kc, c0, 128), start=kc == 0, stop=kc == NKC - 1)
                        for kc in range(NKC):
                            mm(psq[:, 128:256], wblk[:, kc, 128:256], hTv(kc, c0, 128), start=kc == 0, stop=kc == NKC - 1)
                        psfv = PS.get()
                        for kc in range(NKC):
                            mm(psfv[:, 0:256], hTv(kc, c0, 128), wblk[:, kc, 128:384], start=kc == 0, stop=kc == NKC - 1)
                        qsg = T128.get(); act(qsg, psq[:, 0:128], AF.Sigmoid)
                        qs = T128.get(); tt(qs, qsg, psq[:, 0:128], MUL)
                        sgT = T128.get(); act(sgT, psq[:, 128:256], AF.Sigmoid)
                        kT = T128.get(); ts(kT, sgT, nomT[:, h:h + 1], MUL, omT[:, h:h + 1], ADD)
                        sg = T128.get(); act(sg, psfv[:, 0:128], AF.Sigmoid)
                        kt = T128.get(); stt(kt, sg, -1.0, nom_bc[:, h * 128:(h + 1) * 128], ADD, MUL)
                        g = T128.get(); act(g, kt, AF.Ln, bias=1.0, scale=-1.0)
                        vt = T128.get(); cp(vt, psfv[:, 128:256])
                        ps3 = PS.get()
                        mm(ps3[:, 0:128], IU, g)
                        mm(ps3[:, 128:256], g, IU)
                        mm(ps3[:, 256:384], BO, g)
                        eb = T128.get(); act(eb, ps3[:, 128:256], AF.Exp)
                        enb = T128.get(); act(enb, ps3[:, 128:256], AF.Exp, scale=-1.0)
                        qtT = T128.get(); tt(qtT, qs, eb, MUL)
                        ktT = T128.get(); tt(ktT, kT, enb, MUL)
                        bsb = T128.get(); cp(bsb, ps3[:, 0:128])
                        dd = T128.get(); tt(dd, ps3[:, 256:384], bsb, SUB)
                        ed = T128.get(); act(ed, dd, AF.Exp)
                        psA = PS.get()
                        mm(psA[:, 0:128], ktT, qtT)
                        att = T128.get(); tt(att, psA[:, 0:128], IU, MUL)
                        P.end_seg()
                        hsegB = P.begin_seg()
                        psO = PS.get()
                        mm(psO[:, 0:128], att, vt, start=True, stop=False)
                        if not samp:
                            tt(QZ.pat([[192, 2], [1, 64]]), qs.pat([[64, 2], [1, 64]]), eb.pat([[64, 2], [1, 64]]), MUL)
                            kh0 = T128.get(); stt(kh0, kt, C("cm0"), ed, MUL, MUL)
                            kh1 = T128.get(); stt(kh1, kt, C("cm1"), ed, MUL, MUL)
                            psS = PS.get()
                            mm(psS[:, 0:128], kh0, vt)
                            mm(psS[:, 128:256], kh1, vt)
                            mm(psO[:, 0:128], QZ[:, 0:128], Sph, start=False, stop=False)
                            S1 = T128.get(); stt(S1, Sph, eb[:, 63:64], psS[:, 0:128], MUL, ADD)
                            mm(psO[:, 0:128], QZ[:, 128:256], S1, start=False, stop=True)
                            stt(Sph, S1, eb[:, 127:128], psS[:, 128:256], MUL, ADD)
                        else:
                            tt(QM.pat([[136, 16], [1, 8]]), qs.pat([[8, 16], [1, 8]]), eb.pat([[8, 16], [1, 8]]), MUL)
                            for s in range(NS):
                                mm(psO[:, 0:128], QM[:, s * 128:(s + 1) * 128], S0q[s // 4][:, (s % 4) * 128:(s % 4 + 1) * 128], start=False, stop=(s == NS - 1))
                            kh = T128.get(); tt(kh, kt, ed, MUL)
                            for q in range(4):
                                vm = T512.get()
                                tt(vm.pat([[128, 4], [1, 128]]), vt.pat([[0, 4], [1, 128]]), C("sel16").pat([[1, 4], [0, 128]], off=4 * q), MUL)
                                psW = PS.get()
                                mm(psW, kh, vm)
                                tmp = T512.get()
                                tt(tmp.pat([[128, 4], [1, 128]]), S0q[q].pat([[128, 4], [1, 128]]),
                                   eb.pat([[8, 4], [0, 128]], off=7 + 32 * q), MUL)
                                Sn = T512.get()
                                tt(Sn, tmp, psW, ADD)
                                dma(shs[j, 4 * q:4 * q + 4, h].rearrange("s k v -> k s v"), Sn.pat([[128, 4], [1, 128]]))
                        ss = TS8.get()
                        junk = T128.get()
                        act(junk, psO[:, 0:128], AF.Square, acc=ss[:, 0:1])
                        ts(ss[:, 1:2], ss[:, 0:1], 1.0 / 128, MUL, NORM_EPS, ADD)
                        act(ss[:, 2:3], ss[:, 1:2], AF.Ln)
                        act(ss[:, 3:4], ss[:, 2:3], AF.Exp, scale=-0.5)
                        on = T128.get(); stt(on, psO[:, 0:128], ss[:, 3:4], og, MUL, MUL)
                        psT = PS.get()
                        tr(psT[:, 0:128], on, identF)
                        cp(brv(h, c0, 128), psT[:, 0:128])
                        P.end_seg()
                if first_last_prompt_group:
                    for h in range(8):
                        dma(shp[j, h], V(Sp[j].ap[:, h, :], [("Sp", j, h)]))

            def rest_layer(l):
                if 2 not in phases:
                    return
                wq = WR.get()
                dma(wq, wview(w_in[l], 3072, 512), eng="pool")
                for hx in range(XH):
                    qxT = L['qx'][qxi[0] % 2]; qxi[0] += 1
                    for (c0, n) in chunks:
                        ps = PS.get()
                        for kc in range(NKC):
                            mm(ps[:, 0:n], wq[:, kc, hx * 128:(hx + 1) * 128], hTv(kc, c0, n), start=kc == 0, stop=kc == NKC - 1)
                        cp(qxT[:, c0:c0 + n], ps[:, 0:n])
                    if has_s:
                        KVB = L["KVB"]
                        kvi = [0]
                        def kvget():
                            b = KVB[kvi[0] % 3]; kvi[0] += 1
                            return b
                        mkTs = []; vbs = []
                        for hf in range(2):
                            kb = kvget()
                            dma(kb.pat([[256, 8], [128, 2], [1, 128]]), ck[l][8 * hf:8 * hf + 8, :, hx * 128:(hx + 1) * 128].rearrange("s (mc p) d -> p s mc d", p=128), eng="pool")
                            mk_ = kvget()
                            for b4 in range(2):
                                psb = bfv(PS.get())
                                for q in range(8):
                                    idx = b4 * 8 + q
                                    tr(psb[:, q * 128:(q + 1) * 128], kb[:, idx * 128:(idx + 1) * 128], identB)
                                cp(mk_[:, b4 * 1024:(b4 + 1) * 1024], psb)
                            mkTs.append(mk_)
                    for li, t in enumerate(grp):
                        samp = (t == "s")
                        c0 = li * 128
                        ps = PS.get()
                        if not samp:
                            mm(ps[:, 0:256], qxT[:, c0:c0 + 128], mkT[:, l, hx, :])
                        else:
                            cp(QXM.pat([[136, 16], [1, 8]]), qxT[:, c0:c0 + 128].pat([[8, 16], [1, 8]]), eng="dve")
                            for s in range(NS):
                                mm(ps[:, 0:256], QXM[:, s * 128:(s + 1) * 128], mkTs[s // 8][:, (s % 8) * 256:(s % 8 + 1) * 256], start=(s == 0), stop=(s == NS - 1))
                        ss = TS8.get()
                        red(ss[:, 0:1], ps[:, 0:256], MAX, negate=True)
                        ts(ss[:, 1:2], ss[:, 0:1], SC_ATT, MUL)
                        pe_ = T512.get()
                        act(pe_[:, 0:256], ps[:, 0:256], AF.Exp, bias=ss[:, 1:2], scale=SC_ATT, acc=ss[:, 2:3])
                        recip(ss[:, 3:4], ss[:, 2:3])
                        pn = TB256.get()
                        ts(pn, pe_[:, 0:256], ss[:, 3:4], MUL)
                        psb = bfv(PS.get())
                        tr(psb[:, 0:128], pn[:, 0:128], identB)
                        tr(psb[:, 128:256], pn[:, 128:256], identB)
                        pT = TB256.get()
                        cp(pT, psb[:, 0:256])
                        psO = PS.get()
                        if not samp:
                            for mc in range(2):
                                mm(psO[:, 0:128], mvb[:, l, mc, hx * 128:(hx + 1) * 128], pT[:, mc * 128:(mc + 1) * 128], start=mc == 0, stop=mc == 1)
                        else:
                            for hf in range(2):
                                vb_ = kvget()
                                dma(vb_.pat([[256, 8], [128, 2], [1, 128]]), cv[l][8 * hf:8 * hf + 8, :, hx * 128:(hx + 1) * 128].rearrange("s (mc p) d -> p s mc d", p=128), eng="pool")
                                vbs.append(vb_)
                            for s in range(NS):
                                for mc in range(2):
                                    vb = vbs[s // 8]
                                    mm(psO[:, s * 8:(s + 1) * 8], vb[:, ((s % 8) * 2 + mc) * 128:((s % 8) * 2 + mc + 1) * 128],
                                       pT[:, mc * 128 + s * 8:mc * 128 + (s + 1) * 8], start=mc == 0, stop=mc == 1)
                        cp(brv(8 + hx, c0, 128), psO[:, 0:128])
                if 3 not in phases:
                    return
                for gb in range(3):
                    wg = WR.get()
                    dma(wg, wview(w_in[l], 3584 + gb * 512, 512), eng="pool")
                    for sub in range(4):
                        bi = gb * 4 + sub
                        for (c0, n) in chunks:
                            ps = PS.get()
                            for kc in range(NKC):
                                mm(ps[:, 0:n], wg[:, kc, sub * 128:(sub + 1) * 128], hTv(kc, c0, n), start=kc == 0, stop=kc == NKC - 1)
                            sgt = TB512.get()
                            act(sgt[:, 0:n], ps[:, 0:n], AF.Sigmoid)
                            tt(sgt[:, 0:n], sgt[:, 0:n], ps[:, 0:n], MUL)
                            tt(brv(bi, c0, n), brv(bi, c0, n), sgt[:, 0:n], MUL)
                if 4 not in phases:
                    return
                for ob in range(NKC):
                    wo = WO.get()
                    dma(wo, w_out[l][:, ob * 128:(ob + 1) * 128].rearrange("(kc p) c -> p kc c", p=128), eng="pool")
                    for (c0, n) in chunks:
                        ps = PS.get()
                        for kc in range(12):
                            mm(ps[:, 0:n], wo[:, kc, :], brv(kc, c0, n), start=kc == 0, stop=kc == 11)
                        tt(xTv(ob, c0, n), xTv(ob, c0, n), ps[:, 0:n], ADD)

            def rwkv_layer(l, j):
                m = j - 1
                cfgn = "16" if has_s else "1"
                MB = C("mb" + cfgn); SL = C("sl" + cfgn)
                TIC = C("tic" + cfgn); TSC = C("tsc" + cfgn); BOC = C("boc" + cfgn)
                LEV = 3 if has_s else 7
                if levo is not None:
                    LEV = levo
                muT = MUC[:, 0:40].pat([[8, 5], [1, 8]]); omuT = MUC[:, 40:80].pat([[8, 5], [1, 8]])
                dma(muT, rw_mu[j].rearrange("k (kc p) -> p k kc", p=128), slow=True)
                ts(omuT, muT, -1.0, MUL, 1.0, ADD)
                raw = WR.get()
                dma(raw[:, :, 0:64], rw_w1[j].rearrange("(kc p) c -> p kc c", p=128), eng="pool")
                dma(raw[:, :, 64:128], rw_a1[j].rearrange("(kc p) c -> p kc c", p=128), eng="pool")
                if j >= 1:
                    mset(raw[:, :, 160:256], 0.0)
                    dma(raw[:, :, 128:160], rw_v1[m].rearrange("(kc p) c -> p kc c", p=128), eng="pool")
                for (c_lo, kind) in ((0, 1), (64, 4)):
                    src = raw[:, :, c_lo:c_lo + 64]
                    tt(L1a[:, :, c_lo:c_lo + 64], src, omuT[:, kind, :].pat([[1, 8], [0, 64]]), MUL)
                    tt(L1b[:, :, c_lo:c_lo + 64], src, muT[:, kind, :].pat([[1, 8], [0, 64]]), MUL)
                if j >= 1:
                    src = raw[:, :, 128:256]
                    tt(L1va, src, omuT[:, 3, :].pat([[1, 8], [0, 128]]), MUL)
                    tt(L1vb, src, muT[:, 3, :].pat([[1, 8], [0, 128]]), MUL)
                W2 = BIGV[0]; V2 = BIGV[1]
                xin = XIN.get()
                dma(xin[0:64, :], rw_w2[j]); dma(xin[64:128, :], rw_a2[j])
                cp(W2, xin)
                if j >= 1:
                    xin = XIN.get()
                    mset(xin, 0.0)
                    dma(xin[0:32, :], rw_v2[m])
                    cp(V2, xin, eng="dve")
                for (c0, n) in chunks:
                    ps = PS.get()
                    for kc in range(NKC):
                        mm(ps[:, 0:n], L1a[:, kc, :], hTv(kc, c0, n), start=kc == 0, stop=False)
                    for kc in range(NKC):
                        mm(ps[:, 0:n], L1b[:, kc, :], hprevv(kc, c0, n), start=False, stop=kc == NKC - 1)
                    th = T512.get()
                    act(th[:, 0:n], ps[:, 0:n], AF.Tanh)
                    ts(V(l1T.ap[:, 0, c0:c0 + n], tkeys(c0, n, lambda li: ("l1", li))), th[:, 0:n], C("hw"), MUL)
                    ts(V(l1T.ap[:, 1, c0:c0 + n], tkeys(c0, n, lambda li: ("l1", li))), ps[:, 0:n], C("ha"), MUL)
                    if j >= 1:
                        ps = PS.get()
                        for kc in range(NKC):
                            mm(ps[:, 0:n], L1va[:, kc, :], hTv(kc, c0, n), start=kc == 0, stop=False)
                        for kc in range(NKC):
                            mm(ps[:, 0:n], L1vb[:, kc, :], hprevv(kc, c0, n), start=False, stop=kc == NKC - 1)
                        cp(V(l1vT.ap[:, c0:c0 + n], tkeys(c0, n, lambda li: ("l1v", li))), ps[:, 0:n])

                for pr in range(8):
                    cc = pr * 128
                    wa = WM.get(); wb = WM2.get()
                    for q in range(3):
                        dma(wa[:, :, q * 128:(q + 1) * 128], wview(w_in[l], q * 1024 + cc, 128), eng="pool")
                    for q, kind in enumerate((0, 2, 3)):
                        tt(wb[:, :, q * 128:(q + 1) * 128], wa[:, :, q * 128:(q + 1) * 128], muT[:, kind, :].pat([[1, 8], [0, 128]]), MUL)
                    tt(wa, wa, wb, SUB)
                    pv = PVS[pr % 2]
                    vecs = [rw_w0[j:j + 1], rw_a0[j:j + 1], (rw_v0[m:m + 1] if j >= 1 else rw_a0[j:j + 1]), rw_kk[j:j + 1], rw_ka[j:j + 1], rw_rk[j:j + 1], rw_lg[j:j + 1], rw_lb[j:j + 1]]
                    for k8, vv in enumerate(vecs):
                        dma(pv[:, k8, :], vv[:, cc:cc + 128].partition_broadcast(128))
                    Mpp = V(Mp[j].ap[:, pr, :], [("Mp", j, pr)], True)
                    MppB = V(MpB[j].ap[:, pr, :], [("MpB", j, pr)])
                    if has_s:
                        M0b = L["M0b"]
                        M0all = L["M0all"]
                        for q in range(4):
                            SBD = L["SBD"][q % 2]
                            for hh in range(2):
                                dma(SBD[hh * 64:(hh + 1) * 64, :].pat([[128, 4], [1, 64]], off=hh * 64),
                                    strw[j, 4 * q:4 * q + 4, 2 * pr + hh].rearrange("s i k -> i s k"))
                            ps = PS.get()
                            for s4 in range(4):
                                tr(ps[:, s4 * 128:(s4 + 1) * 128], SBD[:, s4 * 128:(s4 + 1) * 128], identF)
                            for hh in range(2):
                                cp(M0all[hh * 64:(hh + 1) * 64, 4 * q:4 * q + 4, :], ps[hh * 64:(hh + 1) * 64, :].pat([[128, 4], [1, 64]], off=hh * 64),
                                   eng=("act" if hh == 0 else "dve"))
                                cp(M0b[hh * 64:(hh + 1) * 64, 4 * q:4 * q + 4, :], ps[hh * 64:(hh + 1) * 64, :].pat([[128, 4], [1, 64]], off=hh * 64),
                                   eng=("dve" if hh == 0 else "act"))
                    segB_prev = None
                    for li, t in enumerate(grp):
                        samp = (t == "s")
                        c0 = li * 128
                        gt = 16 if samp else t
                        segA = P.begin_seg()
                        if segB_prev is not None and PIPE:
                            P.merges.append((segB_prev, segA))
                        psR = PS.get()
                        for kc in range(NKC):
                            mm(psR[:, 0:384], hTv(kc, c0, 128), wa[:, kc, :], start=kc == 0, stop=False)
                        for kc in range(NKC):
                            mm(psR[:, 0:384], hprevv(kc, c0, 128), wb[:, kc, :], start=False, stop=kc == NKC - 1)
                        rP, kP, vP = psR[:, 0:128], psR[:, 128:256], psR[:, 256:384]
                        l1k = ("l1", li)
                        psL = PS.get()
                        mm(psL[:, 0:128], V(l1T.ap[:, 0, c0:c0 + 128], [l1k]), W2[:, cc:cc + 128])
                        mm(psL[:, 128:256], V(l1T.ap[:, 1, c0:c0 + 128], [l1k]), W2[:, cc:cc + 128])
                        if j >= 1:
                            mm(psL[:, 256:384], V(l1vT.ap[:, c0:c0 + 128], [("l1v", li)]), V2[:, cc:cc + 128])
                        npre = 384 if j >= 1 else 256
                        pre = T512.get(); tt(pre[:, 0:npre], psL[:, 0:npre], pv.pat([[1, npre]]), ADD)
                        sgs = T512.get(); act(sgs[:, 0:npre], pre[:, 0:npre], AF.Sigmoid)
                        sg = sgs[:, 0:128]; ag = sgs[:, 128:256]
                        Vt = BT128.get()
                        vkey = ("vfd", gt, pr)
                        if j == 0:
                            cp(Vt, vP)
                            P.op("sp", (lambda e, o=vfd[gt * 128:(gt + 1) * 128, cc:cc + 128], i_=Vt.ap: e.dma_start(out=o, in_=i_)),
                                 reads=list(Vt.keys), writes=[vkey], dma=True)
                        else:
                            vf = NRV.get()
                            P.op("sp", (lambda e, i_=vfd[gt * 128:(gt + 1) * 128, cc:cc + 128], o=vf.ap: e.dma_start(out=o, in_=i_)),
                                 reads=[vkey], writes=list(vf.keys), dma=True)
                            vg = sgs[:, 256:384]
                            d1 = T128.get(); tt(d1, vf, vP, SUB)
                            tt(d1, d1, vg, MUL)
                            tt(Vt, d1, vP, ADD)
                        kk = T128.get(); tt(kk, kP, pv[:, 3, :], MUL)
                        sq = T128.get(); tt(sq, kk, kk, MUL)
                        sm = TS8.get()
                        red(sm[:, 0:2], sq.pat([[64, 2], [1, 64]]), ADD)
                        act(sm[:, 2:4], sm[:, 0:2], AF.Ln)
                        act(sm[:, 4:6], sm[:, 2:4], AF.Exp, scale=-0.5)
                        ts(sm[:, 4:6], sm[:, 4:6], 1e12, ALU.min)
                        kkn = T128.get()
                        tt(kkn.pat([[64, 2], [1, 64]]), kk.pat([[64, 2], [1, 64]]), sm[:, 4:6].pat([[1, 2], [0, 64]]), MUL)
                        t1 = T128.get(); stt(t1, ag, -1.0, pv[:, 4, :], ADD, MUL)
                        BK = T256.get()
                        kp = BK[:, 128:256]; stt(kp, t1, 1.0, kP, ADD, MUL)
                        bb = BK[:, 0:128]; tt(bb, kkn, ag, MUL)
                        t2 = T128.get(); tt(t2, rP, kp, MUL)
                        tt(t2, t2, pv[:, 5, :], MUL)
                        red(sm[:, 6:8], t2.pat([[64, 2], [1, 64]]), ADD)
                        psC = PS.get()
                        mm(psC[:, 0:128], TIC, sg)
                        mm(psC[:, 128:256], TSC, sg)
                        mm(psC[:, 256:384], BOC, sg)
                        nsq = 16 if samp else 1
                        nsq = 16 if samp else 2
                        mm(psC[:, 384:384 + nsq], sg, (C("sel16c") if samp else C("negc")))
                        EE = T512.get(); act(EE[:, 0:384], psC[:, 0:384], AF.Exp)
                        E1 = EE[:, 0:128]; E3 = EE[:, 128:256]; E5 = EE[:, 256:384]
                        E2 = T128.get(); act(E2, psC[:, 0:128], AF.Exp, scale=-1.0)
                        gC = TS8s.get(); act(gC[:, 0:nsq], psC[:, 384:384 + nsq], AF.Exp)
                        dg = [[192, 2], [1, 64]]; nd = [[64, 2], [1, 64]]
                        Am, Rm = ZP; Bm = ZBK[:, 0:256]; Km = ZBK[:, 256:512]
                        ZH = ZBKH[li % 2]; BHm = ZH[:, 0:256]; KHm = ZH[:, 256:512]
                        dg2 = [[256, 2], [192, 2], [1, 64]]
                        stt(Am.pat(dg), kkn.pat(nd), -1.0, E3.pat(nd), MUL, MUL)
                        tt(ZBK.pat(dg2), BK.pat([[128, 2], [64, 2], [1, 64]]), E2.pat([[0, 2], [64, 2], [1, 64]]), MUL)
                        tt(Rm.pat(dg), rP.pat(nd), E1.pat(nd), MUL)
                        tt(ZH.pat(dg2), ZBK.pat(dg2), E5.pat([[0, 2], [64, 2], [1, 64]]), MUL)
                        XT = BT1K.get()
                        psb = bfv(PS.get())
                        for q, Xm in enumerate((Am, Am, Bm, Bm, Km, Km, Rm, Rm)):
                            hh = q % 2
                            tr(psb[:, q * 128:(q + 1) * 128], Xm[:, hh * 128:(hh + 1) * 128], identB)
                        cp(XT[:, 0:512], psb[:, 0:512]); cp(XT[:, 512:1024], psb[:, 512:1024], eng="dve")
                        AT = lambda hh: XT[:, (0 + hh) * 128:(1 + hh) * 128]
                        BT = lambda hh: XT[:, (2 + hh) * 128:(3 + hh) * 128]
                        KT = lambda hh: XT[:, (4 + hh) * 128:(5 + hh) * 128]
                        RT = lambda hh: XT[:, (6 + hh) * 128:(7 + hh) * 128]
                        Mk = []; XY = BT512.get(); WT = BT256.get()
                        for hh in range(2):
                            psM = PS.get()
                            mm(psM[:, 0:128], BT(hh), AT(hh))
                            mm(psM[:, 128:256], KT(hh), AT(hh))
                            mm(psM[:, 256:384], BT(hh), RT(hh))
                            mm(psM[:, 384:512], KT(hh), RT(hh))
                            mk_ = MK[2 * (li % 2) + hh]; tt(mk_, psM, MB, MUL)
                            Mk.append(mk_)
                            psN = PS.get()
                            mm(psN[:, 0:128], AT(hh), BT(hh))
                            tt(XY[:, hh * 256 + 128:hh * 256 + 256], psN[:, 0:128], SL, MUL)
                            tt(WT[:, hh * 128:(hh + 1) * 128], mk_[:, 0:128], identF, ADD)
                        for lev in range(1, LEV):
                            psV = PS.get()
                            for hh in range(2):
                                Xp = Mk[hh][:, 0:128] if lev == 1 else XY[:, hh * 256:hh * 256 + 128]
                                Yp = XY[:, hh * 256 + 128:hh * 256 + 256]
                                mm(psV[:, hh * 256 + 128:hh * 256 + 256], Xp, Yp)
                                if lev < LEV - 1:
                                    mm(psV[:, hh * 256:hh * 256 + 128], Yp, Xp)
                            XYn = BT512.get()
                            if lev < LEV - 1:
                                cp(XYn, psV)
                            else:
                                cp(XYn.pat([[256, 2], [1, 128]], off=128), psV.pat([[256, 2], [1, 128]], off=128))
                            psW = PS.get()
                            for hh in range(2):
                                mm(psW[:, hh * 128:(hh + 1) * 128], XYn[:, hh * 256 + 128:hh * 256 + 256], WT[:, hh * 128:(hh + 1) * 128])
                            WTn = BT256.get()
                            tt(WTn, psW[:, 0:256], WT, ADD)
                            XY = XYn; WT = WTn
                        P.end_seg()
                        segB_prev = P.begin_seg()
                        if samp:
                            cp(Z1b.pat([[136, 16], [1, 8]]), AT(0).pat([[8, 16], [1, 8]]), eng="pool")
                            cp(Z2b.pat([[136, 16], [1, 8]]), AT(1).pat([[8, 16], [1, 8]]), eng="pool")
                        psP = PS.get()
                        for hh in range(2):
                            o_ = psP[:, hh * 64:(hh + 1) * 64]
                            if not samp:
                                mm(o_, AT(hh), MppB, start=True, stop=False)
                            else:
                                ZZ = Z1b if hh == 0 else Z2b
                                for s in range(NS):
                                    mm(o_, ZZ[:, s * 128:(s + 1) * 128], M0b[:, s, :], start=(s == 0), stop=False)
                            mm(o_, Mk[hh][:, 128:256], Vt[:, hh * 64:(hh + 1) * 64], start=False, stop=True)
                        P1 = BT128.get(); cp(P1, psP[:, 0:128])
                        psU = PS.get()
                        for hh in range(2):
                            mm(psU[:, hh * 64:(hh + 1) * 64], WT[:, hh * 128:(hh + 1) * 128], P1[:, hh * 64:(hh + 1) * 64])
                        U = BT128.get(); cp(U, psU[:, 0:128], eng="dve")
                        if samp:
                            cp(Z1b.pat([[136, 16], [1, 8]]), RT(0).pat([[8, 16], [1, 8]]), eng="pool")
                            cp(Z2b.pat([[136, 16], [1, 8]]), RT(1).pat([[8, 16], [1, 8]]), eng="pool")
                        psY = PS.get()
                        for hh in range(2):
                            o_ = psY[:, hh * 64:(hh + 1) * 64]
                            if not samp:
                                mm(o_, RT(hh), MppB, start=True, stop=False)
                            else:
                                ZZ = Z1b if hh == 0 else Z2b
                                for s in range(NS):
                                    mm(o_, ZZ[:, s * 128:(s + 1) * 128], M0b[:, s, :], start=(s == 0), stop=False)
                            mm(o_, Mk[hh][:, 256:384], U[:, hh * 64:(hh + 1) * 64], start=False, stop=False)
                            mm(o_, Mk[hh][:, 384:512], Vt[:, hh * 64:(hh + 1) * 64], start=False, stop=True)
                        if not samp:
                            psS = PS.get()
                            mm(psS[:, 0:64], BHm[:, 0:128], U[:, 0:64], start=True, stop=False)
                            mm(psS[:, 0:64], BHm[:, 128:256], U[:, 64:128], start=False, stop=False)
                            mm(psS[:, 0:64], KHm[:, 0:128], Vt[:, 0:64], start=False, stop=False)
                            mm(psS[:, 0:64], KHm[:, 128:256], Vt[:, 64:128], start=False, stop=True)
                            stt(Mpp, Mpp, gC[:, 0:1], psS[:, 0:64], MUL, ADD)
                            cp(MppB, Mpp)
                        else:
                            for q in range(2):
                                Uw = [BT512.get(), BT512.get()]; Vw = [BT512.get(), BT512.get()]
                                for hh in range(2):
                                    tt(Uw[hh].pat([[64, 8], [1, 64]]), U[:, hh * 64:(hh + 1) * 64].pat([[0, 8], [1, 64]]),
                                       C("sel16").pat([[1, 8], [0, 64]], off=8 * q), MUL)
                                    tt(Vw[hh].pat([[64, 8], [1, 64]]), Vt[:, hh * 64:(hh + 1) * 64].pat([[0, 8], [1, 64]]),
                                       C("sel16").pat([[1, 8], [0, 64]], off=8 * q), MUL, eng="pool")
                                psS = PS.get()
                                mm(psS, BHm[:, 0:128], Uw[0], start=True, stop=False)
                                mm(psS, BHm[:, 128:256], Uw[1], start=False, stop=False)
                                mm(psS, KHm[:, 0:128], Vw[0], start=False, stop=False)
                                mm(psS, KHm[:, 128:256], Vw[1], start=False, stop=True)
                                tmp = T512.get()
                                tt(tmp.pat([[64, 8], [1, 64]]), M0all[:, 8 * q:8 * q + 8, :], gC[:, 8 * q:8 * q + 8].pat([[1, 8], [0, 64]]), MUL)
                                Mn = T512.get()
                                tt(Mn, tmp, psS, ADD)
                                for q2 in range(2):
                                    MBD = L["MBD"][q2]
                                    for hh in range(2):
                                        cp(MBD[hh * 64:(hh + 1) * 64, :].pat([[128, 4], [1, 64]], off=hh * 64),
                                           Mn[hh * 64:(hh + 1) * 64, q2 * 256:(q2 + 1) * 256].pat([[64, 4], [1, 64]]), eng=("act" if hh == 0 else "pool"))
                                    ps = PS.get()
                                    for s4 in range(4):
                                        tr(ps[:, s4 * 128:(s4 + 1) * 128], MBD[:, s4 * 128:(s4 + 1) * 128], identF)
                                    So = T256.get()
                                    for hh in range(2):
                                        cp(So[hh * 64:(hh + 1) * 64, :].pat([[64, 4], [1, 64]]),
                                           ps[hh * 64:(hh + 1) * 64, :].pat([[128, 4], [1, 64]], off=hh * 64), eng=("act" if hh == 0 else "dve"))
                                    sb0 = 8 * q + 4 * q2
                                    for hh in range(2):
                                        dma(srs[j, sb0:sb0 + 4, 2 * pr + hh].rearrange("s i k -> i s k"),
                                            So[hh * 64:(hh + 1) * 64, :].pat([[64, 4], [1, 64]]))
                        g2 = [[64, 2], [1, 64]]
                        red(sm[:, 8:10], psY[:, 0:128].pat(g2), ADD)
                        ysq = T128.get(); act(ysq, psY[:, 0:128], AF.Square)
                        red(sm[:, 10:12], ysq.pat(g2), ADD)
                        ts(sm[:, 12:14], sm[:, 8:10], 1.0 / 64, MUL)
                        tt(sm[:, 14:16], sm[:, 12:14], sm[:, 12:14], MUL)
                        stt(sm[:, 16:18], sm[:, 10:12], 1.0 / 64, sm[:, 14:16], MUL, SUB)
                        act(sm[:, 18:20], sm[:, 16:18], AF.Ln, bias=LNX_EPS, scale=1.0)
                        act(sm[:, 20:22], sm[:, 18:20], AF.Exp, scale=-0.5)
                        yc = T128.get()
                        tt(yc.pat(g2), psY[:, 0:128].pat(g2), sm[:, 12:14].pat([[1, 2], [0, 64]]), SUB)
                        tt(yc.pat(g2), yc.pat(g2), sm[:, 20:22].pat([[1, 2], [0, 64]]), MUL)
                        tt(yc, yc, pv[:, 6, :], MUL)
                        tt(yc, yc, pv[:, 7, :], ADD)
                        yb = T128.get()
                        tt(yb.pat(g2), Vt.pat(g2), sm[:, 6:8].pat([[1, 2], [0, 64]]), MUL)
                        tt(yc, yc, yb, ADD)
                        psT = PS.get()
                        tr(psT[:, 0:128], yc, identF)
                        cp(brv(pr, c0, 128), psT[:, 0:128])
                        P.end_seg()
                if first_last_prompt_group:
                    for pr in range(8):
                        MBD = PMBD
                        Mpp = V(Mp[j].ap[:, pr, :], [("Mp", j, pr)])
                        for hh in range(2):
                            cp(MBD[hh * 64:(hh + 1) * 64, hh * 64:(hh + 1) * 64], Mpp[hh * 64:(hh + 1) * 64, :], eng=("act" if hh == 0 else "pool"))
                        ps = PS.get()
                        tr(ps[:, 0:128], MBD, identF)
                        So = T128.get()
                        for hh in range(2):
                            cp(So[hh * 64:(hh + 1) * 64, 0:64], ps[hh * 64:(hh + 1) * 64, hh * 64:(hh + 1) * 64], eng=("act" if hh == 0 else "dve"))
                        for hh in range(2):
                            dma(srp[j, 2 * pr + hh], So[hh * 64:(hh + 1) * 64, 0:64])

            first_last_prompt_group = (15 in ptiles)
            for l in range(depth):
                is_rw = (l % 2 == 1)
                j = l // 2
                gcol = TS8.get()[:, 0:8]
                dma(gcol, norm_g[l].rearrange("(kc p) -> p kc", p=128), slow=True)
                if is_rw:
                    if first_group:
                        mset(V(hT.ap[:, :, 0:1], [("hT", "c0")]), 0.0)
                    elif npt > 0:
                        cp(V(hT.ap[:, :, 0], [("hT", "c0")]), hcar[j], eng="pool")
                for li, t in enumerate(grp):
                    rstd = rms_tile(li)
                    for kc in range(NKC):
                        stt(V(hT.ap[:, kc, 1 + li * 128:1 + (li + 1) * 128], [hkey(li)]), xTv(kc, li * 128, 128), gcol[:, kc:kc + 1], rstd, MUL, MUL)
                    if is_rw and t == 15:
                        hl = TS8.get()[:, 0:8]
                        ts(hl, V(xT.ap[:, :, li * 128 + 127], [xkey(li)]), rstd[:, 127:128], MUL)
                        tt(hl, hl, gcol, MUL)
                        dma(ssp[j].rearrange("(kc p) -> p kc", p=128), hl, slow=True)
                    if is_rw and t == "s":
                        hl = T128.get()
                        hl3 = hl.pat([[16, 8], [1, 16]])
                        xv = xTt(li).pat([[NTM, 8], [8, 16]], off=7)
                        rv = rstd.pat([[0, 8], [8, 16]], off=7)
                        tt(hl3, xv, rv, MUL)
                        tt(hl3, hl3, gcol.pat([[1, 8], [0, 16]]), MUL)
                        ps = PS.get()
                        tr(ps[:, 0:128], hl, identF)
                        ho = T128.get()
                        cp(ho, ps[:, 0:128])
                        for kc in range(NKC):
                            dma(sss[j, :, kc * 128:(kc + 1) * 128], ho[kc * 16:(kc + 1) * 16, :])
                        stin = XIN.get()
                        mset(stin, 0.0)
                        dma(stin[0:16, :], sts[j])
                        for half in range(2):
                            ps = PS.get()
                            for q in range(4):
                                kc = half * 4 + q
                                tr(ps[:, q * 128:(q + 1) * 128], stin[:, kc * 128:(kc + 1) * 128], identF)
                            cp(hpS.pat([[128, 4], [8, 16]], off=half * 4 * 128), ps.pat([[128, 4], [1, 16]]))
                        hsv = V(hT.ap[:, :, 1 + li * 128:1 + (li + 1) * 128], [hkey(li)])
                        cp(hpS.pat([[128, 8], [8, 16], [1, 7]], off=1), hsv.pat([[HS, 8], [8, 16], [1, 7]]), eng="pool")
                if is_rw and npt > 0:
                    cp(hcar[j], V(hT.ap[:, :, npt * 128], [hkey(npt - 1)]), eng="pool")
                if 1 in phases:
                    if not is_rw:
                        hgrn_layer(l, j)
                    else:
                        rwkv_layer(l, j)
                rest_layer(l)

            fg = TS8.get()[:, 0:8]
            dma(fg, final_g[0].rearrange("(kc p) -> p kc", p=128), slow=True)
            for li, t in enumerate(grp):
                rstd = rms_tile(li)
                yT = T1K.get()
                for kc in range(NKC):
                    stt(yT[:, kc * 128:(kc + 1) * 128], xTv(kc, li * 128, 128), fg[:, kc:kc + 1], rstd, MUL, MUL)
                yo = T1K.get()
                for half in range(2):
                    ps = PS.get()
                    for q in range(4):
                        kc = half * 4 + q
                        tr(ps[:, q * 128:(q + 1) * 128], yT[:, kc * 128:(kc + 1) * 128], identF)
                    cp(yo[:, half * 512:(half + 1) * 512], ps)
                dma(ys if t == "s" else yp[t * 128:(t + 1) * 128, :], yo)

        for gi, grp in enumerate(groups):
            process_group(gi, grp)

        info = P.emit()
    return nc, info


def _shard_inputs(inp):
    cf, cb = make_consts()
    maps = []
    for c in range(8):
        b = slice(16 * c, 16 * c + 16)
        m = {
            "xp": np.ascontiguousarray(inp["x_prompt"][c]),
            "xs": np.ascontiguousarray(inp["x_sample"][b]).reshape(128, D),
            "mem": np.ascontiguousarray(inp["mem_prompt"][c]),
            "sth": np.ascontiguousarray(inp["state_hgrn"][:, b]),
            "strw": np.ascontiguousarray(inp["state_rwkv"][:, b]),
            "sts": np.ascontiguousarray(inp["state_shift"][:, b]),
            "ck": np.ascontiguousarray(inp["cache_mem_k"][:, b]).reshape(4, 16, MEM, 512),
            "cv": np.ascontiguousarray(inp["cache_mem_v"][:, b]).reshape(4, 16, MEM, 512),
            "rw_rk": np.ascontiguousarray(inp["rw_rk"]).reshape(2, D),
            "final_g": np.ascontiguousarray(inp["final_g"]).reshape(1, D),
            "cf": cf, "cb": cb,
        }
        for k in ("norm_g", "w_in", "w_out", "mem_norm_g", "w_mem_kv", "hg_lb", "hg_onorm_g", "rw_mu", "rw_w0", "rw_w1",
                  "rw_w2", "rw_a0", "rw_a1", "rw_a2", "rw_v0", "rw_v1", "rw_v2", "rw_kk", "rw_ka", "rw_lnx_g", "rw_lnx_b"):
            m[k] = np.ascontiguousarray(inp[k])
        maps.append(m)
    return maps


_NC_CACHE = {}


def kernel(**inputs):
    inp = {k: np.asarray(v, dtype=np.float32) for k, v in inputs.items()}
    if "nc" not in _NC_CACHE:
        _NC_CACHE["nc"] = build()[0]
    nc = _NC_CACHE["nc"]
    maps = _shard_inputs(inp)
    res = run_bass_kernel_spmd(nc, maps, core_ids=list(range(8)))
    R = res.results
    y_p = np.stack([R[c]["yp"] for c in range(8)], 0)
    y_s = np.concatenate([R[c]["ys"].reshape(16, 8, D) for c in range(8)], 0)
    sh_p = np.stack([R[c]["shp"] for c in range(8)], 1)
    sr_p = np.stack([R[c]["srp"] for c in range(8)], 1)
    ss_p = np.stack([R[c]["ssp"] for c in range(8)], 1)
    mk_p = np.stack([R[c]["mkp"].reshape(4, MEM, 4, 128) for c in range(8)], 1)
    mv_p = np.stack([R[c]["mvp"].reshape(4, MEM, 4, 128) for c in range(8)], 1)
    sh_s = np.concatenate([R[c]["shs"] for c in range(8)], 1)
    sr_s = np.concatenate([R[c]["srs"] for c in range(8)], 1)
    ss_s = np.concatenate([R[c]["sss"] for c in range(8)], 1)
    return (y_p, y_s, sh_p, sr_p, ss_p, mk_p, mv_p, sh_s, sr_s, ss_s)
```

```python
import contextlib
import numpy as np
import ml_dtypes
import concourse.bass as bass
import concourse.mybir as mybir
from concourse.bass_utils import run_bass_kernel_spmd

F32 = mybir.dt.float32
F32R = mybir.dt.float32r
BF16 = mybir.dt.bfloat16
AF = mybir.ActivationFunctionType
ALU = mybir.AluOpType
AX = mybir.AxisListType

ENGS = ("pe", "act", "dve", "pool", "sp")
DMA_K = 6
SEM_WRAP = 30000
SAME_ENG_RAW_WAITS = True
PIPE = True


class Prog:
    def __init__(self, nc):
        self.nc = nc
        self.ops = []
        self.last_w = {}
        self.readers = {}
        self.excl = set()
        self.seg = None
        self.nseg = 0
        self.merges = []

    def begin_seg(self):
        self.nseg += 1
        self.seg = self.nseg
        return self.seg

    def end_seg(self):
        self.seg = None

    def op(self, eng, fn, reads=(), writes=(), dma=False):
        i = len(self.ops)
        ex = [k for k in reads if k in self.excl]
        if ex:
            writes = list(writes) + ex
        deps = {}
        for k in reads:
            w = self.last_w.get(k)
            if w is not None:
                deps[w] = True
        for k in writes:
            w = self.last_w.get(k)
            if w is not None:
                deps.setdefault(w, False)
            for r in self.readers.get(k, ()):
                deps.setdefault(r, False)
        o = dict(i=i, eng=eng, fn=fn, dma=dma, deps=deps, seg=self.seg)
        self.ops.append(o)
        for k in reads:
            self.readers.setdefault(k, []).append(i)
        for k in writes:
            self.last_w[k] = i
            self.readers[k] = []
        return i

    def barrier(self):
        lastc = {e: None for e in ENGS}
        lastd = {e: [] for e in ENGS}
        for o in self.ops:
            if o["dma"]:
                lastd[o["eng"]].append(o["i"])
            else:
                lastc[o["eng"]] = o["i"]
        extra = [v for v in lastc.values() if v is not None]
        for e in ENGS:
            extra += lastd[e][-DMA_K:]
        for e in ENGS:
            i = self.op(e, lambda eng: eng.nop(), reads=(), writes=())
            for x in extra:
                self.ops[i]["deps"][x] = True

    def _order(self):
        n = len(self.ops)
        order = []
        segops = {}
        for o in self.ops:
            if o["seg"] is not None:
                segops.setdefault(o["seg"], []).append(o["i"])
        partner = {x: y for x, y in self.merges}
        done_seg = set()
        i = 0
        while i < n:
            o = self.ops[i]
            sg = o["seg"]
            if sg is not None and sg in partner and sg not in done_seg:
                X = segops[sg]; Y = segops[partner[sg]]
                assert X[-1] + 1 == Y[0] and X == list(range(X[0], X[-1] + 1)) and Y == list(range(Y[0], Y[-1] + 1))
                emitted = set()
                xi = yi = 0
                xset = set(X)
                turn = 0
                while xi < len(X) or yi < len(Y):
                    pick_y = False
                    if yi < len(Y) and (turn == 1 or xi >= len(X)):
                        yo = self.ops[Y[yi]]
                        if all((d not in xset) or (d in emitted) for d in yo["deps"]):
                            pick_y = True
                    if pick_y:
                        order.append(Y[yi]); yi += 1
                    else:
                        order.append(X[xi]); emitted.add(X[xi]); xi += 1
                    turn ^= 1
                done_seg.add(sg); done_seg.add(partner[sg])
                i = Y[-1] + 1
                continue
            order.append(i)
            i += 1
        assert sorted(order) == list(range(n))
        return order

    def emit(self, final_wait_eng="sp"):
        nc = self.nc
        order = self._order()
        ops = [self.ops[i] for i in order]
        newidx = {o["i"]: k for k, o in enumerate(ops)}
        for k, o in enumerate(ops):
            o["deps"] = {newidx[d]: r for d, r in o["deps"].items()}
            assert all(d < k for d in o["deps"])
            o["i"] = k
        per_eng = {e: [] for e in ENGS}
        ndma = {e: 0 for e in ENGS}
        lastslot = {}
        for o in ops:
            e = o["eng"]
            o["pos"] = len(per_eng[e])
            per_eng[e].append(o["i"])
            if o["dma"]:
                nn = ndma[e]
                o["dslot"] = nn % DMA_K
                o["dval"] = 16 * (nn // DMA_K + 1)
                o["dprev"] = lastslot.get((e, o["dslot"]))
                lastslot[(e, o["dslot"])] = o["i"]
                ndma[e] = nn + 1
        known = {e: {f: -1 for f in ENGS} for e in ENGS}
        known_d = {e: {} for e in ENGS}
        milestone = set()
        for o in ops:
            e = o["eng"]
            wl = []
            deps = dict(o["deps"])
            if o["dma"] and o["dprev"] is not None:
                deps[o["dprev"]] = True
            for j, raw in deps.items():
                p = ops[j]
                f = p["eng"]
                if p["dma"]:
                    key = (f, p["dslot"])
                    if known_d[e].get(key, 0) < p["dval"]:
                        known_d[e][key] = p["dval"]
                        wl.append(("d", f, p["dslot"], p["dval"]))
                else:
                    if f == e and not o["dma"] and e != "pool":
                        if e == "pe" or not raw or not SAME_ENG_RAW_WAITS:
                            continue
                    if known[e][f] < p["pos"]:
                        wl.append(("c", f, p["pos"], j))
            best = {}
            out = []
            for w in wl:
                if w[0] == "c":
                    if w[1] not in best or best[w[1]][2] < w[2]:
                        best[w[1]] = w
                else:
                    out.append(w)
            for f, w in best.items():
                known[e][f] = w[2]
                milestone.add(w[3])
                out.append(w)
            o["waits"] = out
        cnt = {e: 0 for e in ENGS}
        for o in ops:
            if o["dma"]:
                continue
            if o["i"] in milestone:
                cnt[o["eng"]] += 1
                o["ms"] = cnt[o["eng"]]
            else:
                o["ms"] = None
        nsem = {e: max(1, (cnt[e] + SEM_WRAP - 1) // SEM_WRAP) for e in ENGS}
        nwaits = 0
        with contextlib.ExitStack() as st:
            csem = {e: [st.enter_context(nc.semaphore(f"c_{e}_{k}")) for k in range(nsem[e])] for e in ENGS}
            dsem = {e: [st.enter_context(nc.semaphore(f"d_{e}_{k}")) for k in range(DMA_K)]
                    for e in ENGS if ndma[e] > 0}
            block = st.enter_context(nc.Block())

            def mk(ename):
                def body(eng):
                    nonlocal nwaits
                    for i in per_eng[ename]:
                        o = ops[i]
                        for w in o["waits"]:
                            nwaits += 1
                            if w[0] == "c":
                                ms = ops[w[3]]["ms"]
                                k, v = (ms - 1) // SEM_WRAP, (ms - 1) % SEM_WRAP + 1
                                eng.wait_ge(csem[w[1]][k], v)
                            else:
                                eng.wait_ge(dsem[w[1]][w[2]], w[3])
                        ins = o["fn"](eng)
                        if o["dma"]:
                            ins.then_inc(dsem[ename][o["dslot"]], 16)
                        elif o["ms"] is not None:
                            k = (o["ms"] - 1) // SEM_WRAP
                            ins.then_inc(csem[ename][k], 1)
                    if ename == final_wait_eng:
                        for f in dsem:
                            last = {}
                            for j in per_eng[f]:
                                if ops[j]["dma"]:
                                    last[ops[j]["dslot"]] = ops[j]["dval"]
                            for s, v in last.items():
                                eng.wait_ge(dsem[f][s], v)
                return body

            block.tensor(mk("pe"))
            block.scalar(mk("act"))
            block.vector(mk("dve"))
            block.gpsimd(mk("pool"))
            block.sync(mk("sp"))
        return dict(n_ops=len(ops), per_eng={e: len(v) for e, v in per_eng.items()},
                    milestones=cnt, nwaits=nwaits)


class V:
    def __init__(self, ap, keys, r=False):
        self.ap = ap
        self.keys = tuple(keys)
        self.r = r

    def __getitem__(self, idx):
        return V(self.ap[idx], self.keys, self.r)

    def pat(self, pat, off=0):
        a = self.ap
        return V(bass.AP(a.tensor, a.offset + off, [list(a.ap[0])] + [list(p) for p in pat]), self.keys, self.r)

    def k(self, *keys):
        return V(self.ap, keys, self.r)

    def nr(self):
        return V(self.ap, self.keys, False)

    def rr(self):
        return V(self.ap, self.keys, True)


D = 1024
NKC = 8
SEQ = 2048
NPT = 16
NS = 16
TS = 8
MEM = 256
XH = 4
INC = 5120
CDEC = -float(np.exp(-0.5))
NORM_EPS = 1e-6
LNX_EPS = 64e-5
SC_ATT = 128 ** -0.5

CO = {}
_o = 0
for _n, _w in [("ident", 128), ("ones", 128),
               ("mb1", 512), ("sl1", 128), ("tic1", 128), ("tsc1", 128), ("boc1", 128),
               ("mb16", 512), ("sl16", 128), ("bo16", 128), ("tic16", 128), ("tsc16", 128), ("boc16", 128),
               ("iu2", 128), ("bo2", 128), ("sel16", 16), ("sel16c", 16), ("negc", 2), ("cm0", 1), ("cm1", 1),
               ("hw", 1), ("ha", 1)]:
    CO[_n] = (_o, _w)
    _o += _w
NCF = _o
CB = {"ident": (0, 128)}
NCB = 128


def make_consts():
    c = np.zeros((128, NCF), np.float32)
    s = np.arange(128)[:, None]
    t = np.arange(128)[None, :]
    def put(n, a):
        o, w = CO[n]
        c[:, o:o + w] = a
    put("ident", (s == t))
    put("ones", 1.0)
    for cfg, blk in ((1, 128), (16, 8), (2, 64)):
        same = (s // blk) == (t // blk)
        iu = same & (s <= t)
        su = same & (s < t)
        sl = same & (s > t)
        if cfg == 2:
            put("iu2", iu); put("bo2", same)
            continue
        put(f"mb{cfg}", np.concatenate([su, su, iu, iu], 1))
        put(f"sl{cfg}", sl)
        if cfg == 16:
            put("bo16", same)
        put(f"tic{cfg}", CDEC * iu); put(f"tsc{cfg}", CDEC * su); put(f"boc{cfg}", CDEC * same)
    sel = (np.arange(128)[:, None] // 8) == np.arange(16)[None, :]
    put("sel16", sel); put("sel16c", CDEC * sel); put("negc", CDEC)
    put("cm0", (np.arange(128) < 64)[:, None]); put("cm1", (np.arange(128) >= 64)[:, None])
    put("hw", (np.arange(128) < 64)[:, None]); put("ha", (np.arange(128) >= 64)[:, None])
    cb = np.zeros((128, NCB), np.float32)
    cb[:, 0:128] = np.eye(128)
    return c, cb.astype(ml_dtypes.bfloat16)


def build(depth=4, groups=None, dbg=False, memkv=True, phases=(1, 2, 3, 4), mks=9, levo=None):
    if groups is None:
        groups = [[0, 1, 2, 3], [4, 5, 6, 7], [8, 9, 10, 11], [12, 13, 14, 15], ["s"]]
    GMAX = max(len(g) for g in groups)
    NTM = GMAX * 128
    nc = bass.Bass("TRN2", target_bir_lowering=False)
    din = lambda n, s, d=F32: nc.dram_tensor(n, list(s), d, kind="ExternalInput").ap()
    dout = lambda n, s: nc.dram_tensor(n, list(s), F32, kind="ExternalOutput").ap()
    xp = din("xp", [SEQ, D]); xs = din("xs", [128, D]); mem = din("mem", [MEM, D])
    sth = din("sth", [2, NS, 8, 128, 128]); strw = din("strw", [2, NS, 16, 64, 64]); sts = din("sts", [2, NS, D])
    ck = din("ck", [4, NS, MEM, 512]); cv = din("cv", [4, NS, MEM, 512])
    norm_g = din("norm_g", [4, D]); w_in = din("w_in", [4, D, INC]); w_out = din("w_out", [4, 1536, D])
    mem_norm_g = din("mem_norm_g", [4, D]); w_mem_kv = din("w_mem_kv", [4, D, 1024])
    hg_lb = din("hg_lb", [2, D]); hg_og = din("hg_onorm_g", [2, 128])
    rw_mu = din("rw_mu", [2, 5, D]); rw_w0 = din("rw_w0", [2, D]); rw_w1 = din("rw_w1", [2, D, 64]); rw_w2 = din("rw_w2", [2, 64, D])
    rw_a0 = din("rw_a0", [2, D]); rw_a1 = din("rw_a1", [2, D, 64]); rw_a2 = din("rw_a2", [2, 64, D])
    rw_v0 = din("rw_v0", [1, D]); rw_v1 = din("rw_v1", [1, D, 32]); rw_v2 = din("rw_v2", [1, 32, D])
    rw_kk = din("rw_kk", [2, D]); rw_ka = din("rw_ka", [2, D]); rw_rk = din("rw_rk", [2, D])
    rw_lg = din("rw_lnx_g", [2, D]); rw_lb = din("rw_lnx_b", [2, D]); final_g = din("final_g", [1, D])
    cf_d = din("cf", [128, NCF]); cb_d = din("cb", [128, NCB], BF16)
    yp = dout("yp", [SEQ, D]); ys = dout("ys", [128, D])
    shp = dout("shp", [2, 8, 128, 128]); srp = dout("srp", [2, 16, 64, 64]); ssp = dout("ssp", [2, D])
    mkp = dout("mkp", [4, MEM, 512]); mvp = dout("mvp", [4, MEM, 512])
    shs = dout("shs", [2, NS, 8, 128, 128]); srs = dout("srs", [2, NS, 16, 64, 64]); sss = dout("sss", [2, NS, D])

    with contextlib.ExitStack() as st:
        P = Prog(nc)

        def sb(name, shape, dt=F32, r=False):
            t = st.enter_context(nc.sbuf_tensor(name, list(shape), dt))
            return V(t[:], [name], r and dt == F32)

        class Rot:
            def __init__(self, name, n, shape, dt=F32, r=False):
                self.bufs = [sb(f"{name}{i}", shape, dt, r) for i in range(n)]
                self.i = 0

            def get(self):
                b = self.bufs[self.i % len(self.bufs)]
                self.i += 1
                return b

        class PSR:
            def __init__(self):
                self.bufs = []
                for i in range(8):
                    t = st.enter_context(nc.psum_tensor(f"ps{i}", [128, 512], F32))
                    self.bufs.append(V(t[:], [f"ps{i}"]))
                    P.excl.add(f"ps{i}")
                self.i = 0

            def get(self):
                b = self.bufs[self.i % 8]
                self.i += 1
                return b

        PS = PSR()

        def bfv(v):
            return V(v.ap.bitcast(BF16), v.keys)

        def rk(*vs):
            out = []
            for v in vs:
                if isinstance(v, V):
                    out += list(v.keys)
            return out

        def a_(x):
            return x.ap if isinstance(x, V) else x

        def o_(x):
            return x.ap.bitcast(F32R) if x.r else x.ap

        def mm(o, l, r, start=True, stop=True):
            la, ra = l.ap, r.ap
            if l.r and r.r:
                la, ra = la.bitcast(F32R), ra.bitcast(F32R)
            P.op("pe", lambda e: e.matmul(o.ap, lhsT=la, rhs=ra, start=start, stop=stop), reads=rk(l, r), writes=rk(o))

        def tr(o, i, idn):
            P.op("pe", lambda e: e.transpose(out=o.ap, in_=i.ap, identity=idn.ap), reads=rk(i, idn), writes=rk(o))

        def act(o, i, f, bias=None, scale=None, acc=None):
            kw = {}
            if bias is not None:
                kw["bias"] = a_(bias)
            if scale is not None:
                kw["scale"] = a_(scale)
            if acc is not None:
                kw["accum_out"] = acc.ap
            oa = o_(o)
            P.op("act", lambda e: e.activation(out=oa, in_=i.ap, func=f, **kw), reads=rk(i, bias, scale), writes=rk(o, acc))

        def tt(o, a, b, op, eng="dve"):
            if eng == "pool" and o.r:
                eng = "dve"
            oa = o_(o)
            P.op(eng, lambda e: e.tensor_tensor(out=oa, in0=a.ap, in1=b.ap, op=op), reads=rk(a, b), writes=rk(o))

        def ts(o, a, s1, op0, s2=None, op1=None, eng="dve"):
            oa = o_(o)

            def f(e):
                if op1 is None:
                    return e.tensor_scalar(out=oa, in0=a.ap, scalar1=a_(s1), scalar2=None, op0=op0)
                return e.tensor_scalar(out=oa, in0=a.ap, scalar1=a_(s1), scalar2=a_(s2), op0=op0, op1=op1)
            P.op(eng, f, reads=rk(a, s1, s2), writes=rk(o))

        def stt(o, a, s, b, op0, op1):
            oa = o_(o)
            P.op("dve", lambda e: e.scalar_tensor_tensor(out=oa, in0=a.ap, scalar=a_(s), in1=b.ap, op0=op0, op1=op1),
                 reads=rk(a, s, b), writes=rk(o))

        def red(o, a, op, negate=False):
            P.op("dve", lambda e: e.tensor_reduce(out=o.ap, in_=a.ap, axis=AX.X, op=op, negate=negate), reads=rk(a), writes=rk(o))

        def recip(o, a):
            oa = o_(o)
            P.op("dve", lambda e: e.reciprocal(out=oa, in_=a.ap), reads=rk(a), writes=rk(o))

        def cp(o, i, eng="act"):
            if eng == "pool" and o.r:
                eng = "dve"
            oa = o_(o)
            if eng == "act":
                P.op("act", lambda e: e.copy(out=oa, in_=i.ap), reads=rk(i), writes=rk(o))
            else:
                P.op(eng, lambda e: e.tensor_copy(out=oa, in_=i.ap), reads=rk(i), writes=rk(o))

        def mset(o, val, eng="pool"):
            P.op(eng, lambda e: e.memset(o.ap, val), writes=rk(o))

        def dma(o, i, eng="sp", slow=False):
            oa = o.ap if isinstance(o, V) else o
            ia = i.ap if isinstance(i, V) else i
            kw = dict(allow_slow_non_contiguous=True) if slow else {}
            P.op(eng, lambda e: e.dma_start(out=oa, in_=ia, **kw), reads=rk(i), writes=rk(o), dma=True)

        MUL, ADD, SUB, MAX = ALU.mult, ALU.add, ALU.subtract, ALU.max

        cF = sb("cF", [128, NCF], r=True); cB = sb("cB", [128, NCB], BF16)
        dma(cB, cb_d)
        C = lambda n: cF[:, CO[n][0]:CO[n][0] + CO[n][1]]
        identF = C("ident"); identB = cB[:, 0:128]; onesF = C("ones")

        Sp = [sb(f"Sp{j}", [128, 8, 128], r=True) for j in range(2)]
        Mp = [sb(f"Mp{j}", [128, 8, 64], r=True) for j in range(2)]
        hcar = [sb(f"hcar{j}", [128, 8], BF16) for j in range(2)]
        for j in range(2):
            mset(Sp[j], 0.0); mset(Mp[j], 0.0)

        ARENA = 62 * 256
        arena_t = st.enter_context(nc.sbuf_tensor("arena", [128, ARENA], F32))
        arena = arena_t[:]
        aoff = {"P": 0, "S": 0}

        def carve(lay, name, shape, dt=F32):
            n = int(np.prod(shape[1:]))
            n32 = n if dt == F32 else (n + 1) // 2
            o = aoff[lay]
            aoff[lay] = o + n32
            assert aoff[lay] <= ARENA, (lay, name, aoff[lay])
            a = arena[:, o:o + n32]
            if dt != F32:
                a = a.bitcast(dt)
            pat = []
            stride = 1
            for d in reversed(shape[1:]):
                pat.insert(0, [stride, d])
                stride *= d
            return V(bass.AP(a.tensor, a.offset, [list(a.ap[0])] + pat), [name])

        LAY = {}
        for lay, ntm in (("P", GMAX * 128), ("S", 128)):
            Ld = dict(NTM=ntm)
            Ld["l1T"] = carve(lay, lay + "l1T", [128, 2, ntm])
            Ld["l1vT"] = carve(lay, lay + "l1vT", [128, ntm])
            if lay == "P":
                Ld["QZ"] = carve(lay, "QZ", [128, 256])
            Ld["xT"] = carve(lay, lay + "xT", [128, NKC, ntm])
            Ld["hT"] = carve(lay, lay + "hT", [128, NKC, 1 + ntm + 1], BF16)
            Ld["brT"] = carve(lay, lay + "brT", [128, 12, ntm], BF16)
            Ld["qx"] = [carve(lay, lay + f"qx{i}", [128, ntm], BF16) for i in range(2)]
            if lay == "P":
                Ld["mkT"] = carve(lay, "mkT", [128, 4, XH, MEM], BF16)
                Ld["mvb"] = carve(lay, "mvb", [128, 4, 2, 512], BF16)
            else:
                Ld["hpS"] = carve(lay, "hpS", [128, NKC, 128], BF16)
                Ld["Z1"] = carve(lay, "Z1", [128, 2048])
                Ld["Z1b"] = carve(lay, "Z1b", [128, 2048], BF16)
                Ld["Z2b"] = carve(lay, "Z2b", [128, 2048], BF16)
                Ld["M0b"] = carve(lay, "M0b", [128, 16, 64], BF16)
                o_save = aoff[lay]
                Ld["S0q"] = [carve(lay, f"S0q{i}", [128, 512]).k("salias") for i in range(4)]
                aoff[lay] = o_save
                Ld["M0all"] = carve(lay, "M0all", [128, 16, 64]).k("salias")
                Ld["SBD"] = [carve(lay, f"SBD{i}", [128, 512]).k("salias") for i in range(2)]
                Ld["MBD"] = [carve(lay, f"MBD{i}", [128, 512]) for i in range(2)]
                Ld["KVB"] = [carve(lay, f"KVB{i}", [128, 2048], BF16) for i in range(3)]
                Ld["QXM"] = carve(lay, "QXM", [128, 2048], BF16)
            LAY[lay] = Ld
        print("arena use (KiB):", {k: v / 256 for k, v in aoff.items()})

        T128 = Rot("t128_", 16, [128, 128], r=True)
        T256 = Rot("t256_", 3, [128, 256], r=True)
        T512 = Rot("t512_", 4, [128, 512], r=True)
        T1K = Rot("t1k_", 2, [128, 1024], r=True)
        TB256 = Rot("tb256_", 2, [128, 256], BF16)
        TB512 = Rot("tb512_", 2, [128, 512], BF16)
        TS8 = Rot("ts8_", 8, [128, 24])
        TS8s = TS8
        OG = sb("OG", [128, 128]); LBC = sb("LBC", [128, 24])
        BIGV = [sb("bigv0", [128, D], r=True), sb("bigv1", [128, D], r=True)]
        MUC = sb("MUC", [128, 80])
        L1a = sb("L1a", [128, NKC, 128], BF16); L1b = sb("L1b", [128, NKC, 128], BF16)
        L1va = sb("L1va", [128, NKC, 128], BF16); L1vb = sb("L1vb", [128, NKC, 128], BF16)
        PVB = sb("PVB", [128, 8, 128])
        ZP = [sb(f"zp{i}", [128, 256], BF16) for i in range(2)]
        ZBK = sb("zbk", [128, 512], BF16)
        ZBKH = [sb(f"zbkh{i}", [128, 512], BF16) for i in range(2)]
        PMBD = sb("PMBD", [128, 128])
        for zb in ZP + [ZBK, PMBD] + ZBKH:
            mset(zb, 0.0)
        WM = Rot("wm_", 2, [128, NKC, 384], BF16)
        WM2 = Rot("wm2_", 2, [128, NKC, 384], BF16)
        WR = Rot("wr_", 2, [128, NKC, 512], BF16)
        WO = Rot("wo_", 1, [128, 12, 128], BF16)
        XIN = Rot("xin_", 1, [128, D])
        _xa = XIN.bufs[0].ap
        PVS = [PVB, V(bass.AP(_xa.tensor, _xa.offset, [list(_xa.ap[0]), [128, 8], [1, 128]]), XIN.bufs[0].keys)]
        _xb = XIN.bufs[0].ap.bitcast(BF16)
        WO.bufs.append(V(bass.AP(_xb.tensor, _xb.offset, [list(_xb.ap[0]), [128, 12], [1, 128]]), XIN.bufs[0].keys))
        NRV = Rot("nrv_", 2, [128, 128], BF16)
        BT128 = Rot("b128_", 8, [128, 128], BF16)
        BT256 = Rot("b256_", 3, [128, 256], BF16)
        BT512 = Rot("b512_", 5, [128, 512], BF16)
        BT1K = Rot("b1k_", 2, [128, 1024], BF16)
        MK = [sb(f"mk{i}", [128, 512], BF16) for i in range(4)]
        MpB = [sb(f"MpB{j}", [128, 8, 64], BF16) for j in range(2)]
        for j in range(2):
            mset(MpB[j], 0.0)
        vfd = nc.dram_tensor("vfd", [17 * 128, D], BF16, kind="Internal").ap()

        for c0 in range(0, NCF, 1024):
            n = min(1024, NCF - c0)
            xin = XIN.get()
            dma(xin[:, 0:n], cf_d[:, c0:c0 + n])
            cp(cF[:, c0:c0 + n], xin[:, 0:n])

        def wview(w2d, c0, ncols):
            return w2d[:, c0:c0 + ncols].rearrange("(kc p) c -> p kc c", p=128)

        def xkey(li):
            return ("xT", li)

        def hkey(li):
            return ("hT", li)

        def mem_kv():
            LP = LAY["P"]
            mkT, mvb = LP["mkT"], LP["mvb"]
            memnT = V(LP["xT"].ap[:, :, 0:MEM], [xkey(0), xkey(1)])
            hmT = V(LP["hT"].ap[:, :, 1:1 + MEM], [hkey(0), hkey(1)])
            for mc in range(2):
                xin = XIN.get()
                dma(xin, mem[mc * 128:(mc + 1) * 128, :])
                ss = TS8.get()
                junk = T1K.get()
                act(junk, xin, AF.Square, acc=ss[:, 0:1])
                ts(ss[:, 1:2], ss[:, 0:1], 1.0 / D, MUL, NORM_EPS, ADD)
                act(ss[:, 2:3], ss[:, 1:2], AF.Ln)
                act(ss[:, 3:4], ss[:, 2:3], AF.Exp, scale=-0.5)
                xn = T1K.get()
                ts(xn, xin, ss[:, 3:4], MUL)
                for half in range(2):
                    ps = PS.get()
                    for q in range(4):
                        kc = half * 4 + q
                        tr(ps[:, q * 128:(q + 1) * 128], xn[:, kc * 128:(kc + 1) * 128], identF)
                    cp(memnT[:, half * 4:(half + 1) * 4, mc * 128:(mc + 1) * 128], ps.pat([[128, 4], [1, 128]]))
            for l in range(depth):
                if mks < 1:
                    break
                gcol = TS8.get()[:, 0:8]
                dma(gcol, mem_norm_g[l].rearrange("(kc p) -> p kc", p=128), slow=True)
                for kc in range(NKC):
                    ts(hmT[:, kc, :], memnT[:, kc, :], gcol[:, kc:kc + 1], MUL)
                if mks < 2:
                    continue
                for cb in range(2):
                    wb = WR.get()
                    dma(wb, wview(w_mem_kv[l], cb * 512, 512), eng="pool")
                    if mks < 3:
                        continue
                    if cb == 0:
                        for hx in range(XH):
                            ps = PS.get()
                            for kc in range(NKC):
                                mm(ps[:, 0:256], wb[:, kc, hx * 128:(hx + 1) * 128], hmT[:, kc, :], start=kc == 0, stop=kc == NKC - 1)
                            cp(mkT[:, l, hx, :], ps[:, 0:256])
                    if mks < 4:
                        continue
                    for mc in range(2):
                        ps = PS.get()
                        for kc in range(NKC):
                            mm(ps, hmT[:, kc, mc * 128:(mc + 1) * 128], wb[:, kc, :], start=kc == 0, stop=kc == NKC - 1)
                        stg = T512.get()
                        cp(stg, ps)
                        if mks >= 6:
                            dma((mkp if cb == 0 else mvp)[l, mc * 128:(mc + 1) * 128, :], stg)
                        if cb == 1 and mks >= 7:
                            cp(mvb[:, l, mc, :], ps, eng="dve")

        if memkv:
            mem_kv()
        mset(LAY["P"]["QZ"], 0.0)

        def process_group(gi, grp):
            ptiles = [t for t in grp if t != "s"]
            has_s = "s" in grp
            assert not (has_s and ptiles)
            L = LAY["S" if has_s else "P"]
            NTM = L["NTM"]; HS = NTM + 2
            xT, hT, brT, l1T, l1vT = L["xT"], L["hT"], L["brT"], L["l1T"], L["l1vT"]
            if has_s:
                hpS = L["hpS"]; Z1 = L["Z1"]; Z1b = L["Z1b"]; Z2b = L["Z2b"]; QXM = L["QXM"]
                P.barrier()
                for zb in [Z1, Z1b, Z2b, QXM] + L["SBD"] + L["MBD"]:
                    mset(zb, 0.0)
                QM = Z1
            else:
                mkT, mvb, QZ = L["mkT"], L["mvb"], L["QZ"]
            qxi = [0]
            npt = len(ptiles)
            first_group = bool(ptiles) and ptiles[0] == 0
            pchunks = []
            c = 0
            while c < npt * 128:
                n = min(512, npt * 128 - c)
                pchunks.append((c, n))
                c += n
            scol = npt * 128
            chunks = pchunks + ([(scol, 128)] if has_s else [])

            def tkeys(c0, n, fn):
                return tuple(fn(li) for li in range(c0 // 128, (c0 + n + 127) // 128))

            def xTv(kc, c0, n):
                return V(xT.ap[:, kc, c0:c0 + n], tkeys(c0, n, xkey))

            def xTt(li):
                return V(xT.ap[:, :, li * 128:(li + 1) * 128], [xkey(li)])

            def hTv(kc, c0, n):
                return V(hT.ap[:, kc, 1 + c0:1 + c0 + n], tkeys(c0, n, hkey))

            def hprevv(kc, c0, n):
                if has_s and c0 == scol:
                    return hpS[:, kc, :]
                ks = tkeys(c0, n, hkey) + ((hkey(c0 // 128 - 1),) if c0 > 0 else (("hT", "c0"),))
                return V(hT.ap[:, kc, c0:c0 + n], ks)

            def brv(kc, c0, n):
                return V(brT.ap[:, kc, c0:c0 + n], tkeys(c0, n, lambda li: ("br", kc, li)))

            for li, t in enumerate(grp):
                xin = XIN.get()
                dma(xin, xs if t == "s" else xp[t * 128:(t + 1) * 128, :])
                for half in range(2):
                    ps = PS.get()
                    for q in range(4):
                        kc = half * 4 + q
                        tr(ps[:, q * 128:(q + 1) * 128], xin[:, kc * 128:(kc + 1) * 128], identF)
                    cp(V(xT.ap[:, half * 4:(half + 1) * 4, li * 128:(li + 1) * 128], [xkey(li)]), ps.pat([[128, 4], [1, 128]]))

            def rms_tile(li):
                sq = T1K.get()
                act(sq.pat([[128, 8], [1, 128]]), xTt(li), AF.Square)
                ps = PS.get()
                for kc in range(NKC):
                    mm(ps[:, 0:128], onesF, sq[:, kc * 128:(kc + 1) * 128], start=kc == 0, stop=kc == NKC - 1)
                rs = T128.get()
                act(rs, ps[:, 0:128], AF.Ln, bias=NORM_EPS, scale=1.0 / D)
                rstd = T128.get()
                act(rstd, rs, AF.Exp, scale=-0.5)
                return rstd

            def hgrn_layer(l, j):
                og = OG
                dma(og, hg_og[j:j + 1, :].partition_broadcast(128))
                lbT = LBC[:, 0:8]; omT = LBC[:, 8:16]; nomT = LBC[:, 16:24]
                nom_bc = BIGV[0]
                if j == 0:
                    mset(omT, 1.0); mset(nomT, -1.0); mset(nom_bc, -1.0)
                else:
                    a0 = TS8.get()[:, 0:8]; a1 = TS8.get()[:, 0:8]
                    dma(a0, hg_lb[0].rearrange("(h p) -> p h", p=128), slow=True)
                    dma(a1, hg_lb[1].rearrange("(h p) -> p h", p=128), slow=True)
                    tt(a1, a1, a0, SUB)
                    act(lbT, a1, AF.Sigmoid)
                    ts(omT, lbT, -1.0, MUL, 1.0, ADD)
                    ts(nomT, lbT, 1.0, SUB)
                    b0 = XIN.get()
                    dma(b0, hg_lb[1:2, :].partition_broadcast(128))
                    cp(nom_bc, b0)
                    b0 = XIN.get()
                    dma(b0, hg_lb[0:1, :].partition_broadcast(128))
                    tt(nom_bc, nom_bc, b0, SUB)
                    act(nom_bc, nom_bc, AF.Sigmoid)
                    ts(nom_bc, nom_bc, 1.0, SUB)
                for h in range(8):
                    wblk = WM.get()
                    for q in range(3):
                        dma(wblk[:, :, q * 128:(q + 1) * 128], wview(w_in[l], q * 1024 + h * 128, 128), eng="pool")
                    Sph = V(Sp[j].ap[:, h, :], [("Sp", j, h)])
                    if has_s:
                        S0q = L["S0q"]
                        for q in range(4):
                            dma(S0q[q].pat([[128, 4], [1, 128]]), sth[j, 4 * q:4 * q + 4, h].rearrange("s k v -> k s v"))
                    hsegB = None
                    for li, t in enumerate(grp):
                        samp = (t == "s")
                        c0 = li * 128
                        hsegA = P.begin_seg()
                        if hsegB is not None and PIPE:
                            P.merges.append((hsegB, hsegA))
                        IU = C("mb16")[:, 256:384] if samp else C("iu2")
                        BO = C("bo16") if samp else C("bo2")
                        psq = PS.get()
                        for kc in range(NKC):
                            mm(psq[:, 0:128], wblk[:, kc, 0:128], hTv(kc, c0, 128), start=kc == 0, stop=kc == NKC - 1)
                        for kc in range(NKC):
                            mm(psq[:, 128:256], wblk[:, kc, 128:256], hTv(kc, c0, 128), start=kc == 0, stop=kc == NKC - 1)
                        psfv = PS.get()
                        for kc in range(NKC):
                            mm(psfv[:, 0:256], hTv(kc, c0, 128), wblk[:, kc, 128:384], start=kc == 0, stop=kc == NKC - 1)
                        qsg = T128.get(); act(qsg, psq[:, 0:128], AF.Sigmoid)
                        qs = T128.get(); tt(qs, qsg, psq[:, 0:128], MUL)
                        sgT = T128.get(); act(sgT, psq[:, 128:256], AF.Sigmoid)
                        kT = T128.get(); ts(kT, sgT, nomT[:, h:h + 1], MUL, omT[:, h:h + 1], ADD)
                        sg = T128.get(); act(sg, psfv[:, 0:128], AF.Sigmoid)
                        kt = T128.get(); stt(kt, sg, -1.0, nom_bc[:, h * 128:(h + 1) * 128], ADD, MUL)
                        g = T128.get(); act(g, kt, AF.Ln, bias=1.0, scale=-1.0)
                        vt = T128.get(); cp(vt, psfv[:, 128:256])
                        ps3 = PS.get()
                        mm(ps3[:, 0:128], IU, g)
                        mm(ps3[:, 128:256], g, IU)
                        mm(ps3[:, 256:384], BO, g)
                        eb = T128.get(); act(eb, ps3[:, 128:256], AF.Exp)
                        enb = T128.get(); act(enb, ps3[:, 128:256], AF.Exp, scale=-1.0)
                        qtT = T128.get(); tt(qtT, qs, eb, MUL)
                        ktT = T128.get(); tt(ktT, kT, enb, MUL)
                        bsb = T128.get(); cp(bsb, ps3[:, 0:128])
                        dd = T128.get(); tt(dd, ps3[:, 256:384], bsb, SUB)
                        ed = T128.get(); act(ed, dd, AF.Exp)
                        psA = PS.get()
                        mm(psA[:, 0:128], ktT, qtT)
                        att = T128.get(); tt(att, psA[:, 0:128], IU, MUL)
                        P.end_seg()
                        hsegB = P.begin_seg()
                        psO = PS.get()
                        mm(psO[:, 0:128], att, vt, start=True, stop=False)
                        if not samp:
                            tt(QZ.pat([[192, 2], [1, 64]]), qs.pat([[64, 2], [1, 64]]), eb.pat([[64, 2], [1, 64]]), MUL)
                            kh0 = T128.get(); stt(kh0, kt, C("cm0"), ed, MUL, MUL)
                            kh1 = T128.get(); stt(kh1, kt, C("cm1"), ed, MUL, MUL)
                            psS = PS.get()
                            mm(psS[:, 0:128], kh0, vt)
                            mm(psS[:, 128:256], kh1, vt)
                            mm(psO[:, 0:128], QZ[:, 0:128], Sph, start=False, stop=False)
                            S1 = T128.get(); stt(S1, Sph, eb[:, 63:64], psS[:, 0:128], MUL, ADD)
                            mm(psO[:, 0:128], QZ[:, 128:256], S1, start=False, stop=True)
                            stt(Sph, S1, eb[:, 127:128], psS[:, 128:256], MUL, ADD)
                        else:
                            tt(QM.pat([[136, 16], [1, 8]]), qs.pat([[8, 16], [1, 8]]), eb.pat([[8, 16], [1, 8]]), MUL)
                            for s in range(NS):
                                mm(psO[:, 0:128], QM[:, s * 128:(s + 1) * 128], S0q[s // 4][:, (s % 4) * 128:(s % 4 + 1) * 128], start=False, stop=(s == NS - 1))
                            kh = T128.get(); tt(kh, kt, ed, MUL)
                            for q in range(4):
                                vm = T512.get()
                                tt(vm.pat([[128, 4], [1, 128]]), vt.pat([[0, 4], [1, 128]]), C("sel16").pat([[1, 4], [0, 128]], off=4 * q), MUL)
                                psW = PS.get()
                                mm(psW, kh, vm)
                                tmp = T512.get()
                                tt(tmp.pat([[128, 4], [1, 128]]), S0q[q].pat([[128, 4], [1, 128]]),
                                   eb.pat([[8, 4], [0, 128]], off=7 + 32 * q), MUL)
                                Sn = T512.get()
                                tt(Sn, tmp, psW, ADD)
                                dma(shs[j, 4 * q:4 * q + 4, h].rearrange("s k v -> k s v"), Sn.pat([[128, 4], [1, 128]]))
                        ss = TS8.get()
                        junk = T128.get()
                        act(junk, psO[:, 0:128], AF.Square, acc=ss[:, 0:1])
                        ts(ss[:, 1:2], ss[:, 0:1], 1.0 / 128, MUL, NORM_EPS, ADD)
                        act(ss[:, 2:3], ss[:, 1:2], AF.Ln)
                        act(ss[:, 3:4], ss[:, 2:3], AF.Exp, scale=-0.5)
                        on = T128.get(); stt(on, psO[:, 0:128], ss[:, 3:4], og, MUL, MUL)
                        psT = PS.get()
                        tr(psT[:, 0:128], on, identF)
                        cp(brv(h, c0, 128), psT[:, 0:128])
                        P.end_seg()
                if first_last_prompt_group:
                    for h in range(8):
                        dma(shp[j, h], V(Sp[j].ap[:, h, :], [("Sp", j, h)]))

            def rest_layer(l):
                if 2 not in phases:
                    return
                wq = WR.get()
                dma(wq, wview(w_in[l], 3072, 512), eng="pool")
                for hx in range(XH):
                    qxT = L['qx'][qxi[0] % 2]; qxi[0] += 1
                    for (c0, n) in chunks:
                        ps = PS.get()
                        for kc in range(NKC):
                            mm(ps[:, 0:n], wq[:, kc, hx * 128:(hx + 1) * 128], hTv(kc, c0, n), start=kc == 0, stop=kc == NKC - 1)
                        cp(qxT[:, c0:c0 + n], ps[:, 0:n])
                    if has_s:
                        KVB = L["KVB"]
                        kvi = [0]
                        def kvget():
                            b = KVB[kvi[0] % 3]; kvi[0] += 1
                            return b
                        mkTs = []; vbs = []
                        for hf in range(2):
                            kb = kvget()
                            dma(kb.pat([[256, 8], [128, 2], [1, 128]]), ck[l][8 * hf:8 * hf + 8, :, hx * 128:(hx + 1) * 128].rearrange("s (mc p) d -> p s mc d", p=128), eng="pool")
                            mk_ = kvget()
                            for b4 in range(2):
                                psb = bfv(PS.get())
                                for q in range(8):
                                    idx = b4 * 8 + q
                                    tr(psb[:, q * 128:(q + 1) * 128], kb[:, idx * 128:(idx + 1) * 128], identB)
                                cp(mk_[:, b4 * 1024:(b4 + 1) * 1024], psb)
                            mkTs.append(mk_)
                    for li, t in enumerate(grp):
                        samp = (t == "s")
                        c0 = li * 128
                        ps = PS.get()
                        if not samp:
                            mm(ps[:, 0:256], qxT[:, c0:c0 + 128], mkT[:, l, hx, :])
                        else:
                            cp(QXM.pat([[136, 16], [1, 8]]), qxT[:, c0:c0 + 128].pat([[8, 16], [1, 8]]), eng="dve")
                            for s in range(NS):
                                mm(ps[:, 0:256], QXM[:, s * 128:(s + 1) * 128], mkTs[s // 8][:, (s % 8) * 256:(s % 8 + 1) * 256], start=(s == 0), stop=(s == NS - 1))
                        ss = TS8.get()
                        red(ss[:, 0:1], ps[:, 0:256], MAX, negate=True)
                        ts(ss[:, 1:2], ss[:, 0:1], SC_ATT, MUL)
                        pe_ = T512.get()
                        act(pe_[:, 0:256], ps[:, 0:256], AF.Exp, bias=ss[:, 1:2], scale=SC_ATT, acc=ss[:, 2:3])
                        recip(ss[:, 3:4], ss[:, 2:3])
                        pn = TB256.get()
                        ts(pn, pe_[:, 0:256], ss[:, 3:4], MUL)
                        psb = bfv(PS.get())
                        tr(psb[:, 0:128], pn[:, 0:128], identB)
                        tr(psb[:, 128:256], pn[:, 128:256], identB)
                        pT = TB256.get()
                        cp(pT, psb[:, 0:256])
                        psO = PS.get()
                        if not samp:
                            for mc in range(2):
                                mm(psO[:, 0:128], mvb[:, l, mc, hx * 128:(hx + 1) * 128], pT[:, mc * 128:(mc + 1) * 128], start=mc == 0, stop=mc == 1)
                        else:
                            for hf in range(2):
                                vb_ = kvget()
                                dma(vb_.pat([[256, 8], [128, 2], [1, 128]]), cv[l][8 * hf:8 * hf + 8, :, hx * 128:(hx + 1) * 128].rearrange("s (mc p) d -> p s mc d", p=128), eng="pool")
                                vbs.append(vb_)
                            for s in range(NS):
                                for mc in range(2):
                                    vb = vbs[s // 8]
                                    mm(psO[:, s * 8:(s + 1) * 8], vb[:, ((s % 8) * 2 + mc) * 128:((s % 8) * 2 + mc + 1) * 128],
                                       pT[:, mc * 128 + s * 8:mc * 128 + (s + 1) * 8], start=mc == 0, stop=mc == 1)
                        cp(brv(8 + hx, c0, 128), psO[:, 0:128])
                if 3 not in phases:
                    return
                for gb in range(3):
                    wg = WR.get()
                    dma(wg, wview(w_in[l], 3584 + gb * 512, 512), eng="pool")
                    for sub in range(4):
                        bi = gb * 4 + sub
                        for (c0, n) in chunks:
                            ps = PS.get()
                            for kc in range(NKC):
                                mm(ps[:, 0:n], wg[:, kc, sub * 128:(sub + 1) * 128], hTv(kc, c0, n), start=kc == 0, stop=kc == NKC - 1)
                            sgt = TB512.get()
                            act(sgt[:, 0:n], ps[:, 0:n], AF.Sigmoid)
                            tt(sgt[:, 0:n], sgt[:, 0:n], ps[:, 0:n], MUL)
                            tt(brv(bi, c0, n), brv(bi, c0, n), sgt[:, 0:n], MUL)
                if 4 not in phases:
                    return
                for ob in range(NKC):
                    wo = WO.get()
                    dma(wo, w_out[l][:, ob * 128:(ob + 1) * 128].rearrange("(kc p) c -> p kc c", p=128), eng="pool")
                    for (c0, n) in chunks:
                        ps = PS.get()
                        for kc in range(12):
                            mm(ps[:, 0:n], wo[:, kc, :], brv(kc, c0, n), start=kc == 0, stop=kc == 11)
                        tt(xTv(ob, c0, n), xTv(ob, c0, n), ps[:, 0:n], ADD)

            def rwkv_layer(l, j):
                m = j - 1
                cfgn = "16" if has_s else "1"
                MB = C("mb" + cfgn); SL = C("sl" + cfgn)
                TIC = C("tic" + cfgn); TSC = C("tsc" + cfgn); BOC = C("boc" + cfgn)
                LEV = 3 if has_s else 7
                if levo is not None:
                    LEV = levo
                muT = MUC[:, 0:40].pat([[8, 5], [1, 8]]); omuT = MUC[:, 40:80].pat([[8, 5], [1, 8]])
                dma(muT, rw_mu[j].rearrange("k (kc p) -> p k kc", p=128), slow=True)
                ts(omuT, muT, -1.0, MUL, 1.0, ADD)
                raw = WR.get()
                dma(raw[:, :, 0:64], rw_w1[j].rearrange("(kc p) c -> p kc c", p=128), eng="pool")
                dma(raw[:, :, 64:128], rw_a1[j].rearrange("(kc p) c -> p kc c", p=128), eng="pool")
                if j >= 1:
                    mset(raw[:, :, 160:256], 0.0)
                    dma(raw[:, :, 128:160], rw_v1[m].rearrange("(kc p) c -> p kc c", p=128), eng="pool")
                for (c_lo, kind) in ((0, 1), (64, 4)):
                    src = raw[:, :, c_lo:c_lo + 64]
                    tt(L1a[:, :, c_lo:c_lo + 64], src, omuT[:, kind, :].pat([[1, 8], [0, 64]]), MUL)
                    tt(L1b[:, :, c_lo:c_lo + 64], src, muT[:, kind, :].pat([[1, 8], [0, 64]]), MUL)
                if j >= 1:
                    src = raw[:, :, 128:256]
                    tt(L1va, src, omuT[:, 3, :].pat([[1, 8], [0, 128]]), MUL)
                    tt(L1vb, src, muT[:, 3, :].pat([[1, 8], [0, 128]]), MUL)
                W2 = BIGV[0]; V2 = BIGV[1]
                xin = XIN.get()
                dma(xin[0:64, :], rw_w2[j]); dma(xin[64:128, :], rw_a2[j])
                cp(W2, xin)
                if j >= 1:
                    xin = XIN.get()
                    mset(xin, 0.0)
                    dma(xin[0:32, :], rw_v2[m])
                    cp(V2, xin, eng="dve")
                for (c0, n) in chunks:
                    ps = PS.get()
                    for kc in range(NKC):
                        mm(ps[:, 0:n], L1a[:, kc, :], hTv(kc, c0, n), start=kc == 0, stop=False)
                    for kc in range(NKC):
                        mm(ps[:, 0:n], L1b[:, kc, :], hprevv(kc, c0, n), start=False, stop=kc == NKC - 1)
                    th = T512.get()
                    act(th[:, 0:n], ps[:, 0:n], AF.Tanh)
                    ts(V(l1T.ap[:, 0, c0:c0 + n], tkeys(c0, n, lambda li: ("l1", li))), th[:, 0:n], C("hw"), MUL)
                    ts(V(l1T.ap[:, 1, c0:c0 + n], tkeys(c0, n, lambda li: ("l1", li))), ps[:, 0:n], C("ha"), MUL)
                    if j >= 1:
                        ps = PS.get()
                        for kc in range(NKC):
                            mm(ps[:, 0:n], L1va[:, kc, :], hTv(kc, c0, n), start=kc == 0, stop=False)
                        for kc in range(NKC):
                            mm(ps[:, 0:n], L1vb[:, kc, :], hprevv(kc, c0, n), start=False, stop=kc == NKC - 1)
                        cp(V(l1vT.ap[:, c0:c0 + n], tkeys(c0, n, lambda li: ("l1v", li))), ps[:, 0:n])

                for pr in range(8):
                    cc = pr * 128
                    wa = WM.get(); wb = WM2.get()
                    for q in range(3):
                        dma(wa[:, :, q * 128:(q + 1) * 128], wview(w_in[l], q * 1024 + cc, 128), eng="pool")
                    for q, kind in enumerate((0, 2, 3)):
                        tt(wb[:, :, q * 128:(q + 1) * 128], wa[:, :, q * 128:(q + 1) * 128], muT[:, kind, :].pat([[1, 8], [0, 128]]), MUL)
                    tt(wa, wa, wb, SUB)
                    pv = PVS[pr % 2]
                    vecs = [rw_w0[j:j + 1], rw_a0[j:j + 1], (rw_v0[m:m + 1] if j >= 1 else rw_a0[j:j + 1]), rw_kk[j:j + 1], rw_ka[j:j + 1], rw_rk[j:j + 1], rw_lg[j:j + 1], rw_lb[j:j + 1]]
                    for k8, vv in enumerate(vecs):
                        dma(pv[:, k8, :], vv[:, cc:cc + 128].partition_broadcast(128))
                    Mpp = V(Mp[j].ap[:, pr, :], [("Mp", j, pr)], True)
                    MppB = V(MpB[j].ap[:, pr, :], [("MpB", j, pr)])
                    if has_s:
                        M0b = L["M0b"]
                        M0all = L["M0all"]
                        for q in range(4):
                            SBD = L["SBD"][q % 2]
                            for hh in range(2):
                                dma(SBD[hh * 64:(hh + 1) * 64, :].pat([[128, 4], [1, 64]], off=hh * 64),
                                    strw[j, 4 * q:4 * q + 4, 2 * pr + hh].rearrange("s i k -> i s k"))
                            ps = PS.get()
                            for s4 in range(4):
                                tr(ps[:, s4 * 128:(s4 + 1) * 128], SBD[:, s4 * 128:(s4 + 1) * 128], identF)
                            for hh in range(2):
                                cp(M0all[hh * 64:(hh + 1) * 64, 4 * q:4 * q + 4, :], ps[hh * 64:(hh + 1) * 64, :].pat([[128, 4], [1, 64]], off=hh * 64),
                                   eng=("act" if hh == 0 else "dve"))
                                cp(M0b[hh * 64:(hh + 1) * 64, 4 * q:4 * q + 4, :], ps[hh * 64:(hh + 1) * 64, :].pat([[128, 4], [1, 64]], off=hh * 64),
                                   eng=("dve" if hh == 0 else "act"))
                    segB_prev = None
                    for li, t in enumerate(grp):
                        samp = (t == "s")
                        c0 = li * 128
                        gt = 16 if samp else t
                        segA = P.begin_seg()
                        if segB_prev is not None and PIPE:
                            P.merges.append((segB_prev, segA))
                        psR = PS.get()
                        for kc in range(NKC):
                            mm(psR[:, 0:384], hTv(kc, c0, 128), wa[:, kc, :], start=kc == 0, stop=False)
                        for kc in range(NKC):
                            mm(psR[:, 0:384], hprevv(kc, c0, 128), wb[:, kc, :], start=False, stop=kc == NKC - 1)
                        rP, kP, vP = psR[:, 0:128], psR[:, 128:256], psR[:, 256:384]
                        l1k = ("l1", li)
                        psL = PS.get()
                        mm(psL[:, 0:128], V(l1T.ap[:, 0, c0:c0 + 128], [l1k]), W2[:, cc:cc + 128])
                        mm(psL[:, 128:256], V(l1T.ap[:, 1, c0:c0 + 128], [l1k]), W2[:, cc:cc + 128])
                        if j >= 1:
                            mm(psL[:, 256:384], V(l1vT.ap[:, c0:c0 + 128], [("l1v", li)]), V2[:, cc:cc + 128])
                        npre = 384 if j >= 1 else 256
                        pre = T512.get(); tt(pre[:, 0:npre], psL[:, 0:npre], pv.pat([[1, npre]]), ADD)
                        sgs = T512.get(); act(sgs[:, 0:npre], pre[:, 0:npre], AF.Sigmoid)
                        sg = sgs[:, 0:128]; ag = sgs[:, 128:256]
                        Vt = BT128.get()
                        vkey = ("vfd", gt, pr)
                        if j == 0:
                            cp(Vt, vP)
                            P.op("sp", (lambda e, o=vfd[gt * 128:(gt + 1) * 128, cc:cc + 128], i_=Vt.ap: e.dma_start(out=o, in_=i_)),
                                 reads=list(Vt.keys), writes=[vkey], dma=True)
                        else:
                            vf = NRV.get()
                            P.op("sp", (lambda e, i_=vfd[gt * 128:(gt + 1) * 128, cc:cc + 128], o=vf.ap: e.dma_start(out=o, in_=i_)),
                                 reads=[vkey], writes=list(vf.keys), dma=True)
                            vg = sgs[:, 256:384]
                            d1 = T128.get(); tt(d1, vf, vP, SUB)
                            tt(d1, d1, vg, MUL)
                            tt(Vt, d1, vP, ADD)
                        kk = T128.get(); tt(kk, kP, pv[:, 3, :], MUL)
                        sq = T128.get(); tt(sq, kk, kk, MUL)
                        sm = TS8.get()
                        red(sm[:, 0:2], sq.pat([[64, 2], [1, 64]]), ADD)
                        act(sm[:, 2:4], sm[:, 0:2], AF.Ln)
                        act(sm[:, 4:6], sm[:, 2:4], AF.Exp, scale=-0.5)
                        ts(sm[:, 4:6], sm[:, 4:6], 1e12, ALU.min)
                        kkn = T128.get()
                        tt(kkn.pat([[64, 2], [1, 64]]), kk.pat([[64, 2], [1, 64]]), sm[:, 4:6].pat([[1, 2], [0, 64]]), MUL)
                        t1 = T128.get(); stt(t1, ag, -1.0, pv[:, 4, :], ADD, MUL)
                        BK = T256.get()
                        kp = BK[:, 128:256]; stt(kp, t1, 1.0, kP, ADD, MUL)
                        bb = BK[:, 0:128]; tt(bb, kkn, ag, MUL)
                        t2 = T128.get(); tt(t2, rP, kp, MUL)
                        tt(t2, t2, pv[:, 5, :], MUL)
                        red(sm[:, 6:8], t2.pat([[64, 2], [1, 64]]), ADD)
                        psC = PS.get()
                        mm(psC[:, 0:128], TIC, sg)
                        mm(psC[:, 128:256], TSC, sg)
                        mm(psC[:, 256:384], BOC, sg)
                        nsq = 16 if samp else 1
                        nsq = 16 if samp else 2
                        mm(psC[:, 384:384 + nsq], sg, (C("sel16c") if samp else C("negc")))
                        EE = T512.get(); act(EE[:, 0:384], psC[:, 0:384], AF.Exp)
                        E1 = EE[:, 0:128]; E3 = EE[:, 128:256]; E5 = EE[:, 256:384]
                        E2 = T128.get(); act(E2, psC[:, 0:128], AF.Exp, scale=-1.0)
                        gC = TS8s.get(); act(gC[:, 0:nsq], psC[:, 384:384 + nsq], AF.Exp)
                        dg = [[192, 2], [1, 64]]; nd = [[64, 2], [1, 64]]
                        Am, Rm = ZP; Bm = ZBK[:, 0:256]; Km = ZBK[:, 256:512]
                        ZH = ZBKH[li % 2]; BHm = ZH[:, 0:256]; KHm = ZH[:, 256:512]
                        dg2 = [[256, 2], [192, 2], [1, 64]]
                        stt(Am.pat(dg), kkn.pat(nd), -1.0, E3.pat(nd), MUL, MUL)
                        tt(ZBK.pat(dg2), BK.pat([[128, 2], [64, 2], [1, 64]]), E2.pat([[0, 2], [64, 2], [1, 64]]), MUL)
                        tt(Rm.pat(dg), rP.pat(nd), E1.pat(nd), MUL)
                        tt(ZH.pat(dg2), ZBK.pat(dg2), E5.pat([[0, 2], [64, 2], [1, 64]]), MUL)
                        XT = BT1K.get()
                        psb = bfv(PS.get())
                        for q, Xm in enumerate((Am, Am, Bm, Bm, Km, Km, Rm, Rm)):
                            hh = q % 2
                            tr(psb[:, q * 128:(q + 1) * 128], Xm[:, hh * 128:(hh + 1) * 128], identB)
                        cp(XT[:, 0:512], psb[:, 0:512]); cp(XT[:, 512:1024], psb[:, 512:1024], eng="dve")
                        AT = lambda hh: XT[:, (0 + hh) * 128:(1 + hh) * 128]
                        BT = lambda hh: XT[:, (2 + hh) * 128:(3 + hh) * 128]
                        KT = lambda hh: XT[:, (4 + hh) * 128:(5 + hh) * 128]
                        RT = lambda hh: XT[:, (6 + hh) * 128:(7 + hh) * 128]
                        Mk = []; XY = BT512.get(); WT = BT256.get()
                        for hh in range(2):
                            psM = PS.get()
                            mm(psM[:, 0:128], BT(hh), AT(hh))
                            mm(psM[:, 128:256], KT(hh), AT(hh))
                            mm(psM[:, 256:384], BT(hh), RT(hh))
                            mm(psM[:, 384:512], KT(hh), RT(hh))
                            mk_ = MK[2 * (li % 2) + hh]; tt(mk_, psM, MB, MUL)
                            Mk.append(mk_)
                            psN = PS.get()
                            mm(psN[:, 0:128], AT(hh), BT(hh))
                            tt(XY[:, hh * 256 + 128:hh * 256 + 256], psN[:, 0:128], SL, MUL)
                            tt(WT[:, hh * 128:(hh + 1) * 128], mk_[:, 0:128], identF, ADD)
                        for lev in range(1, LEV):
                            psV = PS.get()
                            for hh in range(2):
                                Xp = Mk[hh][:, 0:128] if lev == 1 else XY[:, hh * 256:hh * 256 + 128]
                                Yp = XY[:, hh * 256 + 128:hh * 256 + 256]
                                mm(psV[:, hh * 256 + 128:hh * 256 + 256], Xp, Yp)
                                if lev < LEV - 1:
                                    mm(psV[:, hh * 256:hh * 256 + 128], Yp, Xp)
                            XYn = BT512.get()
                            if lev < LEV - 1:
                                cp(XYn, psV)
                            else:
                                cp(XYn.pat([[256, 2], [1, 128]], off=128), psV.pat([[256, 2], [1, 128]], off=128))
                            psW = PS.get()
                            for hh in range(2):
                                mm(psW[:, hh * 128:(hh + 1) * 128], XYn[:, hh * 256 + 128:hh * 256 + 256], WT[:, hh * 128:(hh + 1) * 128])
                            WTn = BT256.get()
                            tt(WTn, psW[:, 0:256], WT, ADD)
                            XY = XYn; WT = WTn
                        P.end_seg()
                        segB_prev = P.begin_seg()
                        if samp:
                            cp(Z1b.pat([[136, 16], [1, 8]]), AT(0).pat([[8, 16], [1, 8]]), eng="act")
                            cp(Z2b.pat([[136, 16], [1, 8]]), AT(1).pat([[8, 16], [1, 8]]), eng="dve")
                        psP = PS.get()
                        for hh in range(2):
                            o_ = psP[:, hh * 64:(hh + 1) * 64]
                            if not samp:
                                mm(o_, AT(hh), MppB, start=True, stop=False)
                            else:
                                ZZ = Z1b if hh == 0 else Z2b
                                for s in range(NS):
                                    mm(o_, ZZ[:, s * 128:(s + 1) * 128], M0b[:, s, :], start=(s == 0), stop=False)
                            mm(o_, Mk[hh][:, 128:256], Vt[:, hh * 64:(hh + 1) * 64], start=False, stop=True)
                        P1 = BT128.get(); cp(P1, psP[:, 0:128])
                        psU = PS.get()
                        for hh in range(2):
                            mm(psU[:, hh * 64:(hh + 1) * 64], WT[:, hh * 128:(hh + 1) * 128], P1[:, hh * 64:(hh + 1) * 64])
                        U = BT128.get(); cp(U, psU[:, 0:128], eng="dve")
                        if samp:
                            cp(Z1b.pat([[136, 16], [1, 8]]), RT(0).pat([[8, 16], [1, 8]]), eng="act")
                            cp(Z2b.pat([[136, 16], [1, 8]]), RT(1).pat([[8, 16], [1, 8]]), eng="dve")
                        psY = PS.get()
                        for hh in range(2):
                            o_ = psY[:, hh * 64:(hh + 1) * 64]
                            if not samp:
                                mm(o_, RT(hh), MppB, start=True, stop=False)
                            else:
                                ZZ = Z1b if hh == 0 else Z2b
                                for s in range(NS):
                                    mm(o_, ZZ[:, s * 128:(s + 1) * 128], M0b[:, s, :], start=(s == 0), stop=False)
                            mm(o_, Mk[hh][:, 256:384], U[:, hh * 64:(hh + 1) * 64], start=False, stop=False)
                            mm(o_, Mk[hh][:, 384:512], Vt[:, hh * 64:(hh + 1) * 64], start=False, stop=True)
                        if not samp:
                            psS = PS.get()
                            mm(psS[:, 0:64], BHm[:, 0:128], U[:, 0:64], start=True, stop=False)
                            mm(psS[:, 0:64], BHm[:, 128:256], U[:, 64:128], start=False, stop=False)
                            mm(psS[:, 0:64], KHm[:, 0:128], Vt[:, 0:64], start=False, stop=False)
                            mm(psS[:, 0:64], KHm[:, 128:256], Vt[:, 64:128], start=False, stop=True)
                            stt(Mpp, Mpp, gC[:, 0:1], psS[:, 0:64], MUL, ADD)
                            cp(MppB, Mpp)
                        else:
                            for q in range(2):
                                Uw = [BT512.get(), BT512.get()]; Vw = [BT512.get(), BT512.get()]
                                for hh in range(2):
                                    tt(Uw[hh].pat([[64, 8], [1, 64]]), U[:, hh * 64:(hh + 1) * 64].pat([[0, 8], [1, 64]]),
                                       C("sel16").pat([[1, 8], [0, 64]], off=8 * q), MUL)
                                    tt(Vw[hh].pat([[64, 8], [1, 64]]), Vt[:, hh * 64:(hh + 1) * 64].pat([[0, 8], [1, 64]]),
                                       C("sel16").pat([[1, 8], [0, 64]], off=8 * q), MUL)
                                psS = PS.get()
                                mm(psS, BHm[:, 0:128], Uw[0], start=True, stop=False)
                                mm(psS, BHm[:, 128:256], Uw[1], start=False, stop=False)
                                mm(psS, KHm[:, 0:128], Vw[0], start=False, stop=False)
                                mm(psS, KHm[:, 128:256], Vw[1], start=False, stop=True)
                                tmp = T512.get()
                                tt(tmp.pat([[64, 8], [1, 64]]), M0all[:, 8 * q:8 * q + 8, :], gC[:, 8 * q:8 * q + 8].pat([[1, 8], [0, 64]]), MUL)
                                Mn = T512.get()
                                tt(Mn, tmp, psS, ADD)
                                for q2 in range(2):
                                    MBD = L["MBD"][q2]
                                    for hh in range(2):
                                        cp(MBD[hh * 64:(hh + 1) * 64, :].pat([[128, 4], [1, 64]], off=hh * 64),
                                           Mn[hh * 64:(hh + 1) * 64, q2 * 256:(q2 + 1) * 256].pat([[64, 4], [1, 64]]), eng=("act" if hh == 0 else "dve"))
                                    ps = PS.get()
                                    for s4 in range(4):
                                        tr(ps[:, s4 * 128:(s4 + 1) * 128], MBD[:, s4 * 128:(s4 + 1) * 128], identF)
                                    So = T256.get()
                                    for hh in range(2):
                                        cp(So[hh * 64:(hh + 1) * 64, :].pat([[64, 4], [1, 64]]),
                                           ps[hh * 64:(hh + 1) * 64, :].pat([[128, 4], [1, 64]], off=hh * 64), eng=("act" if hh == 0 else "dve"))
                                    sb0 = 8 * q + 4 * q2
                                    for hh in range(2):
                                        dma(srs[j, sb0:sb0 + 4, 2 * pr + hh].rearrange("s i k -> i s k"),
                                            So[hh * 64:(hh + 1) * 64, :].pat([[64, 4], [1, 64]]))
                        g2 = [[64, 2], [1, 64]]
                        red(sm[:, 8:10], psY[:, 0:128].pat(g2), ADD)
                        ysq = T128.get(); act(ysq, psY[:, 0:128], AF.Square)
                        red(sm[:, 10:12], ysq.pat(g2), ADD)
                        ts(sm[:, 12:14], sm[:, 8:10], 1.0 / 64, MUL)
                        tt(sm[:, 14:16], sm[:, 12:14], sm[:, 12:14], MUL)
                        stt(sm[:, 16:18], sm[:, 10:12], 1.0 / 64, sm[:, 14:16], MUL, SUB)
                        act(sm[:, 18:20], sm[:, 16:18], AF.Ln, bias=LNX_EPS, scale=1.0)
                        act(sm[:, 20:22], sm[:, 18:20], AF.Exp, scale=-0.5)
                        yc = T128.get()
                        tt(yc.pat(g2), psY[:, 0:128].pat(g2), sm[:, 12:14].pat([[1, 2], [0, 64]]), SUB)
                        tt(yc.pat(g2), yc.pat(g2), sm[:, 20:22].pat([[1, 2], [0, 64]]), MUL)
                        tt(yc, yc, pv[:, 6, :], MUL)
                        tt(yc, yc, pv[:, 7, :], ADD)
                        yb = T128.get()
                        tt(yb.pat(g2), Vt.pat(g2), sm[:, 6:8].pat([[1, 2], [0, 64]]), MUL)
                        tt(yc, yc, yb, ADD)
                        psT = PS.get()
                        tr(psT[:, 0:128], yc, identF)
                        cp(brv(pr, c0, 128), psT[:, 0:128])
                        P.end_seg()
                if first_last_prompt_group:
                    for pr in range(8):
                        MBD = PMBD
                        Mpp = V(Mp[j].ap[:, pr, :], [("Mp", j, pr)])
                        for hh in range(2):
                            cp(MBD[hh * 64:(hh + 1) * 64, hh * 64:(hh + 1) * 64], Mpp[hh * 64:(hh + 1) * 64, :], eng=("act" if hh == 0 else "pool"))
                        ps = PS.get()
                        tr(ps[:, 0:128], MBD, identF)
                        So = T128.get()
                        for hh in range(2):
                            cp(So[hh * 64:(hh + 1) * 64, 0:64], ps[hh * 64:(hh + 1) * 64, hh * 64:(hh + 1) * 64], eng=("act" if hh == 0 else "dve"))
                        for hh in range(2):
                            dma(srp[j, 2 * pr + hh], So[hh * 64:(hh + 1) * 64, 0:64])

            first_last_prompt_group = (15 in ptiles)
            for l in range(depth):
                is_rw = (l % 2 == 1)
                j = l // 2
                gcol = TS8.get()[:, 0:8]
                dma(gcol, norm_g[l].rearrange("(kc p) -> p kc", p=128), slow=True)
                if is_rw:
                    if first_group:
                        mset(V(hT.ap[:, :, 0:1], [("hT", "c0")]), 0.0)
                    elif npt > 0:
                        cp(V(hT.ap[:, :, 0], [("hT", "c0")]), hcar[j], eng="pool")
                for li, t in enumerate(grp):
                    rstd = rms_tile(li)
                    for kc in range(NKC):
                        stt(V(hT.ap[:, kc, 1 + li * 128:1 + (li + 1) * 128], [hkey(li)]), xTv(kc, li * 128, 128), gcol[:, kc:kc + 1], rstd, MUL, MUL)
                    if is_rw and t == 15:
                        hl = TS8.get()[:, 0:8]
                        ts(hl, V(xT.ap[:, :, li * 128 + 127], [xkey(li)]), rstd[:, 127:128], MUL)
                        tt(hl, hl, gcol, MUL)
                        dma(ssp[j].rearrange("(kc p) -> p kc", p=128), hl, slow=True)
                    if is_rw and t == "s":
                        hl = T128.get()
                        hl3 = hl.pat([[16, 8], [1, 16]])
                        xv = xTt(li).pat([[NTM, 8], [8, 16]], off=7)
                        rv = rstd.pat([[0, 8], [8, 16]], off=7)
                        tt(hl3, xv, rv, MUL)
                        tt(hl3, hl3, gcol.pat([[1, 8], [0, 16]]), MUL)
                        ps = PS.get()
                        tr(ps[:, 0:128], hl, identF)
                        ho = T128.get()
                        cp(ho, ps[:, 0:128])
                        for kc in range(NKC):
                            dma(sss[j, :, kc * 128:(kc + 1) * 128], ho[kc * 16:(kc + 1) * 16, :])
                        stin = XIN.get()
                        mset(stin, 0.0)
                        dma(stin[0:16, :], sts[j])
                        for half in range(2):
                            ps = PS.get()
                            for q in range(4):
                                kc = half * 4 + q
                                tr(ps[:, q * 128:(q + 1) * 128], stin[:, kc * 128:(kc + 1) * 128], identF)
                            cp(hpS.pat([[128, 4], [8, 16]], off=half * 4 * 128), ps.pat([[128, 4], [1, 16]]))
                        hsv = V(hT.ap[:, :, 1 + li * 128:1 + (li + 1) * 128], [hkey(li)])
                        cp(hpS.pat([[128, 8], [8, 16], [1, 7]], off=1), hsv.pat([[HS, 8], [8, 16], [1, 7]]), eng="pool")
                if is_rw and npt > 0:
                    cp(hcar[j], V(hT.ap[:, :, npt * 128], [hkey(npt - 1)]), eng="pool")
                if 1 in phases:
                    if not is_rw:
                        hgrn_layer(l, j)
                    else:
                        rwkv_layer(l, j)
                rest_layer(l)

            fg = TS8.get()[:, 0:8]
            dma(fg, final_g[0].rearrange("(kc p) -> p kc", p=128), slow=True)
            for li, t in enumerate(grp):
                rstd = rms_tile(li)
                yT = T1K.get()
                for kc in range(NKC):
                    stt(yT[:, kc * 128:(kc + 1) * 128], xTv(kc, li * 128, 128), fg[:, kc:kc + 1], rstd, MUL, MUL)
                yo = T1K.get()
                for half in range(2):
                    ps = PS.get()
                    for q in range(4):
                        kc = half * 4 + q
                        tr(ps[:, q * 128:(q + 1) * 128], yT[:, kc * 128:(kc + 1) * 128], identF)
                    cp(yo[:, half * 512:(half + 1) * 512], ps)
                dma(ys if t == "s" else yp[t * 128:(t + 1) * 128, :], yo)

        for gi, grp in enumerate(groups):
            process_group(gi, grp)

        info = P.emit()
    return nc, info


def _shard_inputs(inp):
    cf, cb = make_consts()
    maps = []
    for c in range(8):
        b = slice(16 * c, 16 * c + 16)
        m = {
            "xp": np.ascontiguousarray(inp["x_prompt"][c]),
            "xs": np.ascontiguousarray(inp["x_sample"][b]).reshape(128, D),
            "mem": np.ascontiguousarray(inp["mem_prompt"][c]),
            "sth": np.ascontiguousarray(inp["state_hgrn"][:, b]),
            "strw": np.ascontiguousarray(inp["state_rwkv"][:, b]),
            "sts": np.ascontiguousarray(inp["state_shift"][:, b]),
            "ck": np.ascontiguousarray(inp["cache_mem_k"][:, b]).reshape(4, 16, MEM, 512),
            "cv": np.ascontiguousarray(inp["cache_mem_v"][:, b]).reshape(4, 16, MEM, 512),
            "rw_rk": np.ascontiguousarray(inp["rw_rk"]).reshape(2, D),
            "final_g": np.ascontiguousarray(inp["final_g"]).reshape(1, D),
            "cf": cf, "cb": cb,
        }
        for k in ("norm_g", "w_in", "w_out", "mem_norm_g", "w_mem_kv", "hg_lb", "hg_onorm_g", "rw_mu", "rw_w0", "rw_w1",
                  "rw_w2", "rw_a0", "rw_a1", "rw_a2", "rw_v0", "rw_v1", "rw_v2", "rw_kk", "rw_ka", "rw_lnx_g", "rw_lnx_b"):
            m[k] = np.ascontiguousarray(inp[k])
        maps.append(m)
    return maps


_NC_CACHE = {}


def kernel(**inputs):
    inp = {k: np.asarray(v, dtype=np.float32) for k, v in inputs.items()}
    if "nc" not in _NC_CACHE:
        _NC_CACHE["nc"] = build()[0]
    nc = _NC_CACHE["nc"]
    maps = _shard_inputs(inp)
    res = run_bass_kernel_spmd(nc, maps, core_ids=list(range(8)))
    R = res.results
    y_p = np.stack([R[c]["yp"] for c in range(8)], 0)
    y_s = np.concatenate([R[c]["ys"].reshape(16, 8, D) for c in range(8)], 0)
    sh_p = np.stack([R[c]["shp"] for c in range(8)], 1)
    sr_p = np.stack([R[c]["srp"] for c in range(8)], 1)
    ss_p = np.stack([R[c]["ssp"] for c in range(8)], 1)
    mk_p = np.stack([R[c]["mkp"].reshape(4, MEM, 4, 128) for c in range(8)], 1)
    mv_p = np.stack([R[c]["mvp"].reshape(4, MEM, 4, 128) for c in range(8)], 1)
    sh_s = np.concatenate([R[c]["shs"] for c in range(8)], 1)
    sr_s = np.concatenate([R[c]["srs"] for c in range(8)], 1)
    ss_s = np.concatenate([R[c]["sss"] for c in range(8)], 1)
    return (y_p, y_s, sh_p, sr_p, ss_p, mk_p, mv_p, sh_s, sr_s, ss_s)
```

```python
import contextlib
import numpy as np
import ml_dtypes
import concourse.bass as bass
import concourse.mybir as mybir
from concourse.bass_utils import run_bass_kernel_spmd

F32 = mybir.dt.float32
F32R = mybir.dt.float32r
BF16 = mybir.dt.bfloat16
AF = mybir.ActivationFunctionType
ALU = mybir.AluOpType
AX = mybir.AxisListType

ENGS = ("pe", "act", "dve", "pool", "sp")
DMA_K = 6
SEM_WRAP = 30000
SAME_ENG_RAW_WAITS = True
PIPE = True


class Prog:
    def __init__(self, nc):
        self.nc = nc
        self.ops = []
        self.last_w = {}
        self.readers = {}
        self.excl = set()
        self.seg = None
        self.nseg = 0
        self.merges = []

    def begin_seg(self):
        self.nseg += 1
        self.seg = self.nseg
        return self.seg

    def end_seg(self):
        self.seg = None

    def op(self, eng, fn, reads=(), writes=(), dma=False):
        i = len(self.ops)
        ex = [k for k in reads if k in self.excl]
        if ex:
            writes = list(writes) + ex
        deps = {}
        for k in reads:
            w = self.last_w.get(k)
            if w is not None:
                deps[w] = True
        for k in writes:
            w = self.last_w.get(k)
            if w is not None:
                deps.setdefault(w, False)
            for r in self.readers.get(k, ()):
                deps.setdefault(r, False)
        o = dict(i=i, eng=eng, fn=fn, dma=dma, deps=deps, seg=self.seg)
        self.ops.append(o)
        for k in reads:
            self.readers.setdefault(k, []).append(i)
        for k in writes:
            self.last_w[k] = i
            self.readers[k] = []
        return i

    def barrier(self):
        lastc = {e: None for e in ENGS}
        lastd = {e: [] for e in ENGS}
        for o in self.ops:
            if o["dma"]:
                lastd[o["eng"]].append(o["i"])
            else:
                lastc[o["eng"]] = o["i"]
        extra = [v for v in lastc.values() if v is not None]
        for e in ENGS:
            extra += lastd[e][-DMA_K:]
        for e in ENGS:
            i = self.op(e, lambda eng: eng.nop(), reads=(), writes=())
            for x in extra:
                self.ops[i]["deps"][x] = True

    def _order(self):
        n = len(self.ops)
        order = []
        segops = {}
        for o in self.ops:
            if o["seg"] is not None:
                segops.setdefault(o["seg"], []).append(o["i"])
        partner = {x: y for x, y in self.merges}
        done_seg = set()
        i = 0
        while i < n:
            o = self.ops[i]
            sg = o["seg"]
            if sg is not None and sg in partner and sg not in done_seg:
                X = segops[sg]; Y = segops[partner[sg]]
                assert X[-1] + 1 == Y[0] and X == list(range(X[0], X[-1] + 1)) and Y == list(range(Y[0], Y[-1] + 1))
                emitted = set()
                xi = yi = 0
                xset = set(X)
                turn = 0
                while xi < len(X) or yi < len(Y):
                    pick_y = False
                    if yi < len(Y) and (turn == 1 or xi >= len(X)):
                        yo = self.ops[Y[yi]]
                        if all((d not in xset) or (d in emitted) for d in yo["deps"]):
                            pick_y = True
                    if pick_y:
                        order.append(Y[yi]); yi += 1
                    else:
                        order.append(X[xi]); emitted.add(X[xi]); xi += 1
                    turn ^= 1
                done_seg.add(sg); done_seg.add(partner[sg])
                i = Y[-1] + 1
                continue
            order.append(i)
            i += 1
        assert sorted(order) == list(range(n))
        return order

    def emit(self, final_wait_eng="sp"):
        nc = self.nc
        order = self._order()
        ops = [self.ops[i] for i in order]
        newidx = {o["i"]: k for k, o in enumerate(ops)}
        for k, o in enumerate(ops):
            o["deps"] = {newidx[d]: r for d, r in o["deps"].items()}
            assert all(d < k for d in o["deps"])
            o["i"] = k
        per_eng = {e: [] for e in ENGS}
        ndma = {e: 0 for e in ENGS}
        lastslot = {}
        for o in ops:
            e = o["eng"]
            o["pos"] = len(per_eng[e])
            per_eng[e].append(o["i"])
            if o["dma"]:
                nn = ndma[e]
                o["dslot"] = nn % DMA_K
                o["dval"] = 16 * (nn // DMA_K + 1)
                o["dprev"] = lastslot.get((e, o["dslot"]))
                lastslot[(e, o["dslot"])] = o["i"]
                ndma[e] = nn + 1
        known = {e: {f: -1 for f in ENGS} for e in ENGS}
        known_d = {e: {} for e in ENGS}
        milestone = set()
        for o in ops:
            e = o["eng"]
            wl = []
            deps = dict(o["deps"])
            if o["dma"] and o["dprev"] is not None:
                deps[o["dprev"]] = True
            for j, raw in deps.items():
                p = ops[j]
                f = p["eng"]
                if p["dma"]:
                    key = (f, p["dslot"])
                    if known_d[e].get(key, 0) < p["dval"]:
                        known_d[e][key] = p["dval"]
                        wl.append(("d", f, p["dslot"], p["dval"]))
                else:
                    if f == e and not o["dma"] and e != "pool":
                        if e == "pe" or not raw or not SAME_ENG_RAW_WAITS:
                            continue
                    if known[e][f] < p["pos"]:
                        wl.append(("c", f, p["pos"], j))
            best = {}
            out = []
            for w in wl:
                if w[0] == "c":
                    if w[1] not in best or best[w[1]][2] < w[2]:
                        best[w[1]] = w
                else:
                    out.append(w)
            for f, w in best.items():
                known[e][f] = w[2]
                milestone.add(w[3])
                out.append(w)
            o["waits"] = out
        cnt = {e: 0 for e in ENGS}
        for o in ops:
            if o["dma"]:
                continue
            if o["i"] in milestone:
                cnt[o["eng"]] += 1
                o["ms"] = cnt[o["eng"]]
            else:
                o["ms"] = None
        nsem = {e: max(1, (cnt[e] + SEM_WRAP - 1) // SEM_WRAP) for e in ENGS}
        nwaits = 0
        with contextlib.ExitStack() as st:
            csem = {e: [st.enter_context(nc.semaphore(f"c_{e}_{k}")) for k in range(nsem[e])] for e in ENGS}
            dsem = {e: [st.enter_context(nc.semaphore(f"d_{e}_{k}")) for k in range(DMA_K)]
                    for e in ENGS if ndma[e] > 0}
            block = st.enter_context(nc.Block())

            def mk(ename):
                def body(eng):
                    nonlocal nwaits
                    for i in per_eng[ename]:
                        o = ops[i]
                        for w in o["waits"]:
                            nwaits += 1
                            if w[0] == "c":
                                ms = ops[w[3]]["ms"]
                                k, v = (ms - 1) // SEM_WRAP, (ms - 1) % SEM_WRAP + 1
                                eng.wait_ge(csem[w[1]][k], v)
                            else:
                                eng.wait_ge(dsem[w[1]][w[2]], w[3])
                        ins = o["fn"](eng)
                        if o["dma"]:
                            ins.then_inc(dsem[ename][o["dslot"]], 16)
                        elif o["ms"] is not None:
                            k = (o["ms"] - 1) // SEM_WRAP
                            ins.then_inc(csem[ename][k], 1)
                    if ename == final_wait_eng:
                        for f in dsem:
                            last = {}
                            for j in per_eng[f]:
                                if ops[j]["dma"]:
                                    last[ops[j]["dslot"]] = ops[j]["dval"]
                            for s, v in last.items():
                                eng.wait_ge(dsem[f][s], v)
                return body

            block.tensor(mk("pe"))
            block.scalar(mk("act"))
            block.vector(mk("dve"))
            block.gpsimd(mk("pool"))
            block.sync(mk("sp"))
        return dict(n_ops=len(ops), per_eng={e: len(v) for e, v in per_eng.items()},
                    milestones=cnt, nwaits=nwaits)


class V:
    def __init__(self, ap, keys, r=False):
        self.ap = ap
        self.keys = tuple(keys)
        self.r = r

    def __getitem__(self, idx):
        return V(self.ap[idx], self.keys, self.r)

    def pat(self, pat, off=0):
        a = self.ap
        return V(bass.AP(a.tensor, a.offset + off, [list(a.ap[0])] + [list(p) for p in pat]), self.keys, self.r)

    def k(self, *keys):
        return V(self.ap, keys, self.r)

    def nr(self):
        return V(self.ap, self.keys, False)

    def rr(self):
        return V(self.ap, self.keys, True)


D = 1024
NKC = 8
SEQ = 2048
NPT = 16
NS = 16
TS = 8
MEM = 256
XH = 4
INC = 5120
CDEC = -float(np.exp(-0.5))
NORM_EPS = 1e-6
LNX_EPS = 64e-5
SC_ATT = 128 ** -0.5

CO = {}
_o = 0
for _n, _w in [("ident", 128), ("ones", 128),
               ("mb1", 512), ("sl1", 128), ("tic1", 128), ("tsc1", 128), ("boc1", 128),
               ("mb16", 512), ("sl16", 128), ("bo16", 128), ("tic16", 128), ("tsc16", 128), ("boc16", 128),
               ("iu2", 128), ("bo2", 128), ("sel16", 16), ("sel16c", 16), ("negc", 2), ("cm0", 1), ("cm1", 1),
               ("hw", 1), ("ha", 1)]:
    CO[_n] = (_o, _w)
    _o += _w
NCF = _o
CB = {"ident": (0, 128)}
NCB = 128


def make_consts():
    c = np.zeros((128, NCF), np.float32)
    s = np.arange(128)[:, None]
    t = np.arange(128)[None, :]
    def put(n, a):
        o, w = CO[n]
        c[:, o:o + w] = a
    put("ident", (s == t))
    put("ones", 1.0)
    for cfg, blk in ((1, 128), (16, 8), (2, 64)):
        same = (s // blk) == (t // blk)
        iu = same & (s <= t)
        su = same & (s < t)
        sl = same & (s > t)
        if cfg == 2:
            put("iu2", iu); put("bo2", same)
            continue
        put(f"mb{cfg}", np.concatenate([su, su, iu, iu], 1))
        put(f"sl{cfg}", sl)
        if cfg == 16:
            put("bo16", same)
        put(f"tic{cfg}", CDEC * iu); put(f"tsc{cfg}", CDEC * su); put(f"boc{cfg}", CDEC * same)
    sel = (np.arange(128)[:, None] // 8) == np.arange(16)[None, :]
    put("sel16", sel); put("sel16c", CDEC * sel); put("negc", CDEC)
    put("cm0", (np.arange(128) < 64)[:, None]); put("cm1", (np.arange(128) >= 64)[:, None])
    put("hw", (np.arange(128) < 64)[:, None]); put("ha", (np.arange(128) >= 64)[:, None])
    cb = np.zeros((128, NCB), np.float32)
    cb[:, 0:128] = np.eye(128)
    return c, cb.astype(ml_dtypes.bfloat16)


def build(depth=4, groups=None, dbg=False, memkv=True, phases=(1, 2, 3, 4), mks=9, levo=None):
    if groups is None:
        groups = [[0, 1, 2, 3], [4, 5, 6, 7], [8, 9, 10, 11], [12, 13, 14, 15], ["s"]]
    GMAX = max(len(g) for g in groups)
    NTM = GMAX * 128
    nc = bass.Bass("TRN2", target_bir_lowering=False)
    din = lambda n, s, d=F32: nc.dram_tensor(n, list(s), d, kind="ExternalInput").ap()
    dout = lambda n, s: nc.dram_tensor(n, list(s), F32, kind="ExternalOutput").ap()
    xp = din("xp", [SEQ, D]); xs = din("xs", [128, D]); mem = din("mem", [MEM, D])
    sth = din("sth", [2, NS, 8, 128, 128]); strw = din("strw", [2, NS, 16, 64, 64]); sts = din("sts", [2, NS, D])
    ck = din("ck", [4, NS, MEM, 512]); cv = din("cv", [4, NS, MEM, 512])
    norm_g = din("norm_g", [4, D]); w_in = din("w_in", [4, D, INC]); w_out = din("w_out", [4, 1536, D])
    mem_norm_g = din("mem_norm_g", [4, D]); w_mem_kv = din("w_mem_kv", [4, D, 1024])
    hg_lb = din("hg_lb", [2, D]); hg_og = din("hg_onorm_g", [2, 128])
    rw_mu = din("rw_mu", [2, 5, D]); rw_w0 = din("rw_w0", [2, D]); rw_w1 = din("rw_w1", [2, D, 64]); rw_w2 = din("rw_w2", [2, 64, D])
    rw_a0 = din("rw_a0", [2, D]); rw_a1 = din("rw_a1", [2, D, 64]); rw_a2 = din("rw_a2", [2, 64, D])
    rw_v0 = din("rw_v0", [1, D]); rw_v1 = din("rw_v1", [1, D, 32]); rw_v2 = din("rw_v2", [1, 32, D])
    rw_kk = din("rw_kk", [2, D]); rw_ka = din("rw_ka", [2, D]); rw_rk = din("rw_rk", [2, D])
    rw_lg = din("rw_lnx_g", [2, D]); rw_lb = din("rw_lnx_b", [2, D]); final_g = din("final_g", [1, D])
    cf_d = din("cf", [128, NCF]); cb_d = din("cb", [128, NCB], BF16)
    yp = dout("yp", [SEQ, D]); ys = dout("ys", [128, D])
    shp = dout("shp", [2, 8, 128, 128]); srp = dout("srp", [2, 16, 64, 64]); ssp = dout("ssp", [2, D])
    mkp = dout("mkp", [4, MEM, 512]); mvp = dout("mvp", [4, MEM, 512])
    shs = dout("shs", [2, NS, 8, 128, 128]); srs = dout("srs", [2, NS, 16, 64, 64]); sss = dout("sss", [2, NS, D])

    with contextlib.ExitStack() as st:
        P = Prog(nc)

        def sb(name, shape, dt=F32, r=False):
            t = st.enter_context(nc.sbuf_tensor(name, list(shape), dt))
            return V(t[:], [name], r and dt == F32)

        class Rot:
            def __init__(self, name, n, shape, dt=F32, r=False):
                self.bufs = [sb(f"{name}{i}", shape, dt, r) for i in range(n)]
                self.i = 0

            def get(self):
                b = self.bufs[self.i % len(self.bufs)]
                self.i += 1
                return b

        class PSR:
            def __init__(self):
                self.bufs = []
                for i in range(8):
                    t = st.enter_context(nc.psum_tensor(f"ps{i}", [128, 512], F32))
                    self.bufs.append(V(t[:], [f"ps{i}"]))
                    P.excl.add(f"ps{i}")
                self.i = 0

            def get(self):
                b = self.bufs[self.i % 8]
                self.i += 1
                return b

        PS = PSR()

        def bfv(v):
            return V(v.ap.bitcast(BF16), v.keys)

        def rk(*vs):
            out = []
            for v in vs:
                if isinstance(v, V):
                    out += list(v.keys)
            return out

        def a_(x):
            return x.ap if isinstance(x, V) else x

        def o_(x):
            return x.ap.bitcast(F32R) if x.r else x.ap

        def mm(o, l, r, start=True, stop=True):
            la, ra = l.ap, r.ap
            if l.r and r.r:
                la, ra = la.bitcast(F32R), ra.bitcast(F32R)
            P.op("pe", lambda e: e.matmul(o.ap, lhsT=la, rhs=ra, start=start, stop=stop), reads=rk(l, r), writes=rk(o))

        def tr(o, i, idn):
            P.op("pe", lambda e: e.transpose(out=o.ap, in_=i.ap, identity=idn.ap), reads=rk(i, idn), writes=rk(o))

        def act(o, i, f, bias=None, scale=None, acc=None):
            kw = {}
            if bias is not None:
                kw["bias"] = a_(bias)
            if scale is not None:
                kw["scale"] = a_(scale)
            if acc is not None:
                kw["accum_out"] = acc.ap
            oa = o_(o)
            P.op("act", lambda e: e.activation(out=oa, in_=i.ap, func=f, **kw), reads=rk(i, bias, scale), writes=rk(o, acc))

        def tt(o, a, b, op, eng="dve"):
            if eng == "pool" and o.r:
                eng = "dve"
            oa = o_(o)
            P.op(eng, lambda e: e.tensor_tensor(out=oa, in0=a.ap, in1=b.ap, op=op), reads=rk(a, b), writes=rk(o))

        def ts(o, a, s1, op0, s2=None, op1=None, eng="dve"):
            oa = o_(o)

            def f(e):
                if op1 is None:
                    return e.tensor_scalar(out=oa, in0=a.ap, scalar1=a_(s1), scalar2=None, op0=op0)
                return e.tensor_scalar(out=oa, in0=a.ap, scalar1=a_(s1), scalar2=a_(s2), op0=op0, op1=op1)
            P.op(eng, f, reads=rk(a, s1, s2), writes=rk(o))

        def stt(o, a, s, b, op0, op1):
            oa = o_(o)
            P.op("dve", lambda e: e.scalar_tensor_tensor(out=oa, in0=a.ap, scalar=a_(s), in1=b.ap, op0=op0, op1=op1),
                 reads=rk(a, s, b), writes=rk(o))

        def red(o, a, op, negate=False):
            P.op("dve", lambda e: e.tensor_reduce(out=o.ap, in_=a.ap, axis=AX.X, op=op, negate=negate), reads=rk(a), writes=rk(o))

        def recip(o, a):
            oa = o_(o)
            P.op("dve", lambda e: e.reciprocal(out=oa, in_=a.ap), reads=rk(a), writes=rk(o))

        def cp(o, i, eng="act"):
            if eng == "pool" and o.r:
                eng = "dve"
            oa = o_(o)
            if eng == "act":
                P.op("act", lambda e: e.copy(out=oa, in_=i.ap), reads=rk(i), writes=rk(o))
            else:
                P.op(eng, lambda e: e.tensor_copy(out=oa, in_=i.ap), reads=rk(i), writes=rk(o))

        def mset(o, val, eng="pool"):
            P.op(eng, lambda e: e.memset(o.ap, val), writes=rk(o))

        def dma(o, i, eng="sp", slow=False):
            oa = o.ap if isinstance(o, V) else o
            ia = i.ap if isinstance(i, V) else i
            kw = dict(allow_slow_non_contiguous=True) if slow else {}
            P.op(eng, lambda e: e.dma_start(out=oa, in_=ia, **kw), reads=rk(i), writes=rk(o), dma=True)

        MUL, ADD, SUB, MAX = ALU.mult, ALU.add, ALU.subtract, ALU.max

        cF = sb("cF", [128, NCF], r=True); cB = sb("cB", [128, NCB], BF16)
        dma(cB, cb_d)
        C = lambda n: cF[:, CO[n][0]:CO[n][0] + CO[n][1]]
        identF = C("ident"); identB = cB[:, 0:128]; onesF = C("ones")

        Sp = [sb(f"Sp{j}", [128, 8, 128], r=True) for j in range(2)]
        Mp = [sb(f"Mp{j}", [128, 8, 64], r=True) for j in range(2)]
        hcar = [sb(f"hcar{j}", [128, 8], BF16) for j in range(2)]
        for j in range(2):
            mset(Sp[j], 0.0); mset(Mp[j], 0.0)

        ARENA = 62 * 256
        arena_t = st.enter_context(nc.sbuf_tensor("arena", [128, ARENA], F32))
        arena = arena_t[:]
        aoff = {"P": 0, "S": 0}

        def carve(lay, name, shape, dt=F32):
            n = int(np.prod(shape[1:]))
            n32 = n if dt == F32 else (n + 1) // 2
            o = aoff[lay]
            aoff[lay] = o + n32
            assert aoff[lay] <= ARENA, (lay, name, aoff[lay])
            a = arena[:, o:o + n32]
            if dt != F32:
                a = a.bitcast(dt)
            pat = []
            stride = 1
            for d in reversed(shape[1:]):
                pat.insert(0, [stride, d])
                stride *= d
            return V(bass.AP(a.tensor, a.offset, [list(a.ap[0])] + pat), [name])

        LAY = {}
        for lay, ntm in (("P", GMAX * 128), ("S", 128)):
            Ld = dict(NTM=ntm)
            Ld["l1T"] = carve(lay, lay + "l1T", [128, 2, ntm])
            Ld["l1vT"] = carve(lay, lay + "l1vT", [128, ntm])
            if lay == "P":
                Ld["QZ"] = carve(lay, "QZ", [128, 256])
            Ld["xT"] = carve(lay, lay + "xT", [128, NKC, ntm])
            Ld["hT"] = carve(lay, lay + "hT", [128, NKC, 1 + ntm + 1], BF16)
            Ld["brT"] = carve(lay, lay + "brT", [128, 12, ntm], BF16)
            Ld["qx"] = [carve(lay, lay + f"qx{i}", [128, ntm], BF16) for i in range(2)]
            if lay == "P":
                Ld["mkT"] = carve(lay, "mkT", [128, 4, XH, MEM], BF16)
                Ld["mvb"] = carve(lay, "mvb", [128, 4, 2, 512], BF16)
            else:
                Ld["hpS"] = carve(lay, "hpS", [128, NKC, 128], BF16)
                Ld["Z1"] = carve(lay, "Z1", [128, 2048])
                Ld["Z1b"] = carve(lay, "Z1b", [128, 2048], BF16)
                Ld["Z2b"] = carve(lay, "Z2b", [128, 2048], BF16)
                Ld["M0b"] = carve(lay, "M0b", [128, 16, 64], BF16)
                o_save = aoff[lay]
                Ld["S0q"] = [carve(lay, f"S0q{i}", [128, 512]).k("salias") for i in range(4)]
                aoff[lay] = o_save
                Ld["M0all"] = carve(lay, "M0all", [128, 16, 64]).k("salias")
                Ld["SBD"] = [carve(lay, f"SBD{i}", [128, 512]).k("salias") for i in range(2)]
                Ld["MBD"] = [carve(lay, f"MBD{i}", [128, 512]) for i in range(2)]
                Ld["KVB"] = [carve(lay, f"KVB{i}", [128, 2048], BF16) for i in range(3)]
                Ld["QXM"] = carve(lay, "QXM", [128, 2048], BF16)
            LAY[lay] = Ld
        print("arena use (KiB):", {k: v / 256 for k, v in aoff.items()})

        T128 = Rot("t128_", 16, [128, 128], r=True)
        T256 = Rot("t256_", 3, [128, 256], r=True)
        T512 = Rot("t512_", 4, [128, 512], r=True)
        T1K = Rot("t1k_", 2, [128, 1024], r=True)
        TB256 = Rot("tb256_", 2, [128, 256], BF16)
        TB512 = Rot("tb512_", 2, [128, 512], BF16)
        TS8 = Rot("ts8_", 8, [128, 24])
        TS8s = TS8
        OG = sb("OG", [128, 128]); LBC = sb("LBC", [128, 24])
        BIGV = [sb("bigv0", [128, D], r=True), sb("bigv1", [128, D], r=True)]
        MUC = sb("MUC", [128, 80])
        L1a = sb("L1a", [128, NKC, 128], BF16); L1b = sb("L1b", [128, NKC, 128], BF16)
        L1va = sb("L1va", [128, NKC, 128], BF16); L1vb = sb("L1vb", [128, NKC, 128], BF16)
        PVB = sb("PVB", [128, 8, 128])
        ZP = [sb(f"zp{i}", [128, 256], BF16) for i in range(2)]
        ZBK = sb("zbk", [128, 512], BF16)
        ZBKH = [sb(f"zbkh{i}", [128, 512], BF16) for i in range(2)]
        PMBD = sb("PMBD", [128, 128])
        for zb in ZP + [ZBK, PMBD] + ZBKH:
            mset(zb, 0.0)
        WM = Rot("wm_", 2, [128, NKC, 384], BF16)
        WM2 = Rot("wm2_", 2, [128, NKC, 384], BF16)
        WR = Rot("wr_", 2, [128, NKC, 512], BF16)
        WO = Rot("wo_", 1, [128, 12, 128], BF16)
        XIN = Rot("xin_", 1, [128, D])
        _xa = XIN.bufs[0].ap
        PVS = [PVB, V(bass.AP(_xa.tensor, _xa.offset, [list(_xa.ap[0]), [128, 8], [1, 128]]), XIN.bufs[0].keys)]
        _xb = XIN.bufs[0].ap.bitcast(BF16)
        WO.bufs.append(V(bass.AP(_xb.tensor, _xb.offset, [list(_xb.ap[0]), [128, 12], [1, 128]]), XIN.bufs[0].keys))
        NRV = Rot("nrv_", 2, [128, 128], BF16)
        BT128 = Rot("b128_", 8, [128, 128], BF16)
        BT256 = Rot("b256_", 3, [128, 256], BF16)
        BT512 = Rot("b512_", 5, [128, 512], BF16)
        BT1K = Rot("b1k_", 2, [128, 1024], BF16)
        MK = [sb(f"mk{i}", [128, 512], BF16) for i in range(4)]
        MpB = [sb(f"MpB{j}", [128, 8, 64], BF16) for j in range(2)]
        for j in range(2):
            mset(MpB[j], 0.0)
        vfd = nc.dram_tensor("vfd", [17 * 128, D], BF16, kind="Internal").ap()

        for c0 in range(0, NCF, 1024):
            n = min(1024, NCF - c0)
            xin = XIN.get()
            dma(xin[:, 0:n], cf_d[:, c0:c0 + n])
            cp(cF[:, c0:c0 + n], xin[:, 0:n])

        def wview(w2d, c0, ncols):
            return w2d[:, c0:c0 + ncols].rearrange("(kc p) c -> p kc c", p=128)

        def xkey(li):
            return ("xT", li)

        def hkey(li):
            return ("hT", li)

        def mem_kv():
            LP = LAY["P"]
            mkT, mvb = LP["mkT"], LP["mvb"]
            memnT = V(LP["xT"].ap[:, :, 0:MEM], [xkey(0), xkey(1)])
            hmT = V(LP["hT"].ap[:, :, 1:1 + MEM], [hkey(0), hkey(1)])
            for mc in range(2):
                xin = XIN.get()
                dma(xin, mem[mc * 128:(mc + 1) * 128, :])
                ss = TS8.get()
                junk = T1K.get()
                act(junk, xin, AF.Square, acc=ss[:, 0:1])
                ts(ss[:, 1:2], ss[:, 0:1], 1.0 / D, MUL, NORM_EPS, ADD)
                act(ss[:, 2:3], ss[:, 1:2], AF.Ln)
                act(ss[:, 3:4], ss[:, 2:3], AF.Exp, scale=-0.5)
                xn = T1K.get()
                ts(xn, xin, ss[:, 3:4], MUL)
                for half in range(2):
                    ps = PS.get()
                    for q in range(4):
                        kc = half * 4 + q
                        tr(ps[:, q * 128:(q + 1) * 128], xn[:, kc * 128:(kc + 1) * 128], identF)
                    cp(memnT[:, half * 4:(half + 1) * 4, mc * 128:(mc + 1) * 128], ps.pat([[128, 4], [1, 128]]))
            for l in range(depth):
                if mks < 1:
                    break
                gcol = TS8.get()[:, 0:8]
                dma(gcol, mem_norm_g[l].rearrange("(kc p) -> p kc", p=128), slow=True)
                for kc in range(NKC):
                    ts(hmT[:, kc, :], memnT[:, kc, :], gcol[:, kc:kc + 1], MUL)
                if mks < 2:
                    continue
                for cb in range(2):
                    wb = WR.get()
                    dma(wb, wview(w_mem_kv[l], cb * 512, 512), eng="pool")
                    if mks < 3:
                        continue
                    if cb == 0:
                        for hx in range(XH):
                            ps = PS.get()
                            for kc in range(NKC):
                                mm(ps[:, 0:256], wb[:, kc, hx * 128:(hx + 1) * 128], hmT[:, kc, :], start=kc == 0, stop=kc == NKC - 1)
                            cp(mkT[:, l, hx, :], ps[:, 0:256])
                    if mks < 4:
                        continue
                    for mc in range(2):
                        ps = PS.get()
                        for kc in range(NKC):
                            mm(ps, hmT[:, kc, mc * 128:(mc + 1) * 128], wb[:, kc, :], start=kc == 0, stop=kc == NKC - 1)
                        stg = T512.get()
                        cp(stg, ps)
                        if mks >= 6:
                            dma((mkp if cb == 0 else mvp)[l, mc * 128:(mc + 1) * 128, :], stg)
                        if cb == 1 and mks >= 7:
                            cp(mvb[:, l, mc, :], ps, eng="dve")

        if memkv:
            mem_kv()
        mset(LAY["P"]["QZ"], 0.0)

        def process_group(gi, grp):
            ptiles = [t for t in grp if t != "s"]
            has_s = "s" in grp
            assert not (has_s and ptiles)
            L = LAY["S" if has_s else "P"]
            NTM = L["NTM"]; HS = NTM + 2
            xT, hT, brT, l1T, l1vT = L["xT"], L["hT"], L["brT"], L["l1T"], L["l1vT"]
            if has_s:
                hpS = L["hpS"]; Z1 = L["Z1"]; Z1b = L["Z1b"]; Z2b = L["Z2b"]; QXM = L["QXM"]
                P.barrier()
                for zb in [Z1, Z1b, Z2b, QXM] + L["SBD"] + L["MBD"]:
                    mset(zb, 0.0)
                QM = Z1
            else:
                mkT, mvb, QZ = L["mkT"], L["mvb"], L["QZ"]
            qxi = [0]
            npt = len(ptiles)
            first_group = bool(ptiles) and ptiles[0] == 0
            pchunks = []
            c = 0
            while c < npt * 128:
                n = min(512, npt * 128 - c)
                pchunks.append((c, n))
                c += n
            scol = npt * 128
            chunks = pchunks + ([(scol, 128)] if has_s else [])

            def tkeys(c0, n, fn):
                return tuple(fn(li) for li in range(c0 // 128, (c0 + n + 127) // 128))

            def xTv(kc, c0, n):
                return V(xT.ap[:, kc, c0:c0 + n], tkeys(c0, n, xkey))

            def xTt(li):
                return V(xT.ap[:, :, li * 128:(li + 1) * 128], [xkey(li)])

            def hTv(kc, c0, n):
                return V(hT.ap[:, kc, 1 + c0:1 + c0 + n], tkeys(c0, n, hkey))

            def hprevv(kc, c0, n):
                if has_s and c0 == scol:
                    return hpS[:, kc, :]
                ks = tkeys(c0, n, hkey) + ((hkey(c0 // 128 - 1),) if c0 > 0 else (("hT", "c0"),))
                return V(hT.ap[:, kc, c0:c0 + n], ks)

            def brv(kc, c0, n):
                return V(brT.ap[:, kc, c0:c0 + n], tkeys(c0, n, lambda li: ("br", kc, li)))

            for li, t in enumerate(grp):
                xin = XIN.get()
                dma(xin, xs if t == "s" else xp[t * 128:(t + 1) * 128, :])
                for half in range(2):
                    ps = PS.get()
                    for q in range(4):
                        kc = half * 4 + q
                        tr(ps[:, q * 128:(q + 1) * 128], xin[:, kc * 128:(kc + 1) * 128], identF)
                    cp(V(xT.ap[:, half * 4:(half + 1) * 4, li * 128:(li + 1) * 128], [xkey(li)]), ps.pat([[128, 4], [1, 128]]))

            def rms_tile(li):
                sq = T1K.get()
                act(sq.pat([[128, 8], [1, 128]]), xTt(li), AF.Square)
                ps = PS.get()
                for kc in range(NKC):
                    mm(ps[:, 0:128], onesF, sq[:, kc * 128:(kc + 1) * 128], start=kc == 0, stop=kc == NKC - 1)
                rs = T128.get()
                act(rs, ps[:, 0:128], AF.Ln, bias=NORM_EPS, scale=1.0 / D)
                rstd = T128.get()
                act(rstd, rs, AF.Exp, scale=-0.5)
                return rstd

            def hgrn_layer(l, j):
                og = OG
                dma(og, hg_og[j:j + 1, :].partition_broadcast(128))
                lbT = LBC[:, 0:8]; omT = LBC[:, 8:16]; nomT = LBC[:, 16:24]
                nom_bc = BIGV[0]
                if j == 0:
                    mset(omT, 1.0); mset(nomT, -1.0); mset(nom_bc, -1.0)
                else:
                    a0 = TS8.get()[:, 0:8]; a1 = TS8.get()[:, 0:8]
                    dma(a0, hg_lb[0].rearrange("(h p) -> p h", p=128), slow=True)
                    dma(a1, hg_lb[1].rearrange("(h p) -> p h", p=128), slow=True)
                    tt(a1, a1, a0, SUB)
                    act(lbT, a1, AF.Sigmoid)
                    ts(omT, lbT, -1.0, MUL, 1.0, ADD)
                    ts(nomT, lbT, 1.0, SUB)
                    b0 = XIN.get()
                    dma(b0, hg_lb[1:2, :].partition_broadcast(128))
                    cp(nom_bc, b0)
                    b0 = XIN.get()
                    dma(b0, hg_lb[0:1, :].partition_broadcast(128))
                    tt(nom_bc, nom_bc, b0, SUB)
                    act(nom_bc, nom_bc, AF.Sigmoid)
                    ts(nom_bc, nom_bc, 1.0, SUB)
                for h in range(8):
                    wblk = WM.get()
                    for q in range(3):
                        dma(wblk[:, :, q * 128:(q + 1) * 128], wview(w_in[l], q * 1024 + h * 128, 128), eng="pool")
                    Sph = V(Sp[j].ap[:, h, :], [("Sp", j, h)])
                    if has_s:
                        S0q = L["S0q"]
                        for q in range(4):
                            dma(S0q[q].pat([[128, 4], [1, 128]]), sth[j, 4 * q:4 * q + 4, h].rearrange("s k v -> k s v"))
                    batched = (not has_s) and npt * 128 <= 512
                    if batched:
                        ng = npt * 128
                        q4s = T512.get(); f4s = T512.get(); qs4 = T512.get(); kT4 = T512.get()
                        psq4 = PS.get()
                        for kc in range(NKC):
                            mm(psq4[:, 0:ng], wblk[:, kc, 0:128], hTv(kc, 0, ng), start=kc == 0, stop=kc == NKC - 1)
                        psf4 = PS.get()
                        for kc in range(NKC):
                            mm(psf4[:, 0:ng], wblk[:, kc, 128:256], hTv(kc, 0, ng), start=kc == 0, stop=kc == NKC - 1)
                        act(q4s[:, 0:ng], psq4[:, 0:ng], AF.Sigmoid)
                        tt(qs4[:, 0:ng], q4s[:, 0:ng], psq4[:, 0:ng], MUL)
                        act(f4s[:, 0:ng], psf4[:, 0:ng], AF.Sigmoid)
                        ts(kT4[:, 0:ng], f4s[:, 0:ng], nomT[:, h:h + 1], MUL, omT[:, h:h + 1], ADD)
                    hsegB = None
                    for li, t in enumerate(grp):
                        samp = (t == "s")
                        c0 = li * 128
                        hsegA = P.begin_seg()
                        if hsegB is not None and PIPE:
                            P.merges.append((hsegB, hsegA))
                        IU = C("mb16")[:, 256:384] if samp else C("iu2")
                        BO = C("bo16") if samp else C("bo2")
                        if not batched:
                            psq = PS.get()
                            for kc in range(NKC):
                                mm(psq[:, 0:128], wblk[:, kc, 0:128], hTv(kc, c0, 128), start=kc == 0, stop=kc == NKC - 1)
                            for kc in range(NKC):
                                mm(psq[:, 128:256], wblk[:, kc, 128:256], hTv(kc, c0, 128), start=kc == 0, stop=kc == NKC - 1)
                        psfv = PS.get()
                        for kc in range(NKC):
                            mm(psfv[:, 0:256], hTv(kc, c0, 128), wblk[:, kc, 128:384], start=kc == 0, stop=kc == NKC - 1)
                        if batched:
                            qs = qs4[:, c0:c0 + 128]; kT = kT4[:, c0:c0 + 128]
                        else:
                            qsg = T128.get(); act(qsg, psq[:, 0:128], AF.Sigmoid)
                            qs = T128.get(); tt(qs, qsg, psq[:, 0:128], MUL)
                            sgT = T128.get(); act(sgT, psq[:, 128:256], AF.Sigmoid)
                            kT = T128.get(); ts(kT, sgT, nomT[:, h:h + 1], MUL, omT[:, h:h + 1], ADD)
                        sg = T128.get(); act(sg, psfv[:, 0:128], AF.Sigmoid)
                        kt = T128.get(); stt(kt, sg, -1.0, nom_bc[:, h * 128:(h + 1) * 128], ADD, MUL)
                        g = T128.get(); act(g, kt, AF.Ln, bias=1.0, scale=-1.0)
                        vt = T128.get(); cp(vt, psfv[:, 128:256])
                        ps3 = PS.get()
                        mm(ps3[:, 0:128], IU, g)
                        mm(ps3[:, 128:256], g, IU)
                        mm(ps3[:, 256:384], BO, g)
                        eb = T128.get(); act(eb, ps3[:, 128:256], AF.Exp)
                        enb = T128.get(); act(enb, ps3[:, 128:256], AF.Exp, scale=-1.0)
                        qtT = T128.get(); tt(qtT, qs, eb, MUL)
                        ktT = T128.get(); tt(ktT, kT, enb, MUL)
                        bsb = T128.get(); cp(bsb, ps3[:, 0:128])
                        dd = T128.get(); tt(dd, ps3[:, 256:384], bsb, SUB)
                        ed = T128.get(); act(ed, dd, AF.Exp)
                        psA = PS.get()
                        mm(psA[:, 0:128], ktT, qtT)
                        att = T128.get(); tt(att, psA[:, 0:128], IU, MUL)
                        P.end_seg()
                        hsegB = P.begin_seg()
                        psO = PS.get()
                        mm(psO[:, 0:128], att, vt, start=True, stop=False)
                        if not samp:
                            tt(QZ.pat([[192, 2], [1, 64]]), qs.pat([[64, 2], [1, 64]]), eb.pat([[64, 2], [1, 64]]), MUL)
                            kh0 = T128.get(); stt(kh0, kt, C("cm0"), ed, MUL, MUL)
                            kh1 = T128.get(); stt(kh1, kt, C("cm1"), ed, MUL, MUL)
                            psS = PS.get()
                            mm(psS[:, 0:128], kh0, vt)
                            mm(psS[:, 128:256], kh1, vt)
                            mm(psO[:, 0:128], QZ[:, 0:128], Sph, start=False, stop=False)
                            S1 = T128.get(); stt(S1, Sph, eb[:, 63:64], psS[:, 0:128], MUL, ADD)
                            mm(psO[:, 0:128], QZ[:, 128:256], S1, start=False, stop=True)
                            stt(Sph, S1, eb[:, 127:128], psS[:, 128:256], MUL, ADD)
                        else:
                            tt(QM.pat([[136, 16], [1, 8]]), qs.pat([[8, 16], [1, 8]]), eb.pat([[8, 16], [1, 8]]), MUL)
                            for s in range(NS):
                                mm(psO[:, 0:128], QM[:, s * 128:(s + 1) * 128], S0q[s // 4][:, (s % 4) * 128:(s % 4 + 1) * 128], start=False, stop=(s == NS - 1))
                            kh = T128.get(); tt(kh, kt, ed, MUL)
                            for q in range(4):
                                vm = T512.get()
                                tt(vm.pat([[128, 4], [1, 128]]), vt.pat([[0, 4], [1, 128]]), C("sel16").pat([[1, 4], [0, 128]], off=4 * q), MUL)
                                psW = PS.get()
                                mm(psW, kh, vm)
                                tmp = T512.get()
                                tt(tmp.pat([[128, 4], [1, 128]]), S0q[q].pat([[128, 4], [1, 128]]),
                                   eb.pat([[8, 4], [0, 128]], off=7 + 32 * q), MUL)
                                Sn = T512.get()
                                tt(Sn, tmp, psW, ADD)
                                dma(shs[j, 4 * q:4 * q + 4, h].rearrange("s k v -> k s v"), Sn.pat([[128, 4], [1, 128]]))
                        ss = TS8.get()
                        junk = T128.get()
                        act(junk, psO[:, 0:128], AF.Square, acc=ss[:, 0:1])
                        ts(ss[:, 1:2], ss[:, 0:1], 1.0 / 128, MUL, NORM_EPS, ADD)
                        act(ss[:, 2:3], ss[:, 1:2], AF.Ln)
                        act(ss[:, 3:4], ss[:, 2:3], AF.Exp, scale=-0.5)
                        on = T128.get(); stt(on, psO[:, 0:128], ss[:, 3:4], og, MUL, MUL)
                        psT = PS.get()
                        tr(psT[:, 0:128], on, identF)
                        cp(brv(h, c0, 128), psT[:, 0:128])
                        P.end_seg()
                if first_last_prompt_group:
                    for h in range(8):
                        dma(shp[j, h], V(Sp[j].ap[:, h, :], [("Sp", j, h)]))

            def rest_layer(l):
                if 2 not in phases:
                    return
                wq = WR.get()
                dma(wq, wview(w_in[l], 3072, 512), eng="pool")
                for hx in range(XH):
                    qxT = L['qx'][qxi[0] % 2]; qxi[0] += 1
                    for (c0, n) in chunks:
                        ps = PS.get()
                        for kc in range(NKC):
                            mm(ps[:, 0:n], wq[:, kc, hx * 128:(hx + 1) * 128], hTv(kc, c0, n), start=kc == 0, stop=kc == NKC - 1)
                        cp(qxT[:, c0:c0 + n], ps[:, 0:n])
                    if has_s:
                        KVB = L["KVB"]
                        kvi = [0]
                        def kvget():
                            b = KVB[kvi[0] % 3]; kvi[0] += 1
                            return b
                        mkTs = []; vbs = []
                        for hf in range(2):
                            kb = kvget()
                            dma(kb.pat([[256, 8], [128, 2], [1, 128]]), ck[l][8 * hf:8 * hf + 8, :, hx * 128:(hx + 1) * 128].rearrange("s (mc p) d -> p s mc d", p=128), eng="pool")
                            mk_ = kvget()
                            for b4 in range(2):
                                psb = bfv(PS.get())
                                for q in range(8):
                                    idx = b4 * 8 + q
                                    tr(psb[:, q * 128:(q + 1) * 128], kb[:, idx * 128:(idx + 1) * 128], identB)
                                cp(mk_[:, b4 * 1024:(b4 + 1) * 1024], psb)
                            mkTs.append(mk_)
                    for li, t in enumerate(grp):
                        samp = (t == "s")
                        c0 = li * 128
                        ps = PS.get()
                        if not samp:
                            mm(ps[:, 0:256], qxT[:, c0:c0 + 128], mkT[:, l, hx, :])
                        else:
                            cp(QXM.pat([[136, 16], [1, 8]]), qxT[:, c0:c0 + 128].pat([[8, 16], [1, 8]]), eng="dve")
                            for s in range(NS):
                                mm(ps[:, 0:256], QXM[:, s * 128:(s + 1) * 128], mkTs[s // 8][:, (s % 8) * 256:(s % 8 + 1) * 256], start=(s == 0), stop=(s == NS - 1))
                        ss = TS8.get()
                        red(ss[:, 0:1], ps[:, 0:256], MAX, negate=True)
                        ts(ss[:, 1:2], ss[:, 0:1], SC_ATT, MUL)
                        pe_ = T512.get()
                        act(pe_[:, 0:256], ps[:, 0:256], AF.Exp, bias=ss[:, 1:2], scale=SC_ATT, acc=ss[:, 2:3])
                        recip(ss[:, 3:4], ss[:, 2:3])
                        pn = TB256.get()
                        ts(pn, pe_[:, 0:256], ss[:, 3:4], MUL)
                        psb = bfv(PS.get())
                        tr(psb[:, 0:128], pn[:, 0:128], identB)
                        tr(psb[:, 128:256], pn[:, 128:256], identB)
                        pT = TB256.get()
                        cp(pT, psb[:, 0:256])
                        psO = PS.get()
                        if not samp:
                            for mc in range(2):
                                mm(psO[:, 0:128], mvb[:, l, mc, hx * 128:(hx + 1) * 128], pT[:, mc * 128:(mc + 1) * 128], start=mc == 0, stop=mc == 1)
                        else:
                            for hf in range(2):
                                vb_ = kvget()
                                dma(vb_.pat([[256, 8], [128, 2], [1, 128]]), cv[l][8 * hf:8 * hf + 8, :, hx * 128:(hx + 1) * 128].rearrange("s (mc p) d -> p s mc d", p=128), eng="pool")
                                vbs.append(vb_)
                            for s in range(NS):
                                for mc in range(2):
                                    vb = vbs[s // 8]
                                    mm(psO[:, s * 8:(s + 1) * 8], vb[:, ((s % 8) * 2 + mc) * 128:((s % 8) * 2 + mc + 1) * 128],
                                       pT[:, mc * 128 + s * 8:mc * 128 + (s + 1) * 8], start=mc == 0, stop=mc == 1)
                        cp(brv(8 + hx, c0, 128), psO[:, 0:128])
                if 3 not in phases:
                    return
                for gb in range(3):
                    wg = WR.get()
                    dma(wg, wview(w_in[l], 3584 + gb * 512, 512), eng="pool")
                    for sub in range(4):
                        bi = gb * 4 + sub
                        for (c0, n) in chunks:
                            ps = PS.get()
                            for kc in range(NKC):
                                mm(ps[:, 0:n], wg[:, kc, sub * 128:(sub + 1) * 128], hTv(kc, c0, n), start=kc == 0, stop=kc == NKC - 1)
                            sgt = TB512.get()
                            act(sgt[:, 0:n], ps[:, 0:n], AF.Sigmoid)
                            tt(sgt[:, 0:n], sgt[:, 0:n], ps[:, 0:n], MUL)
                            tt(brv(bi, c0, n), brv(bi, c0, n), sgt[:, 0:n], MUL)
                if 4 not in phases:
                    return
                for ob in range(NKC):
                    wo = WO.get()
                    dma(wo, w_out[l][:, ob * 128:(ob + 1) * 128].rearrange("(kc p) c -> p kc c", p=128), eng="pool")
                    for (c0, n) in chunks:
                        ps = PS.get()
                        for kc in range(12):
                            mm(ps[:, 0:n], wo[:, kc, :], brv(kc, c0, n), start=kc == 0, stop=kc == 11)
                        tt(xTv(ob, c0, n), xTv(ob, c0, n), ps[:, 0:n], ADD)

            def rwkv_layer(l, j):
                m = j - 1
                cfgn = "16" if has_s else "1"
                MB = C("mb" + cfgn); SL = C("sl" + cfgn)
                TIC = C("tic" + cfgn); TSC = C("tsc" + cfgn); BOC = C("boc" + cfgn)
                LEV = 3 if has_s else 7
                if levo is not None:
                    LEV = levo
                muT = MUC[:, 0:40].pat([[8, 5], [1, 8]]); omuT = MUC[:, 40:80].pat([[8, 5], [1, 8]])
                dma(muT, rw_mu[j].rearrange("k (kc p) -> p k kc", p=128), slow=True)
                ts(omuT, muT, -1.0, MUL, 1.0, ADD)
                raw = WR.get()
                dma(raw[:, :, 0:64], rw_w1[j].rearrange("(kc p) c -> p kc c", p=128), eng="pool")
                dma(raw[:, :, 64:128], rw_a1[j].rearrange("(kc p) c -> p kc c", p=128), eng="pool")
                if j >= 1:
                    mset(raw[:, :, 160:256], 0.0)
                    dma(raw[:, :, 128:160], rw_v1[m].rearrange("(kc p) c -> p kc c", p=128), eng="pool")
                for (c_lo, kind) in ((0, 1), (64, 4)):
                    src = raw[:, :, c_lo:c_lo + 64]
                    tt(L1a[:, :, c_lo:c_lo + 64], src, omuT[:, kind, :].pat([[1, 8], [0, 64]]), MUL)
                    tt(L1b[:, :, c_lo:c_lo + 64], src, muT[:, kind, :].pat([[1, 8], [0, 64]]), MUL)
                if j >= 1:
                    src = raw[:, :, 128:256]
                    tt(L1va, src, omuT[:, 3, :].pat([[1, 8], [0, 128]]), MUL)
                    tt(L1vb, src, muT[:, 3, :].pat([[1, 8], [0, 128]]), MUL)
                W2 = BIGV[0]; V2 = BIGV[1]
                xin = XIN.get()
                dma(xin[0:64, :], rw_w2[j]); dma(xin[64:128, :], rw_a2[j])
                cp(W2, xin)
                if j >= 1:
                    xin = XIN.get()
                    mset(xin, 0.0)
                    dma(xin[0:32, :], rw_v2[m])
                    cp(V2, xin, eng="dve")
                for (c0, n) in chunks:
                    ps = PS.get()
                    for kc in range(NKC):
                        mm(ps[:, 0:n], L1a[:, kc, :], hTv(kc, c0, n), start=kc == 0, stop=False)
                    for kc in range(NKC):
                        mm(ps[:, 0:n], L1b[:, kc, :], hprevv(kc, c0, n), start=False, stop=kc == NKC - 1)
                    th = T512.get()
                    act(th[:, 0:n], ps[:, 0:n], AF.Tanh)
                    ts(V(l1T.ap[:, 0, c0:c0 + n], tkeys(c0, n, lambda li: ("l1", li))), th[:, 0:n], C("hw"), MUL)
                    ts(V(l1T.ap[:, 1, c0:c0 + n], tkeys(c0, n, lambda li: ("l1", li))), ps[:, 0:n], C("ha"), MUL)
                    if j >= 1:
                        ps = PS.get()
                        for kc in range(NKC):
                            mm(ps[:, 0:n], L1va[:, kc, :], hTv(kc, c0, n), start=kc == 0, stop=False)
                        for kc in range(NKC):
                            mm(ps[:, 0:n], L1vb[:, kc, :], hprevv(kc, c0, n), start=False, stop=kc == NKC - 1)
                        cp(V(l1vT.ap[:, c0:c0 + n], tkeys(c0, n, lambda li: ("l1v", li))), ps[:, 0:n])

                for pr in range(8):
                    cc = pr * 128
                    wa = WM.get(); wb = WM2.get()
                    for q in range(3):
                        dma(wa[:, :, q * 128:(q + 1) * 128], wview(w_in[l], q * 1024 + cc, 128), eng="pool")
                    for q, kind in enumerate((0, 2, 3)):
                        tt(wb[:, :, q * 128:(q + 1) * 128], wa[:, :, q * 128:(q + 1) * 128], muT[:, kind, :].pat([[1, 8], [0, 128]]), MUL)
                    tt(wa, wa, wb, SUB)
                    pv = PVS[pr % 2]
                    vecs = [rw_w0[j:j + 1], rw_a0[j:j + 1], (rw_v0[m:m + 1] if j >= 1 else rw_a0[j:j + 1]), rw_kk[j:j + 1], rw_ka[j:j + 1], rw_rk[j:j + 1], rw_lg[j:j + 1], rw_lb[j:j + 1]]
                    for k8, vv in enumerate(vecs):
                        dma(pv[:, k8, :], vv[:, cc:cc + 128].partition_broadcast(128))
                    Mpp = V(Mp[j].ap[:, pr, :], [("Mp", j, pr)], True)
                    MppB = V(MpB[j].ap[:, pr, :], [("MpB", j, pr)])
                    if has_s:
                        M0b = L["M0b"]
                        M0all = L["M0all"]
                        for q in range(4):
                            SBD = L["SBD"][q % 2]
                            for hh in range(2):
                                dma(SBD[hh * 64:(hh + 1) * 64, :].pat([[128, 4], [1, 64]], off=hh * 64),
                                    strw[j, 4 * q:4 * q + 4, 2 * pr + hh].rearrange("s i k -> i s k"))
                            ps = PS.get()
                            for s4 in range(4):
                                tr(ps[:, s4 * 128:(s4 + 1) * 128], SBD[:, s4 * 128:(s4 + 1) * 128], identF)
                            for hh in range(2):
                                cp(M0all[hh * 64:(hh + 1) * 64, 4 * q:4 * q + 4, :], ps[hh * 64:(hh + 1) * 64, :].pat([[128, 4], [1, 64]], off=hh * 64),
                                   eng=("act" if hh == 0 else "dve"))
                                cp(M0b[hh * 64:(hh + 1) * 64, 4 * q:4 * q + 4, :], ps[hh * 64:(hh + 1) * 64, :].pat([[128, 4], [1, 64]], off=hh * 64),
                                   eng=("dve" if hh == 0 else "act"))
                    segB_prev = None
                    for li, t in enumerate(grp):
                        samp = (t == "s")
                        c0 = li * 128
                        gt = 16 if samp else t
                        segA = P.begin_seg()
                        if segB_prev is not None and PIPE:
                            P.merges.append((segB_prev, segA))
                        psR = PS.get()
                        for kc in range(NKC):
                            mm(psR[:, 0:384], hTv(kc, c0, 128), wa[:, kc, :], start=kc == 0, stop=False)
                        for kc in range(NKC):
                            mm(psR[:, 0:384], hprevv(kc, c0, 128), wb[:, kc, :], start=False, stop=kc == NKC - 1)
                        rP, kP, vP = psR[:, 0:128], psR[:, 128:256], psR[:, 256:384]
                        l1k = ("l1", li)
                        psL = PS.get()
                        mm(psL[:, 0:128], V(l1T.ap[:, 0, c0:c0 + 128], [l1k]), W2[:, cc:cc + 128])
                        mm(psL[:, 128:256], V(l1T.ap[:, 1, c0:c0 + 128], [l1k]), W2[:, cc:cc + 128])
                        if j >= 1:
                            mm(psL[:, 256:384], V(l1vT.ap[:, c0:c0 + 128], [("l1v", li)]), V2[:, cc:cc + 128])
                        npre = 384 if j >= 1 else 256
                        pre = T512.get(); tt(pre[:, 0:npre], psL[:, 0:npre], pv.pat([[1, npre]]), ADD)
                        sgs = T512.get(); act(sgs[:, 0:npre], pre[:, 0:npre], AF.Sigmoid)
                        sg = sgs[:, 0:128]; ag = sgs[:, 128:256]
                        Vt = BT128.get()
                        vkey = ("vfd", gt, pr)
                        if j == 0:
                            cp(Vt, vP)
                            P.op("sp", (lambda e, o=vfd[gt * 128:(gt + 1) * 128, cc:cc + 128], i_=Vt.ap: e.dma_start(out=o, in_=i_)),
                                 reads=list(Vt.keys), writes=[vkey], dma=True)
                        else:
                            vf = NRV.get()
                            P.op("sp", (lambda e, i_=vfd[gt * 128:(gt + 1) * 128, cc:cc + 128], o=vf.ap: e.dma_start(out=o, in_=i_)),
                                 reads=[vkey], writes=list(vf.keys), dma=True)
                            vg = sgs[:, 256:384]
                            d1 = T128.get(); tt(d1, vf, vP, SUB)
                            tt(d1, d1, vg, MUL)
                            tt(Vt, d1, vP, ADD)
                        kk = T128.get(); tt(kk, kP, pv[:, 3, :], MUL)
                        sq = T128.get(); tt(sq, kk, kk, MUL)
                        sm = TS8.get()
                        red(sm[:, 0:2], sq.pat([[64, 2], [1, 64]]), ADD)
                        act(sm[:, 2:4], sm[:, 0:2], AF.Ln)
                        act(sm[:, 4:6], sm[:, 2:4], AF.Exp, scale=-0.5)
                        ts(sm[:, 4:6], sm[:, 4:6], 1e12, ALU.min)
                        kkn = T128.get()
                        tt(kkn.pat([[64, 2], [1, 64]]), kk.pat([[64, 2], [1, 64]]), sm[:, 4:6].pat([[1, 2], [0, 64]]), MUL)
                        t1 = T128.get(); stt(t1, ag, -1.0, pv[:, 4, :], ADD, MUL)
                        BK = T256.get()
                        kp = BK[:, 128:256]; stt(kp, t1, 1.0, kP, ADD, MUL)
                        bb = BK[:, 0:128]; tt(bb, kkn, ag, MUL)
                        t2 = T128.get(); tt(t2, rP, kp, MUL)
                        tt(t2, t2, pv[:, 5, :], MUL)
                        red(sm[:, 6:8], t2.pat([[64, 2], [1, 64]]), ADD)
                        psC = PS.get()
                        mm(psC[:, 0:128], TIC, sg)
                        mm(psC[:, 128:256], TSC, sg)
                        mm(psC[:, 256:384], BOC, sg)
                        nsq = 16 if samp else 1
                        nsq = 16 if samp else 2
                        mm(psC[:, 384:384 + nsq], sg, (C("sel16c") if samp else C("negc")))
                        EE = T512.get(); act(EE[:, 0:384], psC[:, 0:384], AF.Exp)
                        E1 = EE[:, 0:128]; E3 = EE[:, 128:256]; E5 = EE[:, 256:384]
                        E2 = T128.get(); act(E2, psC[:, 0:128], AF.Exp, scale=-1.0)
                        gC = TS8s.get(); act(gC[:, 0:nsq], psC[:, 384:384 + nsq], AF.Exp)
                        dg = [[192, 2], [1, 64]]; nd = [[64, 2], [1, 64]]
                        Am, Rm = ZP; Bm = ZBK[:, 0:256]; Km = ZBK[:, 256:512]
                        ZH = ZBKH[li % 2]; BHm = ZH[:, 0:256]; KHm = ZH[:, 256:512]
                        dg2 = [[256, 2], [192, 2], [1, 64]]
                        stt(Am.pat(dg), kkn.pat(nd), -1.0, E3.pat(nd), MUL, MUL)
                        tt(ZBK.pat(dg2), BK.pat([[128, 2], [64, 2], [1, 64]]), E2.pat([[0, 2], [64, 2], [1, 64]]), MUL)
                        tt(Rm.pat(dg), rP.pat(nd), E1.pat(nd), MUL)
                        tt(ZH.pat(dg2), ZBK.pat(dg2), E5.pat([[0, 2], [64, 2], [1, 64]]), MUL)
                        XT = BT1K.get()
                        psb = bfv(PS.get())
                        for q, Xm in enumerate((Am, Am, Bm, Bm, Km, Km, Rm, Rm)):
                            hh = q % 2
                            tr(psb[:, q * 128:(q + 1) * 128], Xm[:, hh * 128:(hh + 1) * 128], identB)
                        cp(XT[:, 0:512], psb[:, 0:512]); cp(XT[:, 512:1024], psb[:, 512:1024], eng="dve")
                        AT = lambda hh: XT[:, (0 + hh) * 128:(1 + hh) * 128]
                        BT = lambda hh: XT[:, (2 + hh) * 128:(3 + hh) * 128]
                        KT = lambda hh: XT[:, (4 + hh) * 128:(5 + hh) * 128]
                        RT = lambda hh: XT[:, (6 + hh) * 128:(7 + hh) * 128]
                        Mk = []; XY = BT512.get(); WT = BT256.get()
                        for hh in range(2):
                            psM = PS.get()
                            mm(psM[:, 0:128], BT(hh), AT(hh))
                            mm(psM[:, 128:256], KT(hh), AT(hh))
                            mm(psM[:, 256:384], BT(hh), RT(hh))
                            mm(psM[:, 384:512], KT(hh), RT(hh))
                            mk_ = MK[2 * (li % 2) + hh]; tt(mk_, psM, MB, MUL)
                            Mk.append(mk_)
                            psN = PS.get()
                            mm(psN[:, 0:128], AT(hh), BT(hh))
                            tt(XY[:, hh * 256 + 128:hh * 256 + 256], psN[:, 0:128], SL, MUL)
                            tt(WT[:, hh * 128:(hh + 1) * 128], mk_[:, 0:128], identF, ADD)
                        for lev in range(1, LEV):
                            psV = PS.get()
                            for hh in range(2):
                                Xp = Mk[hh][:, 0:128] if lev == 1 else XY[:, hh * 256:hh * 256 + 128]
                                Yp = XY[:, hh * 256 + 128:hh * 256 + 256]
                                mm(psV[:, hh * 256 + 128:hh * 256 + 256], Xp, Yp)
                                if lev < LEV - 1:
                                    mm(psV[:, hh * 256:hh * 256 + 128], Yp, Xp)
                            XYn = BT512.get()
                            if lev < LEV - 1:
                                cp(XYn, psV)
                            else:
                                cp(XYn.pat([[256, 2], [1, 128]], off=128), psV.pat([[256, 2], [1, 128]], off=128))
                            psW = PS.get()
                            for hh in range(2):
                                mm(psW[:, hh * 128:(hh + 1) * 128], XYn[:, hh * 256 + 128:hh * 256 + 256], WT[:, hh * 128:(hh + 1) * 128])
                            WTn = BT256.get()
                            tt(WTn, psW[:, 0:256], WT, ADD)
                            XY = XYn; WT = WTn
                        P.end_seg()
                        segB_prev = P.begin_seg()
                        if samp:
                            cp(Z1b.pat([[136, 16], [1, 8]]), AT(0).pat([[8, 16], [1, 8]]), eng="act")
                            cp(Z2b.pat([[136, 16], [1, 8]]), AT(1).pat([[8, 16], [1, 8]]), eng="dve")
                        psP = PS.get()
                        for hh in range(2):
                            o_ = psP[:, hh * 64:(hh + 1) * 64]
                            if not samp:
                                mm(o_, AT(hh), MppB, start=True, stop=False)
                            else:
                                ZZ = Z1b if hh == 0 else Z2b
                                for s in range(NS):
                                    mm(o_, ZZ[:, s * 128:(s + 1) * 128], M0b[:, s, :], start=(s == 0), stop=False)
                            mm(o_, Mk[hh][:, 128:256], Vt[:, hh * 64:(hh + 1) * 64], start=False, stop=True)
                        P1 = BT128.get(); cp(P1, psP[:, 0:128])
                        psU = PS.get()
                        for hh in range(2):
                            mm(psU[:, hh * 64:(hh + 1) * 64], WT[:, hh * 128:(hh + 1) * 128], P1[:, hh * 64:(hh + 1) * 64])
                        U = BT128.get(); cp(U, psU[:, 0:128], eng="dve")
                        if samp:
                            cp(Z1b.pat([[136, 16], [1, 8]]), RT(0).pat([[8, 16], [1, 8]]), eng="act")
                            cp(Z2b.pat([[136, 16], [1, 8]]), RT(1).pat([[8, 16], [1, 8]]), eng="dve")
                        psY = PS.get()
                        for hh in range(2):
                            o_ = psY[:, hh * 64:(hh + 1) * 64]
                            if not samp:
                                mm(o_, RT(hh), MppB, start=True, stop=False)
                            else:
                                ZZ = Z1b if hh == 0 else Z2b
                                for s in range(NS):
                                    mm(o_, ZZ[:, s * 128:(s + 1) * 128], M0b[:, s, :], start=(s == 0), stop=False)
                            mm(o_, Mk[hh][:, 256:384], U[:, hh * 64:(hh + 1) * 64], start=False, stop=False)
                            mm(o_, Mk[hh][:, 384:512], Vt[:, hh * 64:(hh + 1) * 64], start=False, stop=True)
                        if not samp:
                            psS = PS.get()
                            mm(psS[:, 0:64], BHm[:, 0:128], U[:, 0:64], start=True, stop=False)
                            mm(psS[:, 0:64], BHm[:, 128:256], U[:, 64:128], start=False, stop=False)
                            mm(psS[:, 0:64], KHm[:, 0:128], Vt[:, 0:64], start=False, stop=False)
                            mm(psS[:, 0:64], KHm[:, 128:256], Vt[:, 64:128], start=False, stop=True)
                            stt(Mpp, Mpp, gC[:, 0:1], psS[:, 0:64], MUL, ADD)
                            cp(MppB, Mpp)
                        else:
                            for q in range(2):
                                Uw = [BT512.get(), BT512.get()]; Vw = [BT512.get(), BT512.get()]
                                for hh in range(2):
                                    tt(Uw[hh].pat([[64, 8], [1, 64]]), U[:, hh * 64:(hh + 1) * 64].pat([[0, 8], [1, 64]]),
                                       C("sel16").pat([[1, 8], [0, 64]], off=8 * q), MUL)
                                    tt(Vw[hh].pat([[64, 8], [1, 64]]), Vt[:, hh * 64:(hh + 1) * 64].pat([[0, 8], [1, 64]]),
                                       C("sel16").pat([[1, 8], [0, 64]], off=8 * q), MUL)
                                psS = PS.get()
                                mm(psS, BHm[:, 0:128], Uw[0], start=True, stop=False)
                                mm(psS, BHm[:, 128:256], Uw[1], start=False, stop=False)
                                mm(psS, KHm[:, 0:128], Vw[0], start=False, stop=False)
                                mm(psS, KHm[:, 128:256], Vw[1], start=False, stop=True)
                                tmp = T512.get()
                                tt(tmp.pat([[64, 8], [1, 64]]), M0all[:, 8 * q:8 * q + 8, :], gC[:, 8 * q:8 * q + 8].pat([[1, 8], [0, 64]]), MUL)
                                Mn = T512.get()
                                tt(Mn, tmp, psS, ADD)
                                for q2 in range(2):
                                    MBD = L["MBD"][q2]
                                    for hh in range(2):
                                        cp(MBD[hh * 64:(hh + 1) * 64, :].pat([[128, 4], [1, 64]], off=hh * 64),
                                           Mn[hh * 64:(hh + 1) * 64, q2 * 256:(q2 + 1) * 256].pat([[64, 4], [1, 64]]), eng=("act" if hh == 0 else "dve"))
                                    ps = PS.get()
                                    for s4 in range(4):
                                        tr(ps[:, s4 * 128:(s4 + 1) * 128], MBD[:, s4 * 128:(s4 + 1) * 128], identF)
                                    So = T256.get()
                                    for hh in range(2):
                                        cp(So[hh * 64:(hh + 1) * 64, :].pat([[64, 4], [1, 64]]),
                                           ps[hh * 64:(hh + 1) * 64, :].pat([[128, 4], [1, 64]], off=hh * 64), eng=("act" if hh == 0 else "dve"))
                                    sb0 = 8 * q + 4 * q2
                                    for hh in range(2):
                                        dma(srs[j, sb0:sb0 + 4, 2 * pr + hh].rearrange("s i k -> i s k"),
                                            So[hh * 64:(hh + 1) * 64, :].pat([[64, 4], [1, 64]]))
                        g2 = [[64, 2], [1, 64]]
                        red(sm[:, 8:10], psY[:, 0:128].pat(g2), ADD)
                        ysq = T128.get(); act(ysq, psY[:, 0:128], AF.Square)
                        red(sm[:, 10:12], ysq.pat(g2), ADD)
                        ts(sm[:, 12:14], sm[:, 8:10], 1.0 / 64, MUL)
                        tt(sm[:, 14:16], sm[:, 12:14], sm[:, 12:14], MUL)
                        stt(sm[:, 16:18], sm[:, 10:12], 1.0 / 64, sm[:, 14:16], MUL, SUB)
                        act(sm[:, 18:20], sm[:, 16:18], AF.Ln, bias=LNX_EPS, scale=1.0)
                        act(sm[:, 20:22], sm[:, 18:20], AF.Exp, scale=-0.5)
                        yc = T128.get()
                        tt(yc.pat(g2), psY[:, 0:128].pat(g2), sm[:, 12:14].pat([[1, 2], [0, 64]]), SUB)
                        tt(yc.pat(g2), yc.pat(g2), sm[:, 20:22].pat([[1, 2], [0, 64]]), MUL)
                        tt(yc, yc, pv[:, 6, :], MUL)
                        tt(yc, yc, pv[:, 7, :], ADD)
                        yb = T128.get()
                        tt(yb.pat(g2), Vt.pat(g2), sm[:, 6:8].pat([[1, 2], [0, 64]]), MUL)
                        tt(yc, yc, yb, ADD)
                        psT = PS.get()
                        tr(psT[:, 0:128], yc, identF)
                        cp(brv(pr, c0, 128), psT[:, 0:128])
                        P.end_seg()
                if first_last_prompt_group:
                    for pr in range(8):
                        MBD = PMBD
                        Mpp = V(Mp[j].ap[:, pr, :], [("Mp", j, pr)])
                        for hh in range(2):
                            cp(MBD[hh * 64:(hh + 1) * 64, hh * 64:(hh + 1) * 64], Mpp[hh * 64:(hh + 1) * 64, :], eng=("act" if hh == 0 else "pool"))
                        ps = PS.get()
                        tr(ps[:, 0:128], MBD, identF)
                        So = T128.get()
                        for hh in range(2):
                            cp(So[hh * 64:(hh + 1) * 64, 0:64], ps[hh * 64:(hh + 1) * 64, hh * 64:(hh + 1) * 64], eng=("act" if hh == 0 else "dve"))
                        for hh in range(2):
                            dma(srp[j, 2 * pr + hh], So[hh * 64:(hh + 1) * 64, 0:64])

            first_last_prompt_group = (15 in ptiles)
            for l in range(depth):
                is_rw = (l % 2 == 1)
                j = l // 2
                gcol = TS8.get()[:, 0:8]
                dma(gcol, norm_g[l].rearrange("(kc p) -> p kc", p=128), slow=True)
                if is_rw:
                    if first_group:
                        mset(V(hT.ap[:, :, 0:1], [("hT", "c0")]), 0.0)
                    elif npt > 0:
                        cp(V(hT.ap[:, :, 0], [("hT", "c0")]), hcar[j], eng="pool")
                for li, t in enumerate(grp):
                    rstd = rms_tile(li)
                    for kc in range(NKC):
                        stt(V(hT.ap[:, kc, 1 + li * 128:1 + (li + 1) * 128], [hkey(li)]), xTv(kc, li * 128, 128), gcol[:, kc:kc + 1], rstd, MUL, MUL)
                    if is_rw and t == 15:
                        hl = TS8.get()[:, 0:8]
                        ts(hl, V(xT.ap[:, :, li * 128 + 127], [xkey(li)]), rstd[:, 127:128], MUL)
                        tt(hl, hl, gcol, MUL)
                        dma(ssp[j].rearrange("(kc p) -> p kc", p=128), hl, slow=True)
                    if is_rw and t == "s":
                        hl = T128.get()
                        hl3 = hl.pat([[16, 8], [1, 16]])
                        xv = xTt(li).pat([[NTM, 8], [8, 16]], off=7)
                        rv = rstd.pat([[0, 8], [8, 16]], off=7)
                        tt(hl3, xv, rv, MUL)
                        tt(hl3, hl3, gcol.pat([[1, 8], [0, 16]]), MUL)
                        ps = PS.get()
                        tr(ps[:, 0:128], hl, identF)
                        ho = T128.get()
                        cp(ho, ps[:, 0:128])
                        for kc in range(NKC):
                            dma(sss[j, :, kc * 128:(kc + 1) * 128], ho[kc * 16:(kc + 1) * 16, :])
                        stin = XIN.get()
                        mset(stin, 0.0)
                        dma(stin[0:16, :], sts[j])
                        for half in range(2):
                            ps = PS.get()
                            for q in range(4):
                                kc = half * 4 + q
                                tr(ps[:, q * 128:(q + 1) * 128], stin[:, kc * 128:(kc + 1) * 128], identF)
                            cp(hpS.pat([[128, 4], [8, 16]], off=half * 4 * 128), ps.pat([[128, 4], [1, 16]]))
                        hsv = V(hT.ap[:, :, 1 + li * 128:1 + (li + 1) * 128], [hkey(li)])
                        cp(hpS.pat([[128, 8], [8, 16], [1, 7]], off=1), hsv.pat([[HS, 8], [8, 16], [1, 7]]), eng="pool")
                if is_rw and npt > 0:
                    cp(hcar[j], V(hT.ap[:, :, npt * 128], [hkey(npt - 1)]), eng="pool")
                if 1 in phases:
                    if not is_rw:
                        hgrn_layer(l, j)
                    else:
                        rwkv_layer(l, j)
                rest_layer(l)

            fg = TS8.get()[:, 0:8]
            dma(fg, final_g[0].rearrange("(kc p) -> p kc", p=128), slow=True)
            for li, t in enumerate(grp):
                rstd = rms_tile(li)
                yT = T1K.get()
                for kc in range(NKC):
                    stt(yT[:, kc * 128:(kc + 1) * 128], xTv(kc, li * 128, 128), fg[:, kc:kc + 1], rstd, MUL, MUL)
                yo = T1K.get()
                for half in range(2):
                    ps = PS.get()
                    for q in range(4):
                        kc = half * 4 + q
                        tr(ps[:, q * 128:(q + 1) * 128], yT[:, kc * 128:(kc + 1) * 128], identF)
                    cp(yo[:, half * 512:(half + 1) * 512], ps)
                dma(ys if t == "s" else yp[t * 128:(t + 1) * 128, :], yo)

        for gi, grp in enumerate(groups):
            process_group(gi, grp)

        info = P.emit()
    return nc, info


def _shard_inputs(inp):
    cf, cb = make_consts()
    maps = []
    for c in range(8):
        b = slice(16 * c, 16 * c + 16)
        m = {
            "xp": np.ascontiguousarray(inp["x_prompt"][c]),
            "xs": np.ascontiguousarray(inp["x_sample"][b]).reshape(128, D),
            "mem": np.ascontiguousarray(inp["mem_prompt"][c]),
            "sth": np.ascontiguousarray(inp["state_hgrn"][:, b]),
            "strw": np.ascontiguousarray(inp["state_rwkv"][:, b]),
            "sts": np.ascontiguousarray(inp["state_shift"][:, b]),
            "ck": np.ascontiguousarray(inp["cache_mem_k"][:, b]).reshape(4, 16, MEM, 512),
            "cv": np.ascontiguousarray(inp["cache_mem_v"][:, b]).reshape(4, 16, MEM, 512),
            "rw_rk": np.ascontiguousarray(inp["rw_rk"]).reshape(2, D),
            "final_g": np.ascontiguousarray(inp["final_g"]).reshape(1, D),
            "cf": cf, "cb": cb,
        }
        for k in ("norm_g", "w_in", "w_out", "mem_norm_g", "w_mem_kv", "hg_lb", "hg_onorm_g", "rw_mu", "rw_w0", "rw_w1",
                  "rw_w2", "rw_a0", "rw_a1", "rw_a2", "rw_v0", "rw_v1", "rw_v2", "rw_kk", "rw_ka", "rw_lnx_g", "rw_lnx_b"):
            m[k] = np.ascontiguousarray(inp[k])
        maps.append(m)
    return maps


_NC_CACHE = {}


def kernel(**inputs):
    inp = {k: np.asarray(v, dtype=np.float32) for k, v in inputs.items()}
    if "nc" not in _NC_CACHE:
        _NC_CACHE["nc"] = build()[0]
    nc = _NC_CACHE["nc"]
    maps = _shard_inputs(inp)
    res = run_bass_kernel_spmd(nc, maps, core_ids=list(range(8)))
    R = res.results
    y_p = np.stack([R[c]["yp"] for c in range(8)], 0)
    y_s = np.concatenate([R[c]["ys"].reshape(16, 8, D) for c in range(8)], 0)
    sh_p = np.stack([R[c]["shp"] for c in range(8)], 1)
    sr_p = np.stack([R[c]["srp"] for c in range(8)], 1)
    ss_p = np.stack([R[c]["ssp"] for c in range(8)], 1)
    mk_p = np.stack([R[c]["mkp"].reshape(4, MEM, 4, 128) for c in range(8)], 1)
    mv_p = np.stack([R[c]["mvp"].reshape(4, MEM, 4, 128) for c in range(8)], 1)
    sh_s = np.concatenate([R[c]["shs"] for c in range(8)], 1)
    sr_s = np.concatenate([R[c]["srs"] for c in range(8)], 1)
    ss_s = np.concatenate([R[c]["sss"] for c in range(8)], 1)
    return (y_p, y_s, sh_p, sr_p, ss_p, mk_p, mv_p, sh_s, sr_s, ss_s)
```

```python
import contextlib
import numpy as np
import ml_dtypes
import concourse.bass as bass
import concourse.mybir as mybir
from concourse.bass_utils import run_bass_kernel_spmd

F32 = mybir.dt.float32
F32R = mybir.dt.float32r
BF16 = mybir.dt.bfloat16
AF = mybir.ActivationFunctionType
ALU = mybir.AluOpType
AX = mybir.AxisListType

ENGS = ("pe", "act", "dve", "pool", "sp")
DMA_K = 6
SEM_WRAP = 30000
SAME_ENG_RAW_WAITS = True
PIPE = True


class Prog:
    def __init__(self, nc):
        self.nc = nc
        self.ops = []
        self.last_w = {}
        self.readers = {}
        self.excl = set()
        self.seg = None
        self.nseg = 0
        self.merges = []

    def begin_seg(self):
        self.nseg += 1
        self.seg = self.nseg
        return self.seg

    def end_seg(self):
        self.seg = None

    def op(self, eng, fn, reads=(), writes=(), dma=False):
        i = len(self.ops)
        ex = [k for k in reads if k in self.excl]
        if ex:
            writes = list(writes) + ex
        deps = {}
        for k in reads:
            w = self.last_w.get(k)
            if w is not None:
                deps[w] = True
        for k in writes:
            w = self.last_w.get(k)
            if w is not None:
                deps.setdefault(w, False)
            for r in self.readers.get(k, ()):
                deps.setdefault(r, False)
        o = dict(i=i, eng=eng, fn=fn, dma=dma, deps=deps, seg=self.seg)
        self.ops.append(o)
        for k in reads:
            self.readers.setdefault(k, []).append(i)
        for k in writes:
            self.last_w[k] = i
            self.readers[k] = []
        return i

    def barrier(self):
        lastc = {e: None for e in ENGS}
        lastd = {e: [] for e in ENGS}
        for o in self.ops:
            if o["dma"]:
                lastd[o["eng"]].append(o["i"])
            else:
                lastc[o["eng"]] = o["i"]
        extra = [v for v in lastc.values() if v is not None]
        for e in ENGS:
            extra += lastd[e][-DMA_K:]
        for e in ENGS:
            i = self.op(e, lambda eng: eng.nop(), reads=(), writes=())
            for x in extra:
                self.ops[i]["deps"][x] = True

    def _order(self):
        n = len(self.ops)
        order = []
        segops = {}
        for o in self.ops:
            if o["seg"] is not None:
                segops.setdefault(o["seg"], []).append(o["i"])
        partner = {x: y for x, y in self.merges}
        done_seg = set()
        i = 0
        while i < n:
            o = self.ops[i]
            sg = o["seg"]
            if sg is not None and sg in partner and sg not in done_seg:
                X = segops[sg]; Y = segops[partner[sg]]
                assert X[-1] + 1 == Y[0] and X == list(range(X[0], X[-1] + 1)) and Y == list(range(Y[0], Y[-1] + 1))
                emitted = set()
                xi = yi = 0
                xset = set(X)
                turn = 0
                while xi < len(X) or yi < len(Y):
                    pick_y = False
                    if yi < len(Y) and (turn == 1 or xi >= len(X)):
                        yo = self.ops[Y[yi]]
                        if all((d not in xset) or (d in emitted) for d in yo["deps"]):
                            pick_y = True
                    if pick_y:
                        order.append(Y[yi]); yi += 1
                    else:
                        order.append(X[xi]); emitted.add(X[xi]); xi += 1
                    turn ^= 1
                done_seg.add(sg); done_seg.add(partner[sg])
                i = Y[-1] + 1
                continue
            order.append(i)
            i += 1
        assert sorted(order) == list(range(n))
        return order

    def emit(self, final_wait_eng="sp"):
        nc = self.nc
        order = self._order()
        ops = [self.ops[i] for i in order]
        newidx = {o["i"]: k for k, o in enumerate(ops)}
        for k, o in enumerate(ops):
            o["deps"] = {newidx[d]: r for d, r in o["deps"].items()}
            assert all(d < k for d in o["deps"])
            o["i"] = k
        per_eng = {e: [] for e in ENGS}
        ndma = {e: 0 for e in ENGS}
        lastslot = {}
        for o in ops:
            e = o["eng"]
            o["pos"] = len(per_eng[e])
            per_eng[e].append(o["i"])
            if o["dma"]:
                nn = ndma[e]
                o["dslot"] = nn % DMA_K
                o["dval"] = 16 * (nn // DMA_K + 1)
                o["dprev"] = lastslot.get((e, o["dslot"]))
                lastslot[(e, o["dslot"])] = o["i"]
                ndma[e] = nn + 1
        known = {e: {f: -1 for f in ENGS} for e in ENGS}
        known_d = {e: {} for e in ENGS}
        milestone = set()
        for o in ops:
            e = o["eng"]
            wl = []
            deps = dict(o["deps"])
            if o["dma"] and o["dprev"] is not None:
                deps[o["dprev"]] = True
            for j, raw in deps.items():
                p = ops[j]
                f = p["eng"]
                if p["dma"]:
                    key = (f, p["dslot"])
                    if known_d[e].get(key, 0) < p["dval"]:
                        known_d[e][key] = p["dval"]
                        wl.append(("d", f, p["dslot"], p["dval"]))
                else:
                    if f == e and not o["dma"] and e != "pool":
                        if e == "pe" or not raw or not SAME_ENG_RAW_WAITS:
                            continue
                    if known[e][f] < p["pos"]:
                        wl.append(("c", f, p["pos"], j))
            best = {}
            out = []
            for w in wl:
                if w[0] == "c":
                    if w[1] not in best or best[w[1]][2] < w[2]:
                        best[w[1]] = w
                else:
                    out.append(w)
            for f, w in best.items():
                known[e][f] = w[2]
                milestone.add(w[3])
                out.append(w)
            o["waits"] = out
        cnt = {e: 0 for e in ENGS}
        for o in ops:
            if o["dma"]:
                continue
            if o["i"] in milestone:
                cnt[o["eng"]] += 1
                o["ms"] = cnt[o["eng"]]
            else:
                o["ms"] = None
        nsem = {e: max(1, (cnt[e] + SEM_WRAP - 1) // SEM_WRAP) for e in ENGS}
        nwaits = 0
        with contextlib.ExitStack() as st:
            csem = {e: [st.enter_context(nc.semaphore(f"c_{e}_{k}")) for k in range(nsem[e])] for e in ENGS}
            dsem = {e: [st.enter_context(nc.semaphore(f"d_{e}_{k}")) for k in range(DMA_K)]
                    for e in ENGS if ndma[e] > 0}
            block = st.enter_context(nc.Block())

            def mk(ename):
                def body(eng):
                    nonlocal nwaits
                    for i in per_eng[ename]:
                        o = ops[i]
                        for w in o["waits"]:
                            nwaits += 1
                            if w[0] == "c":
                                ms = ops[w[3]]["ms"]
                                k, v = (ms - 1) // SEM_WRAP, (ms - 1) % SEM_WRAP + 1
                                eng.wait_ge(csem[w[1]][k], v)
                            else:
                                eng.wait_ge(dsem[w[1]][w[2]], w[3])
                        ins = o["fn"](eng)
                        if o["dma"]:
                            ins.then_inc(dsem[ename][o["dslot"]], 16)
                        elif o["ms"] is not None:
                            k = (o["ms"] - 1) // SEM_WRAP
                            ins.then_inc(csem[ename][k], 1)
                    if ename == final_wait_eng:
                        for f in dsem:
                            last = {}
                            for j in per_eng[f]:
                                if ops[j]["dma"]:
                                    last[ops[j]["dslot"]] = ops[j]["dval"]
                            for s, v in last.items():
                                eng.wait_ge(dsem[f][s], v)
                return body

            block.tensor(mk("pe"))
            block.scalar(mk("act"))
            block.vector(mk("dve"))
            block.gpsimd(mk("pool"))
            block.sync(mk("sp"))
        return dict(n_ops=len(ops), per_eng={e: len(v) for e, v in per_eng.items()},
                    milestones=cnt, nwaits=nwaits)


class V:
    def __init__(self, ap, keys, r=False):
        self.ap = ap
        self.keys = tuple(keys)
        self.r = r

    def __getitem__(self, idx):
        return V(self.ap[idx], self.keys, self.r)

    def pat(self, pat, off=0):
        a = self.ap
        return V(bass.AP(a.tensor, a.offset + off, [list(a.ap[0])] + [list(p) for p in pat]), self.keys, self.r)

    def k(self, *keys):
        return V(self.ap, keys, self.r)

    def nr(self):
        return V(self.ap, self.keys, False)

    def rr(self):
        return V(self.ap, self.keys, True)


D = 1024
NKC = 8
SEQ = 2048
NPT = 16
NS = 16
TS = 8
MEM = 256
XH = 4
INC = 5120
CDEC = -float(np.exp(-0.5))
NORM_EPS = 1e-6
LNX_EPS = 64e-5
SC_ATT = 128 ** -0.5

CO = {}
_o = 0
for _n, _w in [("ident", 128), ("ones", 128),
               ("mb1", 512), ("sl1", 128), ("tic1", 128), ("tsc1", 128), ("boc1", 128),
               ("mb16", 512), ("sl16", 128), ("bo16", 128), ("tic16", 128), ("tsc16", 128), ("boc16", 128),
               ("iu2", 128), ("bo2", 128), ("sel16", 16), ("sel16c", 16), ("negc", 2), ("cm0", 1), ("cm1", 1),
               ("hw", 1), ("ha", 1)]:
    CO[_n] = (_o, _w)
    _o += _w
NCF = _o
CB = {"ident": (0, 128)}
NCB = 128


def make_consts():
    c = np.zeros((128, NCF), np.float32)
    s = np.arange(128)[:, None]
    t = np.arange(128)[None, :]
    def put(n, a):
        o, w = CO[n]
        c[:, o:o + w] = a
    put("ident", (s == t))
    put("ones", 1.0)
    for cfg, blk in ((1, 128), (16, 8), (2, 64)):
        same = (s // blk) == (t // blk)
        iu = same & (s <= t)
        su = same & (s < t)
        sl = same & (s > t)
        if cfg == 2:
            put("iu2", iu); put("bo2", same)
            continue
        put(f"mb{cfg}", np.concatenate([su, su, iu, iu], 1))
        put(f"sl{cfg}", sl)
        if cfg == 16:
            put("bo16", same)
        put(f"tic{cfg}", CDEC * iu); put(f"tsc{cfg}", CDEC * su); put(f"boc{cfg}", CDEC * same)
    sel = (np.arange(128)[:, None] // 8) == np.arange(16)[None, :]
    put("sel16", sel); put("sel16c", CDEC * sel); put("negc", CDEC)
    put("cm0", (np.arange(128) < 64)[:, None]); put("cm1", (np.arange(128) >= 64)[:, None])
    put("hw", (np.arange(128) < 64)[:, None]); put("ha", (np.arange(128) >= 64)[:, None])
    cb = np.zeros((128, NCB), np.float32)
    cb[:, 0:128] = np.eye(128)
    return c, cb.astype(ml_dtypes.bfloat16)


def build(depth=4, groups=None, dbg=False, memkv=True, phases=(1, 2, 3, 4), mks=9, levo=None):
    if groups is None:
        groups = [[0, 1, 2, 3], [4, 5, 6, 7], [8, 9, 10, 11], [12, 13, 14, 15], ["s"]]
    GMAX = max(len(g) for g in groups)
    NTM = GMAX * 128
    nc = bass.Bass("TRN2", target_bir_lowering=False)
    din = lambda n, s, d=F32: nc.dram_tensor(n, list(s), d, kind="ExternalInput").ap()
    dout = lambda n, s: nc.dram_tensor(n, list(s), F32, kind="ExternalOutput").ap()
    xp = din("xp", [SEQ, D]); xs = din("xs", [128, D]); mem = din("mem", [MEM, D])
    sth = din("sth", [2, NS, 8, 128, 128]); strw = din("strw", [2, NS, 16, 64, 64]); sts = din("sts", [2, NS, D])
    ck = din("ck", [4, NS, MEM, 512]); cv = din("cv", [4, NS, MEM, 512])
    norm_g = din("norm_g", [4, D]); w_in = din("w_in", [4, D, INC]); w_out = din("w_out", [4, 1536, D])
    mem_norm_g = din("mem_norm_g", [4, D]); w_mem_kv = din("w_mem_kv", [4, D, 1024])
    hg_lb = din("hg_lb", [2, D]); hg_og = din("hg_onorm_g", [2, 128])
    rw_mu = din("rw_mu", [2, 5, D]); rw_w0 = din("rw_w0", [2, D]); rw_w1 = din("rw_w1", [2, D, 64]); rw_w2 = din("rw_w2", [2, 64, D])
    rw_a0 = din("rw_a0", [2, D]); rw_a1 = din("rw_a1", [2, D, 64]); rw_a2 = din("rw_a2", [2, 64, D])
    rw_v0 = din("rw_v0", [1, D]); rw_v1 = din("rw_v1", [1, D, 32]); rw_v2 = din("rw_v2", [1, 32, D])
    rw_kk = din("rw_kk", [2, D]); rw_ka = din("rw_ka", [2, D]); rw_rk = din("rw_rk", [2, D])
    rw_lg = din("rw_lnx_g", [2, D]); rw_lb = din("rw_lnx_b", [2, D]); final_g = din("final_g", [1, D])
    cf_d = din("cf", [128, NCF]); cb_d = din("cb", [128, NCB], BF16)
    yp = dout("yp", [SEQ, D]); ys = dout("ys", [128, D])
    shp = dout("shp", [2, 8, 128, 128]); srp = dout("srp", [2, 16, 64, 64]); ssp = dout("ssp", [2, D])
    mkp = dout("mkp", [4, MEM, 512]); mvp = dout("mvp", [4, MEM, 512])
    shs = dout("shs", [2, NS, 8, 128, 128]); srs = dout("srs", [2, NS, 16, 64, 64]); sss = dout("sss", [2, NS, D])

    with contextlib.ExitStack() as st:
        P = Prog(nc)

        def sb(name, shape, dt=F32, r=False):
            t = st.enter_context(nc.sbuf_tensor(name, list(shape), dt))
            return V(t[:], [name], r and dt == F32)

        class Rot:
            def __init__(self, name, n, shape, dt=F32, r=False):
                self.bufs = [sb(f"{name}{i}", shape, dt, r) for i in range(n)]
                self.i = 0

            def get(self):
                b = self.bufs[self.i % len(self.bufs)]
                self.i += 1
                return b

        class PSR:
            def __init__(self):
                self.bufs = []
                for i in range(8):
                    t = st.enter_context(nc.psum_tensor(f"ps{i}", [128, 512], F32))
                    self.bufs.append(V(t[:], [f"ps{i}"]))
                    P.excl.add(f"ps{i}")
                self.i = 0

            def get(self):
                b = self.bufs[self.i % 8]
                self.i += 1
                return b

        PS = PSR()

        def bfv(v):
            return V(v.ap.bitcast(BF16), v.keys)

        def rk(*vs):
            out = []
            for v in vs:
                if isinstance(v, V):
                    out += list(v.keys)
            return out

        def a_(x):
            return x.ap if isinstance(x, V) else x

        def o_(x):
            return x.ap.bitcast(F32R) if x.r else x.ap

        def mm(o, l, r, start=True, stop=True):
            la, ra = l.ap, r.ap
            if l.r and r.r:
                la, ra = la.bitcast(F32R), ra.bitcast(F32R)
            P.op("pe", lambda e: e.matmul(o.ap, lhsT=la, rhs=ra, start=start, stop=stop), reads=rk(l, r), writes=rk(o))

        def tr(o, i, idn):
            P.op("pe", lambda e: e.transpose(out=o.ap, in_=i.ap, identity=idn.ap), reads=rk(i, idn), writes=rk(o))

        def act(o, i, f, bias=None, scale=None, acc=None):
            kw = {}
            if bias is not None:
                kw["bias"] = a_(bias)
            if scale is not None:
                kw["scale"] = a_(scale)
            if acc is not None:
                kw["accum_out"] = acc.ap
            oa = o_(o)
            P.op("act", lambda e: e.activation(out=oa, in_=i.ap, func=f, **kw), reads=rk(i, bias, scale), writes=rk(o, acc))

        def tt(o, a, b, op, eng="dve"):
            if eng == "pool" and o.r:
                eng = "dve"
            oa = o_(o)
            P.op(eng, lambda e: e.tensor_tensor(out=oa, in0=a.ap, in1=b.ap, op=op), reads=rk(a, b), writes=rk(o))

        def ts(o, a, s1, op0, s2=None, op1=None, eng="dve"):
            oa = o_(o)

            def f(e):
                if op1 is None:
                    return e.tensor_scalar(out=oa, in0=a.ap, scalar1=a_(s1), scalar2=None, op0=op0)
                return e.tensor_scalar(out=oa, in0=a.ap, scalar1=a_(s1), scalar2=a_(s2), op0=op0, op1=op1)
            P.op(eng, f, reads=rk(a, s1, s2), writes=rk(o))

        def stt(o, a, s, b, op0, op1):
            oa = o_(o)
            P.op("dve", lambda e: e.scalar_tensor_tensor(out=oa, in0=a.ap, scalar=a_(s), in1=b.ap, op0=op0, op1=op1),
                 reads=rk(a, s, b), writes=rk(o))

        def red(o, a, op, negate=False):
            P.op("dve", lambda e: e.tensor_reduce(out=o.ap, in_=a.ap, axis=AX.X, op=op, negate=negate), reads=rk(a), writes=rk(o))

        def recip(o, a):
            oa = o_(o)
            P.op("dve", lambda e: e.reciprocal(out=oa, in_=a.ap), reads=rk(a), writes=rk(o))

        def cp(o, i, eng="act"):
            if eng == "pool" and o.r:
                eng = "dve"
            oa = o_(o)
            if eng == "act":
                P.op("act", lambda e: e.copy(out=oa, in_=i.ap), reads=rk(i), writes=rk(o))
            else:
                P.op(eng, lambda e: e.tensor_copy(out=oa, in_=i.ap), reads=rk(i), writes=rk(o))

        def mset(o, val, eng="pool"):
            P.op(eng, lambda e: e.memset(o.ap, val), writes=rk(o))

        def dma(o, i, eng="sp", slow=False):
            oa = o.ap if isinstance(o, V) else o
            ia = i.ap if isinstance(i, V) else i
            kw = dict(allow_slow_non_contiguous=True) if slow else {}
            P.op(eng, lambda e: e.dma_start(out=oa, in_=ia, **kw), reads=rk(i), writes=rk(o), dma=True)

        MUL, ADD, SUB, MAX = ALU.mult, ALU.add, ALU.subtract, ALU.max

        cF = sb("cF", [128, NCF], r=True); cB = sb("cB", [128, NCB], BF16)
        dma(cB, cb_d)
        C = lambda n: cF[:, CO[n][0]:CO[n][0] + CO[n][1]]
        identF = C("ident"); identB = cB[:, 0:128]; onesF = C("ones")

        Sp = [sb(f"Sp{j}", [128, 8, 128], r=True) for j in range(2)]
        Mp = [sb(f"Mp{j}", [128, 8, 64], r=True) for j in range(2)]
        hcar = [sb(f"hcar{j}", [128, 8], BF16) for j in range(2)]
        for j in range(2):
            mset(Sp[j], 0.0); mset(Mp[j], 0.0)

        ARENA = 62 * 256
        arena_t = st.enter_context(nc.sbuf_tensor("arena", [128, ARENA], F32))
        arena = arena_t[:]
        aoff = {"P": 0, "S": 0}

        def carve(lay, name, shape, dt=F32):
            n = int(np.prod(shape[1:]))
            n32 = n if dt == F32 else (n + 1) // 2
            o = aoff[lay]
            aoff[lay] = o + n32
            assert aoff[lay] <= ARENA, (lay, name, aoff[lay])
            a = arena[:, o:o + n32]
            if dt != F32:
                a = a.bitcast(dt)
            pat = []
            stride = 1
            for d in reversed(shape[1:]):
                pat.insert(0, [stride, d])
                stride *= d
            return V(bass.AP(a.tensor, a.offset, [list(a.ap[0])] + pat), [name])

        LAY = {}
        for lay, ntm in (("P", GMAX * 128), ("S", 128)):
            Ld = dict(NTM=ntm)
            Ld["l1T"] = carve(lay, lay + "l1T", [128, 2, ntm])
            Ld["l1vT"] = carve(lay, lay + "l1vT", [128, ntm])
            if lay == "P":
                Ld["QZ"] = carve(lay, "QZ", [128, 256])
            Ld["xT"] = carve(lay, lay + "xT", [128, NKC, ntm])
            Ld["hT"] = carve(lay, lay + "hT", [128, NKC, 1 + ntm + 1], BF16)
            Ld["brT"] = carve(lay, lay + "brT", [128, 12, ntm], BF16)
            Ld["qx"] = [carve(lay, lay + f"qx{i}", [128, ntm], BF16) for i in range(2)]
            if lay == "P":
                Ld["mkT"] = carve(lay, "mkT", [128, 4, XH, MEM], BF16)
                Ld["mvb"] = carve(lay, "mvb", [128, 4, 2, 512], BF16)
            else:
                Ld["hpS"] = carve(lay, "hpS", [128, NKC, 128], BF16)
                Ld["Z1"] = carve(lay, "Z1", [128, 2048])
                Ld["Z1b"] = carve(lay, "Z1b", [128, 2048], BF16)
                Ld["Z2b"] = carve(lay, "Z2b", [128, 2048], BF16)
                Ld["M0b"] = carve(lay, "M0b", [128, 16, 64], BF16)
                o_save = aoff[lay]
                Ld["S0q"] = [carve(lay, f"S0q{i}", [128, 512]).k("salias") for i in range(4)]
                aoff[lay] = o_save
                Ld["M0all"] = carve(lay, "M0all", [128, 16, 64]).k("salias")
                Ld["SBD"] = [carve(lay, f"SBD{i}", [128, 512]).k("salias") for i in range(2)]
                Ld["MBD"] = [carve(lay, f"MBD{i}", [128, 512]) for i in range(2)]
                Ld["KVB"] = [carve(lay, f"KVB{i}", [128, 2048], BF16) for i in range(3)]
                Ld["QXM"] = carve(lay, "QXM", [128, 2048], BF16)
            LAY[lay] = Ld
        print("arena use (KiB):", {k: v / 256 for k, v in aoff.items()})

        T128 = Rot("t128_", 16, [128, 128], r=True)
        T256 = Rot("t256_", 3, [128, 256], r=True)
        T512 = Rot("t512_", 4, [128, 512], r=True)
        T1K = Rot("t1k_", 2, [128, 1024], r=True)
        TB256 = Rot("tb256_", 2, [128, 256], BF16)
        TB512 = Rot("tb512_", 2, [128, 512], BF16)
        TS8 = Rot("ts8_", 8, [128, 24])
        TS8s = TS8
        OG = sb("OG", [128, 128]); LBC = sb("LBC", [128, 24])
        BIGV = [sb("bigv0", [128, D], r=True), sb("bigv1", [128, D], r=True)]
        MUC = sb("MUC", [128, 80])
        L1a = sb("L1a", [128, NKC, 128], BF16); L1b = sb("L1b", [128, NKC, 128], BF16)
        L1va = sb("L1va", [128, NKC, 128], BF16); L1vb = sb("L1vb", [128, NKC, 128], BF16)
        PVB = sb("PVB", [128, 8, 128])
        ZP = [sb(f"zp{i}", [128, 256], BF16) for i in range(2)]
        ZBK = sb("zbk", [128, 512], BF16)
        ZBKH = [sb(f"zbkh{i}", [128, 512], BF16) for i in range(2)]
        PMBD = sb("PMBD", [128, 128])
        for zb in ZP + [ZBK, PMBD] + ZBKH:
            mset(zb, 0.0)
        WM = Rot("wm_", 2, [128, NKC, 384], BF16)
        WM2 = Rot("wm2_", 2, [128, NKC, 384], BF16)
        WR = Rot("wr_", 2, [128, NKC, 512], BF16)
        WO = Rot("wo_", 1, [128, 12, 128], BF16)
        XIN = Rot("xin_", 1, [128, D])
        _xa = XIN.bufs[0].ap
        PVS = [PVB, V(bass.AP(_xa.tensor, _xa.offset, [list(_xa.ap[0]), [128, 8], [1, 128]]), XIN.bufs[0].keys)]
        _xb = XIN.bufs[0].ap.bitcast(BF16)
        WO.bufs.append(V(bass.AP(_xb.tensor, _xb.offset, [list(_xb.ap[0]), [128, 12], [1, 128]]), XIN.bufs[0].keys))
        NRV = Rot("nrv_", 2, [128, 128], BF16)
        BT128 = Rot("b128_", 8, [128, 128], BF16)
        BT256 = Rot("b256_", 3, [128, 256], BF16)
        BT512 = Rot("b512_", 5, [128, 512], BF16)
        BT1K = Rot("b1k_", 2, [128, 1024], BF16)
        MK = [sb(f"mk{i}", [128, 512], BF16) for i in range(4)]
        MpB = [sb(f"MpB{j}", [128, 8, 64], BF16) for j in range(2)]
        for j in range(2):
            mset(MpB[j], 0.0)
        vfd = nc.dram_tensor("vfd", [17 * 128, D], BF16, kind="Internal").ap()

        for c0 in range(0, NCF, 1024):
            n = min(1024, NCF - c0)
            xin = XIN.get()
            dma(xin[:, 0:n], cf_d[:, c0:c0 + n])
            cp(cF[:, c0:c0 + n], xin[:, 0:n])

        def wview(w2d, c0, ncols):
            return w2d[:, c0:c0 + ncols].rearrange("(kc p) c -> p kc c", p=128)

        def xkey(li):
            return ("xT", li)

        def hkey(li):
            return ("hT", li)

        def mem_kv():
            LP = LAY["P"]
            mkT, mvb = LP["mkT"], LP["mvb"]
            memnT = V(LP["xT"].ap[:, :, 0:MEM], [xkey(0), xkey(1)])
            hmT = V(LP["hT"].ap[:, :, 1:1 + MEM], [hkey(0), hkey(1)])
            for mc in range(2):
                xin = XIN.get()
                dma(xin, mem[mc * 128:(mc + 1) * 128, :])
                ss = TS8.get()
                junk = T1K.get()
                act(junk, xin, AF.Square, acc=ss[:, 0:1])
                ts(ss[:, 1:2], ss[:, 0:1], 1.0 / D, MUL, NORM_EPS, ADD)
                act(ss[:, 2:3], ss[:, 1:2], AF.Ln)
                act(ss[:, 3:4], ss[:, 2:3], AF.Exp, scale=-0.5)
                xn = T1K.get()
                ts(xn, xin, ss[:, 3:4], MUL)
                for half in range(2):
                    ps = PS.get()
                    for q in range(4):
                        kc = half * 4 + q
                        tr(ps[:, q * 128:(q + 1) * 128], xn[:, kc * 128:(kc + 1) * 128], identF)
                    cp(memnT[:, half * 4:(half + 1) * 4, mc * 128:(mc + 1) * 128], ps.pat([[128, 4], [1, 128]]))
            for l in range(depth):
                if mks < 1:
                    break
                gcol = TS8.get()[:, 0:8]
                dma(gcol, mem_norm_g[l].rearrange("(kc p) -> p kc", p=128), slow=True)
                for kc in range(NKC):
                    ts(hmT[:, kc, :], memnT[:, kc, :], gcol[:, kc:kc + 1], MUL)
                if mks < 2:
                    continue
                for cb in range(2):
                    wb = WR.get()
                    dma(wb, wview(w_mem_kv[l], cb * 512, 512), eng="pool")
                    if mks < 3:
                        continue
                    if cb == 0:
                        for hx in range(XH):
                            ps = PS.get()
                            for kc in range(NKC):
                                mm(ps[:, 0:256], wb[:, kc, hx * 128:(hx + 1) * 128], hmT[:, kc, :], start=kc == 0, stop=kc == NKC - 1)
                            cp(mkT[:, l, hx, :], ps[:, 0:256])
                    if mks < 4:
                        continue
                    for mc in range(2):
                        ps = PS.get()
                        for kc in range(NKC):
                            mm(ps, hmT[:, kc, mc * 128:(mc + 1) * 128], wb[:, kc, :], start=kc == 0, stop=kc == NKC - 1)
                        stg = T512.get()
                        cp(stg, ps)
                        if mks >= 6:
                            dma((mkp if cb == 0 else mvp)[l, mc * 128:(mc + 1) * 128, :], stg)
                        if cb == 1 and mks >= 7:
                            cp(mvb[:, l, mc, :], ps, eng="dve")

        if memkv:
            mem_kv()
        mset(LAY["P"]["QZ"], 0.0)

        def process_group(gi, grp):
            ptiles = [t for t in grp if t != "s"]
            has_s = "s" in grp
            assert not (has_s and ptiles)
            L = LAY["S" if has_s else "P"]
            NTM = L["NTM"]; HS = NTM + 2
            xT, hT, brT, l1T, l1vT = L["xT"], L["hT"], L["brT"], L["l1T"], L["l1vT"]
            if has_s:
                hpS = L["hpS"]; Z1 = L["Z1"]; Z1b = L["Z1b"]; Z2b = L["Z2b"]; QXM = L["QXM"]
                P.barrier()
                for zb in [Z1, Z1b, Z2b, QXM] + L["SBD"] + L["MBD"]:
                    mset(zb, 0.0)
                QM = Z1
            else:
                mkT, mvb, QZ = L["mkT"], L["mvb"], L["QZ"]
            qxi = [0]
            npt = len(ptiles)
            first_group = bool(ptiles) and ptiles[0] == 0
            pchunks = []
            c = 0
            while c < npt * 128:
                n = min(512, npt * 128 - c)
                pchunks.append((c, n))
                c += n
            scol = npt * 128
            chunks = pchunks + ([(scol, 128)] if has_s else [])

            def tkeys(c0, n, fn):
                return tuple(fn(li) for li in range(c0 // 128, (c0 + n + 127) // 128))

            def xTv(kc, c0, n):
                return V(xT.ap[:, kc, c0:c0 + n], tkeys(c0, n, xkey))

            def xTt(li):
                return V(xT.ap[:, :, li * 128:(li + 1) * 128], [xkey(li)])

            def hTv(kc, c0, n):
                return V(hT.ap[:, kc, 1 + c0:1 + c0 + n], tkeys(c0, n, hkey))

            def hprevv(kc, c0, n):
                if has_s and c0 == scol:
                    return hpS[:, kc, :]
                ks = tkeys(c0, n, hkey) + ((hkey(c0 // 128 - 1),) if c0 > 0 else (("hT", "c0"),))
                return V(hT.ap[:, kc, c0:c0 + n], ks)

            def brv(kc, c0, n):
                return V(brT.ap[:, kc, c0:c0 + n], tkeys(c0, n, lambda li: ("br", kc, li)))

            for li, t in enumerate(grp):
                xin = XIN.get()
                dma(xin, xs if t == "s" else xp[t * 128:(t + 1) * 128, :])
                for half in range(2):
                    ps = PS.get()
                    for q in range(4):
                        kc = half * 4 + q
                        tr(ps[:, q * 128:(q + 1) * 128], xin[:, kc * 128:(kc + 1) * 128], identF)
                    cp(V(xT.ap[:, half * 4:(half + 1) * 4, li * 128:(li + 1) * 128], [xkey(li)]), ps.pat([[128, 4], [1, 128]]))

            def rms_tile(li):
                sq = T1K.get()
                act(sq.pat([[128, 8], [1, 128]]), xTt(li), AF.Square)
                ps = PS.get()
                for kc in range(NKC):
                    mm(ps[:, 0:128], onesF, sq[:, kc * 128:(kc + 1) * 128], start=kc == 0, stop=kc == NKC - 1)
                rs = T128.get()
                act(rs, ps[:, 0:128], AF.Ln, bias=NORM_EPS, scale=1.0 / D)
                rstd = T128.get()
                act(rstd, rs, AF.Exp, scale=-0.5)
                return rstd

            def hgrn_layer(l, j):
                og = OG
                dma(og, hg_og[j:j + 1, :].partition_broadcast(128))
                lbT = LBC[:, 0:8]; omT = LBC[:, 8:16]; nomT = LBC[:, 16:24]
                nom_bc = BIGV[0]
                if j == 0:
                    mset(omT, 1.0); mset(nomT, -1.0); mset(nom_bc, -1.0)
                else:
                    a0 = TS8.get()[:, 0:8]; a1 = TS8.get()[:, 0:8]
                    dma(a0, hg_lb[0].rearrange("(h p) -> p h", p=128), slow=True)
                    dma(a1, hg_lb[1].rearrange("(h p) -> p h", p=128), slow=True)
                    tt(a1, a1, a0, SUB)
                    act(lbT, a1, AF.Sigmoid)
                    ts(omT, lbT, -1.0, MUL, 1.0, ADD)
                    ts(nomT, lbT, 1.0, SUB)
                    b0 = XIN.get()
                    dma(b0, hg_lb[1:2, :].partition_broadcast(128))
                    cp(nom_bc, b0)
                    b0 = XIN.get()
                    dma(b0, hg_lb[0:1, :].partition_broadcast(128))
                    tt(nom_bc, nom_bc, b0, SUB)
                    act(nom_bc, nom_bc, AF.Sigmoid)
                    ts(nom_bc, nom_bc, 1.0, SUB)
                for h in range(8):
                    wblk = WM.get()
                    for q in range(3):
                        dma(wblk[:, :, q * 128:(q + 1) * 128], wview(w_in[l], q * 1024 + h * 128, 128), eng="pool")
                    Sph = V(Sp[j].ap[:, h, :], [("Sp", j, h)])
                    if has_s:
                        S0q = L["S0q"]
                        for q in range(4):
                            dma(S0q[q].pat([[128, 4], [1, 128]]), sth[j, 4 * q:4 * q + 4, h].rearrange("s k v -> k s v"))
                    batched = (not has_s) and npt * 128 <= 512
                    if batched:
                        ng = npt * 128
                        q4s = T512.get(); f4s = T512.get(); qs4 = T512.get(); kT4 = T512.get()
                        psq4 = PS.get()
                        for kc in range(NKC):
                            mm(psq4[:, 0:ng], wblk[:, kc, 0:128], hTv(kc, 0, ng), start=kc == 0, stop=kc == NKC - 1)
                        psf4 = PS.get()
                        for kc in range(NKC):
                            mm(psf4[:, 0:ng], wblk[:, kc, 128:256], hTv(kc, 0, ng), start=kc == 0, stop=kc == NKC - 1)
                        act(q4s[:, 0:ng], psq4[:, 0:ng], AF.Sigmoid)
                        tt(qs4[:, 0:ng], q4s[:, 0:ng], psq4[:, 0:ng], MUL)
                        act(f4s[:, 0:ng], psf4[:, 0:ng], AF.Sigmoid)
                        ts(kT4[:, 0:ng], f4s[:, 0:ng], nomT[:, h:h + 1], MUL, omT[:, h:h + 1], ADD)
                    hsegB = None
                    for li, t in enumerate(grp):
                        samp = (t == "s")
                        c0 = li * 128
                        hsegA = P.begin_seg()
                        if hsegB is not None and PIPE:
                            P.merges.append((hsegB, hsegA))
                        IU = C("mb16")[:, 256:384] if samp else C("iu2")
                        BO = C("bo16") if samp else C("bo2")
                        if not batched:
                            psq = PS.get()
                            for kc in range(NKC):
                                mm(psq[:, 0:128], wblk[:, kc, 0:128], hTv(kc, c0, 128), start=kc == 0, stop=kc == NKC - 1)
                            for kc in range(NKC):
                                mm(psq[:, 128:256], wblk[:, kc, 128:256], hTv(kc, c0, 128), start=kc == 0, stop=kc == NKC - 1)
                        psfv = PS.get()
                        for kc in range(NKC):
                            mm(psfv[:, 0:256], hTv(kc, c0, 128), wblk[:, kc, 128:384], start=kc == 0, stop=kc == NKC - 1)
                        if batched:
                            qs = qs4[:, c0:c0 + 128]; kT = kT4[:, c0:c0 + 128]
                        else:
                            qsg = T128.get(); act(qsg, psq[:, 0:128], AF.Sigmoid)
                            qs = T128.get(); tt(qs, qsg, psq[:, 0:128], MUL)
                            sgT = T128.get(); act(sgT, psq[:, 128:256], AF.Sigmoid)
                            kT = T128.get(); ts(kT, sgT, nomT[:, h:h + 1], MUL, omT[:, h:h + 1], ADD)
                        sg = T128.get(); act(sg, psfv[:, 0:128], AF.Sigmoid)
                        kt = T128.get(); stt(kt, sg, -1.0, nom_bc[:, h * 128:(h + 1) * 128], ADD, MUL)
                        g = T128.get(); act(g, kt, AF.Ln, bias=1.0, scale=-1.0)
                        vt = T128.get(); cp(vt, psfv[:, 128:256])
                        ps3 = PS.get()
                        mm(ps3[:, 0:128], IU, g)
                        mm(ps3[:, 128:256], g, IU)
                        mm(ps3[:, 256:384], BO, g)
                        eb = T128.get(); act(eb, ps3[:, 128:256], AF.Exp)
                        enb = T128.get(); act(enb, ps3[:, 128:256], AF.Exp, scale=-1.0)
                        qtT = T128.get(); tt(qtT, qs, eb, MUL)
                        ktT = T128.get(); tt(ktT, kT, enb, MUL)
                        bsb = T128.get(); cp(bsb, ps3[:, 0:128])
                        dd = T128.get(); tt(dd, ps3[:, 256:384], bsb, SUB)
                        ed = T128.get(); act(ed, dd, AF.Exp)
                        psA = PS.get()
                        mm(psA[:, 0:128], ktT, qtT)
                        att = T128.get(); tt(att, psA[:, 0:128], IU, MUL)
                        P.end_seg()
                        hsegB = P.begin_seg()
                        psO = PS.get()
                        mm(psO[:, 0:128], att, vt, start=True, stop=False)
                        if not samp:
                            tt(QZ.pat([[192, 2], [1, 64]]), qs.pat([[64, 2], [1, 64]]), eb.pat([[64, 2], [1, 64]]), MUL)
                            kh0 = T128.get(); stt(kh0, kt, C("cm0"), ed, MUL, MUL)
                            kh1 = T128.get(); stt(kh1, kt, C("cm1"), ed, MUL, MUL)
                            psS = PS.get()
                            mm(psS[:, 0:128], kh0, vt)
                            mm(psS[:, 128:256], kh1, vt)
                            mm(psO[:, 0:128], QZ[:, 0:128], Sph, start=False, stop=False)
                            S1 = T128.get(); stt(S1, Sph, eb[:, 63:64], psS[:, 0:128], MUL, ADD)
                            mm(psO[:, 0:128], QZ[:, 128:256], S1, start=False, stop=True)
                            stt(Sph, S1, eb[:, 127:128], psS[:, 128:256], MUL, ADD)
                        else:
                            tt(QM.pat([[136, 16], [1, 8]]), qs.pat([[8, 16], [1, 8]]), eb.pat([[8, 16], [1, 8]]), MUL)
                            for s in range(NS):
                                mm(psO[:, 0:128], QM[:, s * 128:(s + 1) * 128], S0q[s // 4][:, (s % 4) * 128:(s % 4 + 1) * 128], start=False, stop=(s == NS - 1))
                            kh = T128.get(); tt(kh, kt, ed, MUL)
                            for q in range(4):
                                vm = T512.get()
                                tt(vm.pat([[128, 4], [1, 128]]), vt.pat([[0, 4], [1, 128]]), C("sel16").pat([[1, 4], [0, 128]], off=4 * q), MUL)
                                psW = PS.get()
                                mm(psW, kh, vm)
                                tmp = T512.get()
                                tt(tmp.pat([[128, 4], [1, 128]]), S0q[q].pat([[128, 4], [1, 128]]),
                                   eb.pat([[8, 4], [0, 128]], off=7 + 32 * q), MUL)
                                Sn = T512.get()
                                tt(Sn, tmp, psW, ADD)
                                dma(shs[j, 4 * q:4 * q + 4, h].rearrange("s k v -> k s v"), Sn.pat([[128, 4], [1, 128]]))
                        ss = TS8.get()
                        junk = T128.get()
                        act(junk, psO[:, 0:128], AF.Square, acc=ss[:, 0:1])
                        ts(ss[:, 1:2], ss[:, 0:1], 1.0 / 128, MUL, NORM_EPS, ADD)
                        act(ss[:, 2:3], ss[:, 1:2], AF.Ln)
                        act(ss[:, 3:4], ss[:, 2:3], AF.Exp, scale=-0.5)
                        on = T128.get(); stt(on, psO[:, 0:128], ss[:, 3:4], og, MUL, MUL)
                        psT = PS.get()
                        tr(psT[:, 0:128], on, identF)
                        cp(brv(h, c0, 128), psT[:, 0:128])
                        P.end_seg()
                if first_last_prompt_group:
                    for h in range(8):
                        dma(shp[j, h], V(Sp[j].ap[:, h, :], [("Sp", j, h)]))

            def rest_layer(l):
                if 2 not in phases:
                    return
                wq = WR.get()
                dma(wq, wview(w_in[l], 3072, 512), eng="pool")
                for hx in range(XH):
                    qxT = L['qx'][qxi[0] % 2]; qxi[0] += 1
                    for (c0, n) in chunks:
                        ps = PS.get()
                        for kc in range(NKC):
                            mm(ps[:, 0:n], wq[:, kc, hx * 128:(hx + 1) * 128], hTv(kc, c0, n), start=kc == 0, stop=kc == NKC - 1)
                        cp(qxT[:, c0:c0 + n], ps[:, 0:n])
                    if has_s:
                        KVB = L["KVB"]
                        kvi = [0]
                        def kvget():
                            b = KVB[kvi[0] % 3]; kvi[0] += 1
                            return b
                        mkTs = []; vbs = []
                        for hf in range(2):
                            kb = kvget()
                            dma(kb.pat([[256, 8], [128, 2], [1, 128]]), ck[l][8 * hf:8 * hf + 8, :, hx * 128:(hx + 1) * 128].rearrange("s (mc p) d -> p s mc d", p=128), eng="pool")
                            mk_ = kvget()
                            for b4 in range(2):
                                psb = bfv(PS.get())
                                for q in range(8):
                                    idx = b4 * 8 + q
                                    tr(psb[:, q * 128:(q + 1) * 128], kb[:, idx * 128:(idx + 1) * 128], identB)
                                cp(mk_[:, b4 * 1024:(b4 + 1) * 1024], psb)
                            mkTs.append(mk_)
                    xseg_prev = None
                    for li, t in enumerate(grp):
                        samp = (t == "s")
                        c0 = li * 128
                        xseg = P.begin_seg()
                        if PIPE and xseg_prev is not None and li % 2 == 1:
                            P.merges.append((xseg_prev, xseg))
                        xseg_prev = xseg
                        ps = PS.get()
                        if not samp:
                            mm(ps[:, 0:256], qxT[:, c0:c0 + 128], mkT[:, l, hx, :])
                        else:
                            cp(QXM.pat([[136, 16], [1, 8]]), qxT[:, c0:c0 + 128].pat([[8, 16], [1, 8]]), eng="dve")
                            for s in range(NS):
                                mm(ps[:, 0:256], QXM[:, s * 128:(s + 1) * 128], mkTs[s // 8][:, (s % 8) * 256:(s % 8 + 1) * 256], start=(s == 0), stop=(s == NS - 1))
                        ss = TS8.get()
                        red(ss[:, 0:1], ps[:, 0:256], MAX, negate=True)
                        ts(ss[:, 1:2], ss[:, 0:1], SC_ATT, MUL)
                        pe_ = T512.get()
                        act(pe_[:, 0:256], ps[:, 0:256], AF.Exp, bias=ss[:, 1:2], scale=SC_ATT, acc=ss[:, 2:3])
                        recip(ss[:, 3:4], ss[:, 2:3])
                        pn = TB256.get()
                        ts(pn, pe_[:, 0:256], ss[:, 3:4], MUL)
                        psb = bfv(PS.get())
                        tr(psb[:, 0:128], pn[:, 0:128], identB)
                        tr(psb[:, 128:256], pn[:, 128:256], identB)
                        pT = TB256.get()
                        cp(pT, psb[:, 0:256])
                        psO = PS.get()
                        if not samp:
                            for mc in range(2):
                                mm(psO[:, 0:128], mvb[:, l, mc, hx * 128:(hx + 1) * 128], pT[:, mc * 128:(mc + 1) * 128], start=mc == 0, stop=mc == 1)
                        else:
                            for hf in range(2):
                                vb_ = kvget()
                                dma(vb_.pat([[256, 8], [128, 2], [1, 128]]), cv[l][8 * hf:8 * hf + 8, :, hx * 128:(hx + 1) * 128].rearrange("s (mc p) d -> p s mc d", p=128), eng="pool")
                                vbs.append(vb_)
                            for s in range(NS):
                                for mc in range(2):
                                    vb = vbs[s // 8]
                                    mm(psO[:, s * 8:(s + 1) * 8], vb[:, ((s % 8) * 2 + mc) * 128:((s % 8) * 2 + mc + 1) * 128],
                                       pT[:, mc * 128 + s * 8:mc * 128 + (s + 1) * 8], start=mc == 0, stop=mc == 1)
                        cp(brv(8 + hx, c0, 128), psO[:, 0:128])
                        P.end_seg()
                if 3 not in phases:
                    return
                for gb in range(3):
                    wg = WR.get()
                    dma(wg, wview(w_in[l], 3584 + gb * 512, 512), eng="pool")
                    for sub in range(4):
                        bi = gb * 4 + sub
                        for (c0, n) in chunks:
                            ps = PS.get()
                            for kc in range(NKC):
                                mm(ps[:, 0:n], wg[:, kc, sub * 128:(sub + 1) * 128], hTv(kc, c0, n), start=kc == 0, stop=kc == NKC - 1)
                            sgt = TB512.get()
                            act(sgt[:, 0:n], ps[:, 0:n], AF.Sigmoid)
                            tt(sgt[:, 0:n], sgt[:, 0:n], ps[:, 0:n], MUL)
                            tt(brv(bi, c0, n), brv(bi, c0, n), sgt[:, 0:n], MUL)
                if 4 not in phases:
                    return
                for ob in range(NKC):
                    wo = WO.get()
                    dma(wo, w_out[l][:, ob * 128:(ob + 1) * 128].rearrange("(kc p) c -> p kc c", p=128), eng="pool")
                    for (c0, n) in chunks:
                        ps = PS.get()
                        for kc in range(12):
                            mm(ps[:, 0:n], wo[:, kc, :], brv(kc, c0, n), start=kc == 0, stop=kc == 11)
                        tt(xTv(ob, c0, n), xTv(ob, c0, n), ps[:, 0:n], ADD)

            def rwkv_layer(l, j):
                m = j - 1
                cfgn = "16" if has_s else "1"
                MB = C("mb" + cfgn); SL = C("sl" + cfgn)
                TIC = C("tic" + cfgn); TSC = C("tsc" + cfgn); BOC = C("boc" + cfgn)
                LEV = 3 if has_s else 7
                if levo is not None:
                    LEV = levo
                muT = MUC[:, 0:40].pat([[8, 5], [1, 8]]); omuT = MUC[:, 40:80].pat([[8, 5], [1, 8]])
                dma(muT, rw_mu[j].rearrange("k (kc p) -> p k kc", p=128), slow=True)
                ts(omuT, muT, -1.0, MUL, 1.0, ADD)
                raw = WR.get()
                dma(raw[:, :, 0:64], rw_w1[j].rearrange("(kc p) c -> p kc c", p=128), eng="pool")
                dma(raw[:, :, 64:128], rw_a1[j].rearrange("(kc p) c -> p kc c", p=128), eng="pool")
                if j >= 1:
                    mset(raw[:, :, 160:256], 0.0)
                    dma(raw[:, :, 128:160], rw_v1[m].rearrange("(kc p) c -> p kc c", p=128), eng="pool")
                for (c_lo, kind) in ((0, 1), (64, 4)):
                    src = raw[:, :, c_lo:c_lo + 64]
                    tt(L1a[:, :, c_lo:c_lo + 64], src, omuT[:, kind, :].pat([[1, 8], [0, 64]]), MUL)
                    tt(L1b[:, :, c_lo:c_lo + 64], src, muT[:, kind, :].pat([[1, 8], [0, 64]]), MUL)
                if j >= 1:
                    src = raw[:, :, 128:256]
                    tt(L1va, src, omuT[:, 3, :].pat([[1, 8], [0, 128]]), MUL)
                    tt(L1vb, src, muT[:, 3, :].pat([[1, 8], [0, 128]]), MUL)
                W2 = BIGV[0]; V2 = BIGV[1]
                xin = XIN.get()
                dma(xin[0:64, :], rw_w2[j]); dma(xin[64:128, :], rw_a2[j])
                cp(W2, xin)
                if j >= 1:
                    xin = XIN.get()
                    mset(xin, 0.0)
                    dma(xin[0:32, :], rw_v2[m])
                    cp(V2, xin, eng="dve")
                for (c0, n) in chunks:
                    ps = PS.get()
                    for kc in range(NKC):
                        mm(ps[:, 0:n], L1a[:, kc, :], hTv(kc, c0, n), start=kc == 0, stop=False)
                    for kc in range(NKC):
                        mm(ps[:, 0:n], L1b[:, kc, :], hprevv(kc, c0, n), start=False, stop=kc == NKC - 1)
                    th = T512.get()
                    act(th[:, 0:n], ps[:, 0:n], AF.Tanh)
                    ts(V(l1T.ap[:, 0, c0:c0 + n], tkeys(c0, n, lambda li: ("l1", li))), th[:, 0:n], C("hw"), MUL)
                    ts(V(l1T.ap[:, 1, c0:c0 + n], tkeys(c0, n, lambda li: ("l1", li))), ps[:, 0:n], C("ha"), MUL)
                    if j >= 1:
                        ps = PS.get()
                        for kc in range(NKC):
                            mm(ps[:, 0:n], L1va[:, kc, :], hTv(kc, c0, n), start=kc == 0, stop=False)
                        for kc in range(NKC):
                            mm(ps[:, 0:n], L1vb[:, kc, :], hprevv(kc, c0, n), start=False, stop=kc == NKC - 1)
                        cp(V(l1vT.ap[:, c0:c0 + n], tkeys(c0, n, lambda li: ("l1v", li))), ps[:, 0:n])

                for pr in range(8):
                    cc = pr * 128
                    wa = WM.get(); wb = WM2.get()
                    for q in range(3):
                        dma(wa[:, :, q * 128:(q + 1) * 128], wview(w_in[l], q * 1024 + cc, 128), eng="pool")
                    for q, kind in enumerate((0, 2, 3)):
                        tt(wb[:, :, q * 128:(q + 1) * 128], wa[:, :, q * 128:(q + 1) * 128], muT[:, kind, :].pat([[1, 8], [0, 128]]), MUL)
                    tt(wa, wa, wb, SUB)
                    pv = PVS[pr % 2]
                    vecs = [rw_w0[j:j + 1], rw_a0[j:j + 1], (rw_v0[m:m + 1] if j >= 1 else rw_a0[j:j + 1]), rw_kk[j:j + 1], rw_ka[j:j + 1], rw_rk[j:j + 1], rw_lg[j:j + 1], rw_lb[j:j + 1]]
                    for k8, vv in enumerate(vecs):
                        dma(pv[:, k8, :], vv[:, cc:cc + 128].partition_broadcast(128))
                    Mpp = V(Mp[j].ap[:, pr, :], [("Mp", j, pr)], True)
                    MppB = V(MpB[j].ap[:, pr, :], [("MpB", j, pr)])
                    if has_s:
                        M0b = L["M0b"]
                        M0all = L["M0all"]
                        for q in range(4):
                            SBD = L["SBD"][q % 2]
                            for hh in range(2):
                                dma(SBD[hh * 64:(hh + 1) * 64, :].pat([[128, 4], [1, 64]], off=hh * 64),
                                    strw[j, 4 * q:4 * q + 4, 2 * pr + hh].rearrange("s i k -> i s k"))
                            ps = PS.get()
                            for s4 in range(4):
                                tr(ps[:, s4 * 128:(s4 + 1) * 128], SBD[:, s4 * 128:(s4 + 1) * 128], identF)
                            for hh in range(2):
                                cp(M0all[hh * 64:(hh + 1) * 64, 4 * q:4 * q + 4, :], ps[hh * 64:(hh + 1) * 64, :].pat([[128, 4], [1, 64]], off=hh * 64),
                                   eng=("act" if hh == 0 else "dve"))
                                cp(M0b[hh * 64:(hh + 1) * 64, 4 * q:4 * q + 4, :], ps[hh * 64:(hh + 1) * 64, :].pat([[128, 4], [1, 64]], off=hh * 64),
                                   eng=("dve" if hh == 0 else "act"))
                    segB_prev = None
                    for li, t in enumerate(grp):
                        samp = (t == "s")
                        c0 = li * 128
                        gt = 16 if samp else t
                        segA = P.begin_seg()
                        if segB_prev is not None and PIPE:
                            P.merges.append((segB_prev, segA))
                        psR = PS.get()
                        for kc in range(NKC):
                            mm(psR[:, 0:384], hTv(kc, c0, 128), wa[:, kc, :], start=kc == 0, stop=False)
                        for kc in range(NKC):
                            mm(psR[:, 0:384], hprevv(kc, c0, 128), wb[:, kc, :], start=False, stop=kc == NKC - 1)
                        rP, kP, vP = psR[:, 0:128], psR[:, 128:256], psR[:, 256:384]
                        l1k = ("l1", li)
                        psL = PS.get()
                        mm(psL[:, 0:128], V(l1T.ap[:, 0, c0:c0 + 128], [l1k]), W2[:, cc:cc + 128])
                        mm(psL[:, 128:256], V(l1T.ap[:, 1, c0:c0 + 128], [l1k]), W2[:, cc:cc + 128])
                        if j >= 1:
                            mm(psL[:, 256:384], V(l1vT.ap[:, c0:c0 + 128], [("l1v", li)]), V2[:, cc:cc + 128])
                        npre = 384 if j >= 1 else 256
                        pre = T512.get(); tt(pre[:, 0:npre], psL[:, 0:npre], pv.pat([[1, npre]]), ADD)
                        sgs = T512.get(); act(sgs[:, 0:npre], pre[:, 0:npre], AF.Sigmoid)
                        sg = sgs[:, 0:128]; ag = sgs[:, 128:256]
                        Vt = BT128.get()
                        vkey = ("vfd", gt, pr)
                        if j == 0:
                            cp(Vt, vP)
                            P.op("sp", (lambda e, o=vfd[gt * 128:(gt + 1) * 128, cc:cc + 128], i_=Vt.ap: e.dma_start(out=o, in_=i_)),
                                 reads=list(Vt.keys), writes=[vkey], dma=True)
                        else:
                            vf = NRV.get()
                            P.op("sp", (lambda e, i_=vfd[gt * 128:(gt + 1) * 128, cc:cc + 128], o=vf.ap: e.dma_start(out=o, in_=i_)),
                                 reads=[vkey], writes=list(vf.keys), dma=True)
                            vg = sgs[:, 256:384]
                            d1 = T128.get(); tt(d1, vf, vP, SUB)
                            tt(d1, d1, vg, MUL)
                            tt(Vt, d1, vP, ADD)
                        kk = T128.get(); tt(kk, kP, pv[:, 3, :], MUL)
                        sq = T128.get(); tt(sq, kk, kk, MUL)
                        sm = TS8.get()
                        red(sm[:, 0:2], sq.pat([[64, 2], [1, 64]]), ADD)
                        act(sm[:, 2:4], sm[:, 0:2], AF.Ln)
                        act(sm[:, 4:6], sm[:, 2:4], AF.Exp, scale=-0.5)
                        ts(sm[:, 4:6], sm[:, 4:6], 1e12, ALU.min)
                        kkn = T128.get()
                        tt(kkn.pat([[64, 2], [1, 64]]), kk.pat([[64, 2], [1, 64]]), sm[:, 4:6].pat([[1, 2], [0, 64]]), MUL)
                        t1 = T128.get(); stt(t1, ag, -1.0, pv[:, 4, :], ADD, MUL)
                        BK = T256.get()
                        kp = BK[:, 128:256]; stt(kp, t1, 1.0, kP, ADD, MUL)
                        bb = BK[:, 0:128]; tt(bb, kkn, ag, MUL)
                        t2 = T128.get(); tt(t2, rP, kp, MUL)
                        tt(t2, t2, pv[:, 5, :], MUL)
                        red(sm[:, 6:8], t2.pat([[64, 2], [1, 64]]), ADD)
                        psC = PS.get()
                        mm(psC[:, 0:128], TIC, sg)
                        mm(psC[:, 128:256], TSC, sg)
                        mm(psC[:, 256:384], BOC, sg)
                        nsq = 16 if samp else 1
                        nsq = 16 if samp else 2
                        mm(psC[:, 384:384 + nsq], sg, (C("sel16c") if samp else C("negc")))
                        EE = T512.get(); act(EE[:, 0:384], psC[:, 0:384], AF.Exp)
                        E1 = EE[:, 0:128]; E3 = EE[:, 128:256]; E5 = EE[:, 256:384]
                        E2 = T128.get(); act(E2, psC[:, 0:128], AF.Exp, scale=-1.0)
                        gC = TS8s.get(); act(gC[:, 0:nsq], psC[:, 384:384 + nsq], AF.Exp)
                        dg = [[192, 2], [1, 64]]; nd = [[64, 2], [1, 64]]
                        Am, Rm = ZP; Bm = ZBK[:, 0:256]; Km = ZBK[:, 256:512]
                        ZH = ZBKH[li % 2]; BHm = ZH[:, 0:256]; KHm = ZH[:, 256:512]
                        dg2 = [[256, 2], [192, 2], [1, 64]]
                        stt(Am.pat(dg), kkn.pat(nd), -1.0, E3.pat(nd), MUL, MUL)
                        tt(ZBK.pat(dg2), BK.pat([[128, 2], [64, 2], [1, 64]]), E2.pat([[0, 2], [64, 2], [1, 64]]), MUL)
                        tt(Rm.pat(dg), rP.pat(nd), E1.pat(nd), MUL)
                        tt(ZH.pat(dg2), ZBK.pat(dg2), E5.pat([[0, 2], [64, 2], [1, 64]]), MUL)
                        XT = BT1K.get()
                        psb = bfv(PS.get())
                        for q, Xm in enumerate((Am, Am, Bm, Bm, Km, Km, Rm, Rm)):
                            hh = q % 2
                            tr(psb[:, q * 128:(q + 1) * 128], Xm[:, hh * 128:(hh + 1) * 128], identB)
                        cp(XT[:, 0:512], psb[:, 0:512]); cp(XT[:, 512:1024], psb[:, 512:1024], eng="dve")
                        AT = lambda hh: XT[:, (0 + hh) * 128:(1 + hh) * 128]
                        BT = lambda hh: XT[:, (2 + hh) * 128:(3 + hh) * 128]
                        KT = lambda hh: XT[:, (4 + hh) * 128:(5 + hh) * 128]
                        RT = lambda hh: XT[:, (6 + hh) * 128:(7 + hh) * 128]
                        Mk = []; XY = BT512.get(); WT = BT256.get()
                        for hh in range(2):
                            psM = PS.get()
                            mm(psM[:, 0:128], BT(hh), AT(hh))
                            mm(psM[:, 128:256], KT(hh), AT(hh))
                            mm(psM[:, 256:384], BT(hh), RT(hh))
                            mm(psM[:, 384:512], KT(hh), RT(hh))
                            mk_ = MK[2 * (li % 2) + hh]; tt(mk_, psM, MB, MUL)
                            Mk.append(mk_)
                            psN = PS.get()
                            mm(psN[:, 0:128], AT(hh), BT(hh))
                            tt(XY[:, hh * 256 + 128:hh * 256 + 256], psN[:, 0:128], SL, MUL)
                            tt(WT[:, hh * 128:(hh + 1) * 128], mk_[:, 0:128], identF, ADD)
                        for lev in range(1, LEV):
                            psV = PS.get()
                            for hh in range(2):
                                Xp = Mk[hh][:, 0:128] if lev == 1 else XY[:, hh * 256:hh * 256 + 128]
                                Yp = XY[:, hh * 256 + 128:hh * 256 + 256]
                                mm(psV[:, hh * 256 + 128:hh * 256 + 256], Xp, Yp)
                                if lev < LEV - 1:
                                    mm(psV[:, hh * 256:hh * 256 + 128], Yp, Xp)
                            XYn = BT512.get()
                            if lev < LEV - 1:
                                cp(XYn, psV)
                            else:
                                cp(XYn.pat([[256, 2], [1, 128]], off=128), psV.pat([[256, 2], [1, 128]], off=128))
                            psW = PS.get()
                            for hh in range(2):
                                mm(psW[:, hh * 128:(hh + 1) * 128], XYn[:, hh * 256 + 128:hh * 256 + 256], WT[:, hh * 128:(hh + 1) * 128])
                            WTn = BT256.get()
                            tt(WTn, psW[:, 0:256], WT, ADD)
                            XY = XYn; WT = WTn
                        P.end_seg()
                        segB_prev = P.begin_seg()
                        if samp:
                            cp(Z1b.pat([[136, 16], [1, 8]]), AT(0).pat([[8, 16], [1, 8]]), eng="act")
                            cp(Z2b.pat([[136, 16], [1, 8]]), AT(1).pat([[8, 16], [1, 8]]), eng="dve")
                        psP = PS.get()
                        for hh in range(2):
                            o_ = psP[:, hh * 64:(hh + 1) * 64]
                            if not samp:
                                mm(o_, AT(hh), MppB, start=True, stop=False)
                            else:
                                ZZ = Z1b if hh == 0 else Z2b
                                for s in range(NS):
                                    mm(o_, ZZ[:, s * 128:(s + 1) * 128], M0b[:, s, :], start=(s == 0), stop=False)
                            mm(o_, Mk[hh][:, 128:256], Vt[:, hh * 64:(hh + 1) * 64], start=False, stop=True)
                        P1 = BT128.get(); cp(P1, psP[:, 0:128])
                        psU = PS.get()
                        for hh in range(2):
                            mm(psU[:, hh * 64:(hh + 1) * 64], WT[:, hh * 128:(hh + 1) * 128], P1[:, hh * 64:(hh + 1) * 64])
                        U = BT128.get(); cp(U, psU[:, 0:128], eng="dve")
                        if samp:
                            cp(Z1b.pat([[136, 16], [1, 8]]), RT(0).pat([[8, 16], [1, 8]]), eng="act")
                            cp(Z2b.pat([[136, 16], [1, 8]]), RT(1).pat([[8, 16], [1, 8]]), eng="dve")
                        psY = PS.get()
                        for hh in range(2):
                            o_ = psY[:, hh * 64:(hh + 1) * 64]
                            if not samp:
                                mm(o_, RT(hh), MppB, start=True, stop=False)
                            else:
                                ZZ = Z1b if hh == 0 else Z2b
                                for s in range(NS):
                                    mm(o_, ZZ[:, s * 128:(s + 1) * 128], M0b[:, s, :], start=(s == 0), stop=False)
                            mm(o_, Mk[hh][:, 256:384], U[:, hh * 64:(hh + 1) * 64], start=False, stop=False)
                            mm(o_, Mk[hh][:, 384:512], Vt[:, hh * 64:(hh + 1) * 64], start=False, stop=True)
                        if not samp:
                            psS = PS.get()
                            mm(psS[:, 0:64], BHm[:, 0:128], U[:, 0:64], start=True, stop=False)
                            mm(psS[:, 0:64], BHm[:, 128:256], U[:, 64:128], start=False, stop=False)
                            mm(psS[:, 0:64], KHm[:, 0:128], Vt[:, 0:64], start=False, stop=False)
                            mm(psS[:, 0:64], KHm[:, 128:256], Vt[:, 64:128], start=False, stop=True)
                            stt(Mpp, Mpp, gC[:, 0:1], psS[:, 0:64], MUL, ADD)
                            cp(MppB, Mpp)
                        else:
                            for q in range(2):
                                Uw = [BT512.get(), BT512.get()]; Vw = [BT512.get(), BT512.get()]
                                for hh in range(2):
                                    tt(Uw[hh].pat([[64, 8], [1, 64]]), U[:, hh * 64:(hh + 1) * 64].pat([[0, 8], [1, 64]]),
                                       C("sel16").pat([[1, 8], [0, 64]], off=8 * q), MUL)
                                    tt(Vw[hh].pat([[64, 8], [1, 64]]), Vt[:, hh * 64:(hh + 1) * 64].pat([[0, 8], [1, 64]]),
                                       C("sel16").pat([[1, 8], [0, 64]], off=8 * q), MUL)
                                psS = PS.get()
                                mm(psS, BHm[:, 0:128], Uw[0], start=True, stop=False)
                                mm(psS, BHm[:, 128:256], Uw[1], start=False, stop=False)
                                mm(psS, KHm[:, 0:128], Vw[0], start=False, stop=False)
                                mm(psS, KHm[:, 128:256], Vw[1], start=False, stop=True)
                                tmp = T512.get()
                                tt(tmp.pat([[64, 8], [1, 64]]), M0all[:, 8 * q:8 * q + 8, :], gC[:, 8 * q:8 * q + 8].pat([[1, 8], [0, 64]]), MUL)
                                Mn = T512.get()
                                tt(Mn, tmp, psS, ADD)
                                for q2 in range(2):
                                    MBD = L["MBD"][q2]
                                    for hh in range(2):
                                        cp(MBD[hh * 64:(hh + 1) * 64, :].pat([[128, 4], [1, 64]], off=hh * 64),
                                           Mn[hh * 64:(hh + 1) * 64, q2 * 256:(q2 + 1) * 256].pat([[64, 4], [1, 64]]), eng=("act" if hh == 0 else "dve"))
                                    ps = PS.get()
                                    for s4 in range(4):
                                        tr(ps[:, s4 * 128:(s4 + 1) * 128], MBD[:, s4 * 128:(s4 + 1) * 128], identF)
                                    So = T256.get()
                                    for hh in range(2):
                                        cp(So[hh * 64:(hh + 1) * 64, :].pat([[64, 4], [1, 64]]),
                                           ps[hh * 64:(hh + 1) * 64, :].pat([[128, 4], [1, 64]], off=hh * 64), eng=("act" if hh == 0 else "dve"))
                                    sb0 = 8 * q + 4 * q2
                                    for hh in range(2):
                                        dma(srs[j, sb0:sb0 + 4, 2 * pr + hh].rearrange("s i k -> i s k"),
                                            So[hh * 64:(hh + 1) * 64, :].pat([[64, 4], [1, 64]]))
                        g2 = [[64, 2], [1, 64]]
                        red(sm[:, 8:10], psY[:, 0:128].pat(g2), ADD)
                        ysq = T128.get(); act(ysq, psY[:, 0:128], AF.Square)
                        red(sm[:, 10:12], ysq.pat(g2), ADD)
                        ts(sm[:, 12:14], sm[:, 8:10], 1.0 / 64, MUL)
                        tt(sm[:, 14:16], sm[:, 12:14], sm[:, 12:14], MUL)
                        stt(sm[:, 16:18], sm[:, 10:12], 1.0 / 64, sm[:, 14:16], MUL, SUB)
                        act(sm[:, 18:20], sm[:, 16:18], AF.Ln, bias=LNX_EPS, scale=1.0)
                        act(sm[:, 20:22], sm[:, 18:20], AF.Exp, scale=-0.5)
                        yc = T128.get()
                        tt(yc.pat(g2), psY[:, 0:128].pat(g2), sm[:, 12:14].pat([[1, 2], [0, 64]]), SUB)
                        tt(yc.pat(g2), yc.pat(g2), sm[:, 20:22].pat([[1, 2], [0, 64]]), MUL)
                        tt(yc, yc, pv[:, 6, :], MUL)
                        tt(yc, yc, pv[:, 7, :], ADD)
                        yb = T128.get()
                        tt(yb.pat(g2), Vt.pat(g2), sm[:, 6:8].pat([[1, 2], [0, 64]]), MUL)
                        tt(yc, yc, yb, ADD)
                        psT = PS.get()
                        tr(psT[:, 0:128], yc, identF)
                        cp(brv(pr, c0, 128), psT[:, 0:128])
                        P.end_seg()
                if first_last_prompt_group:
                    for pr in range(8):
                        MBD = PMBD
                        Mpp = V(Mp[j].ap[:, pr, :], [("Mp", j, pr)])
                        for hh in range(2):
                            cp(MBD[hh * 64:(hh + 1) * 64, hh * 64:(hh + 1) * 64], Mpp[hh * 64:(hh + 1) * 64, :], eng=("act" if hh == 0 else "pool"))
                        ps = PS.get()
                        tr(ps[:, 0:128], MBD, identF)
                        So = T128.get()
                        for hh in range(2):
                            cp(So[hh * 64:(hh + 1) * 64, 0:64], ps[hh * 64:(hh + 1) * 64, hh * 64:(hh + 1) * 64], eng=("act" if hh == 0 else "dve"))
                        for hh in range(2):
                            dma(srp[j, 2 * pr + hh], So[hh * 64:(hh + 1) * 64, 0:64])

            first_last_prompt_group = (15 in ptiles)
            for l in range(depth):
                is_rw = (l % 2 == 1)
                j = l // 2
                gcol = TS8.get()[:, 0:8]
                dma(gcol, norm_g[l].rearrange("(kc p) -> p kc", p=128), slow=True)
                if is_rw:
                    if first_group:
                        mset(V(hT.ap[:, :, 0:1], [("hT", "c0")]), 0.0)
                    elif npt > 0:
                        cp(V(hT.ap[:, :, 0], [("hT", "c0")]), hcar[j], eng="pool")
                for li, t in enumerate(grp):
                    rstd = rms_tile(li)
                    for kc in range(NKC):
                        stt(V(hT.ap[:, kc, 1 + li * 128:1 + (li + 1) * 128], [hkey(li)]), xTv(kc, li * 128, 128), gcol[:, kc:kc + 1], rstd, MUL, MUL)
                    if is_rw and t == 15:
                        hl = TS8.get()[:, 0:8]
                        ts(hl, V(xT.ap[:, :, li * 128 + 127], [xkey(li)]), rstd[:, 127:128], MUL)
                        tt(hl, hl, gcol, MUL)
                        dma(ssp[j].rearrange("(kc p) -> p kc", p=128), hl, slow=True)
                    if is_rw and t == "s":
                        hl = T128.get()
                        hl3 = hl.pat([[16, 8], [1, 16]])
                        xv = xTt(li).pat([[NTM, 8], [8, 16]], off=7)
                        rv = rstd.pat([[0, 8], [8, 16]], off=7)
                        tt(hl3, xv, rv, MUL)
                        tt(hl3, hl3, gcol.pat([[1, 8], [0, 16]]), MUL)
                        ps = PS.get()
                        tr(ps[:, 0:128], hl, identF)
                        ho = T128.get()
                        cp(ho, ps[:, 0:128])
                        for kc in range(NKC):
                            dma(sss[j, :, kc * 128:(kc + 1) * 128], ho[kc * 16:(kc + 1) * 16, :])
                        stin = XIN.get()
                        mset(stin, 0.0)
                        dma(stin[0:16, :], sts[j])
                        for half in range(2):
                            ps = PS.get()
                            for q in range(4):
                                kc = half * 4 + q
                                tr(ps[:, q * 128:(q + 1) * 128], stin[:, kc * 128:(kc + 1) * 128], identF)
                            cp(hpS.pat([[128, 4], [8, 16]], off=half * 4 * 128), ps.pat([[128, 4], [1, 16]]))
                        hsv = V(hT.ap[:, :, 1 + li * 128:1 + (li + 1) * 128], [hkey(li)])
                        cp(hpS.pat([[128, 8], [8, 16], [1, 7]], off=1), hsv.pat([[HS, 8], [8, 16], [1, 7]]), eng="pool")
                if is_rw and npt > 0:
                    cp(hcar[j], V(hT.ap[:, :, npt * 128], [hkey(npt - 1)]), eng="pool")
                if 1 in phases:
                    if not is_rw:
                        hgrn_layer(l, j)
                    else:
                        rwkv_layer(l, j)
                rest_layer(l)

            fg = TS8.get()[:, 0:8]
            dma(fg, final_g[0].rearrange("(kc p) -> p kc", p=128), slow=True)
            for li, t in enumerate(grp):
                rstd = rms_tile(li)
                yT = T1K.get()
                for kc in range(NKC):
                    stt(yT[:, kc * 128:(kc + 1) * 128], xTv(kc, li * 128, 128), fg[:, kc:kc + 1], rstd, MUL, MUL)
                yo = T1K.get()
                for half in range(2):
                    ps = PS.get()
                    for q in range(4):
                        kc = half * 4 + q
                        tr(ps[:, q * 128:(q + 1) * 128], yT[:, kc * 128:(kc + 1) * 128], identF)
                    cp(yo[:, half * 512:(half + 1) * 512], ps)
                dma(ys if t == "s" else yp[t * 128:(t + 1) * 128, :], yo)

        for gi, grp in enumerate(groups):
            process_group(gi, grp)

        info = P.emit()
    return nc, info


def _shard_inputs(inp):
    cf, cb = make_consts()
    maps = []
    for c in range(8):
        b = slice(16 * c, 16 * c + 16)
        m = {
            "xp": np.ascontiguousarray(inp["x_prompt"][c]),
            "xs": np.ascontiguousarray(inp["x_sample"][b]).reshape(128, D),
            "mem": np.ascontiguousarray(inp["mem_prompt"][c]),
            "sth": np.ascontiguousarray(inp["state_hgrn"][:, b]),
            "strw": np.ascontiguousarray(inp["state_rwkv"][:, b]),
            "sts": np.ascontiguousarray(inp["state_shift"][:, b]),
            "ck": np.ascontiguousarray(inp["cache_mem_k"][:, b]).reshape(4, 16, MEM, 512),
            "cv": np.ascontiguousarray(inp["cache_mem_v"][:, b]).reshape(4, 16, MEM, 512),
            "rw_rk": np.ascontiguousarray(inp["rw_rk"]).reshape(2, D),
            "final_g": np.ascontiguousarray(inp["final_g"]).reshape(1, D),
            "cf": cf, "cb": cb,
        }
        for k in ("norm_g", "w_in", "w_out", "mem_norm_g", "w_mem_kv", "hg_lb", "hg_onorm_g", "rw_mu", "rw_w0", "rw_w1",
                  "rw_w2", "rw_a0", "rw_a1", "rw_a2", "rw_v0", "rw_v1", "rw_v2", "rw_kk", "rw_ka", "rw_lnx_g", "rw_lnx_b"):
            m[k] = np.ascontiguousarray(inp[k])
        maps.append(m)
    return maps


_NC_CACHE = {}


def kernel(**inputs):
    inp = {k: np.asarray(v, dtype=np.float32) for k, v in inputs.items()}
    if "nc" not in _NC_CACHE:
        _NC_CACHE["nc"] = build()[0]
    nc = _NC_CACHE["nc"]
    maps = _shard_inputs(inp)
    res = run_bass_kernel_spmd(nc, maps, core_ids=list(range(8)))
    R = res.results
    y_p = np.stack([R[c]["yp"] for c in range(8)], 0)
    y_s = np.concatenate([R[c]["ys"].reshape(16, 8, D) for c in range(8)], 0)
    sh_p = np.stack([R[c]["shp"] for c in range(8)], 1)
    sr_p = np.stack([R[c]["srp"] for c in range(8)], 1)
    ss_p = np.stack([R[c]["ssp"] for c in range(8)], 1)
    mk_p = np.stack([R[c]["mkp"].reshape(4, MEM, 4, 128) for c in range(8)], 1)
    mv_p = np.stack([R[c]["mvp"].reshape(4, MEM, 4, 128) for c in range(8)], 1)
    sh_s = np.concatenate([R[c]["shs"] for c in range(8)], 1)
    sr_s = np.concatenate([R[c]["srs"] for c in range(8)], 1)
    ss_s = np.concatenate([R[c]["sss"] for c in range(8)], 1)
    return (y_p, y_s, sh_p, sr_p, ss_p, mk_p, mv_p, sh_s, sr_s, ss_s)
```
